# Optimizing a Trainium2 kernel written in Bass

```python
import math
import jax
import jax.numpy as jnp
from jax import lax
import numpy as np

D_MODEL = 1024
BATCH = 32
SEQ = 256
DEPTH = 2
DEC_BATCH = 2
DEC_SEQ = 1024
PAST_LEN = 512

GRID_W = 64
N_BRANCH = 4
BRANCH_W = D_MODEL // N_BRANCH
N_HEADS = 4
GLA_DK = BRANCH_W // (2 * N_HEADS)
GLA_DV = BRANCH_W // N_HEADS
GLA_KW = N_HEADS * GLA_DK
GLA_LOWRANK = 16
GLA_TAU = 16.0
DN_DK = BRANCH_W // N_HEADS
DN_DV = BRANCH_W // N_HEADS
DN_CONV = 3
HG_DK = BRANCH_W // N_HEADS
HG_DV = BRANCH_W // N_HEADS
HG_W = N_HEADS * HG_DK
DF_DH = BRANCH_W // (2 * N_HEADS)
ROPE_BASE = 10000.0
D_FF = ((8 * D_MODEL // 3 + 255) // 256) * 256
N_MOD = 9
CHUNK = 64
Q_BLOCK = 128
EPS = 1e-6
IN_SIZES = (GLA_KW, GLA_KW, BRANCH_W, BRANCH_W, 2 * GLA_LOWRANK,
            3 * BRANCH_W, 2 * N_HEADS, 2 * N_HEADS, BRANCH_W,
            HG_W, 2 * HG_W, BRANCH_W, BRANCH_W,
            BRANCH_W, BRANCH_W, BRANCH_W)
N_IN = sum(IN_SIZES)

kernel_name = 'hybrid_diffusion_prefix_step'


def rms_norm(x, w):
    xf = x.astype(jnp.float32)
    y = xf * lax.rsqrt(jnp.mean(xf * xf, axis=-1, keepdims=True) + EPS)
    return (y * w.astype(jnp.float32)).astype(x.dtype)


def l2norm(x):
    xf = x.astype(jnp.float32)
    return xf * lax.rsqrt(jnp.sum(xf * xf, axis=-1, keepdims=True) + EPS)


def split_cols(a, sizes):
    out, start = [], 0
    for s in sizes:
        out.append(a[..., start:start + s])
        start += s
    return out


def split_heads(a):
    b, t, _ = a.shape
    return a.reshape(b, t, N_HEADS, -1).transpose(0, 2, 1, 3)


def merge_heads(a):
    b, h, t, d = a.shape
    return a.transpose(0, 2, 1, 3).reshape(b, t, h * d)


def flip_t(a):
    return jnp.flip(a, axis=2)


def gated_head_norm(o, w, gate):
    return merge_heads(rms_norm(o, w)) * jax.nn.silu(gate)


def to_chunks(a):
    b, h, t = a.shape[:3]
    return jnp.moveaxis(a.reshape(b, h, t // CHUNK, CHUNK, *a.shape[3:]), 2, 0)


def from_chunks(o):
    n, b, h, c, d = o.shape
    return jnp.moveaxis(o, 0, 2).reshape(b, h, n * c, d)


def gla_chunk_scan(q, k, v, g, s0):
    f32 = jnp.float32
    mask = jnp.tril(jnp.ones((CHUNK, CHUNK), bool))[:, :, None]

    def step(s, inp):
        qc, kc, vc, gc = [a.astype(f32) for a in inp]
        b = jnp.cumsum(gc, axis=-2)
        rel = b[..., :, None, :] - b[..., None, :, :]
        dec = jnp.exp(jnp.where(mask, rel, -jnp.inf))
        attn = jnp.einsum('bhid,bhjd,bhijd->bhij', qc, kc, dec)
        o = jnp.einsum('bhid,bhde->bhie', qc * jnp.exp(b), s) + jnp.einsum('bhij,bhje->bhie', attn, vc)
        b_last = b[..., -1:, :]
        s_new = s * jnp.exp(b[..., -1, :])[..., None] + jnp.einsum('bhjd,bhje->bhde', kc * jnp.exp(b_last - b), vc)
        return s_new, o

    s_fin, o = lax.scan(step, s0.astype(f32), (to_chunks(q), to_chunks(k), to_chunks(v), to_chunks(g)))
    return from_chunks(o).astype(v.dtype), s_fin


def delta_chunk_scan(q, k, v, beta, g, s0):
    f32 = jnp.float32
    dv = v.shape[-1]
    strict = jnp.tril(jnp.ones((CHUNK, CHUNK), bool), -1)
    incl = jnp.tril(jnp.ones((CHUNK, CHUNK), bool))
    eye = jnp.eye(CHUNK, dtype=f32)

    def step(s, inp):
        qc, kc, vc, bc, gc = [a.astype(f32) for a in inp]
        d = jnp.cumsum(gc, axis=-1)
        rel = d[..., :, None] - d[..., None, :]
        dec_s = jnp.exp(jnp.where(strict, rel, -jnp.inf))
        dec_i = jnp.exp(jnp.where(incl, rel, -jnp.inf))
        a_mat = eye + bc[..., :, None] * jnp.einsum('bhid,bhjd->bhij', kc, kc) * dec_s
        rhs = jnp.concatenate([vc * bc[..., None], kc * (bc * jnp.exp(d))[..., None]], axis=-1)
        sol = lax.linalg.triangular_solve(a_mat, rhs, left_side=True, lower=True)
        u, w = sol[..., :dv], sol[..., dv:]
        v_new = u - jnp.einsum('bhid,bhde->bhie', w, s)
        qk = jnp.einsum('bhid,bhjd->bhij', qc, kc) * dec_i
        o = jnp.einsum('bhid,bhde->bhie', qc * jnp.exp(d)[..., None], s) + jnp.einsum('bhij,bhje->bhie', qk, v_new)
        d_last = d[..., -1:]
        s_new = s * jnp.exp(d_last)[..., None] + jnp.einsum('bhjd,bhje->bhde', kc * jnp.exp(d_last - d)[..., None], v_new)
        return s_new, o

    s_fin, o = lax.scan(step, s0.astype(f32), (to_chunks(q), to_chunks(k), to_chunks(v), to_chunks(beta), to_chunks(g)))
    return from_chunks(o).astype(v.dtype), s_fin


def centred_dwconv(x, w):
    pad = w.shape[0] // 2
    return lax.conv_general_dilated(x, w[:, None, :].astype(x.dtype), window_strides=(1,),
                                    padding=((pad, pad),), dimension_numbers=('NWC', 'WIO', 'NWC'),
                                    feature_group_count=x.shape[-1])


def axial_rope_tables(t_len, dh):
    f32 = jnp.float32
    rows = t_len // GRID_W
    row = jnp.repeat(jnp.arange(rows), GRID_W).astype(f32)
    col = jnp.tile(jnp.arange(GRID_W), rows).astype(f32)
    half = dh // 2
    inv = ROPE_BASE ** (-jnp.arange(0, half, 2, dtype=f32) / half)
    def angles(pos):
        a = pos[:, None] * inv[None, :]
        return jnp.concatenate([a, a], axis=-1)
    ang = jnp.concatenate([angles(row), angles(col)], axis=-1)
    return jnp.cos(ang), jnp.sin(ang)


def apply_axial_rope(x, cos, sin):
    half = x.shape[-1] // 2
    qt = half // 2
    rot = lambda a: jnp.concatenate([-a[..., qt:], a[..., :qt]], axis=-1)
    xrot = jnp.concatenate([rot(x[..., :half]), rot(x[..., half:])], axis=-1)
    return (x.astype(jnp.float32) * cos + xrot.astype(jnp.float32) * sin).astype(x.dtype)


def diff_attention(q1, q2, k1, k2, v, lam):
    b, h, tq, dh = q1.shape
    nb = tq // Q_BLOCK
    scale = dh ** -0.5
    blk = lambda a: jnp.moveaxis(a.reshape(b, h, nb, Q_BLOCK, dh), 2, 0)

    def one(qs):
        qb1, qb2 = qs
        p1 = jax.nn.softmax(jnp.einsum('bhqd,bhkd->bhqk', qb1, k1).astype(jnp.float32) * scale, axis=-1)
        p2 = jax.nn.softmax(jnp.einsum('bhqd,bhkd->bhqk', qb2, k2).astype(jnp.float32) * scale, axis=-1)
        a = (p1 - lam * p2).astype(v.dtype)
        return jnp.einsum('bhqk,bhke->bhqe', a, v)

    o = lax.map(one, (blk(q1), blk(q2)))
    return jnp.moveaxis(o, 0, 2).reshape(b, h, tq, 2 * dh)


def token_mixers(xn, p, ctx, rope):
    f32 = jnp.float32
    bsz, t_len, _ = xn.shape
    (ga_q, ga_k, ga_v, ga_r, ga_lr, dn_x, dn_b, dn_a, dn_g,
     hg_q, hg_f, hg_i, hg_g, df_q, df_k, df_v) = split_cols(xn @ p['w_in'], IN_SIZES)
    if ctx is None:
        zeros = lambda dk, dv: jnp.zeros((bsz, 2, N_HEADS, dk, dv), f32)
        s_gla, s_dn, s_hg = zeros(GLA_DK, GLA_DV), zeros(DN_DK, DN_DV), zeros(HG_DK, HG_DV)
    else:
        s_gla, s_dn, s_hg = ctx['gla'], ctx['dn'], ctx['hg']

    q = split_heads(ga_q) * GLA_DK ** -0.5
    k = split_heads(ga_k)
    v = split_heads(ga_v)
    lr = ga_lr.reshape(bsz, t_len, 2, GLA_LOWRANK)
    log_a = jax.nn.log_sigmoid(jnp.einsum('btnr,nrk->nbtk', lr, p['gla_w2']).astype(f32)
                               + p['gla_b'][:, None, None, :].astype(f32)) / GLA_TAU
    o_f, sf = gla_chunk_scan(q, k, v, split_heads(log_a[0]), s_gla[:, 0])
    o_b, sb = gla_chunk_scan(flip_t(q), flip_t(k), flip_t(v), flip_t(split_heads(log_a[1])), s_gla[:, 1])
    out_a = gated_head_norm(o_f + flip_t(o_b), p['gla_norm'], ga_r)
    st_a = jnp.stack([sf, sb], axis=1)

    cq, ck, cv = split_cols(jax.nn.silu(centred_dwconv(dn_x, p['dn_conv'])), (BRANCH_W, BRANCH_W, BRANCH_W))
    q = l2norm(split_heads(cq)) * DN_DK ** -0.5
    k = l2norm(split_heads(ck))
    v = split_heads(cv)
    beta = jax.nn.sigmoid(dn_b.astype(f32)).reshape(bsz, t_len, 2, N_HEADS).transpose(2, 0, 3, 1)
    a_in = dn_a.astype(f32).reshape(bsz, t_len, 2, N_HEADS) + p['dn_dt_bias'].astype(f32)
    g = (-jnp.exp(p['dn_a_log'].astype(f32)) * jax.nn.softplus(a_in)).transpose(2, 0, 3, 1)
    o_f, sf = delta_chunk_scan(q, k, v, beta[0], g[0], s_dn[:, 0])
    o_b, sb = delta_chunk_scan(flip_t(q), flip_t(k), flip_t(v), flip_t(beta[1]), flip_t(g[1]), s_dn[:, 1])
    out_b = gated_head_norm(o_f + flip_t(o_b), p['dn_norm'], dn_g)
    st_b = jnp.stack([sf, sb], axis=1)

    q = jax.nn.silu(split_heads(hg_q)) * HG_DK ** -0.5
    v = split_heads(hg_i)
    f = p['hg_lb'] + (1.0 - p['hg_lb']) * jax.nn.sigmoid(hg_f.astype(f32).reshape(bsz, t_len, 2, HG_W))
    log_f = jnp.log(f)
    o_f, sf = gla_chunk_scan(q, split_heads(1.0 - f[:, :, 0]), v, split_heads(log_f[:, :, 0]), s_hg[:, 0])
    o_b, sb = gla_chunk_scan(flip_t(q), flip_t(split_heads(1.0 - f[:, :, 1])), flip_t(v),
                             flip_t(split_heads(log_f[:, :, 1])), s_hg[:, 1])
    out_c = gated_head_norm(o_f + flip_t(o_b), p['hg_norm'], hg_g)
    st_c = jnp.stack([sf, sb], axis=1)

    dq = split_heads(df_q)
    df_kh = split_heads(df_k)
    df_vh = split_heads(df_v)
    q1, q2 = dq[..., :DF_DH], dq[..., DF_DH:]
    k1, k2 = df_kh[..., :DF_DH], df_kh[..., DF_DH:]
    v = df_vh
    if ctx is not None:
        cos, sin = rope
        q1, q2, k1, k2 = [apply_axial_rope(a, cos, sin) for a in (q1, q2, k1, k2)]
        ck_cache, cv_cache = ctx['diff_k'], ctx['diff_v']
        k1 = jnp.concatenate([ck_cache[..., :DF_DH], k1], axis=2)
        k2 = jnp.concatenate([ck_cache[..., DF_DH:], k2], axis=2)
        v = jnp.concatenate([cv_cache, v], axis=2)
    lv = p['diff_lambda'].astype(f32)
    lam = jnp.exp(jnp.sum(lv[0] * lv[1])) - jnp.exp(jnp.sum(lv[2] * lv[3])) + p['lam_init']
    o = diff_attention(q1, q2, k1, k2, v, lam)
    out_d = merge_heads(rms_norm(o, p['diff_norm'])) * (1.0 - p['lam_init'])

    br = jnp.stack([out_a, out_b, out_c, out_d], axis=0)
    up = jnp.einsum('nbtc,ncd->nbtd', br, p['w_branch'])
    gate = jax.nn.sigmoid(xn @ p['w_mgate']).reshape(bsz, t_len, N_BRANCH, D_MODEL)
    out = jnp.einsum('btnd,nbtd->btd', gate, up) @ p['w_out']
    if ctx is None:
        return out, (df_kh, df_vh, st_a, st_b, st_c)
    return out, None


def swiglu(x, w_up, w_down):
    gu = x @ w_up
    return (jax.nn.silu(gu[..., :D_FF]) * gu[..., D_FF:]) @ w_down


def trunk_layer(x, mod, p, ctx, rope):
    sh1, sc1, ga1, shm, scm, gam, sh2, sc2, ga2 = jnp.split(mod, N_MOD, axis=-1)
    h = rms_norm(x, p['norm'][0]) * (1.0 + sc1) + sh1
    x = x + 0.5 * ga1 * swiglu(h, p['ffn1_in'], p['ffn1_down'])
    h = rms_norm(x, p['norm'][1]) * (1.0 + scm) + shm
    mix, new_ctx = token_mixers(h, p, ctx, rope)
    x = x + gam * mix
    h = rms_norm(x, p['norm'][2]) * (1.0 + sc2) + sh2
    x = x + 0.5 * ga2 * swiglu(h, p['ffn2_in'], p['ffn2_down'])
    return x, new_ctx


def setup_inputs(seed: int = 0) -> dict:
    key = jax.random.key(seed)
    ks = jax.random.split(key, 40)
    f32 = jnp.float32
    nrm = lambda k, shape, s: jax.random.normal(k, shape, f32) * s
    H = N_HEADS
    dt = jnp.exp(jax.random.uniform(ks[20], (DEPTH, 2, H), f32, math.log(1e-3), math.log(1e-1)))
    return {
        'x_prompt': nrm(ks[0], (BATCH, SEQ, D_MODEL), 1.0),
        'x_sample': nrm(ks[1], (DEC_BATCH, DEC_SEQ, D_MODEL), 1.0),
        'cache_diff_k': nrm(ks[2], (DEC_BATCH, DEPTH, H, PAST_LEN, 2 * DF_DH), 1.0),
        'cache_diff_v': nrm(ks[3], (DEC_BATCH, DEPTH, H, PAST_LEN, 2 * DF_DH), 1.0),
        'state_gla': nrm(ks[4], (DEC_BATCH, DEPTH, 2, H, GLA_DK, GLA_DV), 0.5),
        'state_dn': nrm(ks[5], (DEC_BATCH, DEPTH, 2, H, DN_DK, DN_DV), 0.5),
        'state_hgrn': nrm(ks[6], (DEC_BATCH, DEPTH, 2, H, HG_DK, HG_DV), 0.5),
        'c': nrm(ks[7], (DEC_BATCH, D_MODEL), 1.0),
        'c_ctx': nrm(ks[8], (D_MODEL,), 1.0),
        'norm_w': 1.0 + nrm(ks[9], (DEPTH, 3, D_MODEL), 0.02),
        'w_mod': nrm(ks[10], (DEPTH, D_MODEL, N_MOD * D_MODEL), 0.5 * D_MODEL ** -0.5),
        'b_mod': nrm(ks[11], (DEPTH, N_MOD * D_MODEL), 0.02),
        'ffn1_in': nrm(ks[12], (DEPTH, D_MODEL, 2 * D_FF), D_MODEL ** -0.5),
        'ffn1_down': nrm(ks[13], (DEPTH, D_FF, D_MODEL), D_FF ** -0.5),
        'ffn2_in': nrm(ks[14], (DEPTH, D_MODEL, 2 * D_FF), D_MODEL ** -0.5),
        'ffn2_down': nrm(ks[15], (DEPTH, D_FF, D_MODEL), D_FF ** -0.5),
        'w_in': nrm(ks[16], (DEPTH, D_MODEL, N_IN), D_MODEL ** -0.5),
        'gla_w2': nrm(ks[17], (DEPTH, 2, GLA_LOWRANK, GLA_KW), GLA_LOWRANK ** -0.5),
        'gla_b': nrm(ks[18], (DEPTH, 2, GLA_KW), 0.1),
        'gla_norm': 1.0 + nrm(ks[19], (DEPTH, GLA_DV), 0.02),
        'dn_conv': nrm(ks[21], (DEPTH, DN_CONV, 3 * BRANCH_W), DN_CONV ** -0.5),
        'dn_a_log': jnp.log(jax.random.uniform(ks[22], (DEPTH, 2, H), f32, 1.0, 16.0)),
        'dn_dt_bias': dt + jnp.log(-jnp.expm1(-dt)),
        'dn_norm': 1.0 + nrm(ks[23], (DEPTH, DN_DV), 0.02),
        'hg_lb_logits': nrm(ks[24], (DEPTH, 2, HG_W), 0.5),
        'hg_norm': 1.0 + nrm(ks[25], (DEPTH, HG_DV), 0.02),
        'diff_lambda': nrm(ks[26], (DEPTH, 4, DF_DH), 0.1),
        'diff_norm': 1.0 + nrm(ks[27], (DEPTH, 2 * DF_DH), 0.02),
        'w_branch': nrm(ks[28], (DEPTH, N_BRANCH, BRANCH_W, D_MODEL), BRANCH_W ** -0.5),
        'w_mgate': nrm(ks[29], (DEPTH, D_MODEL, N_BRANCH * D_MODEL), D_MODEL ** -0.5),
        'w_out': nrm(ks[30], (DEPTH, D_MODEL, D_MODEL), D_MODEL ** -0.5),
        'final_norm': 1.0 + nrm(ks[31], (D_MODEL,), 0.02),
    }


def reference(x_prompt, x_sample, cache_diff_k, cache_diff_v, state_gla, state_dn, state_hgrn, c, c_ctx,
              norm_w, w_mod, b_mod, ffn1_in, ffn1_down, ffn2_in, ffn2_down, w_in, gla_w2, gla_b, gla_norm,
              dn_conv, dn_a_log, dn_dt_bias, dn_norm, hg_lb_logits, hg_norm, diff_lambda, diff_norm,
              w_branch, w_mgate, w_out, final_norm):
    f32 = jnp.float32
    lb_p = jax.nn.softmax(hg_lb_logits.astype(f32), axis=0)
    hg_lb = jnp.cumsum(lb_p, axis=0) - lb_p[0:1]
    rope = axial_rope_tables(x_sample.shape[1], DF_DH)
    xp, xs = x_prompt, x_sample
    new_k, new_v, new_gla, new_dn, new_hg = [], [], [], [], []
    for l in range(DEPTH):
        p = {'norm': norm_w[l], 'ffn1_in': ffn1_in[l], 'ffn1_down': ffn1_down[l],
             'ffn2_in': ffn2_in[l], 'ffn2_down': ffn2_down[l], 'w_in': w_in[l],
             'gla_w2': gla_w2[l], 'gla_b': gla_b[l], 'gla_norm': gla_norm[l],
             'dn_conv': dn_conv[l], 'dn_a_log': dn_a_log[l], 'dn_dt_bias': dn_dt_bias[l], 'dn_norm': dn_norm[l],
             'hg_lb': hg_lb[l], 'hg_norm': hg_norm[l], 'diff_lambda': diff_lambda[l], 'diff_norm': diff_norm[l],
             'w_branch': w_branch[l], 'w_mgate': w_mgate[l], 'w_out': w_out[l],
             'lam_init': 0.8 - 0.6 * math.exp(-0.3 * l)}
        mod_ctx = (jax.nn.silu(c_ctx) @ w_mod[l] + b_mod[l])[None, None, :]
        mod_lat = (jax.nn.silu(c) @ w_mod[l] + b_mod[l])[:, None, :]
        xp, (k_l, v_l, sg, sd, sh) = trunk_layer(xp, mod_ctx, p, None, None)
        new_k.append(k_l)
        new_v.append(v_l)
        new_gla.append(sg)
        new_dn.append(sd)
        new_hg.append(sh)
        ctx = {'diff_k': cache_diff_k[:, l], 'diff_v': cache_diff_v[:, l],
               'gla': state_gla[:, l], 'dn': state_dn[:, l], 'hg': state_hgrn[:, l]}
        xs, _ = trunk_layer(xs, mod_lat, p, ctx, rope)
    y_prompt = rms_norm(xp, final_norm)
    y_sample = rms_norm(xs, final_norm)
    return (y_prompt, y_sample, jnp.stack(new_k, axis=1), jnp.stack(new_v, axis=1),
            jnp.stack(new_gla, axis=1), jnp.stack(new_dn, axis=1), jnp.stack(new_hg, axis=1))
```

```python
import math
from contextlib import ExitStack
import numpy as np
import concourse.bass as bass
import concourse.mybir as mybir
from concourse.bass_utils import run_bass_kernel_spmd

F32 = mybir.dt.float32
BF16 = mybir.dt.bfloat16
AF = mybir.ActivationFunctionType
ALU = mybir.AluOpType
ENGS = ("pe", "act", "dve", "pool", "sp")

D = 1024
NT = 1024
DFF = 2816
NIN = 3888
EPS = 1e-6
DEPTH = 2
CH = 64
NCH = NT // CH
FFN_STOP = 0
DF_STOP = 0
DN_STOP = 0


class Buf:
    __slots__ = ("name", "w", "r")

    def __init__(self, name=""):
        self.name = name
        self.w = None
        self.r = {}


class Prog:
    def __init__(self):
        self.ops = {e: [] for e in ENGS}
        self.cnt = {e: 0 for e in ENGS}
        self.seen = {e: {} for e in ENGS}
        self.dma_cnt = {}

    def _need(self, eng, tick, waits):
        if tick is None:
            return
        k, v = tick
        if self.seen[eng].get(k, 0) >= v:
            return
        waits[k] = max(waits.get(k, 0), v)

    def _deps(self, eng, reads, writes, is_dma):
        waits = {}
        for b in reads:
            self._need(eng, b.w, waits)
        for b in writes:
            if b.w is not None and (is_dma or b.w[0] != eng):
                self._need(eng, b.w, waits)
            for k, v in b.r.items():
                if is_dma or k != eng:
                    self._need(eng, (k, v), waits)
        for k, v in waits.items():
            self.seen[eng][k] = v
        return tuple(waits.items())

    def op(self, eng, fn, reads=(), writes=(), inc=True):
        waits = self._deps(eng, reads, writes, False)
        if inc:
            self.cnt[eng] += 1
            tv = self.cnt[eng]
        else:
            tv = self.cnt[eng] + 1
        self.ops[eng].append((waits, fn, (eng, 1) if inc else None))
        for b in reads:
            b.r[eng] = tv
        for b in writes:
            b.w = (eng, tv)
            b.r = {}

    def dma(self, eng, fn, semkey, reads=(), writes=(), n=1):
        waits = self._deps(eng, reads, writes, True)
        self.dma_cnt[semkey] = self.dma_cnt.get(semkey, 0) + 16 * n
        tick = (semkey, self.dma_cnt[semkey])
        self.ops[eng].append((waits, fn, (semkey, 16)))
        for b in reads:
            b.r[semkey] = tick[1]
        for b in writes:
            b.w = tick
            b.r = {}
        return tick

    def barrier(self, skip=lambda k: k.startswith("w") and not k.startswith("d_")):
        for e in ENGS:
            waits = {}
            for k in ENGS:
                if k != e and self.cnt[k] > 0:
                    self._need(e, (k, self.cnt[k]), waits)
            for k, v in self.dma_cnt.items():
                if not skip(k):
                    self._need(e, (k, v), waits)
            for k, v in waits.items():
                self.seen[e][k] = v
            if waits:
                self.ops[e].append((tuple(waits.items()), None, None))

    def final_wait(self, eng):
        waits = {}
        for k in ENGS:
            if k != eng and self.cnt[k] > 0:
                self._need(eng, (k, self.cnt[k]), waits)
        for k, v in self.dma_cnt.items():
            self._need(eng, (k, v), waits)
        self.ops[eng].append((tuple(waits.items()), None, None))


def replay(prog, block, sems):
    def run(name):
        def body(eng):
            for waits, fn, inc in prog.ops[name]:
                for k, v in waits:
                    eng.wait_ge(sems[k], v)
                if fn is None:
                    continue
                res = fn(eng)
                if inc is not None:
                    if isinstance(res, (list, tuple)):
                        for r in res:
                            r.then_inc(sems[inc[0]], inc[1])
                    else:
                        res.then_inc(sems[inc[0]], inc[1])
        return body
    block.tensor(run("pe"))
    block.scalar(run("act"))
    block.vector(run("dve"))
    block.gpsimd(run("pool"))
    block.sync(run("sp"))


IN_OFF = {}
_o = 0
for _n, _s in [("ga_q", 128), ("ga_k", 128), ("ga_v", 256), ("ga_r", 256), ("ga_lr", 32), ("dn_x", 768), ("dn_b", 8),
               ("dn_a", 8), ("dn_g", 256), ("hg_q", 256), ("hg_f", 512), ("hg_i", 256), ("hg_g", 256), ("df_q", 256),
               ("df_k", 256), ("df_v", 256)]:
    IN_OFF[_n] = _o
    _o += _s
assert _o == NIN

ROW = {}
_r = 0
for _n, _s in [("norm_w", DEPTH * 3 * 8), ("b_mod", DEPTH * 72), ("final", 8), ("c_ctx", 8), ("c_lat", 8), ("gla_b", DEPTH * 2),
               ("dn_conv", DEPTH * 3 * 6), ("hg_lb", DEPTH * 2 * 2), ("hnorm", DEPTH * 4)]:
    ROW[_n] = _r
    _r += _s
NROW = 384
assert _r <= NROW

CST = {}
_c = 0
for _n, _s in [("ident", 128), ("ones", 128), ("bd64", 128), ("rot", 128), ("sel_f", 128), ("sel_b", 128), ("id8", 512),
               ("mf", 256), ("mb", 256), ("rm", 1024), ("hm_gla", 4), ("hm_hg", 4),
               ("sm_gla", 256), ("sm_hg", 256), ("qm", 4), ("CSTA_END", 0),
               ("nm_c", 512), ("nm_b", 512), ("NM_END", 0),
               ("cos", 1024), ("sin", 1024)]:
    CST[_n] = (_c, _s)
    _c += _s
NCST = _c
NCSTA = CST["CSTA_END"][0]
NCSTB = CST["mf"][0]


def make_consts():
    c = np.zeros((128, NCST), np.float32)

    def put(name, arr):
        o, s = CST[name]
        a = np.zeros((128, s), np.float32)
        a[:arr.shape[0], :arr.shape[1]] = arr
        c[:, o:o + s] = a
    put("ident", np.eye(128))
    put("ones", np.ones((128, 128)))
    bd = np.zeros((128, 128)); bd[:64, :64] = 1; bd[64:, 64:] = 1
    put("bd64", bd)
    j = np.arange(64)[:, None]; i = np.arange(64)[None, :]
    put("mf", np.tile((j <= i).astype(np.float32), (1, 4)))
    put("mb", np.tile((j >= i).astype(np.float32), (1, 4)))
    rm = np.ones((128, 1024)); rm[:, ::64] = 0
    put("rm", rm)
    d = np.arange(128)[:, None]; h = np.arange(4)[None, :]
    put("hm_gla", (d // 32 == h).astype(np.float32))
    put("hm_hg", (d // 64 == h % 2).astype(np.float32))
    put("sm_gla", np.repeat((d // 32 == h).astype(np.float32), 64, axis=1))
    put("sm_hg", np.repeat((d // 64 == h % 2).astype(np.float32), 64, axis=1))
    NEG = -30000.0
    put("nm_c", np.concatenate([np.tile(np.where(j <= i, 0.0, NEG), (1, 4)), np.tile(np.where(j >= i, 0.0, NEG), (1, 4))], axis=1))
    put("nm_b", np.concatenate([np.tile(np.where(j > i, 0.0, NEG), (1, 4)), np.tile(np.where(j < i, 0.0, NEG), (1, 4))], axis=1))
    t = np.arange(1024)
    row = (t // 64).astype(np.float32); col = (t % 64).astype(np.float32)
    half = 16
    inv = (10000.0 ** (-np.arange(0, half, 2, dtype=np.float32) / half)).astype(np.float32)
    ang_r = np.concatenate([row[:, None] * inv[None, :]] * 2, axis=1)
    ang_c = np.concatenate([col[:, None] * inv[None, :]] * 2, axis=1)
    ang = np.concatenate([ang_r, ang_c], axis=1).astype(np.float32)
    cos32 = np.cos(ang).T; sin32 = np.sin(ang).T
    put("cos", np.tile(cos32, (4, 1)))
    put("sin", np.tile(sin32, (4, 1)))
    R = np.zeros((128, 128), np.float32)
    for p in range(128):
        b16 = (p // 16) * 16; dd = p % 16
        if dd < 8:
            R[b16 + dd + 8, p] = -1.0
        else:
            R[b16 + dd - 8, p] = 1.0
    put("rot", R)
    put("qm", (d // 32 == h).astype(np.float32))
    sf = np.zeros((64, 128)); sf[63, :] = 1
    sb_ = np.zeros((64, 128)); sb_[0, :] = 1
    put("sel_f", sf); put("sel_b", sb_)
    put("id8", np.tile(np.eye(64), (1, 8)))
    return c


def pack_pvec(inp, b):
    rows = np.zeros((NROW, 128), np.float32)

    def put(name, arr):
        a = np.asarray(arr, np.float32).reshape(-1, 128)
        rows[ROW[name]:ROW[name] + a.shape[0]] = a
    put("norm_w", inp["norm_w"])
    put("b_mod", inp["b_mod"])
    put("final", inp["final_norm"])
    put("c_ctx", inp["c_ctx"])
    put("c_lat", inp["c"][b])
    put("gla_b", inp["gla_b"])
    put("dn_conv", inp["dn_conv"])
    put("hg_lb", inp["hg_lb_logits"])
    hn = np.stack([np.stack([np.tile(inp[k][l], 2) for k in ("gla_norm", "dn_norm", "hg_norm", "diff_norm")]) for l in range(DEPTH)])
    put("hnorm", hn)
    return rows


class Tn:
    __slots__ = ("t", "b")

    def __init__(self, t, b):
        self.t = t
        self.b = b


class Phase:
    def __init__(self, k):
        self.k = k
        self.es = ExitStack()
        self.rots = {}

    def __enter__(self):
        self.es.__enter__()
        return self

    def sb(self, name, shape, dt=F32):
        self.k.uid += 1
        t = self.es.enter_context(self.k.nc.sbuf_tensor(f"{name}_{self.k.uid}", list(shape), dt))
        return Tn(t, Buf(name))

    def rot(self, name, shape, dt=F32, n=2):
        if name not in self.rots:
            self.rots[name] = [[self.sb(f"{name}{i}", shape, dt) for i in range(n)], 0]
        lst = self.rots[name]
        t = lst[0][lst[1] % n]
        lst[1] += 1
        return t

    def __exit__(self, *a):
        self.k.P.barrier()
        return self.es.__exit__(*a)


class K:
    def __init__(self, wplan=None, taps=(), stages=None):
        self.nc = bass.Bass("TRN2", target_bir_lowering=False)
        self.P = Prog()
        self.es = ExitStack()
        self.uid = 0
        self.wplan = wplan
        self.wrec = []
        self.wi = 0
        self.wissued = 0
        self.taps = set(taps)
        self.tap_out = {}
        self.stages = stages
        self.din = {}
        self.dout = {}
        self.bank_i = 0

    def inp(self, name, shape):
        self.din[name] = self.nc.dram_tensor(name, list(shape), F32, kind="ExternalInput").ap()
        return self.din[name]

    def outp(self, name, shape):
        self.dout[name] = self.nc.dram_tensor(name, list(shape), F32, kind="ExternalOutput").ap()
        return self.dout[name]

    def sb(self, name, shape, dt=F32):
        self.uid += 1
        t = self.es.enter_context(self.nc.sbuf_tensor(f"{name}_{self.uid}", list(shape), dt))
        return Tn(t, Buf(name))

    def phase(self):
        return Phase(self)

    def mm(self, out, lhsT, rhs, start, stop, rd, wr, inc=None):
        inc = True
        self.P.op("pe", lambda e: e.matmul(out, lhsT=lhsT, rhs=rhs, start=start, stop=stop), reads=rd, writes=wr, inc=inc)

    def tr(self, out, in_, ident, rd, wr):
        self.P.op("pe", lambda e: e.transpose(out, in_, ident), reads=rd, writes=wr)

    def act(self, out, in_, func, rd, wr, bias=0.0, scale=1.0, accum=None):
        if accum is None:
            self.P.op("act", lambda e: e.activation(out=out, in_=in_, func=func, bias=bias, scale=scale), reads=rd, writes=wr)
        else:
            self.P.op("act", lambda e: e.activation(out=out, in_=in_, func=func, bias=bias, scale=scale, accum_out=accum), reads=rd, writes=wr)

    def tt(self, eng, out, a, b, op, rd, wr):
        if eng == "pool" and op not in (ALU.add, ALU.subtract, ALU.mult):
            eng = "dve"
        self.P.op(eng, lambda e: e.tensor_tensor(out=out, in0=a, in1=b, op=op), reads=rd, writes=wr)

    def ts(self, eng, out, a, s1, s2, op0, op1, rd, wr):
        self.P.op(eng, lambda e: e.tensor_scalar(out=out, in0=a, scalar1=s1, scalar2=s2, op0=op0, op1=op1), reads=rd, writes=wr)

    def stt(self, eng, out, a, s, b, op0, op1, rd, wr):
        eng = "dve"
        self.P.op(eng, lambda e: e.scalar_tensor_tensor(out=out, in0=a, scalar=s, in1=b, op0=op0, op1=op1), reads=rd, writes=wr)

    def cp(self, eng, out, in_, rd, wr):
        if eng == "act":
            self.P.op("act", lambda e: e.copy(out=out, in_=in_), reads=rd, writes=wr)
        else:
            self.P.op(eng, lambda e: e.tensor_copy(out=out, in_=in_), reads=rd, writes=wr)

    def memset(self, eng, ap, val, wr):
        self.P.op(eng, lambda e: e.memset(ap, val), writes=wr)

    def dma(self, eng, out, in_, key, rd=(), wr=()):
        key = "d_" + (wr[0].name if wr else rd[0].name)
        return self.P.dma(eng, lambda e: e.dma_start(out=out, in_=in_), key, reads=rd, writes=wr)

    def bank(self):
        b = self.banks[self.bank_i % len(self.banks)]
        self.bank_i += 1
        return b

    def tap(self, name, tn_ap, shape, rd):
        if name not in self.taps:
            return
        o = self.outp("tap_" + name, shape)
        self.P.dma("pool", lambda e: e.dma_start(out=o, in_=tn_ap), "tap_" + name, reads=rd)

    def wsrc(self, spec, slot):
        kind = spec[0]
        if kind == "mat":
            _, name, idx, nk, ranges = spec
            W = self.din[name]
            for i in idx:
                W = W[i]
            tot = sum(n for _, n in ranges)
            view = slot.t[:, 0:nk * tot].rearrange("p (k c) -> p k c", c=tot)
            pieces = []
            o = 0
            for c0, n in ranges:
                pieces.append((view[:, :, o:o + n], W[:, c0:c0 + n].rearrange("(k p) c -> p k c", p=128)))
                o += n
            return pieces, view
        if kind == "branch":
            _, l, n = spec
            W = self.din["w_branch"][l][n]
            view = slot.t[0:64, 0:4096].rearrange("p (h f) -> p h f", f=1024)
            return [(view, W.rearrange("(h e) f -> e h f", e=64))], view
        raise ValueError(kind)

    def _wissue(self, j, spec):
        slot = self.wslots[j % len(self.wslots)]
        pieces, view = self.wsrc(spec, slot)
        key = f"w{j % len(self.wslots)}"
        self.P.dma("pool", lambda e: [e.dma_start(out=o, in_=i) for o, i in pieces], key, writes=[slot.b], n=len(pieces))

    def wget(self, spec):
        j = self.wi
        self.wi += 1
        self.wrec.append(spec)
        slot = self.wslots[j % len(self.wslots)]
        if self.wplan is None:
            self._wissue(j, spec)
        else:
            assert self.wplan[j] == spec, (j, spec, self.wplan[j])
            while self.wissued < min(j + len(self.wslots) - 1, len(self.wplan)):
                self._wissue(self.wissued, self.wplan[self.wissued])
                self.wissued += 1
        _, view = self.wsrc(spec, slot)
        return view, slot.b


def CS(k, name, lo=0, n=None):
    o, s = CST[name]
    n = s - lo if n is None else n
    return k.cst.t[:, o + lo:o + lo + n]


def CSB(k, name, lo=0, n=None):
    o, s = CST[name]
    n = s - lo if n is None else n
    return k.cstb.t[:, o + lo:o + lo + n]


def build(wplan=None, taps=(), stages=("all",)):
    k = K(wplan, taps, stages)
    nc = k.nc
    es = k.es
    st = set(stages)
    ALL = "all" in st
    xin = [k.inp("xp", [NT, D]), k.inp("xs", [NT, D])]
    k.inp("ck", [DEPTH, 4, 512, 64]); k.inp("cv", [DEPTH, 4, 512, 64])
    k.inp("sg", [DEPTH, 2, 4, 32, 64]); k.inp("sd", [DEPTH, 2, 4, 64, 64]); k.inp("sh", [DEPTH, 2, 4, 64, 64])
    k.inp("pvec", [NROW, 128]); k.inp("cst", [128, NCST])
    k.inp("dn_al", [DEPTH, 8]); k.inp("dn_dt", [DEPTH, 8]); k.inp("dlam", [DEPTH, 128]); k.inp("gla_w2", [DEPTH, 2, 16, 128])
    k.inp("w_mod", [DEPTH, D, 9 * D])
    for n_ in ("ffn1_in", "ffn2_in"):
        k.inp(n_, [DEPTH, D, 2 * DFF])
    for n_ in ("ffn1_down", "ffn2_down"):
        k.inp(n_, [DEPTH, DFF, D])
    k.inp("w_in", [DEPTH, D, NIN]); k.inp("w_branch", [DEPTH, 4, 256, D]); k.inp("w_mgate", [DEPTH, D, 4 * D]); k.inp("w_out", [DEPTH, D, D])
    yout = [k.outp("yp", [NT, D]), k.outp("ys", [NT, D])]
    k.outp("nk", [4, DEPTH, 4, 256, 64]); k.outp("nv", [4, DEPTH, 4, 256, 64])
    k.outp("ng", [4, DEPTH, 2, 4, 32, 64]); k.outp("nd", [4, DEPTH, 2, 4, 64, 64]); k.outp("nh", [4, DEPTH, 2, 4, 64, 64])

    k.cst = k.sb("cst", [128, NCSTA])
    k.cstb = k.sb("cstb", [128, NCSTB], BF16)
    k.pt = k.sb("pt", [128, NROW])
    k.mod = k.sb("mod", [128, DEPTH, 72, 2])
    k.der = k.sb("der", [128, DEPTH, 2, 9, 8])
    k.XT = k.sb("XT", [128, 8, NT])
    k.xtb = [Buf(f"xt{c}") for c in range(8)]
    k.wslots = [k.sb(f"wslot{i}", [128, 4096], BF16) for i in range(3)]
    k.xscr = nc.dram_tensor("xscr", [128, 8 * NT], F32, kind="Internal").ap()
    k.big = [es.enter_context(nc.psum_tensor(f"big{i}", [128, 1536], F32)) for i in range(2)]
    k.banks = [Tn(k.big[i // 3][:, (i % 3) * 512:(i % 3 + 1) * 512], Buf(f"bank{i}")) for i in range(6)]
    k.bbanks = [Tn(es.enter_context(nc.psum_tensor(f"bbank{i}", [128, 1024], BF16)), Buf(f"bbank{i}")) for i in range(2)]
    k.bb_i = 0
    P = k.P
    IDF = lambda: CS(k, "ident")
    IDB = lambda: CSB(k, "ident")
    ONESB = lambda: CSB(k, "ones")

    k.dma("sp", k.cst.t[:], k.din["cst"][:, 0:NCSTA], "ld", wr=[k.cst.b])
    k.dma("pool", k.cstb.t[:], k.din["cst"][:, 0:NCSTB], "ldb", wr=[k.cstb.b])
    with k.phase() as ph:
        for r in range(3):
            stg = ph.rot("pstg", [128, 128], F32, 3)
            k.dma("sp", stg.t[:], k.din["pvec"][r * 128:(r + 1) * 128, :], "ld", wr=[stg.b])
            bk = k.bank()
            k.tr(bk.t[:, 0:128], stg.t[:], IDF(), [stg.b, k.cst.b], [bk.b])
            k.cp("dve", k.pt.t[:, r * 128:(r + 1) * 128], bk.t[:, 0:128], [bk.b], [k.pt.b])
        cs = ph.sb("cs", [128, 8, 2], BF16)
        k.act(cs.t[:, :, 0], k.pt.t[:, ROW["c_ctx"]:ROW["c_ctx"] + 8], AF.Silu, [k.pt.b], [cs.b])
        k.act(cs.t[:, :, 1], k.pt.t[:, ROW["c_lat"]:ROW["c_lat"] + 8], AF.Silu, [k.pt.b], [cs.b])
        for l in range(DEPTH):
            bk = k.bank()
            for tl in range(18):
                wv, wb = k.wget(("mat", "w_mod", (l,), 8, ((tl * 512, 512),)))
                for s4 in range(4):
                    j = tl * 4 + s4
                    for kc in range(8):
                        k.mm(bk.t[:, j * 2:j * 2 + 2], wv[:, kc, s4 * 128:(s4 + 1) * 128], cs.t[:, kc, :], kc == 0, kc == 7, [wb, cs.b], [bk.b])
            o0 = ROW["b_mod"] + l * 72
            k.tt("dve", k.mod.t[:, l], bk.t[:, 0:144].rearrange("p (j w) -> p j w", w=2),
                 k.pt.t[:, o0:o0 + 72].unsqueeze(2).to_broadcast([128, 72, 2]), ALU.add, [bk.b, k.pt.b], [k.mod.b])
            for w in range(2):
                for i in range(3):
                    nw0 = ROW["norm_w"] + (l * 3 + i) * 8
                    k.stt("dve", k.der.t[:, l, w, i, :], k.mod.t[:, l, (3 * i + 1) * 8:(3 * i + 2) * 8, w], 1.0, k.pt.t[:, nw0:nw0 + 8],
                          ALU.add, ALU.mult, [k.mod.b, k.pt.b], [k.der.b])
                    k.cp("dve", k.der.t[:, l, w, 3 + i, :], k.mod.t[:, l, (3 * i) * 8:(3 * i + 1) * 8, w], [k.mod.b], [k.der.b])
                    k.ts("dve", k.der.t[:, l, w, 6 + i, :], k.mod.t[:, l, (3 * i + 2) * 8:(3 * i + 3) * 8, w], 1.0 if i == 1 else 0.5, 0.0,
                         ALU.mult, ALU.add, [k.mod.b], [k.der.b])
    k.tap("mod", k.mod.t[:].rearrange("p l j w -> p (l j w)"), [128, DEPTH * 144], [k.mod.b])

    XT = k.XT

    def xload(g):
        with k.phase() as ph:
            for tl in range(8):
                stg = ph.rot("xstg", [128, D], F32, 2)
                k.dma("sp", stg.t[:], xin[g][tl * 128:(tl + 1) * 128, :], "ldx", wr=[stg.b])
                for hb in range(2):
                    bk = k.bank()
                    for c4 in range(4):
                        c = hb * 4 + c4
                        k.tr(bk.t[:, c4 * 128:(c4 + 1) * 128], stg.t[:, c * 128:(c + 1) * 128], IDF(), [stg.b, k.cst.b], [bk.b])
                    eng = "act" if hb == 0 else "dve"
                    k.cp(eng, XT.t[:, hb * 4:(hb + 1) * 4, tl * 128:(tl + 1) * 128], bk.t[:].rearrange("p (c t) -> p c t", t=128), [bk.b],
                         [k.xtb[c] for c in range(hb * 4, hb * 4 + 4)])

    def rstd_of(ph, src_fn, nchunk, nparts, lhsT, scale, name, rd):
        rstd = ph.sb(name, [128, NT])
        for hb in range(2):
            bk = k.bank()
            for c in range(nchunk):
                sq = ph.rot("sq", [128, 512], BF16, 3)
                k.act(sq.t[0:nparts, :], src_fn(c, hb), AF.Square, rd, [sq.b])
                k.mm(bk.t[0:nparts, :], lhsT, sq.t[0:nparts, :], c == 0, c == nchunk - 1, [sq.b, k.cstb.b], [bk.b])
            sl = rstd.t[0:nparts, hb * 512:(hb + 1) * 512]
            k.ts("dve", sl, bk.t[0:nparts, :], scale, EPS, ALU.mult, ALU.add, [bk.b], [rstd.b])
            k.act(sl, sl, AF.Sqrt, [rstd.b], [rstd.b])
            k.P.op("dve", (lambda sl_: (lambda e: e.reciprocal(out=sl_, in_=sl_)))(sl), reads=[rstd.b], writes=[rstd.b])
        return rstd

    def norm_mod(ph, l, w, i, HN):
        rstd = rstd_of(ph, lambda c, hb: XT.t[:, c, hb * 512:(hb + 1) * 512], 8, 128, ONESB(), 1.0 / D, "rstd", k.xtb)
        for c in range(8):
            tmp = ph.rot("ntmp", [128, NT], F32, 2)
            k.tt("dve", tmp.t[:], XT.t[:, c, :], rstd.t[:], ALU.mult, [k.xtb[c], rstd.b], [tmp.b])
            k.act(HN.t[:, c, :], tmp.t[:], AF.Identity, [tmp.b, k.der.b], [HN.b],
                  bias=k.der.t[:, l, w, 3 + i, c:c + 1], scale=k.der.t[:, l, w, i, c:c + 1])

    def ffn(l, w, i, win, wdown):
        with k.phase() as ph:
            HN = ph.sb("HN", [128, 8, NT], BF16)
            FA = ph.sb("FA", [128, 22, NT], BF16)
            norm_mod(ph, l, w, i, HN)
            if FFN_STOP == 1:
                return
            for j in range(11 if FFN_STOP != 2 else 1):
                wv, wb = k.wget(("mat", win, (l,), 8, ((j * 256, 256), (DFF + j * 256, 256))))
                for sub in range(2):
                    for hb in range(2):
                        bg = k.bank(); bu = k.bank()
                        for kc in range(8):
                            k.mm(bg.t[:], wv[:, kc, sub * 128:(sub + 1) * 128], HN.t[:, kc, hb * 512:(hb + 1) * 512], kc == 0, kc == 7, [wb, HN.b], [bg.b])
                        for kc in range(8):
                            k.mm(bu.t[:], wv[:, kc, 256 + sub * 128:256 + (sub + 1) * 128], HN.t[:, kc, hb * 512:(hb + 1) * 512], kc == 0, kc == 7, [wb, HN.b], [bu.b])
                        sg = ph.rot("sg", [128, 512], F32, 3)
                        k.act(sg.t[:], bg.t[:], AF.Silu, [bg.b], [sg.b])
                        k.tt("dve", FA.t[:, j * 2 + sub, hb * 512:(hb + 1) * 512], sg.t[:], bu.t[:], ALU.mult, [sg.b, bu.b], [FA.b])
            if FFN_STOP in (2, 3):
                return
            for fo in range(8):
                wv, wb = k.wget(("mat", wdown, (l,), 22, ((fo * 128, 128),)))
                for hb in range(2):
                    bk = k.bank()
                    for kc in range(22):
                        k.mm(bk.t[:], wv[:, kc, :], FA.t[:, kc, hb * 512:(hb + 1) * 512], kc == 0, kc == 21, [wb, FA.b], [bk.b])
                    xs_ = XT.t[:, fo, hb * 512:(hb + 1) * 512]
                    k.stt("dve", xs_, bk.t[:], k.der.t[:, l, w, 6 + i, fo:fo + 1], xs_, ALU.mult, ALU.add, [bk.b, k.der.b, k.xtb[fo]], [k.xtb[fo]])

    def final(g):
        with k.phase() as ph:
            rstd = rstd_of(ph, lambda c, hb: XT.t[:, c, hb * 512:(hb + 1) * 512], 8, 128, ONESB(), 1.0 / D, "rstd", k.xtb)
            f0 = ROW["final"]
            for c in range(8):
                tmp = ph.rot("ntmp", [128, NT], F32, 2)
                k.stt("dve", tmp.t[:], XT.t[:, c, :], k.pt.t[:, f0 + c:f0 + c + 1], rstd.t[:], ALU.mult, ALU.mult, [k.xtb[c], rstd.b, k.pt.b], [tmp.b])
                for tl in range(8):
                    pass
                k.cp("pool", XT.t[:, c, :], tmp.t[:], [tmp.b], [k.xtb[c]])
            for tl in range(8):
                stg = ph.rot("ystg", [128, D], F32, 2)
                for hb in range(2):
                    bk = k.bank()
                    for c4 in range(4):
                        c = hb * 4 + c4
                        k.tr(bk.t[:, c4 * 128:(c4 + 1) * 128], XT.t[:, c, tl * 128:(tl + 1) * 128], IDF(), [k.xtb[c], k.cst.b], [bk.b])
                    k.cp("act" if hb == 0 else "dve", stg.t[:, hb * 512:(hb + 1) * 512], bk.t[:], [bk.b], [stg.b])
                k.dma("sp", yout[g][tl * 128:(tl + 1) * 128, :], stg.t[:], "sty", rd=[stg.b])


    def proj_fm(wv, wb, c0, n, HN, evac):
        for hb in range(2):
            bk = k.bank()
            for kc in range(8):
                k.mm(bk.t[0:n, :], wv[:, kc, c0:c0 + n], HN.t[:, kc, hb * 512:(hb + 1) * 512], kc == 0, kc == 7, [wb, HN.b], [bk.b])
            evac(hb, bk)

    def proj_tm(wv, wb, c0, n, HN, tok0, M, evac):
        bk = k.bank()
        for kc in range(8):
            k.mm(bk.t[0:M, 0:n], HN.t[:, kc, tok0:tok0 + M], wv[:, kc, c0:c0 + n], kc == 0, kc == 7, [HN.b, wb], [bk.b])
        evac(bk)

    def win_tile(l, c0, n):
        return k.wget(("mat", "w_in", (l,), 8, ((c0, n),)))

    def evac_to(dst, dstb, func=None, n=128, eng="act", scale=1.0):
        def f(hb, bk):
            o = dst[0:n, hb * 512:(hb + 1) * 512]
            if func is not None:
                k.act(o, bk.t[0:n, :], func, [bk.b], [dstb], scale=scale)
            else:
                k.cp(eng, o, bk.t[0:n, :], [bk.b], [dstb])
        return f

    def head_norm(ph, OACC, l, n, BR, gate=None, extra=1.0):
        w0 = ROW["hnorm"] + l * 4 + n
        for h in range(4):
            with k.phase() as ph2:
                rstd = k.rstd_of(ph2, lambda c, hb: OACC.t[0:64, h, hb * 512:(hb + 1) * 512], 1, 64, CSB(k, "ones")[0:64, 0:64], 1.0 / 64, "hrstd", [OACC.b])
                t1 = ph2.sb("hn_t1", [64, NT], F32)
                k.tt("dve", t1.t[:], OACC.t[0:64, h, :], rstd.t[0:64, :], ALU.mult, [OACC.b, rstd.b], [t1.b])
                if gate is not None:
                    k.stt("pool", BR.t[0:64, n, h, :], t1.t[:], k.pt.t[0:64, w0:w0 + 1], gate.t[0:64, h, :], ALU.mult, ALU.mult, [t1.b, k.pt.b, gate.b], [BR.b])
                else:
                    k.ts("pool", BR.t[0:64, n, h, :], t1.t[:], k.pt.t[0:64, w0:w0 + 1], extra, ALU.mult, ALU.mult, [t1.b, k.pt.b], [BR.b])

    def gate_proj(ph, l, c0, HN):
        gate = ph.sb("gate", [64, 4, NT], BF16)
        wv, wb = win_tile(l, c0, 256)
        for h in range(4):
            proj_fm(wv, wb, h * 64, 64, HN, evac_to(gate.t[:, h, :], gate.b, AF.Silu, 64))
        return gate

    def gla_scan(ph, g, l, n, KG, qT, qscale, kT, gT, vtm, hmname, smname, sin_name, sout_name, dk, OACC):
        nseq = 4 if g == 0 else 1
        cps = NCH // nseq
        HPG = 4 // KG
        MASK = {0: CS(k, "mf"), 1: CS(k, "mb")}
        SMv = CS(k, smname)
        for dr in range(2):
            with k.phase() as p2:
                qg = [p2.sb(f"qg{kg}", [128, NT], BF16) for kg in range(KG)]
                kx = [p2.sb(f"kx{h}", [128, NT], BF16) for h in range(4)]
                kd = [p2.sb(f"kd{kg}", [128, NT], BF16) for kg in range(KG)]
                eref = [p2.sb(f"eref{kg}", [128, NCH]) for kg in range(KG)]
                elast = [p2.sb(f"elast{kg}", [128, NCH]) for kg in range(KG)]
                tl = 63 if dr == 0 else 0
                for kg in range(KG):
                    gsrc = gT[dr][kg]; ksrc = kT[dr][kg]
                    cum = p2.rot("cum", [128, NT], F32, 2)
                    k.P.op("dve", (lambda o_, g_: (lambda e: e.tensor_tensor_scan(out=o_, data0=CS(k, "rm"), data1=g_, initial=0.0, op0=ALU.mult, op1=ALU.add)))(cum.t[:], gsrc.t[:]),
                           reads=[gsrc.b, k.cst.b], writes=[cum.b])
                    c3 = lambda t_: t_.t[:].rearrange("p (c j) -> p c j", j=64)
                    if dr == 1:
                        c2 = p2.rot("cum", [128, NT], F32, 2)
                        k.tt("dve", c2.t[:], gsrc.t[:], cum.t[:], ALU.subtract, [gsrc.b, cum.b], [c2.b])
                        k.tt("dve", c3(c2), c3(c2), c3(cum)[:, :, 63:64].to_broadcast([128, NCH, 64]), ALU.add, [c2.b, cum.b], [c2.b])
                        cum = c2
                    bm = p2.rot("bm", [128, NT], F32, 1)
                    k.tt("dve", c3(bm), c3(cum), c3(cum)[:, :, 32:33].to_broadcast([128, NCH, 64]), ALU.subtract, [cum.b], [bm.b])
                    e1 = p2.rot("ee", [128, NT], F32, 2)
                    k.act(e1.t[:], bm.t[:], AF.Exp, [bm.b], [e1.b])
                    k.stt("pool", qg[kg].t[:], qT[kg].t[:], qscale, e1.t[:], ALU.mult, ALU.mult, [qT[kg].b, e1.b], [qg[kg].b])
                    e2 = p2.rot("ee", [128, NT], F32, 2)
                    k.act(e2.t[:], bm.t[:], AF.Exp, [bm.b], [e2.b], scale=-1.0)
                    for hh in range(HPG):
                        h = kg * HPG + hh
                        k.stt("dve" if hh % 2 == 0 else "pool", kx[h].t[:], ksrc.t[:], CS(k, hmname, h, 1), e2.t[:], ALU.mult, ALU.mult, [ksrc.b, e2.b, k.cst.b], [kx[h].b])
                    bl = p2.rot("bm", [128, NT], F32, 1)
                    k.tt("dve", c3(bl), c3(cum)[:, :, tl:tl + 1].to_broadcast([128, NCH, 64]), c3(cum), ALU.subtract, [cum.b], [bl.b])
                    e3 = p2.rot("ee", [128, NT], F32, 2)
                    k.act(e3.t[:], bl.t[:], AF.Exp, [bl.b], [e3.b])
                    k.tt("pool", kd[kg].t[:], ksrc.t[:], e3.t[:], ALU.mult, [ksrc.b, e3.b], [kd[kg].b])
                    k.act(eref[kg].t[:], c3(cum)[:, :, 32], AF.Exp, [cum.b], [eref[kg].b])
                    k.act(elast[kg].t[:], c3(cum)[:, :, tl], AF.Exp, [cum.b], [elast[kg].b])
                S32 = p2.sb("S32", [128, 256])
                ATs = [p2.sb(f"AT{i_}", [64, 256], BF16) for i_ in range(3)]
                at_i = 0
                for a_ in ATs:
                    k.memset("pool", a_.t[:], 0.0, [a_.b])
                for s_ in range(nseq):
                    b_in = None
                    if g == 0:
                        k.memset("pool", S32.t[:], 0.0, [S32.b])
                    else:
                        k.memset("pool", S32.t[:], 0.0, [S32.b])
                        for h in range(4):
                            kg = h // HPG; hh = h % HPG
                            k.dma("sp", S32.t[hh * dk:(hh + 1) * dk, h * 64:(h + 1) * 64] if KG == 2 else S32.t[h * dk:(h + 1) * dk, h * 64:(h + 1) * 64],
                                  k.din[sin_name][l, dr, h], "ldS", wr=[S32.b])
                    order = range(cps) if dr == 0 else range(cps - 1, -1, -1)
                    for ci in order:
                        ch = s_ * cps + ci
                        tok = slice(ch * 64, (ch + 1) * 64)
                        bb = k.bbanks[k.bb_i % 2]; k.bb_i += 1
                        for kg in range(KG):
                            k.tr(bb.t[0:64, kg * 128:(kg + 1) * 128], kd[kg].t[:, tok], CSB(k, "ident"), [kd[kg].b, k.cstb.b], [bb.b])
                        kdT = p2.rot("kdT", [64, 256], BF16, 3)
                        k.cp("act", kdT.t[:, 0:KG * 128], bb.t[0:64, 0:KG * 128], [bb.b], [kdT.b])
                        pa = k.bank()
                        t0 = ch * 64
                        lo = slice(t0, t0 + 32); hi = slice(t0 + 32, t0 + 64)
                        for h in range(4):
                            qh = qg[h // HPG]
                            rdm = [kx[h].b, qh.b]
                            if dr == 0:
                                k.mm(pa.t[0:32, h * 64:(h + 1) * 64], kx[h].t[:, lo], qh.t[:, tok], True, True, rdm, [pa.b])
                                k.mm(pa.t[32:64, h * 64 + 32:(h + 1) * 64], kx[h].t[:, hi], qh.t[:, hi], True, True, rdm, [pa.b])
                            else:
                                k.mm(pa.t[0:32, h * 64:h * 64 + 32], kx[h].t[:, lo], qh.t[:, lo], True, True, rdm, [pa.b])
                                k.mm(pa.t[32:64, h * 64:(h + 1) * 64], kx[h].t[:, hi], qh.t[:, tok], True, True, rdm, [pa.b])
                        AT = ATs[at_i % 3]; at_i += 1
                        v4 = lambda ap_: ap_.rearrange("p (h i) -> p h i", i=64)
                        if dr == 0:
                            k.tt("dve", AT.t[0:32, :], pa.t[0:32, 0:256], MASK[dr][0:32, :], ALU.mult, [pa.b, k.cst.b], [AT.b])
                            k.tt("dve", v4(AT.t[32:64, :])[:, :, 32:64], v4(pa.t[32:64, 0:256])[:, :, 32:64], v4(MASK[dr][32:64, :])[:, :, 32:64], ALU.mult, [pa.b, k.cst.b], [AT.b])
                        else:
                            k.tt("dve", v4(AT.t[0:32, :])[:, :, 0:32], v4(pa.t[0:32, 0:256])[:, :, 0:32], v4(MASK[dr][0:32, :])[:, :, 0:32], ALU.mult, [pa.b, k.cst.b], [AT.b])
                            k.tt("dve", AT.t[32:64, :], pa.t[32:64, 0:256], MASK[dr][32:64, :], ALU.mult, [pa.b, k.cst.b], [AT.b])
                        Sbf = p2.rot("Sbf", [128, 256], BF16, 3)
                        if KG == 1:
                            k.stt("pool", Sbf.t[:], S32.t[:], eref[0].t[:, ch:ch + 1], SMv, ALU.mult, ALU.mult, [S32.b, eref[0].b, k.cst.b], [Sbf.b])
                        else:
                            for kg in range(2):
                                k.stt("pool", Sbf.t[:, kg * 128:(kg + 1) * 128], S32.t[:, kg * 128:(kg + 1) * 128], eref[kg].t[:, ch:ch + 1], SMv[:, kg * 128:(kg + 1) * 128],
                                      ALU.mult, ALU.mult, [S32.b, eref[kg].b, k.cst.b], [Sbf.b])
                        po = k.bank()
                        for h in range(4):
                            o_ = po.t[0:64, h * 64:(h + 1) * 64]
                            k.mm(o_, vtm.t[:, ch, h * 64:(h + 1) * 64], AT.t[:, h * 64:(h + 1) * 64], True, False, [vtm.b, AT.b], [po.b])
                            k.mm(o_, Sbf.t[:, h * 64:(h + 1) * 64], qg[h // HPG].t[:, tok], False, True, [Sbf.b, qg[h // HPG].b], [po.b])
                        oview = OACC.t[0:64, :, tok]
                        pview = po.t[0:64, 0:256].rearrange("p (h i) -> p h i", i=64)
                        if dr == 0:
                            k.cp("act", oview, pview, [po.b], [OACC.b])
                        else:
                            k.tt("dve", oview, oview, pview, ALU.add, [po.b, OACC.b], [OACC.b])
                        ps_ = k.bank()
                        if KG == 1:
                            k.mm(ps_.t[:, 0:256], kdT.t[:, 0:128], vtm.t[:, ch, :], True, True, [kdT.b, vtm.b], [ps_.b])
                            k.stt("dve", S32.t[:], S32.t[:], elast[0].t[:, ch:ch + 1], ps_.t[:, 0:256], ALU.mult, ALU.add, [S32.b, elast[0].b, ps_.b], [S32.b])
                        else:
                            for kg in range(2):
                                k.mm(ps_.t[:, kg * 128:(kg + 1) * 128], kdT.t[:, kg * 128:(kg + 1) * 128], vtm.t[:, ch, kg * 128:(kg + 1) * 128], True, True, [kdT.b, vtm.b], [ps_.b])
                            for kg in range(2):
                                sl = slice(kg * 128, (kg + 1) * 128)
                                k.stt("dve", S32.t[:, sl], S32.t[:, sl], elast[kg].t[:, ch:ch + 1], ps_.t[:, sl], ALU.mult, ALU.add, [S32.b, elast[kg].b, ps_.b], [S32.b])
                    if g == 0:
                        for h in range(4):
                            hh = h % HPG
                            src = S32.t[hh * dk:(hh + 1) * dk, h * 64:(h + 1) * 64] if KG == 2 else S32.t[h * dk:(h + 1) * dk, h * 64:(h + 1) * 64]
                            k.dma("sp", k.dout[sout_name][s_, l, dr, h], src, "stS", rd=[S32.b])

    def mixer_gla(ph, g, l, HN, BR):
        with k.phase() as p1:
            o = IN_OFF["ga_q"]
            wv, wb = win_tile(l, 0, 512)
            qT = p1.sb("qT", [128, NT]); kTt = p1.sb("kT", [128, NT])
            proj_fm(wv, wb, 0, 128, HN, evac_to(qT.t, qT.b))
            proj_fm(wv, wb, 128, 128, HN, evac_to(kTt.t, kTt.b, eng="dve"))
            vtm = p1.sb("vtm", [64, NCH, 256], BF16)
            for ch in range(NCH):
                proj_tm(wv, wb, 256, 256, HN, ch * 64, 64, (lambda ch_: (lambda bk: k.cp("act" if ch_ % 2 else "dve", vtm.t[:, ch_, :], bk.t[0:64, 0:256], [bk.b], [vtm.b])))(ch))
            wv2, wb2 = win_tile(l, 768, 32)
            lrT = p1.sb("lrT", [32, NT], BF16)
            proj_fm(wv2, wb2, 0, 32, HN, evac_to(lrT.t, lrT.b, n=32))
            w2p = p1.sb("w2p", [32, 2, 128], BF16)
            k.memset("pool", w2p.t[:], 0.0, [w2p.b])
            for dr in range(2):
                k.dma("pool", w2p.t[dr * 16:(dr + 1) * 16, dr, :], k.din["gla_w2"][l, dr], "ldw2", wr=[w2p.b])
            negb = p1.sb("negb", [128, 2])
            gb0 = ROW["gla_b"] + l * 2
            k.ts("dve", negb.t[:], k.pt.t[:, gb0:gb0 + 2], -1.0, 0.0, ALU.mult, ALU.add, [k.pt.b], [negb.b])
            gT = [[Tn(k.XT.t[:, 4 + dr, :], Buf(f"gT{dr}"))] for dr in range(2)]
            for dr in range(2):
                for hb in range(2):
                    bk = k.bank()
                    k.mm(bk.t[:], w2p.t[:, dr, :], lrT.t[:, hb * 512:(hb + 1) * 512], True, True, [w2p.b, lrT.b], [bk.b])
                    sl = gT[dr][0].t[:, hb * 512:(hb + 1) * 512]
                    k.act(sl, bk.t[:], AF.Exp, [bk.b, negb.b], [gT[dr][0].b], bias=negb.t[:, dr:dr + 1], scale=-1.0)
                    k.act(sl, sl, AF.Ln, [gT[dr][0].b], [gT[dr][0].b], bias=1.0)
                    k.ts("dve", sl, sl, -1.0 / 16.0, 0.0, ALU.mult, ALU.add, [gT[dr][0].b], [gT[dr][0].b])
            OACC = Tn(k.XT.t[0:64, 0:4, :], Buf("OACC"))
            gla_scan(p1, g, l, 0, 1, [qT], 32 ** -0.5, [[kTt], [kTt]], gT, vtm, "hm_gla", "sm_gla", "sg", "ng", 32, OACC)
            gate = gate_proj(p1, l, 512, HN)
            head_norm(p1, OACC, l, 0, BR, gate)

    def mixer_hg(ph, g, l, HN, BR):
        with k.phase() as p1:
            c0 = IN_OFF["hg_q"]
            wv, wb = win_tile(l, c0, 512)
            wv2, wb2 = win_tile(l, c0 + 512, 512)
            qT = [p1.sb(f"hq{kg}", [128, NT], BF16) for kg in range(2)]
            for kg in range(2):
                proj_fm(wv, wb, kg * 128, 128, HN, evac_to(qT[kg].t, qT[kg].b, AF.Silu))
            vtm = p1.sb("vtm", [64, NCH, 256], BF16)
            for ch in range(NCH):
                proj_tm(wv2, wb2, 256, 256, HN, ch * 64, 64, (lambda ch_: (lambda bk: k.cp("act" if ch_ % 2 else "dve", vtm.t[:, ch_, :], bk.t[0:64, 0:256], [bk.b], [vtm.b])))(ch))
            lb = p1.sb("lb", [128, 4]); oml = p1.sb("oml", [128, 4])
            r0 = ROW["hg_lb"]
            if l == 0:
                k.memset("pool", lb.t[:], 0.0, [lb.b])
            else:
                ex = p1.sb("lbex", [128, 8])
                k.act(ex.t[:], k.pt.t[:, r0:r0 + 8], AF.Exp, [k.pt.b], [ex.b])
                sm_ = p1.sb("lbsum", [128, 4])
                k.tt("dve", sm_.t[:], ex.t[:, 0:4], ex.t[:, 4:8], ALU.add, [ex.b], [sm_.b])
                k.P.op("dve", lambda e: e.reciprocal(out=sm_.t[:], in_=sm_.t[:]), reads=[sm_.b], writes=[sm_.b])
                k.tt("dve", lb.t[:], ex.t[:, 4:8], sm_.t[:], ALU.mult, [ex.b, sm_.b], [lb.b])
            k.ts("dve", oml.t[:], lb.t[:], -1.0, 1.0, ALU.mult, ALU.add, [lb.b], [oml.b])
            kT = [[p1.sb(f"hk{dr}{kg}", [128, NT], BF16) for kg in range(2)] for dr in range(2)]
            gT = [[Tn(k.XT.t[:, 4 + dr * 2 + kg, :], Buf(f"hg{dr}{kg}")) for kg in range(2)] for dr in range(2)]
            for dr in range(2):
                for kg in range(2):
                    wsrc_, wbuf_, cc = (wv, wb, 256 + kg * 128) if dr == 0 else (wv2, wb2, kg * 128)
                    idx = dr * 2 + kg

                    def ev(hb, bk, dr=dr, kg=kg, idx=idx):
                        sl = slice(hb * 512, (hb + 1) * 512)
                        f_ = p1.rot("hf", [128, 512], F32, 1)
                        k.act(f_.t[:], bk.t[:], AF.Sigmoid, [bk.b], [f_.b])
                        k.ts("dve", f_.t[:], f_.t[:], oml.t[:, idx:idx + 1], lb.t[:, idx:idx + 1], ALU.mult, ALU.add, [f_.b, oml.b, lb.b], [f_.b])
                        k.act(gT[dr][kg].t[:, sl], f_.t[:], AF.Ln, [f_.b], [gT[dr][kg].b])
                        k.ts("pool", kT[dr][kg].t[:, sl], f_.t[:], -1.0, 1.0, ALU.mult, ALU.add, [f_.b], [kT[dr][kg].b])
                    proj_fm(wsrc_, wbuf_, cc, 128, HN, ev)
            OACC = Tn(k.XT.t[0:64, 0:4, :], Buf("OACC"))
            gla_scan(p1, g, l, 2, 2, qT, 0.125, kT, gT, vtm, "hm_hg", "sm_hg", "sh", "nh", 64, OACC)
            gate = gate_proj(p1, l, IN_OFF["hg_g"], HN)
            head_norm(p1, OACC, l, 2, BR, gate)


    def lam_setup(ph, l):
        dl = ph.sb("dl", [128, 128])
        k.dma("sp", dl.t[:], k.din["dlam"][l:l + 1, :].partition_broadcast(128), "lddl", wr=[dl.b])
        pr = ph.sb("dlp", [128, 2, 32])
        k.tt("dve", pr.t[:, 0, :], dl.t[:, 0:32], dl.t[:, 32:64], ALU.mult, [dl.b], [pr.b])
        k.tt("dve", pr.t[:, 1, :], dl.t[:, 64:96], dl.t[:, 96:128], ALU.mult, [dl.b], [pr.b])
        sm_ = ph.sb("dls", [128, 2])
        k.P.op("dve", lambda e: e.reduce_sum(out=sm_.t[:], in_=pr.t[:], axis=mybir.AxisListType.X), reads=[pr.b], writes=[sm_.b])
        k.act(sm_.t[:], sm_.t[:], AF.Exp, [sm_.b], [sm_.b])
        lam = ph.sb("lam", [128, 2])
        lam_init = 0.8 - 0.6 * math.exp(-0.3 * l)
        k.stt("dve", lam.t[:, 0:1], sm_.t[:, 0:1], lam_init, sm_.t[:, 1:2], ALU.add, ALU.subtract, [sm_.b], [lam.b])
        k.ts("dve", lam.t[:, 1:2], lam.t[:, 0:1], -1.0, 0.0, ALU.mult, ALU.add, [lam.b], [lam.b])
        return lam, lam_init

    def mixer_df(ph, g, l, HN, BR):
        with k.phase() as p1:
            lam, lam_init = lam_setup(p1, l)
            c0 = IN_OFF["df_q"]
            wv, wb = win_tile(l, c0, 512)
            wv2, wb2 = win_tile(l, c0 + 512, 256)
            NK = 256 if g == 0 else 1536
            NKT = NK // 128
            koff = 0 if g == 0 else 512
            qT = [Tn(k.XT.t[:, 6 + cp, :], Buf(f"dq{cp}")) for cp in range(2)]
            kTf = [Tn(k.XT.t[:, 4 + cp, :], Buf(f"dkf{cp}")) for cp in range(2)]
            KT = [p1.sb(f"dK{cp}", [128, koff + NT], BF16) for cp in range(2)]
            for cp in range(2):
                proj_fm(wv, wb, cp * 128, 128, HN, evac_to(qT[cp].t, qT[cp].b))
                proj_fm(wv, wb, 256 + cp * 128, 128, HN, evac_to(kTf[cp].t, kTf[cp].b, eng="dve"))
            if DF_STOP == 1:
                return
            if g == 1:
                rope = p1.sb("rope", [128, 2048])
                k.dma("sp", rope.t[:], k.din["cst"][:, CST["cos"][0]:CST["cos"][0] + 2048], "ldrope", wr=[rope.b])
                for src in qT + kTf:
                    xb_ = p1.rot("ropexb", [128, NT], BF16, 2)
                    k.cp("pool", xb_.t[:], src.t[:], [src.b], [xb_.b])
                    for hb in range(2):
                        sl = slice(hb * 512, (hb + 1) * 512)
                        bk = k.bank()
                        k.mm(bk.t[:], CSB(k, "rot"), xb_.t[:, sl], True, True, [xb_.b, k.cstb.b], [bk.b])
                        t2 = p1.rot("ropet", [128, 512], F32, 2)
                        k.tt("dve", t2.t[:], bk.t[:], rope.t[:, 1024 + hb * 512:1024 + (hb + 1) * 512], ALU.mult, [bk.b, rope.b], [t2.b])
                        k.tt("pool", src.t[:, sl], src.t[:, sl], rope.t[:, sl], ALU.mult, [src.b, rope.b], [src.b])
                        k.tt("dve", src.t[:, sl], src.t[:, sl], t2.t[:], ALU.add, [src.b, t2.b], [src.b])
                for cp in range(2):
                    for kt in range(4):
                        stg = p1.rot("ckstg", [128, 2, 64], F32, 2)
                        k.dma("sp", stg.t[:], k.din["ck"][l, 2 * cp:2 * cp + 2, kt * 128:(kt + 1) * 128, :].rearrange("h t d -> t h d"), "ldck", wr=[stg.b])
                        bk = k.bank()
                        k.tr(bk.t[:, 0:128], stg.t[:].rearrange("p h d -> p (h d)"), CS(k, "ident"), [stg.b, k.cst.b], [bk.b])
                        k.cp("act", KT[cp].t[:, kt * 128:(kt + 1) * 128], bk.t[:, 0:128], [bk.b], [KT[cp].b])
            for cp in range(2):
                k.cp("pool", KT[cp].t[:, koff:koff + NT], kTf[cp].t[:], [kTf[cp].b], [KT[cp].b])
            if DF_STOP == 6:
                return
            NVT = (koff + NT) // 128
            Vtm = p1.sb("Vtm", [128, NVT, 256], BF16)
            if g == 1:
                for kt in range(4):
                    k.dma("pool", Vtm.t[:, kt, :].rearrange("p (h d) -> p h d", d=64), k.din["cv"][l, :, kt * 128:(kt + 1) * 128, :].rearrange("h t d -> t h d"), "ldcv", wr=[Vtm.b])
            for tl in range(8):
                def evv(bk, tl=tl):
                    if g == 0:
                        so = p1.rot("vout", [128, 256], F32, 2)
                        k.cp("dve", so.t[:], bk.t[:, 0:256], [bk.b], [so.b])
                        k.cp("act", Vtm.t[:, koff // 128 + tl, :], so.t[:], [so.b], [Vtm.b])
                        s_ = tl // 2; t0 = (tl % 2) * 128
                        k.dma("sp", k.dout["nv"][s_, l, :, t0:t0 + 128, :].rearrange("h t d -> t h d"), so.t[:].rearrange("p (h d) -> p h d", d=64), "stv", rd=[so.b])
                    else:
                        k.cp("act", Vtm.t[:, koff // 128 + tl, :], bk.t[:, 0:256], [bk.b], [Vtm.b])
                proj_tm(wv2, wb2, 0, 256, HN, tl * 128, 128, evv)
                if g == 0 and DF_STOP not in (7, 8):
                    def evk(bk, tl=tl):
                        so = p1.rot("kout", [128, 256], F32, 2)
                        k.cp("dve", so.t[:], bk.t[:, 0:256], [bk.b], [so.b])
                        s_ = tl // 2; t0 = (tl % 2) * 128
                        if True:
                            k.dma("sp", k.dout["nk"][s_, l, :, t0:t0 + 128, :].rearrange("h t d -> t h d"), so.t[:].rearrange("p (h d) -> p h d", d=64), "stk", rd=[so.b])
                    proj_tm(wv, wb, 256, 256, HN, tl * 128, 128, evk)
            if DF_STOP in (2, 5, 7, 8):
                return
            QX = [[p1.sb(f"QX{cp}{i}", [128, NT], BF16) for i in range(4)] for cp in range(2)]
            for cp in range(2):
                for i in range(4):
                    k.ts("dve" if i % 2 else "pool", QX[cp][i].t[:], qT[cp].t[:], CS(k, "qm", i, 1), 0.0, ALU.mult, ALU.add, [qT[cp].b, k.cst.b], [QX[cp][i].b])
            if DF_STOP == 3:
                return
            OACC = Tn(k.XT.t[0:64, 0:4, :], Buf("OACC"))
            scale = 32 ** -0.5
            bigb = [[k.banks[i * 3 + j].b for j in range(3)] for i in range(2)]
            for qb in range(8):
                qs = slice(qb * 128, (qb + 1) * 128)
                k0 = (qb // 2) * 256 if g == 0 else 0
                for h in range(4):
                    cp = h // 2; hh = h % 2
                    Pm = []; rr = []
                    for m in range(2):
                        big = k.big[m]; bb_ = bigb[m]
                        for sg_ in range((NK + 511) // 512):
                            n_ = min(512, NK - sg_ * 512)
                            k.mm(big[:, sg_ * 512:sg_ * 512 + n_], QX[cp][hh * 2 + m].t[:, qs], KT[cp].t[:, k0 + sg_ * 512:k0 + sg_ * 512 + n_], True, True,
                                 [QX[cp][hh * 2 + m].b, KT[cp].b], bb_)
                        mx = p1.rot("mx", [128, 1], F32, 4)
                        k.P.op("dve", (lambda o_, i_: (lambda e: e.reduce_max(out=o_, in_=i_, axis=mybir.AxisListType.X)))(mx.t[:], big[:, 0:NK]), reads=bb_, writes=[mx.b])
                        k.ts("dve", mx.t[:], mx.t[:], -scale, 0.0, ALU.mult, ALU.add, [mx.b], [mx.b])
                        Pt = p1.rot("Pt", [128, NK], BF16, 2)
                        rs = p1.rot("rs", [128, 1], F32, 4)
                        k.act(Pt.t[:], big[:, 0:NK], AF.Exp, bb_ + [mx.b], [Pt.b, rs.b], bias=mx.t[:], scale=scale, accum=rs.t[:])
                        k.P.op("dve", (lambda o_: (lambda e: e.reciprocal(out=o_, in_=o_)))(rs.t[:]), reads=[rs.b], writes=[rs.b])
                        if m == 1:
                            k.tt("dve", rs.t[:], rs.t[:], lam.t[:, 1:2], ALU.mult, [rs.b, lam.b], [rs.b])
                        Pm.append(Pt); rr.append(rs)
                    A = p1.rot("Amat", [128, NK], BF16, 1)
                    k.ts("pool", A.t[:], Pm[0].t[:], rr[0].t[:], 0.0, ALU.mult, ALU.add, [Pm[0].b, rr[0].b], [A.b])
                    k.stt("dve", A.t[:], Pm[1].t[:], rr[1].t[:], A.t[:], ALU.mult, ALU.add, [Pm[1].b, rr[1].b, A.b], [A.b])
                    ATm = p1.rot("ATm", [128, NKT, 128], BF16, 1)
                    for t8 in range((NKT + 7) // 8):
                        bb = k.bbanks[k.bb_i % 2]; k.bb_i += 1
                        nt_ = min(8, NKT - t8 * 8)
                        for kt in range(nt_):
                            k.tr(bb.t[:, kt * 128:(kt + 1) * 128], A.t[:, (t8 * 8 + kt) * 128:(t8 * 8 + kt + 1) * 128], CSB(k, "ident"), [A.b, k.cstb.b], [bb.b])
                        k.cp("act", ATm.t[:, t8 * 8:t8 * 8 + nt_, :], bb.t[:, 0:nt_ * 128].rearrange("p (t q) -> p t q", q=128), [bb.b], [ATm.b])
                    po = k.bank()
                    vt0 = k0 // 128
                    for kt in range(NKT):
                        k.mm(po.t[0:64, 0:128], Vtm.t[:, vt0 + kt, h * 64:(h + 1) * 64], ATm.t[:, kt, :], kt == 0, kt == NKT - 1, [Vtm.b, ATm.b], [po.b])
                    k.cp("act", OACC.t[0:64, h, qs], po.t[0:64, 0:128], [po.b], [OACC.b])
                    if DF_STOP == 4:
                        return
            head_norm(p1, OACC, l, 3, BR, None, 1.0 - lam_init)

    def merge(g, l, HN, BR):
        w = g
        with k.phase() as p1:
            MG = p1.sb("MG", [128, 8, NT], BF16)
            for kq in range(2):
                acc = p1.sb(f"mgacc{kq}", [128, 4, NT])
                for n in range(4):
                    wv, wb = k.wget(("mat", "w_mgate", (l,), 8, ((n * 1024 + kq * 512, 512),)))
                    bv, bbuf = k.wget(("branch", l, n))
                    for k4 in range(4):
                        kc = kq * 4 + k4
                        for hb in range(2):
                            sl = slice(hb * 512, (hb + 1) * 512)
                            bg = k.bank(); bu = k.bank()
                            for kk in range(8):
                                k.mm(bg.t[:], wv[:, kk, k4 * 128:(k4 + 1) * 128], HN.t[:, kk, sl], kk == 0, kk == 7, [wb, HN.b], [bg.b])
                            for h in range(4):
                                k.mm(bu.t[:], bv[:, h, kc * 128:(kc + 1) * 128], BR.t[0:64, n, h, sl], h == 0, h == 3, [bbuf, BR.b], [bu.b])
                            sg = p1.rot("msg", [128, 512], F32, 3)
                            k.act(sg.t[:], bg.t[:], AF.Sigmoid, [bg.b], [sg.b])
                            if n == 0:
                                k.tt("dve", acc.t[:, k4, sl], sg.t[:], bu.t[:], ALU.mult, [sg.b, bu.b], [acc.b])
                            else:
                                k.tt("dve", sg.t[:], sg.t[:], bu.t[:], ALU.mult, [sg.b, bu.b], [sg.b])
                                if n < 3:
                                    k.tt("pool", acc.t[:, k4, sl], acc.t[:, k4, sl], sg.t[:], ALU.add, [acc.b, sg.b], [acc.b])
                                else:
                                    k.tt("pool", MG.t[:, kc, sl], acc.t[:, k4, sl], sg.t[:], ALU.add, [acc.b, sg.b], [MG.b])
            for t2 in range(2):
                wv, wb = k.wget(("mat", "w_out", (l,), 8, ((t2 * 512, 512),)))
                for f4 in range(4):
                    fo = t2 * 4 + f4
                    for hb in range(2):
                        sl = slice(hb * 512, (hb + 1) * 512)
                        bk = k.bank()
                        for kk in range(8):
                            k.mm(bk.t[:], wv[:, kk, f4 * 128:(f4 + 1) * 128], MG.t[:, kk, sl], kk == 0, kk == 7, [wb, MG.b], [bk.b])
                        xs_ = XT.t[:, fo, sl]
                        k.stt("dve", xs_, bk.t[:], k.der.t[:, l, w, 7, fo:fo + 1], xs_, ALU.mult, ALU.add, [bk.b, k.der.b, k.xtb[fo]], [k.xtb[fo]])

    def layer(g, l, mixers=("A", "B", "C", "D")):
        w = g
        ffn(l, w, 0, "ffn1_in", "ffn1_down")
        with k.phase() as ph:
            HN = ph.sb("HN", [128, 8, NT], BF16)
            BR = ph.sb("BR", [64, 4, 4, NT], BF16)
            norm_mod(ph, l, w, 1, HN)
            k.dma("sp", k.xscr[:, :], XT.t[:].rearrange("p c t -> p (c t)"), "spill", rd=k.xtb)
            k.P.barrier()
            if "A" in mixers:
                mixer_gla(ph, g, l, HN, BR)
            else:
                k.memset("pool", BR.t[:, 0], 0.0, [BR.b])
            if "B" in mixers:
                k.mixer_dn(ph, g, l, HN, BR)
            else:
                k.memset("pool", BR.t[:, 1], 0.0, [BR.b])
            if "C" in mixers:
                mixer_hg(ph, g, l, HN, BR)
            else:
                k.memset("pool", BR.t[:, 2], 0.0, [BR.b])
            if "D" in mixers:
                mixer_df(ph, g, l, HN, BR)
            else:
                k.memset("pool", BR.t[:, 3], 0.0, [BR.b])
            for n_, nm_ in enumerate("ABCD"):
                k.tap(f"br{nm_}{g}{l}", BR.t[0:64, n_].rearrange("p h t -> p (h t)"), [64, 4 * NT], [BR.b])
            k.P.barrier()
            k.P.dma("sp", lambda e: e.dma_start(out=XT.t[:].rearrange("p c t -> p (c t)"), in_=k.xscr[:, :]), "d_xt0", writes=k.xtb)
            merge(g, l, HN, BR)
        k.tap(f"xm{g}{l}", XT.t[:].rearrange("p c t -> p (c t)"), [128, 8 * NT], k.xtb)
        ffn(l, w, 2, "ffn2_in", "ffn2_down")

    k.layer = layer
    k.mixer_df = mixer_df
    k.merge = merge

    def mixer_dn(ph, g, l, HN, BR):
        nseq = 4 if g == 0 else 1
        cps = NCH // nseq
        T = NT // nseq
        with k.phase() as p1:
            c0 = IN_OFF["dn_x"]
            wv, wb = win_tile(l, c0, 512)
            wv2, wb2 = win_tile(l, c0 + 512, 272)
            nmc = p1.sb("nmc", [64, 1024], BF16)
            k.dma("pool", nmc.t[:], k.din["cst"][0:64, CST["nm_c"][0]:CST["nm_c"][0] + 1024], "ldnm", wr=[nmc.b])
            QT = [p1.sb(f"nQ{p}", [128, NT], BF16) for p in range(2)]
            KT = [p1.sb(f"nK{p}", [128, NT], BF16) for p in range(2)]
            Ktm = p1.sb("Ktm", [64, NCH, 256], BF16); Vtm = p1.sb("nVtm", [64, NCH, 256], BF16)
            with k.phase() as p2:
                VT = [p2.sb(f"nV{p}", [128, NT], BF16) for p in range(2)]
                for c in range(6):
                    src_w, src_b, cc = (wv, wb, c * 128) if c < 4 else (wv2, wb2, (c - 4) * 128)
                    x = p2.rot("cx", [128, NT], F32, 2)
                    proj_fm(src_w, src_b, cc, 128, HN, evac_to(x.t, x.b))
                    y = p2.rot("cy", [128, NT], F32, 2)
                    wrow = lambda tap: k.pt.t[:, ROW["dn_conv"] + (l * 3 + tap) * 6 + c:ROW["dn_conv"] + (l * 3 + tap) * 6 + c + 1]
                    k.ts("dve", y.t[:], x.t[:], wrow(1), 0.0, ALU.mult, ALU.add, [x.b, k.pt.b], [y.b])
                    x3 = x.t[:].rearrange("p (s t) -> p s t", t=T); y3 = y.t[:].rearrange("p (s t) -> p s t", t=T)
                    k.stt("dve", y3[:, :, 1:T], x3[:, :, 0:T - 1], wrow(0), y3[:, :, 1:T], ALU.mult, ALU.add, [x.b, y.b, k.pt.b], [y.b])
                    k.stt("dve", y3[:, :, 0:T - 1], x3[:, :, 1:T], wrow(2), y3[:, :, 0:T - 1], ALU.mult, ALU.add, [x.b, y.b, k.pt.b], [y.b])
                    if c >= 4:
                        k.act(VT[c - 4].t[:], y.t[:], AF.Silu, [y.b], [VT[c - 4].b])
                    else:
                        k.act(y.t[:], y.t[:], AF.Silu, [y.b], [y.b])
                        rn = k.rstd_of(p2, lambda c_, hb: y.t[:, hb * 512:(hb + 1) * 512], 1, 128, CSB(k, "bd64"), 1.0, f"rn{c % 2}", [y.b])
                        dst = QT[c] if c < 2 else KT[c - 2]
                        k.stt("dve", dst.t[:], y.t[:], 0.125 if c < 2 else 1.0, rn.t[:], ALU.mult, ALU.mult, [y.b, rn.b], [dst.b])
                for ch in range(NCH):
                    tok = slice(ch * 64, (ch + 1) * 64)
                    bb = k.bbanks[k.bb_i % 2]; k.bb_i += 1
                    for p in range(2):
                        k.tr(bb.t[0:64, p * 128:(p + 1) * 128], KT[p].t[:, tok], CSB(k, "ident"), [KT[p].b, k.cstb.b], [bb.b])
                        k.tr(bb.t[0:64, 256 + p * 128:256 + (p + 1) * 128], VT[p].t[:, tok], CSB(k, "ident"), [VT[p].b, k.cstb.b], [bb.b])
                    k.cp("act", Ktm.t[:, ch, :], bb.t[0:64, 0:256], [bb.b], [Ktm.b])
                    k.cp("act", Vtm.t[:, ch, :], bb.t[0:64, 256:512], [bb.b], [Vtm.b])

            if DN_STOP == 2:
                return
            zz = p1.sb("zz", [64, NCH, 16])
            for ch in range(NCH):
                proj_tm(wv2, wb2, 256, 16, HN, ch * 64, 64, (lambda ch_: (lambda bk: k.cp("dve", zz.t[:, ch_, :], bk.t[0:64, 0:16], [bk.b], [zz.b])))(ch))
            par = p1.sb("dnpar", [64, 16])
            k.dma("sp", par.t[:, 0:8], k.din["dn_al"][l:l + 1, :].partition_broadcast(64), "ldpar", wr=[par.b])
            k.dma("sp", par.t[:, 8:16], k.din["dn_dt"][l:l + 1, :].partition_broadcast(64), "ldpar", wr=[par.b])
            k.act(par.t[:, 0:8], par.t[:, 0:8], AF.Exp, [par.b], [par.b])
            k.ts("dve", par.t[:, 0:8], par.t[:, 0:8], -1.0, 0.0, ALU.mult, ALU.add, [par.b], [par.b])
            bcol = p1.sb("bcol", [64, NCH, 8]); lnb = p1.sb("lnb", [64, NCH, 8]); gcol = p1.sb("gcol", [64, NCH, 8])
            k.act(bcol.t[:], zz.t[:, :, 0:8], AF.Sigmoid, [zz.b], [bcol.b])
            k.act(lnb.t[:], bcol.t[:], AF.Ln, [bcol.b], [lnb.b])
            k.tt("dve", gcol.t[:], zz.t[:, :, 8:16], par.t[:, 8:16].unsqueeze(1).to_broadcast([64, NCH, 8]), ALU.add, [zz.b, par.b], [gcol.b])
            k.act(gcol.t[:], gcol.t[:], AF.Exp, [gcol.b], [gcol.b])
            k.act(gcol.t[:], gcol.t[:], AF.Ln, [gcol.b], [gcol.b], bias=1.0)
            k.tt("dve", gcol.t[:], gcol.t[:], par.t[:, 0:8].unsqueeze(1).to_broadcast([64, NCH, 8]), ALU.mult, [gcol.b, par.b], [gcol.b])
            dcol = p1.sb("dcol", [64, NCH, 8]); dlast = p1.sb("dlast", [128, NCH, 8])
            bk = k.bank()
            for ch in range(NCH):
                for dr in range(2):
                    k.mm(bk.t[0:64, ch * 8 + dr * 4:ch * 8 + dr * 4 + 4], CS(k, "mf" if dr == 0 else "mb")[0:64, 0:64], gcol.t[:, ch, dr * 4:dr * 4 + 4], True, True, [gcol.b, k.cst.b], [bk.b])
            k.cp("dve", dcol.t[:].rearrange("p c h -> p (c h)"), bk.t[0:64, 0:NCH * 8], [bk.b], [dcol.b])
            bk = k.bank()
            for ch in range(NCH):
                for dr in range(2):
                    k.mm(bk.t[:, ch * 8 + dr * 4:ch * 8 + dr * 4 + 4], CS(k, "sel_f" if dr == 0 else "sel_b")[0:64, :], dcol.t[:, ch, dr * 4:dr * 4 + 4], True, True, [dcol.b, k.cst.b], [bk.b])
            k.cp("dve", dlast.t[:].rearrange("p c h -> p (c h)"), bk.t[:, 0:NCH * 8], [bk.b], [dlast.b])
            acol = p1.sb("acol", [64, NCH, 8]); expd = p1.sb("expd", [64, NCH, 8]); nexpd = p1.sb("nexpd", [64, NCH, 8])
            edl = p1.sb("edl", [64, NCH, 8]); edlast = p1.sb("edlast", [128, NCH, 8])
            k.tt("dve", acol.t[:], dcol.t[:], lnb.t[:], ALU.add, [dcol.b, lnb.b], [acol.b])
            k.act(expd.t[:], dcol.t[:], AF.Exp, [dcol.b], [expd.b])
            k.ts("dve", nexpd.t[:], expd.t[:], -1.0, 0.0, ALU.mult, ALU.add, [expd.b], [nexpd.b])
            k.tt("dve", edl.t[:], dlast.t[0:64], dcol.t[:], ALU.subtract, [dlast.b, dcol.b], [edl.b])
            k.act(edl.t[:], edl.t[:], AF.Exp, [edl.b], [edl.b])
            k.act(edlast.t[:], dlast.t[:], AF.Exp, [dlast.b], [edlast.b])
            if DN_STOP == 3:
                return
            OACC = Tn(k.XT.t[0:64, 0:4, :], Buf("OACC"))
            bc8 = lambda t_, ch_, lo, n_: t_.t[:, ch_, lo:lo + n_].unsqueeze(2).to_broadcast([64, n_, 64])
            v3 = lambda ap_, n_: ap_.rearrange("p (h i) -> p h i", i=64)
            for dr in range(2):
                h0 = dr * 4
                with k.phase() as p2:
                    TbT = p2.sb("TbT", [64, NCH, 256], BF16); Aqk = p2.sb("Aqk", [64, NCH, 256], BF16)
                    for ch in range(NCH):
                        tok = slice(ch * 64, (ch + 1) * 64)
                        dg = p2.rot("dg", [64, 256], F32, 1)
                        k.tt("pool", v3(dg.t[:], 4), v3(CS(k, "id8")[0:64, 0:256], 4), bc8(dcol, ch, h0, 4), ALU.mult, [dcol.b, k.cst.b], [dg.b])
                        rb = k.bank()
                        k.mm(rb.t[0:64, 0:256], CS(k, "ones")[0:64, 0:64], dg.t[:], True, True, [dg.b, k.cst.b], [rb.b])
                        if DN_STOP == 5:
                            return
                        ec = p2.rot("ec", [64, 256], F32, 1); eb = p2.rot("eb", [64, 256], F32, 1)
                        k.tt("dve", v3(ec.t[:], 4), v3(rb.t[0:64, 0:256], 4), bc8(dcol, ch, h0, 4), ALU.subtract, [rb.b, dcol.b], [ec.b])
                        k.tt("dve", ec.t[:], ec.t[:], nmc.t[:, dr * 256:(dr + 1) * 256], ALU.min, [ec.b, nmc.b], [ec.b])
                        k.act(ec.t[:], ec.t[:], AF.Exp, [ec.b], [ec.b])
                        k.stt("dve", v3(eb.t[:], 4), v3(rb.t[0:64, 0:256], 4), -1.0, bc8(acol, ch, h0, 4), ALU.mult, ALU.add, [rb.b, acol.b], [eb.b])
                        k.tt("pool", eb.t[:], eb.t[:], nmc.t[:, 512 + dr * 256:512 + (dr + 1) * 256], ALU.min, [eb.b, nmc.b], [eb.b])
                        k.act(eb.t[:], eb.t[:], AF.Exp, [eb.b], [eb.b])
                        if DN_STOP == 6:
                            return
                        gk = k.bank(); gq = k.bank()
                        KTm = p2.rot("KTm", [128, 256], BF16, 1)
                        for h in range(4):
                            k.ts("pool" if h % 2 else "dve", KTm.t[:, h * 64:(h + 1) * 64], KT[h // 2].t[:, tok], CS(k, "hm_hg", h, 1), 0.0, ALU.mult, ALU.add,
                                 [KT[h // 2].b, k.cst.b], [KTm.b])
                        for h in range(4):
                            p = h // 2
                            k.mm(gk.t[0:64, h * 64:(h + 1) * 64], KTm.t[:, h * 64:(h + 1) * 64], KT[p].t[:, tok], True, True, [KT[p].b, KTm.b], [gk.b])
                            k.mm(gq.t[0:64, h * 64:(h + 1) * 64], KTm.t[:, h * 64:(h + 1) * 64], QT[p].t[:, tok], True, True, [KTm.b, QT[p].b], [gq.b])
                        XN = p2.rot("XN", [64, 256], F32, 2); XTt = p2.rot("XTt", [64, 256], F32, 2)
                        k.stt("dve", XN.t[:], gk.t[0:64, 0:256], -1.0, eb.t[:], ALU.mult, ALU.mult, [gk.b, eb.b], [XN.b])
                        k.tt("dve", Aqk.t[:, ch, :], gq.t[0:64, 0:256], ec.t[:], ALU.mult, [gq.b, ec.b], [Aqk.b])
                        if DN_STOP == 7:
                            return
                        bt = k.bank()
                        for h in range(4):
                            k.tr(bt.t[0:64, h * 64:(h + 1) * 64], XN.t[:, h * 64:(h + 1) * 64], CS(k, "ident")[0:64, 0:64], [XN.b, k.cst.b], [bt.b])
                        k.cp("act", XTt.t[:], bt.t[0:64, 0:256], [bt.b], [XTt.b])
                        Pm = p2.rot("Pm", [64, 256], F32, 2)
                        k.tt("pool", Pm.t[:], XTt.t[:], CS(k, "id8")[0:64, 0:256], ALU.add, [XTt.b, k.cst.b], [Pm.b])
                        if DN_STOP == 8:
                            return
                        for lev in range(5):
                            hsl = lambda h: slice(h * 64, (h + 1) * 64)
                            if lev < 4:
                                b1 = k.bank()
                                for h in range(4):
                                    k.mm(b1.t[0:64, hsl(h)], XN.t[:, hsl(h)], XTt.t[:, hsl(h)], True, True, [XN.b, XTt.b], [b1.b])
                            b2 = k.bank()
                            for h in range(4):
                                k.mm(b2.t[0:64, hsl(h)], XTt.t[:, hsl(h)], XN.t[:, hsl(h)], True, True, [XN.b, XTt.b], [b2.b])
                            XN2 = p2.rot("XN", [64, 256], F32, 2)
                            k.cp("act", XN2.t[:], b2.t[0:64, 0:256], [b2.b], [XN2.b])
                            b3 = k.bank()
                            for h in range(4):
                                k.mm(b3.t[0:64, hsl(h)], XN2.t[:, hsl(h)], Pm.t[:, hsl(h)], True, True, [XN2.b, Pm.b], [b3.b])
                            Pn = p2.rot("Pm", [64, 256], F32, 2)
                            k.tt("dve", Pn.t[:], Pm.t[:], b3.t[0:64, 0:256], ALU.add, [Pm.b, b3.b], [Pn.b])
                            Pm = Pn
                            if lev < 4:
                                XT2 = p2.rot("XTt", [64, 256], F32, 2)
                                k.cp("act", XT2.t[:], b1.t[0:64, 0:256], [b1.b], [XT2.b])
                                XTt = XT2
                            XN = XN2
                        k.tt("dve", v3(TbT.t[:, ch, :], 4), v3(Pm.t[:], 4), bc8(bcol, ch, h0, 4), ALU.mult, [Pm.b, bcol.b], [TbT.b])
                    if DN_STOP == 4:
                        return
                    S32 = p2.sb("S32", [128, 256])
                    for s_ in range(nseq):
                        k.memset("pool", S32.t[:], 0.0, [S32.b])
                        if g == 1:
                            for h in range(4):
                                k.dma("sp", S32.t[(h % 2) * 64:(h % 2 + 1) * 64, h * 64:(h + 1) * 64], k.din["sd"][l, dr, h], "ldS", wr=[S32.b])
                        order = range(cps) if dr == 0 else range(cps - 1, -1, -1)
                        for ci in order:
                            ch = s_ * cps + ci
                            tok = slice(ch * 64, (ch + 1) * 64)
                            Sbf = p2.rot("Sbf", [128, 256], BF16, 3)
                            k.tt("pool", Sbf.t[:], S32.t[:], CS(k, "sm_hg"), ALU.mult, [S32.b, k.cst.b], [Sbf.b])
                            pk = k.bank(); pq = k.bank()
                            for h in range(4):
                                p = h // 2
                                k.mm(pk.t[0:64, h * 64:(h + 1) * 64], KT[p].t[:, tok], Sbf.t[:, h * 64:(h + 1) * 64], True, True, [KT[p].b, Sbf.b], [pk.b])
                                k.mm(pq.t[0:64, h * 64:(h + 1) * 64], QT[p].t[:, tok], Sbf.t[:, h * 64:(h + 1) * 64], True, True, [QT[p].b, Sbf.b], [pq.b])
                            Y = p2.rot("Y", [64, 256], F32, 1); Yb = p2.rot("Yb", [64, 256], BF16, 2)
                            k.tt("dve", v3(Y.t[:], 4), v3(pk.t[0:64, 0:256], 4), bc8(nexpd, ch, h0, 4), ALU.mult, [pk.b, nexpd.b], [Y.b])
                            k.tt("dve", Yb.t[:], Y.t[:], Vtm.t[:, ch, :], ALU.add, [Y.b, Vtm.b], [Yb.b])
                            pv = k.bank()
                            for h in range(4):
                                k.mm(pv.t[0:64, h * 64:(h + 1) * 64], TbT.t[:, ch, h * 64:(h + 1) * 64], Yb.t[:, h * 64:(h + 1) * 64], True, True, [TbT.b, Yb.b], [pv.b])
                            VN = p2.rot("VN", [64, 256], BF16, 2); VNs = p2.rot("VNs", [64, 256], BF16, 2)
                            k.cp("dve", VN.t[:], pv.t[0:64, 0:256], [pv.b], [VN.b])
                            k.tt("dve", v3(VNs.t[:], 4), v3(pv.t[0:64, 0:256], 4), bc8(edl, ch, h0, 4), ALU.mult, [pv.b, edl.b], [VNs.b])
                            pa = k.bank()
                            for h in range(4):
                                k.mm(pa.t[0:64, h * 64:(h + 1) * 64], Aqk.t[:, ch, h * 64:(h + 1) * 64], VN.t[:, h * 64:(h + 1) * 64], True, True, [Aqk.b, VN.b], [pa.b])
                            ot = p2.rot("ot", [64, 256], F32, 1)
                            k.tt("dve", v3(ot.t[:], 4), v3(pq.t[0:64, 0:256], 4), bc8(expd, ch, h0, 4), ALU.mult, [pq.b, expd.b], [ot.b])
                            k.tt("dve", ot.t[:], ot.t[:], pa.t[0:64, 0:256], ALU.add, [ot.b, pa.b], [ot.b])
                            pt_ = k.bank()
                            for h in range(4):
                                k.tr(pt_.t[0:64, h * 64:(h + 1) * 64], ot.t[:, h * 64:(h + 1) * 64], CS(k, "ident")[0:64, 0:64], [ot.b, k.cst.b], [pt_.b])
                            oview = OACC.t[0:64, :, tok]
                            if dr == 0:
                                k.cp("act", oview, v3(pt_.t[0:64, 0:256], 4), [pt_.b], [OACC.b])
                            else:
                                k.tt("dve", oview, oview, v3(pt_.t[0:64, 0:256], 4), ALU.add, [pt_.b, OACC.b], [OACC.b])
                            pS = k.bank()
                            for p in range(2):
                                k.mm(pS.t[:, p * 128:(p + 1) * 128], Ktm.t[:, ch, p * 128:(p + 1) * 128], VNs.t[:, p * 128:(p + 1) * 128], True, True, [Ktm.b, VNs.b], [pS.b])
                            for h in range(4):
                                sl = slice(h * 64, (h + 1) * 64)
                                k.stt("dve", S32.t[:, sl], S32.t[:, sl], edlast.t[:, ch, h0 + h:h0 + h + 1], pS.t[:, sl], ALU.mult, ALU.add,
                                      [S32.b, edlast.b, pS.b], [S32.b])
                        if g == 0:
                            for h in range(4):
                                k.dma("sp", k.dout["nd"][s_, l, dr, h], S32.t[(h % 2) * 64:(h % 2 + 1) * 64, h * 64:(h + 1) * 64], "stS", rd=[S32.b])
            gate = gate_proj(p1, l, IN_OFF["dn_g"], HN)
            head_norm(p1, OACC, l, 1, BR, gate)

    k.mixer_dn = mixer_dn
    k.mixer_gla = mixer_gla
    k.mixer_hg = mixer_hg
    k.gla_scan = gla_scan
    k.head_norm = head_norm
    k.gate_proj = gate_proj
    k.proj_fm = proj_fm
    k.proj_tm = proj_tm
    k.win_tile = win_tile
    k.evac_to = evac_to
    k.ffn = ffn
    k.xload = xload
    k.final = final
    k.rstd_of = rstd_of
    k.norm_mod = norm_mod
    return k


def finish(k):
    k.P.final_wait("sp")
    keys = list(ENGS) + list(k.P.dma_cnt.keys())
    sems = {key: k.es.enter_context(k.nc.semaphore("s_" + key)) for key in keys}
    with k.nc.Block() as block:
        replay(k.P, block, sems)
    k.es.close()
    return k.nc


def host_inputs(inp, core):
    b = core // 4
    f = lambda a: np.ascontiguousarray(np.asarray(a, np.float32))
    m = {
        "xp": f(inp["x_prompt"][core * 4:(core + 1) * 4].reshape(NT, D)),
        "xs": f(inp["x_sample"][b]),
        "ck": f(inp["cache_diff_k"][b]), "cv": f(inp["cache_diff_v"][b]),
        "sg": f(inp["state_gla"][b]), "sd": f(inp["state_dn"][b]), "sh": f(inp["state_hgrn"][b]),
        "pvec": pack_pvec(inp, b), "cst": make_consts(),
        "dn_al": f(inp["dn_a_log"].reshape(DEPTH, 8)), "dn_dt": f(inp["dn_dt_bias"].reshape(DEPTH, 8)),
        "dlam": f(inp["diff_lambda"].reshape(DEPTH, 128)), "gla_w2": f(inp["gla_w2"]),
    }
    for n_ in ("w_mod", "ffn1_in", "ffn2_in", "ffn1_down", "ffn2_down", "w_in", "w_branch", "w_mgate", "w_out"):
        m[n_] = f(inp[n_])
    return m


def program(k, mixers=("A", "B", "C", "D")):
    for g in range(2):
        k.xload(g)
        for l in range(DEPTH):
            k.layer(g, l, mixers)
        k.final(g)


_CACHE = {}


def get_nc():
    if "nc" not in _CACHE:
        k1 = build(None)
        program(k1)
        k2 = build(list(k1.wrec))
        program(k2)
        _CACHE["nc"] = finish(k2)
    return _CACHE["nc"]


def kernel(**inp):
    inp = {n: np.asarray(v) for n, v in inp.items()}
    nc = get_nc()
    in_maps = [host_inputs(inp, c) for c in range(8)]
    res = run_bass_kernel_spmd(nc, in_maps, core_ids=list(range(8)))
    R = res.results
    y_prompt = np.concatenate([R[c]["yp"].reshape(4, 256, D) for c in range(8)], axis=0)
    y_sample = np.stack([R[0]["ys"], R[4]["ys"]], axis=0)
    cat = lambda n_: np.concatenate([R[c][n_] for c in range(8)], axis=0)
    return (y_prompt.astype(np.float32), y_sample.astype(np.float32), cat("nk"), cat("nv"), cat("ng"), cat("nd"), cat("nh"))
```

```python
import math
from contextlib import ExitStack
import numpy as np
import concourse.bass as bass
import concourse.mybir as mybir
from concourse.bass_utils import run_bass_kernel_spmd

F32 = mybir.dt.float32
BF16 = mybir.dt.bfloat16
AF = mybir.ActivationFunctionType
ALU = mybir.AluOpType
ENGS = ("pe", "act", "dve", "pool", "sp")

D = 1024
NT = 1024
DFF = 2816
NIN = 3888
EPS = 1e-6
DEPTH = 2
CH = 64
NCH = NT // CH
FFN_STOP = 0
DF_STOP = 0
DN_STOP = 0


class Buf:
    __slots__ = ("name", "w", "r")

    def __init__(self, name=""):
        self.name = name
        self.w = None
        self.r = {}


class Prog:
    def __init__(self):
        self.ops = {e: [] for e in ENGS}
        self.cnt = {e: 0 for e in ENGS}
        self.seen = {e: {} for e in ENGS}
        self.dma_cnt = {}

    def _need(self, eng, tick, waits):
        if tick is None:
            return
        k, v = tick
        if self.seen[eng].get(k, 0) >= v:
            return
        waits[k] = max(waits.get(k, 0), v)

    def _deps(self, eng, reads, writes, is_dma):
        waits = {}
        for b in reads:
            self._need(eng, b.w, waits)
        for b in writes:
            if b.w is not None and (is_dma or b.w[0] != eng):
                self._need(eng, b.w, waits)
            for k, v in b.r.items():
                if is_dma or k != eng:
                    self._need(eng, (k, v), waits)
        for k, v in waits.items():
            self.seen[eng][k] = v
        return tuple(waits.items())

    def op(self, eng, fn, reads=(), writes=(), inc=True):
        waits = self._deps(eng, reads, writes, False)
        if inc:
            self.cnt[eng] += 1
            tv = self.cnt[eng]
        else:
            tv = self.cnt[eng] + 1
        self.ops[eng].append((waits, fn, (eng, 1) if inc else None))
        for b in reads:
            b.r[eng] = tv
        for b in writes:
            b.w = (eng, tv)
            b.r = {}

    def dma(self, eng, fn, semkey, reads=(), writes=(), n=1):
        waits = self._deps(eng, reads, writes, True)
        self.dma_cnt[semkey] = self.dma_cnt.get(semkey, 0) + 16 * n
        tick = (semkey, self.dma_cnt[semkey])
        self.ops[eng].append((waits, fn, (semkey, 16)))
        for b in reads:
            b.r[semkey] = tick[1]
        for b in writes:
            b.w = tick
            b.r = {}
        return tick

    def barrier(self, skip=lambda k: k.startswith("w") and not k.startswith("d_")):
        for e in ENGS:
            waits = {}
            for k in ENGS:
                if k != e and self.cnt[k] > 0:
                    self._need(e, (k, self.cnt[k]), waits)
            for k, v in self.dma_cnt.items():
                if not skip(k):
                    self._need(e, (k, v), waits)
            for k, v in waits.items():
                self.seen[e][k] = v
            if waits:
                self.ops[e].append((tuple(waits.items()), None, None))

    def final_wait(self, eng):
        waits = {}
        for k in ENGS:
            if k != eng and self.cnt[k] > 0:
                self._need(eng, (k, self.cnt[k]), waits)
        for k, v in self.dma_cnt.items():
            self._need(eng, (k, v), waits)
        self.ops[eng].append((tuple(waits.items()), None, None))


def replay(prog, block, sems):
    def run(name):
        def body(eng):
            for waits, fn, inc in prog.ops[name]:
                for k, v in waits:
                    eng.wait_ge(sems[k], v)
                if fn is None:
                    continue
                res = fn(eng)
                if inc is not None:
                    if isinstance(res, (list, tuple)):
                        for r in res:
                            r.then_inc(sems[inc[0]], inc[1])
                    else:
                        res.then_inc(sems[inc[0]], inc[1])
        return body
    block.tensor(run("pe"))
    block.scalar(run("act"))
    block.vector(run("dve"))
    block.gpsimd(run("pool"))
    block.sync(run("sp"))


IN_OFF = {}
_o = 0
for _n, _s in [("ga_q", 128), ("ga_k", 128), ("ga_v", 256), ("ga_r", 256), ("ga_lr", 32), ("dn_x", 768), ("dn_b", 8),
               ("dn_a", 8), ("dn_g", 256), ("hg_q", 256), ("hg_f", 512), ("hg_i", 256), ("hg_g", 256), ("df_q", 256),
               ("df_k", 256), ("df_v", 256)]:
    IN_OFF[_n] = _o
    _o += _s
assert _o == NIN

ROW = {}
_r = 0
for _n, _s in [("norm_w", DEPTH * 3 * 8), ("b_mod", DEPTH * 72), ("final", 8), ("c_ctx", 8), ("c_lat", 8), ("gla_b", DEPTH * 2),
               ("dn_conv", DEPTH * 3 * 6), ("hg_lb", DEPTH * 2 * 2), ("hnorm", DEPTH * 4)]:
    ROW[_n] = _r
    _r += _s
NROW = 384
assert _r <= NROW

CST = {}
_c = 0
for _n, _s in [("ident", 128), ("ones", 128), ("bd64", 128), ("rot", 128), ("sel_f", 128), ("sel_b", 128), ("id8", 512),
               ("mf", 256), ("mb", 256), ("rm", 1024), ("hm_gla", 4), ("hm_hg", 4),
               ("sm_gla", 256), ("sm_hg", 256), ("qm", 4), ("CSTA_END", 0),
               ("nm_c", 512), ("nm_b", 512), ("NM_END", 0),
               ("cos", 1024), ("sin", 1024)]:
    CST[_n] = (_c, _s)
    _c += _s
NCST = _c
NCSTA = CST["CSTA_END"][0]
NCSTB = CST["mf"][0]


def make_consts():
    c = np.zeros((128, NCST), np.float32)

    def put(name, arr):
        o, s = CST[name]
        a = np.zeros((128, s), np.float32)
        a[:arr.shape[0], :arr.shape[1]] = arr
        c[:, o:o + s] = a
    put("ident", np.eye(128))
    put("ones", np.ones((128, 128)))
    bd = np.zeros((128, 128)); bd[:64, :64] = 1; bd[64:, 64:] = 1
    put("bd64", bd)
    j = np.arange(64)[:, None]; i = np.arange(64)[None, :]
    put("mf", np.tile((j <= i).astype(np.float32), (1, 4)))
    put("mb", np.tile((j >= i).astype(np.float32), (1, 4)))
    rm = np.ones((128, 1024)); rm[:, ::64] = 0
    put("rm", rm)
    d = np.arange(128)[:, None]; h = np.arange(4)[None, :]
    put("hm_gla", (d // 32 == h).astype(np.float32))
    put("hm_hg", (d // 64 == h % 2).astype(np.float32))
    put("sm_gla", np.repeat((d // 32 == h).astype(np.float32), 64, axis=1))
    put("sm_hg", np.repeat((d // 64 == h % 2).astype(np.float32), 64, axis=1))
    NEG = -30000.0
    put("nm_c", np.concatenate([np.tile(np.where(j <= i, 0.0, NEG), (1, 4)), np.tile(np.where(j >= i, 0.0, NEG), (1, 4))], axis=1))
    put("nm_b", np.concatenate([np.tile(np.where(j > i, 0.0, NEG), (1, 4)), np.tile(np.where(j < i, 0.0, NEG), (1, 4))], axis=1))
    t = np.arange(1024)
    row = (t // 64).astype(np.float32); col = (t % 64).astype(np.float32)
    half = 16
    inv = (10000.0 ** (-np.arange(0, half, 2, dtype=np.float32) / half)).astype(np.float32)
    ang_r = np.concatenate([row[:, None] * inv[None, :]] * 2, axis=1)
    ang_c = np.concatenate([col[:, None] * inv[None, :]] * 2, axis=1)
    ang = np.concatenate([ang_r, ang_c], axis=1).astype(np.float32)
    cos32 = np.cos(ang).T; sin32 = np.sin(ang).T
    put("cos", np.tile(cos32, (4, 1)))
    put("sin", np.tile(sin32, (4, 1)))
    R = np.zeros((128, 128), np.float32)
    for p in range(128):
        b16 = (p // 16) * 16; dd = p % 16
        if dd < 8:
            R[b16 + dd + 8, p] = -1.0
        else:
            R[b16 + dd - 8, p] = 1.0
    put("rot", R)
    put("qm", (d // 32 == h).astype(np.float32))
    sf = np.zeros((64, 128)); sf[63, :] = 1
    sb_ = np.zeros((64, 128)); sb_[0, :] = 1
    put("sel_f", sf); put("sel_b", sb_)
    put("id8", np.tile(np.eye(64), (1, 8)))
    return c


def pack_pvec(inp, b):
    rows = np.zeros((NROW, 128), np.float32)

    def put(name, arr):
        a = np.asarray(arr, np.float32).reshape(-1, 128)
        rows[ROW[name]:ROW[name] + a.shape[0]] = a
    put("norm_w", inp["norm_w"])
    put("b_mod", inp["b_mod"])
    put("final", inp["final_norm"])
    put("c_ctx", inp["c_ctx"])
    put("c_lat", inp["c"][b])
    put("gla_b", inp["gla_b"])
    put("dn_conv", inp["dn_conv"])
    put("hg_lb", inp["hg_lb_logits"])
    hn = np.stack([np.stack([np.tile(inp[k][l], 2) for k in ("gla_norm", "dn_norm", "hg_norm", "diff_norm")]) for l in range(DEPTH)])
    put("hnorm", hn)
    return rows


class Tn:
    __slots__ = ("t", "b")

    def __init__(self, t, b):
        self.t = t
        self.b = b


class Phase:
    def __init__(self, k):
        self.k = k
        self.es = ExitStack()
        self.rots = {}

    def __enter__(self):
        self.es.__enter__()
        return self

    def sb(self, name, shape, dt=F32):
        self.k.uid += 1
        t = self.es.enter_context(self.k.nc.sbuf_tensor(f"{name}_{self.k.uid}", list(shape), dt))
        return Tn(t, Buf(name))

    def rot(self, name, shape, dt=F32, n=2):
        if name not in self.rots:
            self.rots[name] = [[self.sb(f"{name}{i}", shape, dt) for i in range(n)], 0]
        lst = self.rots[name]
        t = lst[0][lst[1] % n]
        lst[1] += 1
        return t

    def __exit__(self, *a):
        self.k.P.barrier()
        return self.es.__exit__(*a)


class K:
    def __init__(self, wplan=None, taps=(), stages=None):
        self.nc = bass.Bass("TRN2", target_bir_lowering=False)
        self.P = Prog()
        self.es = ExitStack()
        self.uid = 0
        self.wplan = wplan
        self.wrec = []
        self.wi = 0
        self.wissued = 0
        self.taps = set(taps)
        self.tap_out = {}
        self.stages = stages
        self.din = {}
        self.dout = {}
        self.bank_i = 0

    def inp(self, name, shape):
        self.din[name] = self.nc.dram_tensor(name, list(shape), F32, kind="ExternalInput").ap()
        return self.din[name]

    def outp(self, name, shape):
        self.dout[name] = self.nc.dram_tensor(name, list(shape), F32, kind="ExternalOutput").ap()
        return self.dout[name]

    def sb(self, name, shape, dt=F32):
        self.uid += 1
        t = self.es.enter_context(self.nc.sbuf_tensor(f"{name}_{self.uid}", list(shape), dt))
        return Tn(t, Buf(name))

    def phase(self):
        return Phase(self)

    def mm(self, out, lhsT, rhs, start, stop, rd, wr, inc=None):
        inc = True
        self.P.op("pe", lambda e: e.matmul(out, lhsT=lhsT, rhs=rhs, start=start, stop=stop), reads=rd, writes=wr, inc=inc)

    def tr(self, out, in_, ident, rd, wr):
        self.P.op("pe", lambda e: e.transpose(out, in_, ident), reads=rd, writes=wr)

    def act(self, out, in_, func, rd, wr, bias=0.0, scale=1.0, accum=None):
        if accum is None:
            self.P.op("act", lambda e: e.activation(out=out, in_=in_, func=func, bias=bias, scale=scale), reads=rd, writes=wr)
        else:
            self.P.op("act", lambda e: e.activation(out=out, in_=in_, func=func, bias=bias, scale=scale, accum_out=accum), reads=rd, writes=wr)

    def tt(self, eng, out, a, b, op, rd, wr):
        if eng == "pool" and op not in (ALU.add, ALU.subtract, ALU.mult):
            eng = "dve"
        self.P.op(eng, lambda e: e.tensor_tensor(out=out, in0=a, in1=b, op=op), reads=rd, writes=wr)

    def ts(self, eng, out, a, s1, s2, op0, op1, rd, wr):
        self.P.op(eng, lambda e: e.tensor_scalar(out=out, in0=a, scalar1=s1, scalar2=s2, op0=op0, op1=op1), reads=rd, writes=wr)

    def stt(self, eng, out, a, s, b, op0, op1, rd, wr):
        eng = "dve"
        self.P.op(eng, lambda e: e.scalar_tensor_tensor(out=out, in0=a, scalar=s, in1=b, op0=op0, op1=op1), reads=rd, writes=wr)

    def cp(self, eng, out, in_, rd, wr):
        if eng == "act":
            self.P.op("act", lambda e: e.copy(out=out, in_=in_), reads=rd, writes=wr)
        else:
            self.P.op(eng, lambda e: e.tensor_copy(out=out, in_=in_), reads=rd, writes=wr)

    def memset(self, eng, ap, val, wr):
        self.P.op(eng, lambda e: e.memset(ap, val), writes=wr)

    def dma(self, eng, out, in_, key, rd=(), wr=()):
        key = "d_" + (wr[0].name if wr else rd[0].name)
        return self.P.dma(eng, lambda e: e.dma_start(out=out, in_=in_), key, reads=rd, writes=wr)

    def bank(self):
        b = self.banks[self.bank_i % len(self.banks)]
        self.bank_i += 1
        return b

    def tap(self, name, tn_ap, shape, rd):
        if name not in self.taps:
            return
        o = self.outp("tap_" + name, shape)
        self.P.dma("pool", lambda e: e.dma_start(out=o, in_=tn_ap), "tap_" + name, reads=rd)

    def wsrc(self, spec, slot):
        kind = spec[0]
        if kind == "mat":
            _, name, idx, nk, ranges = spec
            W = self.din[name]
            for i in idx:
                W = W[i]
            tot = sum(n for _, n in ranges)
            view = slot.t[:, 0:nk * tot].rearrange("p (k c) -> p k c", c=tot)
            pieces = []
            o = 0
            for c0, n in ranges:
                pieces.append((view[:, :, o:o + n], W[:, c0:c0 + n].rearrange("(k p) c -> p k c", p=128)))
                o += n
            return pieces, view
        if kind == "branch":
            _, l, n = spec
            W = self.din["w_branch"][l][n]
            view = slot.t[0:64, 0:4096].rearrange("p (h f) -> p h f", f=1024)
            return [(view, W.rearrange("(h e) f -> e h f", e=64))], view
        raise ValueError(kind)

    def _wissue(self, j, spec):
        slot = self.wslots[j % len(self.wslots)]
        pieces, view = self.wsrc(spec, slot)
        key = f"w{j % len(self.wslots)}"
        self.P.dma("pool", lambda e: [e.dma_start(out=o, in_=i) for o, i in pieces], key, writes=[slot.b], n=len(pieces))

    def wget(self, spec):
        j = self.wi
        self.wi += 1
        self.wrec.append(spec)
        slot = self.wslots[j % len(self.wslots)]
        if self.wplan is None:
            self._wissue(j, spec)
        else:
            assert self.wplan[j] == spec, (j, spec, self.wplan[j])
            while self.wissued < min(j + len(self.wslots) - 1, len(self.wplan)):
                self._wissue(self.wissued, self.wplan[self.wissued])
                self.wissued += 1
        _, view = self.wsrc(spec, slot)
        return view, slot.b


def CS(k, name, lo=0, n=None):
    o, s = CST[name]
    n = s - lo if n is None else n
    return k.cst.t[:, o + lo:o + lo + n]


def CSB(k, name, lo=0, n=None):
    o, s = CST[name]
    n = s - lo if n is None else n
    return k.cstb.t[:, o + lo:o + lo + n]


def build(wplan=None, taps=(), stages=("all",)):
    k = K(wplan, taps, stages)
    nc = k.nc
    es = k.es
    st = set(stages)
    ALL = "all" in st
    xin = [k.inp("xp", [NT, D]), k.inp("xs", [NT, D])]
    k.inp("ck", [DEPTH, 4, 512, 64]); k.inp("cv", [DEPTH, 4, 512, 64])
    k.inp("sg", [DEPTH, 2, 4, 32, 64]); k.inp("sd", [DEPTH, 2, 4, 64, 64]); k.inp("sh", [DEPTH, 2, 4, 64, 64])
    k.inp("pvec", [NROW, 128]); k.inp("cst", [128, NCST])
    k.inp("dn_al", [DEPTH, 8]); k.inp("dn_dt", [DEPTH, 8]); k.inp("dlam", [DEPTH, 128]); k.inp("gla_w2", [DEPTH, 2, 16, 128])
    k.inp("w_mod", [DEPTH, D, 9 * D])
    for n_ in ("ffn1_in", "ffn2_in"):
        k.inp(n_, [DEPTH, D, 2 * DFF])
    for n_ in ("ffn1_down", "ffn2_down"):
        k.inp(n_, [DEPTH, DFF, D])
    k.inp("w_in", [DEPTH, D, NIN]); k.inp("w_branch", [DEPTH, 4, 256, D]); k.inp("w_mgate", [DEPTH, D, 4 * D]); k.inp("w_out", [DEPTH, D, D])
    yout = [k.outp("yp", [NT, D]), k.outp("ys", [NT, D])]
    k.outp("nk", [4, DEPTH, 4, 256, 64]); k.outp("nv", [4, DEPTH, 4, 256, 64])
    k.outp("ng", [4, DEPTH, 2, 4, 32, 64]); k.outp("nd", [4, DEPTH, 2, 4, 64, 64]); k.outp("nh", [4, DEPTH, 2, 4, 64, 64])

    k.cst = k.sb("cst", [128, NCSTA])
    k.cstb = k.sb("cstb", [128, NCSTB], BF16)
    k.pt = k.sb("pt", [128, NROW])
    k.mod = k.sb("mod", [128, DEPTH, 72, 2])
    k.der = k.sb("der", [128, DEPTH, 2, 9, 8])
    k.XT = k.sb("XT", [128, 8, NT])
    k.xtb = [Buf(f"xt{c}") for c in range(8)]
    k.wslots = [k.sb(f"wslot{i}", [128, 4096], BF16) for i in range(3)]
    k.xscr = nc.dram_tensor("xscr", [128, 8 * NT], F32, kind="Internal").ap()
    k.big = [es.enter_context(nc.psum_tensor(f"big{i}", [128, 1536], F32)) for i in range(2)]
    k.banks = [Tn(k.big[i // 3][:, (i % 3) * 512:(i % 3 + 1) * 512], Buf(f"bank{i}")) for i in range(6)]
    k.bbanks = [Tn(es.enter_context(nc.psum_tensor(f"bbank{i}", [128, 1024], BF16)), Buf(f"bbank{i}")) for i in range(2)]
    k.bb_i = 0
    P = k.P
    IDF = lambda: CS(k, "ident")
    IDB = lambda: CSB(k, "ident")
    ONESB = lambda: CSB(k, "ones")

    k.dma("sp", k.cst.t[:], k.din["cst"][:, 0:NCSTA], "ld", wr=[k.cst.b])
    k.dma("pool", k.cstb.t[:], k.din["cst"][:, 0:NCSTB], "ldb", wr=[k.cstb.b])
    with k.phase() as ph:
        for r in range(3):
            stg = ph.rot("pstg", [128, 128], F32, 3)
            k.dma("sp", stg.t[:], k.din["pvec"][r * 128:(r + 1) * 128, :], "ld", wr=[stg.b])
            bk = k.bank()
            k.tr(bk.t[:, 0:128], stg.t[:], IDF(), [stg.b, k.cst.b], [bk.b])
            k.cp("dve", k.pt.t[:, r * 128:(r + 1) * 128], bk.t[:, 0:128], [bk.b], [k.pt.b])
        cs = ph.sb("cs", [128, 8, 2], BF16)
        k.act(cs.t[:, :, 0], k.pt.t[:, ROW["c_ctx"]:ROW["c_ctx"] + 8], AF.Silu, [k.pt.b], [cs.b])
        k.act(cs.t[:, :, 1], k.pt.t[:, ROW["c_lat"]:ROW["c_lat"] + 8], AF.Silu, [k.pt.b], [cs.b])
        for l in range(DEPTH):
            bk = k.bank()
            for tl in range(18):
                wv, wb = k.wget(("mat", "w_mod", (l,), 8, ((tl * 512, 512),)))
                for s4 in range(4):
                    j = tl * 4 + s4
                    for kc in range(8):
                        k.mm(bk.t[:, j * 2:j * 2 + 2], wv[:, kc, s4 * 128:(s4 + 1) * 128], cs.t[:, kc, :], kc == 0, kc == 7, [wb, cs.b], [bk.b])
            o0 = ROW["b_mod"] + l * 72
            k.tt("dve", k.mod.t[:, l], bk.t[:, 0:144].rearrange("p (j w) -> p j w", w=2),
                 k.pt.t[:, o0:o0 + 72].unsqueeze(2).to_broadcast([128, 72, 2]), ALU.add, [bk.b, k.pt.b], [k.mod.b])
            for w in range(2):
                for i in range(3):
                    nw0 = ROW["norm_w"] + (l * 3 + i) * 8
                    k.stt("dve", k.der.t[:, l, w, i, :], k.mod.t[:, l, (3 * i + 1) * 8:(3 * i + 2) * 8, w], 1.0, k.pt.t[:, nw0:nw0 + 8],
                          ALU.add, ALU.mult, [k.mod.b, k.pt.b], [k.der.b])
                    k.cp("dve", k.der.t[:, l, w, 3 + i, :], k.mod.t[:, l, (3 * i) * 8:(3 * i + 1) * 8, w], [k.mod.b], [k.der.b])
                    k.ts("dve", k.der.t[:, l, w, 6 + i, :], k.mod.t[:, l, (3 * i + 2) * 8:(3 * i + 3) * 8, w], 1.0 if i == 1 else 0.5, 0.0,
                         ALU.mult, ALU.add, [k.mod.b], [k.der.b])
    k.tap("mod", k.mod.t[:].rearrange("p l j w -> p (l j w)"), [128, DEPTH * 144], [k.mod.b])

    XT = k.XT

    def xload(g):
        with k.phase() as ph:
            for tl in range(8):
                stg = ph.rot("xstg", [128, D], F32, 2)
                k.dma("sp", stg.t[:], xin[g][tl * 128:(tl + 1) * 128, :], "ldx", wr=[stg.b])
                for hb in range(2):
                    bk = k.bank()
                    for c4 in range(4):
                        c = hb * 4 + c4
                        k.tr(bk.t[:, c4 * 128:(c4 + 1) * 128], stg.t[:, c * 128:(c + 1) * 128], IDF(), [stg.b, k.cst.b], [bk.b])
                    eng = "act" if hb == 0 else "dve"
                    k.cp(eng, XT.t[:, hb * 4:(hb + 1) * 4, tl * 128:(tl + 1) * 128], bk.t[:].rearrange("p (c t) -> p c t", t=128), [bk.b],
                         [k.xtb[c] for c in range(hb * 4, hb * 4 + 4)])

    def rstd_of(ph, src_fn, nchunk, nparts, lhsT, scale, name, rd, dst=None):
        rstd = dst if dst is not None else ph.sb(name, [128, NT])
        for hb in range(2):
            bk = k.bank()
            for c in range(nchunk):
                sq = ph.rot("sq", [128, 512], BF16, 3)
                k.act(sq.t[0:nparts, :], src_fn(c, hb), AF.Square, rd, [sq.b])
                k.mm(bk.t[0:nparts, :], lhsT, sq.t[0:nparts, :], c == 0, c == nchunk - 1, [sq.b, k.cstb.b], [bk.b])
            sl = rstd.t[0:nparts, hb * 512:(hb + 1) * 512]
            k.ts("dve", sl, bk.t[0:nparts, :], scale, EPS, ALU.mult, ALU.add, [bk.b], [rstd.b])
            k.act(sl, sl, AF.Sqrt, [rstd.b], [rstd.b])
            k.P.op("dve", (lambda sl_: (lambda e: e.reciprocal(out=sl_, in_=sl_)))(sl), reads=[rstd.b], writes=[rstd.b])
        return rstd

    def norm_mod(ph, l, w, i, HN):
        rstd = rstd_of(ph, lambda c, hb: XT.t[:, c, hb * 512:(hb + 1) * 512], 8, 128, ONESB(), 1.0 / D, "rstd", k.xtb)
        for c in range(8):
            tmp = ph.rot("ntmp", [128, NT], F32, 2)
            k.tt("dve", tmp.t[:], XT.t[:, c, :], rstd.t[:], ALU.mult, [k.xtb[c], rstd.b], [tmp.b])
            k.act(HN.t[:, c, :], tmp.t[:], AF.Identity, [tmp.b, k.der.b], [HN.b],
                  bias=k.der.t[:, l, w, 3 + i, c:c + 1], scale=k.der.t[:, l, w, i, c:c + 1])

    def ffn(l, w, i, win, wdown):
        with k.phase() as ph:
            HN = ph.sb("HN", [128, 8, NT], BF16)
            FA = ph.sb("FA", [128, 22, NT], BF16)
            norm_mod(ph, l, w, i, HN)
            if FFN_STOP == 1:
                return
            for j in range(11 if FFN_STOP != 2 else 1):
                wv, wb = k.wget(("mat", win, (l,), 8, ((j * 256, 256), (DFF + j * 256, 256))))
                for sub in range(2):
                    for hb in range(2):
                        bg = k.bank(); bu = k.bank()
                        for kc in range(8):
                            k.mm(bg.t[:], wv[:, kc, sub * 128:(sub + 1) * 128], HN.t[:, kc, hb * 512:(hb + 1) * 512], kc == 0, kc == 7, [wb, HN.b], [bg.b])
                        for kc in range(8):
                            k.mm(bu.t[:], wv[:, kc, 256 + sub * 128:256 + (sub + 1) * 128], HN.t[:, kc, hb * 512:(hb + 1) * 512], kc == 0, kc == 7, [wb, HN.b], [bu.b])
                        sg = ph.rot("sg", [128, 512], F32, 3)
                        k.act(sg.t[:], bg.t[:], AF.Silu, [bg.b], [sg.b])
                        k.tt("dve", FA.t[:, j * 2 + sub, hb * 512:(hb + 1) * 512], sg.t[:], bu.t[:], ALU.mult, [sg.b, bu.b], [FA.b])
            if FFN_STOP in (2, 3):
                return
            for fo in range(8):
                wv, wb = k.wget(("mat", wdown, (l,), 22, ((fo * 128, 128),)))
                for hb in range(2):
                    bk = k.bank()
                    for kc in range(22):
                        k.mm(bk.t[:], wv[:, kc, :], FA.t[:, kc, hb * 512:(hb + 1) * 512], kc == 0, kc == 21, [wb, FA.b], [bk.b])
                    xs_ = XT.t[:, fo, hb * 512:(hb + 1) * 512]
                    k.stt("dve", xs_, bk.t[:], k.der.t[:, l, w, 6 + i, fo:fo + 1], xs_, ALU.mult, ALU.add, [bk.b, k.der.b, k.xtb[fo]], [k.xtb[fo]])

    def final(g):
        with k.phase() as ph:
            rstd = rstd_of(ph, lambda c, hb: XT.t[:, c, hb * 512:(hb + 1) * 512], 8, 128, ONESB(), 1.0 / D, "rstd", k.xtb)
            f0 = ROW["final"]
            for c in range(8):
                tmp = ph.rot("ntmp", [128, NT], F32, 2)
                k.stt("dve", tmp.t[:], XT.t[:, c, :], k.pt.t[:, f0 + c:f0 + c + 1], rstd.t[:], ALU.mult, ALU.mult, [k.xtb[c], rstd.b, k.pt.b], [tmp.b])
                for tl in range(8):
                    pass
                k.cp("pool", XT.t[:, c, :], tmp.t[:], [tmp.b], [k.xtb[c]])
            for tl in range(8):
                stg = ph.rot("ystg", [128, D], F32, 2)
                for hb in range(2):
                    bk = k.bank()
                    for c4 in range(4):
                        c = hb * 4 + c4
                        k.tr(bk.t[:, c4 * 128:(c4 + 1) * 128], XT.t[:, c, tl * 128:(tl + 1) * 128], IDF(), [k.xtb[c], k.cst.b], [bk.b])
                    k.cp("act" if hb == 0 else "dve", stg.t[:, hb * 512:(hb + 1) * 512], bk.t[:], [bk.b], [stg.b])
                k.dma("sp", yout[g][tl * 128:(tl + 1) * 128, :], stg.t[:], "sty", rd=[stg.b])


    def proj_fm(wv, wb, c0, n, HN, evac):
        for hb in range(2):
            bk = k.bank()
            for kc in range(8):
                k.mm(bk.t[0:n, :], wv[:, kc, c0:c0 + n], HN.t[:, kc, hb * 512:(hb + 1) * 512], kc == 0, kc == 7, [wb, HN.b], [bk.b])
            evac(hb, bk)

    def proj_tm(wv, wb, c0, n, HN, tok0, M, evac):
        bk = k.bank()
        for kc in range(8):
            k.mm(bk.t[0:M, 0:n], HN.t[:, kc, tok0:tok0 + M], wv[:, kc, c0:c0 + n], kc == 0, kc == 7, [HN.b, wb], [bk.b])
        evac(bk)

    def win_tile(l, c0, n):
        return k.wget(("mat", "w_in", (l,), 8, ((c0, n),)))

    def evac_to(dst, dstb, func=None, n=128, eng="act", scale=1.0):
        def f(hb, bk):
            o = dst[0:n, hb * 512:(hb + 1) * 512]
            if func is not None:
                k.act(o, bk.t[0:n, :], func, [bk.b], [dstb], scale=scale)
            else:
                k.cp(eng, o, bk.t[0:n, :], [bk.b], [dstb])
        return f

    def head_norm(ph, OACC, l, n, BR, gate=None, extra=1.0):
        w0 = ROW["hnorm"] + l * 4 + n
        with k.phase() as ph2:
            rstd = ph2.sb("hrstd", [128, NT])
            for h in range(4):
                k.rstd_of(ph2, lambda c, hb: OACC.t[0:64, h, hb * 512:(hb + 1) * 512], 1, 64, CSB(k, "ones")[0:64, 0:64], 1.0 / 64, "hrstd", [OACC.b], dst=rstd)
                t1 = ph2.rot("hn_t1", [64, NT], F32, 1)
                k.tt("dve", t1.t[:], OACC.t[0:64, h, :], rstd.t[0:64, :], ALU.mult, [OACC.b, rstd.b], [t1.b])
                if gate is not None:
                    k.stt("dve", BR.t[0:64, n, h, :], t1.t[:], k.pt.t[0:64, w0:w0 + 1], gate.t[0:64, h, :], ALU.mult, ALU.mult, [t1.b, k.pt.b, gate.b], [BR.b])
                else:
                    k.ts("pool", BR.t[0:64, n, h, :], t1.t[:], k.pt.t[0:64, w0:w0 + 1], extra, ALU.mult, ALU.mult, [t1.b, k.pt.b], [BR.b])

    def gate_proj(ph, l, c0, HN):
        gate = ph.sb("gate", [64, 4, NT], BF16)
        wv, wb = win_tile(l, c0, 256)
        for h in range(4):
            proj_fm(wv, wb, h * 64, 64, HN, evac_to(gate.t[:, h, :], gate.b, AF.Silu, 64))
        return gate

    def gla_scan(ph, g, l, n, KG, qT, qscale, kT, gT, vtm, hmname, smname, sin_name, sout_name, dk, OACC):
        nseq = 4 if g == 0 else 1
        cps = NCH // nseq
        HPG = 4 // KG
        MASK = {0: CS(k, "mf"), 1: CS(k, "mb")}
        SMv = CS(k, smname)
        for dr in range(2):
            with k.phase() as p2:
                qg = [p2.sb(f"qg{kg}", [128, NT], BF16) for kg in range(KG)]
                kx = [p2.sb(f"kx{h}", [128, NT], BF16) for h in range(4)]
                kd = [p2.sb(f"kd{kg}", [128, NT], BF16) for kg in range(KG)]
                eref = [p2.sb(f"eref{kg}", [128, NCH]) for kg in range(KG)]
                elast = [p2.sb(f"elast{kg}", [128, NCH]) for kg in range(KG)]
                tl = 63 if dr == 0 else 0
                for kg in range(KG):
                    gsrc = gT[dr][kg]; ksrc = kT[dr][kg]
                    cum = p2.rot("cum", [128, NT], F32, 2)
                    k.P.op("dve", (lambda o_, g_: (lambda e: e.tensor_tensor_scan(out=o_, data0=CS(k, "rm"), data1=g_, initial=0.0, op0=ALU.mult, op1=ALU.add)))(cum.t[:], gsrc.t[:]),
                           reads=[gsrc.b, k.cst.b], writes=[cum.b])
                    c3 = lambda t_: t_.t[:].rearrange("p (c j) -> p c j", j=64)
                    if dr == 1:
                        c2 = p2.rot("cum", [128, NT], F32, 2)
                        k.tt("dve", c2.t[:], gsrc.t[:], cum.t[:], ALU.subtract, [gsrc.b, cum.b], [c2.b])
                        k.tt("dve", c3(c2), c3(c2), c3(cum)[:, :, 63:64].to_broadcast([128, NCH, 64]), ALU.add, [c2.b, cum.b], [c2.b])
                        cum = c2
                    bm = p2.rot("bm", [128, NT], F32, 1)
                    k.tt("dve", c3(bm), c3(cum), c3(cum)[:, :, 32:33].to_broadcast([128, NCH, 64]), ALU.subtract, [cum.b], [bm.b])
                    e1 = p2.rot("ee", [128, NT], F32, 2)
                    k.act(e1.t[:], bm.t[:], AF.Exp, [bm.b], [e1.b])
                    k.stt("pool", qg[kg].t[:], qT[kg].t[:], qscale, e1.t[:], ALU.mult, ALU.mult, [qT[kg].b, e1.b], [qg[kg].b])
                    e2 = p2.rot("ee", [128, NT], F32, 2)
                    k.act(e2.t[:], bm.t[:], AF.Exp, [bm.b], [e2.b], scale=-1.0)
                    for hh in range(HPG):
                        h = kg * HPG + hh
                        k.stt("dve" if hh % 2 == 0 else "pool", kx[h].t[:], ksrc.t[:], CS(k, hmname, h, 1), e2.t[:], ALU.mult, ALU.mult, [ksrc.b, e2.b, k.cst.b], [kx[h].b])
                    bl = p2.rot("bm", [128, NT], F32, 1)
                    k.tt("dve", c3(bl), c3(cum)[:, :, tl:tl + 1].to_broadcast([128, NCH, 64]), c3(cum), ALU.subtract, [cum.b], [bl.b])
                    e3 = p2.rot("ee", [128, NT], F32, 2)
                    k.act(e3.t[:], bl.t[:], AF.Exp, [bl.b], [e3.b])
                    k.tt("pool", kd[kg].t[:], ksrc.t[:], e3.t[:], ALU.mult, [ksrc.b, e3.b], [kd[kg].b])
                    k.act(eref[kg].t[:], c3(cum)[:, :, 32], AF.Exp, [cum.b], [eref[kg].b])
                    k.act(elast[kg].t[:], c3(cum)[:, :, tl], AF.Exp, [cum.b], [elast[kg].b])
                S32 = p2.sb("S32", [128, 256])
                ATs = [p2.sb(f"AT{i_}", [64, 256], BF16) for i_ in range(3)]
                for a_ in ATs:
                    k.memset("pool", a_.t[:], 0.0, [a_.b])
                kdTs = [p2.sb(f"kdT{i_}", [64, 256], BF16) for i_ in range(4)]
                ATs = ATs + [p2.sb("AT3", [64, 256], BF16)]
                k.memset("pool", ATs[3].t[:], 0.0, [ATs[3].b])
                v4 = lambda ap_: ap_.rearrange("p (h i) -> p h i", i=64)

                def stageA(ch, slot):
                    tok = slice(ch * 64, (ch + 1) * 64)
                    bb = k.bbanks[k.bb_i % 2]; k.bb_i += 1
                    for kg in range(KG):
                        k.tr(bb.t[0:64, kg * 128:(kg + 1) * 128], kd[kg].t[:, tok], CSB(k, "ident"), [kd[kg].b, k.cstb.b], [bb.b])
                    kdT = kdTs[slot % 4]
                    k.cp("act", kdT.t[:, 0:KG * 128], bb.t[0:64, 0:KG * 128], [bb.b], [kdT.b])
                    pa = k.bank()
                    t0 = ch * 64
                    lo = slice(t0, t0 + 32); hi = slice(t0 + 32, t0 + 64)
                    for h in range(4):
                        qh = qg[h // HPG]
                        rdm = [kx[h].b, qh.b]
                        if dr == 0:
                            k.mm(pa.t[0:32, h * 64:(h + 1) * 64], kx[h].t[:, lo], qh.t[:, tok], True, True, rdm, [pa.b])
                            k.mm(pa.t[32:64, h * 64 + 32:(h + 1) * 64], kx[h].t[:, hi], qh.t[:, hi], True, True, rdm, [pa.b])
                        else:
                            k.mm(pa.t[0:32, h * 64:h * 64 + 32], kx[h].t[:, lo], qh.t[:, lo], True, True, rdm, [pa.b])
                            k.mm(pa.t[32:64, h * 64:(h + 1) * 64], kx[h].t[:, hi], qh.t[:, tok], True, True, rdm, [pa.b])
                    AT = ATs[slot % 4]
                    if dr == 0:
                        k.tt("dve", AT.t[0:32, :], pa.t[0:32, 0:256], MASK[dr][0:32, :], ALU.mult, [pa.b, k.cst.b], [AT.b])
                        k.tt("dve", v4(AT.t[32:64, :])[:, :, 32:64], v4(pa.t[32:64, 0:256])[:, :, 32:64], v4(MASK[dr][32:64, :])[:, :, 32:64], ALU.mult, [pa.b, k.cst.b], [AT.b])
                    else:
                        k.tt("dve", v4(AT.t[0:32, :])[:, :, 0:32], v4(pa.t[0:32, 0:256])[:, :, 0:32], v4(MASK[dr][0:32, :])[:, :, 0:32], ALU.mult, [pa.b, k.cst.b], [AT.b])
                        k.tt("dve", AT.t[32:64, :], pa.t[32:64, 0:256], MASK[dr][32:64, :], ALU.mult, [pa.b, k.cst.b], [AT.b])
                    return kdT, AT

                def stageB(ch, kdT, AT):
                    tok = slice(ch * 64, (ch + 1) * 64)
                    ps_ = k.bank()
                    if KG == 1:
                        k.mm(ps_.t[:, 0:256], kdT.t[:, 0:128], vtm.t[:, ch, :], True, True, [kdT.b, vtm.b], [ps_.b])
                    else:
                        for kg in range(2):
                            k.mm(ps_.t[:, kg * 128:(kg + 1) * 128], kdT.t[:, kg * 128:(kg + 1) * 128], vtm.t[:, ch, kg * 128:(kg + 1) * 128], True, True, [kdT.b, vtm.b], [ps_.b])
                    Sbf = p2.rot("Sbf", [128, 256], BF16, 3)
                    if KG == 1:
                        k.stt("dve", Sbf.t[:], S32.t[:], eref[0].t[:, ch:ch + 1], SMv, ALU.mult, ALU.mult, [S32.b, eref[0].b, k.cst.b], [Sbf.b])
                    else:
                        for kg in range(2):
                            k.stt("dve", Sbf.t[:, kg * 128:(kg + 1) * 128], S32.t[:, kg * 128:(kg + 1) * 128], eref[kg].t[:, ch:ch + 1], SMv[:, kg * 128:(kg + 1) * 128],
                                  ALU.mult, ALU.mult, [S32.b, eref[kg].b, k.cst.b], [Sbf.b])
                    if KG == 1:
                        k.stt("dve", S32.t[:], S32.t[:], elast[0].t[:, ch:ch + 1], ps_.t[:, 0:256], ALU.mult, ALU.add, [S32.b, elast[0].b, ps_.b], [S32.b])
                    else:
                        for kg in range(2):
                            sl = slice(kg * 128, (kg + 1) * 128)
                            k.stt("dve", S32.t[:, sl], S32.t[:, sl], elast[kg].t[:, ch:ch + 1], ps_.t[:, sl], ALU.mult, ALU.add, [S32.b, elast[kg].b, ps_.b], [S32.b])
                    po = k.bank()
                    for h in range(4):
                        o_ = po.t[0:64, h * 64:(h + 1) * 64]
                        k.mm(o_, vtm.t[:, ch, h * 64:(h + 1) * 64], AT.t[:, h * 64:(h + 1) * 64], True, False, [vtm.b, AT.b], [po.b])
                        k.mm(o_, Sbf.t[:, h * 64:(h + 1) * 64], qg[h // HPG].t[:, tok], False, True, [Sbf.b, qg[h // HPG].b], [po.b])
                    oview = OACC.t[0:64, :, tok]
                    pview = po.t[0:64, 0:256].rearrange("p (h i) -> p h i", i=64)
                    if dr == 0:
                        k.cp("act", oview, pview, [po.b], [OACC.b])
                    else:
                        k.tt("dve", oview, oview, pview, ALU.add, [po.b, OACC.b], [OACC.b])

                seqlist = []
                for s_ in range(nseq):
                    order = range(cps) if dr == 0 else range(cps - 1, -1, -1)
                    for ii, ci in enumerate(order):
                        seqlist.append((s_, s_ * cps + ci, ii == 0, ii == cps - 1))
                DEPTH_A = 2
                pend = {}
                for j in range(min(DEPTH_A, len(seqlist))):
                    pend[j] = stageA(seqlist[j][1], j)
                for j, (s_, ch, first, last) in enumerate(seqlist):
                    if first:
                        k.memset("pool", S32.t[:], 0.0, [S32.b])
                        if g == 1:
                            for h in range(4):
                                hh = h % HPG
                                k.dma("sp", S32.t[hh * dk:(hh + 1) * dk, h * 64:(h + 1) * 64] if KG == 2 else S32.t[h * dk:(h + 1) * dk, h * 64:(h + 1) * 64],
                                      k.din[sin_name][l, dr, h], "ldS", wr=[S32.b])
                    kdT, AT = pend.pop(j)
                    stageB(ch, kdT, AT)
                    if j + DEPTH_A < len(seqlist):
                        pend[j + DEPTH_A] = stageA(seqlist[j + DEPTH_A][1], j + DEPTH_A)
                    if last and g == 0:
                        for h in range(4):
                            hh = h % HPG
                            src = S32.t[hh * dk:(hh + 1) * dk, h * 64:(h + 1) * 64] if KG == 2 else S32.t[h * dk:(h + 1) * dk, h * 64:(h + 1) * 64]
                            k.dma("sp", k.dout[sout_name][s_, l, dr, h], src, "stS", rd=[S32.b])

    def mixer_gla(ph, g, l, HN, BR):
        with k.phase() as p1:
            o = IN_OFF["ga_q"]
            wv, wb = win_tile(l, 0, 512)
            qT = p1.sb("qT", [128, NT]); kTt = p1.sb("kT", [128, NT])
            proj_fm(wv, wb, 0, 128, HN, evac_to(qT.t, qT.b))
            proj_fm(wv, wb, 128, 128, HN, evac_to(kTt.t, kTt.b, eng="dve"))
            vtm = p1.sb("vtm", [64, NCH, 256], BF16)
            for ch in range(NCH):
                proj_tm(wv, wb, 256, 256, HN, ch * 64, 64, (lambda ch_: (lambda bk: k.cp("act" if ch_ % 2 else "dve", vtm.t[:, ch_, :], bk.t[0:64, 0:256], [bk.b], [vtm.b])))(ch))
            wv2, wb2 = win_tile(l, 768, 32)
            lrT = p1.sb("lrT", [32, NT], BF16)
            proj_fm(wv2, wb2, 0, 32, HN, evac_to(lrT.t, lrT.b, n=32))
            w2p = p1.sb("w2p", [32, 2, 128], BF16)
            k.memset("pool", w2p.t[:], 0.0, [w2p.b])
            for dr in range(2):
                k.dma("pool", w2p.t[dr * 16:(dr + 1) * 16, dr, :], k.din["gla_w2"][l, dr], "ldw2", wr=[w2p.b])
            negb = p1.sb("negb", [128, 2])
            gb0 = ROW["gla_b"] + l * 2
            k.ts("dve", negb.t[:], k.pt.t[:, gb0:gb0 + 2], -1.0, 0.0, ALU.mult, ALU.add, [k.pt.b], [negb.b])
            gT = [[Tn(k.XT.t[:, 4 + dr, :], Buf(f"gT{dr}"))] for dr in range(2)]
            for dr in range(2):
                for hb in range(2):
                    bk = k.bank()
                    k.mm(bk.t[:], w2p.t[:, dr, :], lrT.t[:, hb * 512:(hb + 1) * 512], True, True, [w2p.b, lrT.b], [bk.b])
                    sl = gT[dr][0].t[:, hb * 512:(hb + 1) * 512]
                    k.act(sl, bk.t[:], AF.Exp, [bk.b, negb.b], [gT[dr][0].b], bias=negb.t[:, dr:dr + 1], scale=-1.0)
                    k.act(sl, sl, AF.Ln, [gT[dr][0].b], [gT[dr][0].b], bias=1.0)
                    k.ts("dve", sl, sl, -1.0 / 16.0, 0.0, ALU.mult, ALU.add, [gT[dr][0].b], [gT[dr][0].b])
            OACC = Tn(k.XT.t[0:64, 0:4, :], Buf("OACC"))
            gla_scan(p1, g, l, 0, 1, [qT], 32 ** -0.5, [[kTt], [kTt]], gT, vtm, "hm_gla", "sm_gla", "sg", "ng", 32, OACC)
            gate = gate_proj(p1, l, 512, HN)
            head_norm(p1, OACC, l, 0, BR, gate)

    def mixer_hg(ph, g, l, HN, BR):
        with k.phase() as p1:
            c0 = IN_OFF["hg_q"]
            wv, wb = win_tile(l, c0, 512)
            wv2, wb2 = win_tile(l, c0 + 512, 512)
            qT = [p1.sb(f"hq{kg}", [128, NT], BF16) for kg in range(2)]
            for kg in range(2):
                proj_fm(wv, wb, kg * 128, 128, HN, evac_to(qT[kg].t, qT[kg].b, AF.Silu))
            vtm = p1.sb("vtm", [64, NCH, 256], BF16)
            for ch in range(NCH):
                proj_tm(wv2, wb2, 256, 256, HN, ch * 64, 64, (lambda ch_: (lambda bk: k.cp("act" if ch_ % 2 else "dve", vtm.t[:, ch_, :], bk.t[0:64, 0:256], [bk.b], [vtm.b])))(ch))
            lb = p1.sb("lb", [128, 4]); oml = p1.sb("oml", [128, 4])
            r0 = ROW["hg_lb"]
            if l == 0:
                k.memset("pool", lb.t[:], 0.0, [lb.b])
            else:
                ex = p1.sb("lbex", [128, 8])
                k.act(ex.t[:], k.pt.t[:, r0:r0 + 8], AF.Exp, [k.pt.b], [ex.b])
                sm_ = p1.sb("lbsum", [128, 4])
                k.tt("dve", sm_.t[:], ex.t[:, 0:4], ex.t[:, 4:8], ALU.add, [ex.b], [sm_.b])
                k.P.op("dve", lambda e: e.reciprocal(out=sm_.t[:], in_=sm_.t[:]), reads=[sm_.b], writes=[sm_.b])
                k.tt("dve", lb.t[:], ex.t[:, 4:8], sm_.t[:], ALU.mult, [ex.b, sm_.b], [lb.b])
            k.ts("dve", oml.t[:], lb.t[:], -1.0, 1.0, ALU.mult, ALU.add, [lb.b], [oml.b])
            kT = [[p1.sb(f"hk{dr}{kg}", [128, NT], BF16) for kg in range(2)] for dr in range(2)]
            gT = [[Tn(k.XT.t[:, 4 + dr * 2 + kg, :], Buf(f"hg{dr}{kg}")) for kg in range(2)] for dr in range(2)]
            for dr in range(2):
                for kg in range(2):
                    wsrc_, wbuf_, cc = (wv, wb, 256 + kg * 128) if dr == 0 else (wv2, wb2, kg * 128)
                    idx = dr * 2 + kg

                    def ev(hb, bk, dr=dr, kg=kg, idx=idx):
                        sl = slice(hb * 512, (hb + 1) * 512)
                        f_ = p1.rot("hf", [128, 512], F32, 1)
                        k.act(f_.t[:], bk.t[:], AF.Sigmoid, [bk.b], [f_.b])
                        k.ts("dve", f_.t[:], f_.t[:], oml.t[:, idx:idx + 1], lb.t[:, idx:idx + 1], ALU.mult, ALU.add, [f_.b, oml.b, lb.b], [f_.b])
                        k.act(gT[dr][kg].t[:, sl], f_.t[:], AF.Ln, [f_.b], [gT[dr][kg].b])
                        k.ts("pool", kT[dr][kg].t[:, sl], f_.t[:], -1.0, 1.0, ALU.mult, ALU.add, [f_.b], [kT[dr][kg].b])
                    proj_fm(wsrc_, wbuf_, cc, 128, HN, ev)
            OACC = Tn(k.XT.t[0:64, 0:4, :], Buf("OACC"))
            gla_scan(p1, g, l, 2, 2, qT, 0.125, kT, gT, vtm, "hm_hg", "sm_hg", "sh", "nh", 64, OACC)
            gate = gate_proj(p1, l, IN_OFF["hg_g"], HN)
            head_norm(p1, OACC, l, 2, BR, gate)


    def lam_setup(ph, l):
        dl = ph.sb("dl", [128, 128])
        k.dma("sp", dl.t[:], k.din["dlam"][l:l + 1, :].partition_broadcast(128), "lddl", wr=[dl.b])
        pr = ph.sb("dlp", [128, 2, 32])
        k.tt("dve", pr.t[:, 0, :], dl.t[:, 0:32], dl.t[:, 32:64], ALU.mult, [dl.b], [pr.b])
        k.tt("dve", pr.t[:, 1, :], dl.t[:, 64:96], dl.t[:, 96:128], ALU.mult, [dl.b], [pr.b])
        sm_ = ph.sb("dls", [128, 2])
        k.P.op("dve", lambda e: e.reduce_sum(out=sm_.t[:], in_=pr.t[:], axis=mybir.AxisListType.X), reads=[pr.b], writes=[sm_.b])
        k.act(sm_.t[:], sm_.t[:], AF.Exp, [sm_.b], [sm_.b])
        lam = ph.sb("lam", [128, 2])
        lam_init = 0.8 - 0.6 * math.exp(-0.3 * l)
        k.stt("dve", lam.t[:, 0:1], sm_.t[:, 0:1], lam_init, sm_.t[:, 1:2], ALU.add, ALU.subtract, [sm_.b], [lam.b])
        k.ts("dve", lam.t[:, 1:2], lam.t[:, 0:1], -1.0, 0.0, ALU.mult, ALU.add, [lam.b], [lam.b])
        return lam, lam_init

    def mixer_df(ph, g, l, HN, BR):
        with k.phase() as p1:
            lam, lam_init = lam_setup(p1, l)
            c0 = IN_OFF["df_q"]
            wv, wb = win_tile(l, c0, 512)
            wv2, wb2 = win_tile(l, c0 + 512, 256)
            NK = 256 if g == 0 else 1536
            NKT = NK // 128
            koff = 0 if g == 0 else 512
            qT = [Tn(k.XT.t[:, 6 + cp, :], Buf(f"dq{cp}")) for cp in range(2)]
            kTf = [Tn(k.XT.t[:, 4 + cp, :], Buf(f"dkf{cp}")) for cp in range(2)]
            KT = [p1.sb(f"dK{cp}", [128, koff + NT], BF16) for cp in range(2)]
            for cp in range(2):
                proj_fm(wv, wb, cp * 128, 128, HN, evac_to(qT[cp].t, qT[cp].b))
                proj_fm(wv, wb, 256 + cp * 128, 128, HN, evac_to(kTf[cp].t, kTf[cp].b, eng="dve"))
            if DF_STOP == 1:
                return
            if g == 1:
                rope = p1.sb("rope", [128, 2048])
                k.dma("sp", rope.t[:], k.din["cst"][:, CST["cos"][0]:CST["cos"][0] + 2048], "ldrope", wr=[rope.b])
                for src in qT + kTf:
                    xb_ = p1.rot("ropexb", [128, NT], BF16, 2)
                    k.cp("pool", xb_.t[:], src.t[:], [src.b], [xb_.b])
                    for hb in range(2):
                        sl = slice(hb * 512, (hb + 1) * 512)
                        bk = k.bank()
                        k.mm(bk.t[:], CSB(k, "rot"), xb_.t[:, sl], True, True, [xb_.b, k.cstb.b], [bk.b])
                        t2 = p1.rot("ropet", [128, 512], F32, 2)
                        k.tt("dve", t2.t[:], bk.t[:], rope.t[:, 1024 + hb * 512:1024 + (hb + 1) * 512], ALU.mult, [bk.b, rope.b], [t2.b])
                        k.tt("pool", src.t[:, sl], src.t[:, sl], rope.t[:, sl], ALU.mult, [src.b, rope.b], [src.b])
                        k.tt("dve", src.t[:, sl], src.t[:, sl], t2.t[:], ALU.add, [src.b, t2.b], [src.b])
                for cp in range(2):
                    for kt in range(4):
                        stg = p1.rot("ckstg", [128, 2, 64], F32, 2)
                        k.dma("sp", stg.t[:], k.din["ck"][l, 2 * cp:2 * cp + 2, kt * 128:(kt + 1) * 128, :].rearrange("h t d -> t h d"), "ldck", wr=[stg.b])
                        bk = k.bank()
                        k.tr(bk.t[:, 0:128], stg.t[:].rearrange("p h d -> p (h d)"), CS(k, "ident"), [stg.b, k.cst.b], [bk.b])
                        k.cp("act", KT[cp].t[:, kt * 128:(kt + 1) * 128], bk.t[:, 0:128], [bk.b], [KT[cp].b])
            for cp in range(2):
                k.cp("pool", KT[cp].t[:, koff:koff + NT], kTf[cp].t[:], [kTf[cp].b], [KT[cp].b])
            if DF_STOP == 6:
                return
            NVT = (koff + NT) // 128
            Vtm = p1.sb("Vtm", [128, NVT, 256], BF16)
            if g == 1:
                for kt in range(4):
                    k.dma("pool", Vtm.t[:, kt, :].rearrange("p (h d) -> p h d", d=64), k.din["cv"][l, :, kt * 128:(kt + 1) * 128, :].rearrange("h t d -> t h d"), "ldcv", wr=[Vtm.b])
            for tl in range(8):
                def evv(bk, tl=tl):
                    if g == 0:
                        so = p1.rot("vout", [128, 256], F32, 2)
                        k.cp("dve", so.t[:], bk.t[:, 0:256], [bk.b], [so.b])
                        k.cp("act", Vtm.t[:, koff // 128 + tl, :], so.t[:], [so.b], [Vtm.b])
                        s_ = tl // 2; t0 = (tl % 2) * 128
                        k.dma("sp", k.dout["nv"][s_, l, :, t0:t0 + 128, :].rearrange("h t d -> t h d"), so.t[:].rearrange("p (h d) -> p h d", d=64), "stv", rd=[so.b])
                    else:
                        k.cp("act", Vtm.t[:, koff // 128 + tl, :], bk.t[:, 0:256], [bk.b], [Vtm.b])
                proj_tm(wv2, wb2, 0, 256, HN, tl * 128, 128, evv)
                if g == 0 and DF_STOP not in (7, 8):
                    def evk(bk, tl=tl):
                        so = p1.rot("kout", [128, 256], F32, 2)
                        k.cp("dve", so.t[:], bk.t[:, 0:256], [bk.b], [so.b])
                        s_ = tl // 2; t0 = (tl % 2) * 128
                        if True:
                            k.dma("sp", k.dout["nk"][s_, l, :, t0:t0 + 128, :].rearrange("h t d -> t h d"), so.t[:].rearrange("p (h d) -> p h d", d=64), "stk", rd=[so.b])
                    proj_tm(wv, wb, 256, 256, HN, tl * 128, 128, evk)
            if DF_STOP in (2, 5, 7, 8):
                return
            QX = [[p1.sb(f"QX{cp}{i}", [128, NT], BF16) for i in range(4)] for cp in range(2)]
            for cp in range(2):
                for i in range(4):
                    k.ts("dve" if i % 2 else "pool", QX[cp][i].t[:], qT[cp].t[:], CS(k, "qm", i, 1), 0.0, ALU.mult, ALU.add, [qT[cp].b, k.cst.b], [QX[cp][i].b])
            if DF_STOP == 3:
                return
            OACC = Tn(k.XT.t[0:64, 0:4, :], Buf("OACC"))
            scale = 32 ** -0.5
            bigb = [[k.banks[i * 3 + j].b for j in range(3)] for i in range(2)]
            for qb in range(8):
                qs = slice(qb * 128, (qb + 1) * 128)
                k0 = (qb // 2) * 256 if g == 0 else 0
                for h in range(4):
                    cp = h // 2; hh = h % 2
                    Pm = []; rr = []
                    for m in range(2):
                        big = k.big[m]; bb_ = bigb[m]
                        for sg_ in range((NK + 511) // 512):
                            n_ = min(512, NK - sg_ * 512)
                            k.mm(big[:, sg_ * 512:sg_ * 512 + n_], QX[cp][hh * 2 + m].t[:, qs], KT[cp].t[:, k0 + sg_ * 512:k0 + sg_ * 512 + n_], True, True,
                                 [QX[cp][hh * 2 + m].b, KT[cp].b], bb_)
                        mx = p1.rot("mx", [128, 1], F32, 4)
                        k.P.op("dve", (lambda o_, i_: (lambda e: e.reduce_max(out=o_, in_=i_, axis=mybir.AxisListType.X)))(mx.t[:], big[:, 0:NK]), reads=bb_, writes=[mx.b])
                        k.ts("dve", mx.t[:], mx.t[:], -scale, 0.0, ALU.mult, ALU.add, [mx.b], [mx.b])
                        Pt = p1.rot("Pt", [128, NK], BF16, 2)
                        rs = p1.rot("rs", [128, 1], F32, 4)
                        k.act(Pt.t[:], big[:, 0:NK], AF.Exp, bb_ + [mx.b], [Pt.b, rs.b], bias=mx.t[:], scale=scale, accum=rs.t[:])
                        k.P.op("dve", (lambda o_: (lambda e: e.reciprocal(out=o_, in_=o_)))(rs.t[:]), reads=[rs.b], writes=[rs.b])
                        if m == 1:
                            k.tt("dve", rs.t[:], rs.t[:], lam.t[:, 1:2], ALU.mult, [rs.b, lam.b], [rs.b])
                        Pm.append(Pt); rr.append(rs)
                    A = p1.rot("Amat", [128, NK], BF16, 1)
                    k.ts("pool", A.t[:], Pm[0].t[:], rr[0].t[:], 0.0, ALU.mult, ALU.add, [Pm[0].b, rr[0].b], [A.b])
                    k.stt("dve", A.t[:], Pm[1].t[:], rr[1].t[:], A.t[:], ALU.mult, ALU.add, [Pm[1].b, rr[1].b, A.b], [A.b])
                    ATm = p1.rot("ATm", [128, NKT, 128], BF16, 1)
                    for t8 in range((NKT + 7) // 8):
                        bb = k.bbanks[k.bb_i % 2]; k.bb_i += 1
                        nt_ = min(8, NKT - t8 * 8)
                        for kt in range(nt_):
                            k.tr(bb.t[:, kt * 128:(kt + 1) * 128], A.t[:, (t8 * 8 + kt) * 128:(t8 * 8 + kt + 1) * 128], CSB(k, "ident"), [A.b, k.cstb.b], [bb.b])
                        k.cp("act", ATm.t[:, t8 * 8:t8 * 8 + nt_, :], bb.t[:, 0:nt_ * 128].rearrange("p (t q) -> p t q", q=128), [bb.b], [ATm.b])
                    po = k.bank()
                    vt0 = k0 // 128
                    for kt in range(NKT):
                        k.mm(po.t[0:64, 0:128], Vtm.t[:, vt0 + kt, h * 64:(h + 1) * 64], ATm.t[:, kt, :], kt == 0, kt == NKT - 1, [Vtm.b, ATm.b], [po.b])
                    k.cp("act", OACC.t[0:64, h, qs], po.t[0:64, 0:128], [po.b], [OACC.b])
                    if DF_STOP == 4:
                        return
            head_norm(p1, OACC, l, 3, BR, None, 1.0 - lam_init)

    def merge(g, l, HN, BR):
        w = g
        with k.phase() as p1:
            MG = p1.sb("MG", [128, 8, NT], BF16)
            for kq in range(2):
                acc = p1.sb(f"mgacc{kq}", [128, 4, NT])
                for n in range(4):
                    wv, wb = k.wget(("mat", "w_mgate", (l,), 8, ((n * 1024 + kq * 512, 512),)))
                    bv, bbuf = k.wget(("branch", l, n))
                    for k4 in range(4):
                        kc = kq * 4 + k4
                        for hb in range(2):
                            sl = slice(hb * 512, (hb + 1) * 512)
                            bg = k.bank(); bu = k.bank()
                            for kk in range(8):
                                k.mm(bg.t[:], wv[:, kk, k4 * 128:(k4 + 1) * 128], HN.t[:, kk, sl], kk == 0, kk == 7, [wb, HN.b], [bg.b])
                            for h in range(4):
                                k.mm(bu.t[:], bv[:, h, kc * 128:(kc + 1) * 128], BR.t[0:64, n, h, sl], h == 0, h == 3, [bbuf, BR.b], [bu.b])
                            sg = p1.rot("msg", [128, 512], F32, 3)
                            k.act(sg.t[:], bg.t[:], AF.Sigmoid, [bg.b], [sg.b])
                            if n == 0:
                                k.tt("dve", acc.t[:, k4, sl], sg.t[:], bu.t[:], ALU.mult, [sg.b, bu.b], [acc.b])
                            else:
                                k.tt("dve", sg.t[:], sg.t[:], bu.t[:], ALU.mult, [sg.b, bu.b], [sg.b])
                                if n < 3:
                                    k.tt("pool", acc.t[:, k4, sl], acc.t[:, k4, sl], sg.t[:], ALU.add, [acc.b, sg.b], [acc.b])
                                else:
                                    k.tt("pool", MG.t[:, kc, sl], acc.t[:, k4, sl], sg.t[:], ALU.add, [acc.b, sg.b], [MG.b])
            for t2 in range(2):
                wv, wb = k.wget(("mat", "w_out", (l,), 8, ((t2 * 512, 512),)))
                for f4 in range(4):
                    fo = t2 * 4 + f4
                    for hb in range(2):
                        sl = slice(hb * 512, (hb + 1) * 512)
                        bk = k.bank()
                        for kk in range(8):
                            k.mm(bk.t[:], wv[:, kk, f4 * 128:(f4 + 1) * 128], MG.t[:, kk, sl], kk == 0, kk == 7, [wb, MG.b], [bk.b])
                        xs_ = XT.t[:, fo, sl]
                        k.stt("dve", xs_, bk.t[:], k.der.t[:, l, w, 7, fo:fo + 1], xs_, ALU.mult, ALU.add, [bk.b, k.der.b, k.xtb[fo]], [k.xtb[fo]])

    def layer(g, l, mixers=("A", "B", "C", "D")):
        w = g
        ffn(l, w, 0, "ffn1_in", "ffn1_down")
        with k.phase() as ph:
            HN = ph.sb("HN", [128, 8, NT], BF16)
            BR = ph.sb("BR", [64, 4, 4, NT], BF16)
            norm_mod(ph, l, w, 1, HN)
            k.dma("sp", k.xscr[:, :], XT.t[:].rearrange("p c t -> p (c t)"), "spill", rd=k.xtb)
            k.P.barrier()
            if "A" in mixers:
                mixer_gla(ph, g, l, HN, BR)
            else:
                k.memset("pool", BR.t[:, 0], 0.0, [BR.b])
            if "B" in mixers:
                k.mixer_dn(ph, g, l, HN, BR)
            else:
                k.memset("pool", BR.t[:, 1], 0.0, [BR.b])
            if "C" in mixers:
                mixer_hg(ph, g, l, HN, BR)
            else:
                k.memset("pool", BR.t[:, 2], 0.0, [BR.b])
            if "D" in mixers:
                mixer_df(ph, g, l, HN, BR)
            else:
                k.memset("pool", BR.t[:, 3], 0.0, [BR.b])
            for n_, nm_ in enumerate("ABCD"):
                k.tap(f"br{nm_}{g}{l}", BR.t[0:64, n_].rearrange("p h t -> p (h t)"), [64, 4 * NT], [BR.b])
            k.P.barrier()
            k.P.dma("sp", lambda e: e.dma_start(out=XT.t[:].rearrange("p c t -> p (c t)"), in_=k.xscr[:, :]), "d_xt0", writes=k.xtb)
            merge(g, l, HN, BR)
        k.tap(f"xm{g}{l}", XT.t[:].rearrange("p c t -> p (c t)"), [128, 8 * NT], k.xtb)
        ffn(l, w, 2, "ffn2_in", "ffn2_down")

    k.layer = layer
    k.mixer_df = mixer_df
    k.merge = merge

    def mixer_dn(ph, g, l, HN, BR):
        nseq = 4 if g == 0 else 1
        cps = NCH // nseq
        T = NT // nseq
        with k.phase() as p1:
            c0 = IN_OFF["dn_x"]
            wv, wb = win_tile(l, c0, 512)
            wv2, wb2 = win_tile(l, c0 + 512, 272)
            nmc = p1.sb("nmc", [64, 1024], BF16)
            k.dma("pool", nmc.t[:], k.din["cst"][0:64, CST["nm_c"][0]:CST["nm_c"][0] + 1024], "ldnm", wr=[nmc.b])
            QT = [p1.sb(f"nQ{p}", [128, NT], BF16) for p in range(2)]
            KT = [p1.sb(f"nK{p}", [128, NT], BF16) for p in range(2)]
            Ktm = p1.sb("Ktm", [64, NCH, 256], BF16); Vtm = p1.sb("nVtm", [64, NCH, 256], BF16)
            with k.phase() as p2:
                VT = [p2.sb(f"nV{p}", [128, NT], BF16) for p in range(2)]
                for c in range(6):
                    src_w, src_b, cc = (wv, wb, c * 128) if c < 4 else (wv2, wb2, (c - 4) * 128)
                    x = p2.rot("cx", [128, NT], F32, 2)
                    proj_fm(src_w, src_b, cc, 128, HN, evac_to(x.t, x.b))
                    y = p2.rot("cy", [128, NT], F32, 2)
                    wrow = lambda tap: k.pt.t[:, ROW["dn_conv"] + (l * 3 + tap) * 6 + c:ROW["dn_conv"] + (l * 3 + tap) * 6 + c + 1]
                    k.ts("dve", y.t[:], x.t[:], wrow(1), 0.0, ALU.mult, ALU.add, [x.b, k.pt.b], [y.b])
                    x3 = x.t[:].rearrange("p (s t) -> p s t", t=T); y3 = y.t[:].rearrange("p (s t) -> p s t", t=T)
                    k.stt("dve", y3[:, :, 1:T], x3[:, :, 0:T - 1], wrow(0), y3[:, :, 1:T], ALU.mult, ALU.add, [x.b, y.b, k.pt.b], [y.b])
                    k.stt("dve", y3[:, :, 0:T - 1], x3[:, :, 1:T], wrow(2), y3[:, :, 0:T - 1], ALU.mult, ALU.add, [x.b, y.b, k.pt.b], [y.b])
                    if c >= 4:
                        k.act(VT[c - 4].t[:], y.t[:], AF.Silu, [y.b], [VT[c - 4].b])
                    else:
                        k.act(y.t[:], y.t[:], AF.Silu, [y.b], [y.b])
                        rn = k.rstd_of(p2, lambda c_, hb: y.t[:, hb * 512:(hb + 1) * 512], 1, 128, CSB(k, "bd64"), 1.0, f"rn{c % 2}", [y.b])
                        dst = QT[c] if c < 2 else KT[c - 2]
                        k.stt("dve", dst.t[:], y.t[:], 0.125 if c < 2 else 1.0, rn.t[:], ALU.mult, ALU.mult, [y.b, rn.b], [dst.b])
                for ch in range(NCH):
                    tok = slice(ch * 64, (ch + 1) * 64)
                    bb = k.bbanks[k.bb_i % 2]; k.bb_i += 1
                    for p in range(2):
                        k.tr(bb.t[0:64, p * 128:(p + 1) * 128], KT[p].t[:, tok], CSB(k, "ident"), [KT[p].b, k.cstb.b], [bb.b])
                        k.tr(bb.t[0:64, 256 + p * 128:256 + (p + 1) * 128], VT[p].t[:, tok], CSB(k, "ident"), [VT[p].b, k.cstb.b], [bb.b])
                    k.cp("act", Ktm.t[:, ch, :], bb.t[0:64, 0:256], [bb.b], [Ktm.b])
                    k.cp("act", Vtm.t[:, ch, :], bb.t[0:64, 256:512], [bb.b], [Vtm.b])

            if DN_STOP == 2:
                return
            zz = p1.sb("zz", [64, NCH, 16])
            for ch in range(NCH):
                proj_tm(wv2, wb2, 256, 16, HN, ch * 64, 64, (lambda ch_: (lambda bk: k.cp("dve", zz.t[:, ch_, :], bk.t[0:64, 0:16], [bk.b], [zz.b])))(ch))
            par = p1.sb("dnpar", [64, 16])
            k.dma("sp", par.t[:, 0:8], k.din["dn_al"][l:l + 1, :].partition_broadcast(64), "ldpar", wr=[par.b])
            k.dma("sp", par.t[:, 8:16], k.din["dn_dt"][l:l + 1, :].partition_broadcast(64), "ldpar", wr=[par.b])
            k.act(par.t[:, 0:8], par.t[:, 0:8], AF.Exp, [par.b], [par.b])
            k.ts("dve", par.t[:, 0:8], par.t[:, 0:8], -1.0, 0.0, ALU.mult, ALU.add, [par.b], [par.b])
            bcol = p1.sb("bcol", [64, NCH, 8]); lnb = p1.sb("lnb", [64, NCH, 8]); gcol = p1.sb("gcol", [64, NCH, 8])
            k.act(bcol.t[:], zz.t[:, :, 0:8], AF.Sigmoid, [zz.b], [bcol.b])
            k.act(lnb.t[:], bcol.t[:], AF.Ln, [bcol.b], [lnb.b])
            k.tt("dve", gcol.t[:], zz.t[:, :, 8:16], par.t[:, 8:16].unsqueeze(1).to_broadcast([64, NCH, 8]), ALU.add, [zz.b, par.b], [gcol.b])
            k.act(gcol.t[:], gcol.t[:], AF.Exp, [gcol.b], [gcol.b])
            k.act(gcol.t[:], gcol.t[:], AF.Ln, [gcol.b], [gcol.b], bias=1.0)
            k.tt("dve", gcol.t[:], gcol.t[:], par.t[:, 0:8].unsqueeze(1).to_broadcast([64, NCH, 8]), ALU.mult, [gcol.b, par.b], [gcol.b])
            dcol = p1.sb("dcol", [64, NCH, 8]); dlast = p1.sb("dlast", [128, NCH, 8])
            bk = k.bank()
            for ch in range(NCH):
                for dr in range(2):
                    k.mm(bk.t[0:64, ch * 8 + dr * 4:ch * 8 + dr * 4 + 4], CS(k, "mf" if dr == 0 else "mb")[0:64, 0:64], gcol.t[:, ch, dr * 4:dr * 4 + 4], True, True, [gcol.b, k.cst.b], [bk.b])
            k.cp("dve", dcol.t[:].rearrange("p c h -> p (c h)"), bk.t[0:64, 0:NCH * 8], [bk.b], [dcol.b])
            bk = k.bank()
            for ch in range(NCH):
                for dr in range(2):
                    k.mm(bk.t[:, ch * 8 + dr * 4:ch * 8 + dr * 4 + 4], CS(k, "sel_f" if dr == 0 else "sel_b")[0:64, :], dcol.t[:, ch, dr * 4:dr * 4 + 4], True, True, [dcol.b, k.cst.b], [bk.b])
            k.cp("dve", dlast.t[:].rearrange("p c h -> p (c h)"), bk.t[:, 0:NCH * 8], [bk.b], [dlast.b])
            acol = p1.sb("acol", [64, NCH, 8]); expd = p1.sb("expd", [64, NCH, 8]); nexpd = p1.sb("nexpd", [64, NCH, 8])
            edl = p1.sb("edl", [64, NCH, 8]); edlast = p1.sb("edlast", [128, NCH, 8])
            k.tt("dve", acol.t[:], dcol.t[:], lnb.t[:], ALU.add, [dcol.b, lnb.b], [acol.b])
            k.act(expd.t[:], dcol.t[:], AF.Exp, [dcol.b], [expd.b])
            k.ts("dve", nexpd.t[:], expd.t[:], -1.0, 0.0, ALU.mult, ALU.add, [expd.b], [nexpd.b])
            k.tt("dve", edl.t[:], dlast.t[0:64], dcol.t[:], ALU.subtract, [dlast.b, dcol.b], [edl.b])
            k.act(edl.t[:], edl.t[:], AF.Exp, [edl.b], [edl.b])
            k.act(edlast.t[:], dlast.t[:], AF.Exp, [dlast.b], [edlast.b])
            if DN_STOP == 3:
                return
            OACC = Tn(k.XT.t[0:64, 0:4, :], Buf("OACC"))
            bc8 = lambda t_, ch_, lo, n_: t_.t[:, ch_, lo:lo + n_].unsqueeze(2).to_broadcast([64, n_, 64])
            v3 = lambda ap_, n_: ap_.rearrange("p (h i) -> p h i", i=64)
            for dr in range(2):
                h0 = dr * 4
                with k.phase() as p2:
                    TbT = p2.sb("TbT", [64, NCH, 256], BF16); Aqk = p2.sb("Aqk", [64, NCH, 256], BF16)
                    NG = 3
                    atile = lambda i_: Tn(k.XT.t[0:64, 4 + i_ // 4, (i_ % 4) * 256:(i_ % 4 + 1) * 256], Buf(f"dnal{i_}"))
                    slots = []
                    for si in range(NG):
                        if si == 0:
                            tl_ = [p2.sb(f"tp{j_}", [64, 256], F32) for j_ in range(8)]
                        else:
                            tl_ = [atile((si - 1) * 8 + j_) for j_ in range(8)]
                        slots.append({"ec": tl_[0], "eb": tl_[1], "XN": tl_[2:4], "XT": tl_[4:6], "PM": tl_[6:8], "dg": tl_[7],
                                      "KTm": p2.sb(f"KTm{si}", [128, 256], BF16)})
                    hsl = lambda h: slice(h * 64, (h + 1) * 64)

                    def tprep(ch, B):
                        tok = slice(ch * 64, (ch + 1) * 64)
                        dg = B["dg"]; ec = B["ec"]; eb = B["eb"]; KTm = B["KTm"]
                        k.tt("pool", v3(dg.t[:], 4), v3(CS(k, "id8")[0:64, 0:256], 4), bc8(dcol, ch, h0, 4), ALU.mult, [dcol.b, k.cst.b], [dg.b])
                        for h in range(4):
                            k.ts("pool" if h % 2 else "dve", KTm.t[:, h * 64:(h + 1) * 64], KT[h // 2].t[:, tok], CS(k, "hm_hg", h, 1), 0.0, ALU.mult, ALU.add,
                                 [KT[h // 2].b, k.cst.b], [KTm.b])
                        yield
                        rb = k.bank()
                        k.mm(rb.t[0:64, 0:256], CS(k, "ones")[0:64, 0:64], dg.t[:], True, True, [dg.b, k.cst.b], [rb.b])
                        yield
                        k.tt("dve", v3(ec.t[:], 4), v3(rb.t[0:64, 0:256], 4), bc8(dcol, ch, h0, 4), ALU.subtract, [rb.b, dcol.b], [ec.b])
                        k.tt("dve", ec.t[:], ec.t[:], nmc.t[:, dr * 256:(dr + 1) * 256], ALU.min, [ec.b, nmc.b], [ec.b])
                        k.stt("dve", v3(eb.t[:], 4), v3(rb.t[0:64, 0:256], 4), -1.0, bc8(acol, ch, h0, 4), ALU.mult, ALU.add, [rb.b, acol.b], [eb.b])
                        k.tt("dve", eb.t[:], eb.t[:], nmc.t[:, 512 + dr * 256:512 + (dr + 1) * 256], ALU.min, [eb.b, nmc.b], [eb.b])
                        gk = k.bank(); gq = k.bank()
                        for h in range(4):
                            p = h // 2
                            k.mm(gk.t[0:64, hsl(h)], KTm.t[:, hsl(h)], KT[p].t[:, tok], True, True, [KT[p].b, KTm.b], [gk.b])
                            k.mm(gq.t[0:64, hsl(h)], KTm.t[:, hsl(h)], QT[p].t[:, tok], True, True, [KTm.b, QT[p].b], [gq.b])
                        yield
                        k.act(ec.t[:], ec.t[:], AF.Exp, [ec.b], [ec.b])
                        k.act(eb.t[:], eb.t[:], AF.Exp, [eb.b], [eb.b])
                        yield
                        XN = B["XN"][0]; XTt = B["XT"][0]; Pm = B["PM"][0]
                        k.stt("dve", XN.t[:], gk.t[0:64, 0:256], -1.0, eb.t[:], ALU.mult, ALU.mult, [gk.b, eb.b], [XN.b])
                        k.tt("dve", Aqk.t[:, ch, :], gq.t[0:64, 0:256], ec.t[:], ALU.mult, [gq.b, ec.b], [Aqk.b])
                        yield
                        bt = k.bank()
                        for h in range(4):
                            k.tr(bt.t[0:64, hsl(h)], XN.t[:, hsl(h)], CS(k, "ident")[0:64, 0:64], [XN.b, k.cst.b], [bt.b])
                        yield
                        k.cp("act", XTt.t[:], bt.t[0:64, 0:256], [bt.b], [XTt.b])
                        yield
                        k.tt("pool", Pm.t[:], XTt.t[:], CS(k, "id8")[0:64, 0:256], ALU.add, [XTt.b, k.cst.b], [Pm.b])
                        for lev in range(5):
                            b1 = None
                            if lev < 4:
                                b1 = k.bank()
                                for h in range(4):
                                    k.mm(b1.t[0:64, hsl(h)], XN.t[:, hsl(h)], XTt.t[:, hsl(h)], True, True, [XN.b, XTt.b], [b1.b])
                            b2 = k.bank()
                            for h in range(4):
                                k.mm(b2.t[0:64, hsl(h)], XTt.t[:, hsl(h)], XN.t[:, hsl(h)], True, True, [XN.b, XTt.b], [b2.b])
                            yield
                            XN2 = B["XN"][(lev + 1) % 2]
                            k.cp("act", XN2.t[:], b2.t[0:64, 0:256], [b2.b], [XN2.b])
                            if lev < 4:
                                XT2 = B["XT"][(lev + 1) % 2]
                                k.cp("act", XT2.t[:], b1.t[0:64, 0:256], [b1.b], [XT2.b])
                                XTt = XT2
                            XN = XN2
                            yield
                            b3 = k.bank()
                            for h in range(4):
                                k.mm(b3.t[0:64, hsl(h)], XN.t[:, hsl(h)], Pm.t[:, hsl(h)], True, True, [XN.b, Pm.b], [b3.b])
                            yield
                            Pn = B["PM"][(lev + 1) % 2]
                            k.tt("dve", Pn.t[:], Pm.t[:], b3.t[0:64, 0:256], ALU.add, [Pm.b, b3.b], [Pn.b])
                            Pm = Pn
                            yield
                        k.tt("dve", v3(TbT.t[:, ch, :], 4), v3(Pm.t[:], 4), bc8(bcol, ch, h0, 4), ALU.mult, [Pm.b, bcol.b], [TbT.b])

                    for c0_ in range(0, NCH, NG):
                        gens = [tprep(ch, slots[i_]) for i_, ch in enumerate(range(c0_, min(NCH, c0_ + NG)))]
                        while gens:
                            nxt = []
                            for g_ in gens:
                                try:
                                    next(g_)
                                    nxt.append(g_)
                                except StopIteration:
                                    pass
                            gens = nxt
                    if DN_STOP == 4:
                        return
                    S32 = p2.sb("S32", [128, 256])
                    for s_ in range(nseq):
                        k.memset("pool", S32.t[:], 0.0, [S32.b])
                        if g == 1:
                            for h in range(4):
                                k.dma("sp", S32.t[(h % 2) * 64:(h % 2 + 1) * 64, h * 64:(h + 1) * 64], k.din["sd"][l, dr, h], "ldS", wr=[S32.b])
                        order = range(cps) if dr == 0 else range(cps - 1, -1, -1)
                        for ci in order:
                            ch = s_ * cps + ci
                            tok = slice(ch * 64, (ch + 1) * 64)
                            Sbf = p2.rot("Sbf", [128, 256], BF16, 3)
                            k.tt("pool", Sbf.t[:], S32.t[:], CS(k, "sm_hg"), ALU.mult, [S32.b, k.cst.b], [Sbf.b])
                            pk = k.bank(); pq = k.bank()
                            for h in range(4):
                                p = h // 2
                                k.mm(pk.t[0:64, h * 64:(h + 1) * 64], KT[p].t[:, tok], Sbf.t[:, h * 64:(h + 1) * 64], True, True, [KT[p].b, Sbf.b], [pk.b])
                                k.mm(pq.t[0:64, h * 64:(h + 1) * 64], QT[p].t[:, tok], Sbf.t[:, h * 64:(h + 1) * 64], True, True, [QT[p].b, Sbf.b], [pq.b])
                            Y = p2.rot("Y", [64, 256], F32, 1); Yb = p2.rot("Yb", [64, 256], BF16, 2)
                            k.tt("dve", v3(Y.t[:], 4), v3(pk.t[0:64, 0:256], 4), bc8(nexpd, ch, h0, 4), ALU.mult, [pk.b, nexpd.b], [Y.b])
                            k.tt("dve", Yb.t[:], Y.t[:], Vtm.t[:, ch, :], ALU.add, [Y.b, Vtm.b], [Yb.b])
                            pv = k.bank()
                            for h in range(4):
                                k.mm(pv.t[0:64, h * 64:(h + 1) * 64], TbT.t[:, ch, h * 64:(h + 1) * 64], Yb.t[:, h * 64:(h + 1) * 64], True, True, [TbT.b, Yb.b], [pv.b])
                            VN = p2.rot("VN", [64, 256], BF16, 2); VNs = p2.rot("VNs", [64, 256], BF16, 2)
                            k.cp("dve", VN.t[:], pv.t[0:64, 0:256], [pv.b], [VN.b])
                            k.tt("dve", v3(VNs.t[:], 4), v3(pv.t[0:64, 0:256], 4), bc8(edl, ch, h0, 4), ALU.mult, [pv.b, edl.b], [VNs.b])
                            pa = k.bank()
                            for h in range(4):
                                k.mm(pa.t[0:64, h * 64:(h + 1) * 64], Aqk.t[:, ch, h * 64:(h + 1) * 64], VN.t[:, h * 64:(h + 1) * 64], True, True, [Aqk.b, VN.b], [pa.b])
                            ot = p2.rot("ot", [64, 256], F32, 1)
                            k.tt("dve", v3(ot.t[:], 4), v3(pq.t[0:64, 0:256], 4), bc8(expd, ch, h0, 4), ALU.mult, [pq.b, expd.b], [ot.b])
                            k.tt("dve", ot.t[:], ot.t[:], pa.t[0:64, 0:256], ALU.add, [ot.b, pa.b], [ot.b])
                            pt_ = k.bank()
                            for h in range(4):
                                k.tr(pt_.t[0:64, h * 64:(h + 1) * 64], ot.t[:, h * 64:(h + 1) * 64], CS(k, "ident")[0:64, 0:64], [ot.b, k.cst.b], [pt_.b])
                            oview = OACC.t[0:64, :, tok]
                            if dr == 0:
                                k.cp("act", oview, v3(pt_.t[0:64, 0:256], 4), [pt_.b], [OACC.b])
                            else:
                                k.tt("dve", oview, oview, v3(pt_.t[0:64, 0:256], 4), ALU.add, [pt_.b, OACC.b], [OACC.b])
                            pS = k.bank()
                            for p in range(2):
                                k.mm(pS.t[:, p * 128:(p + 1) * 128], Ktm.t[:, ch, p * 128:(p + 1) * 128], VNs.t[:, p * 128:(p + 1) * 128], True, True, [Ktm.b, VNs.b], [pS.b])
                            for h in range(4):
                                sl = slice(h * 64, (h + 1) * 64)
                                k.stt("dve", S32.t[:, sl], S32.t[:, sl], edlast.t[:, ch, h0 + h:h0 + h + 1], pS.t[:, sl], ALU.mult, ALU.add,
                                      [S32.b, edlast.b, pS.b], [S32.b])
                        if g == 0:
                            for h in range(4):
                                k.dma("sp", k.dout["nd"][s_, l, dr, h], S32.t[(h % 2) * 64:(h % 2 + 1) * 64, h * 64:(h + 1) * 64], "stS", rd=[S32.b])
            gate = gate_proj(p1, l, IN_OFF["dn_g"], HN)
            head_norm(p1, OACC, l, 1, BR, gate)

    k.mixer_dn = mixer_dn
    k.mixer_gla = mixer_gla
    k.mixer_hg = mixer_hg
    k.gla_scan = gla_scan
    k.head_norm = head_norm
    k.gate_proj = gate_proj
    k.proj_fm = proj_fm
    k.proj_tm = proj_tm
    k.win_tile = win_tile
    k.evac_to = evac_to
    k.ffn = ffn
    k.xload = xload
    k.final = final
    k.rstd_of = rstd_of
    k.norm_mod = norm_mod
    return k


def finish(k):
    k.P.final_wait("sp")
    keys = list(ENGS) + list(k.P.dma_cnt.keys())
    sems = {key: k.es.enter_context(k.nc.semaphore("s_" + key)) for key in keys}
    with k.nc.Block() as block:
        replay(k.P, block, sems)
    k.es.close()
    return k.nc


def host_inputs(inp, core):
    b = core // 4
    f = lambda a: np.ascontiguousarray(np.asarray(a, np.float32))
    m = {
        "xp": f(inp["x_prompt"][core * 4:(core + 1) * 4].reshape(NT, D)),
        "xs": f(inp["x_sample"][b]),
        "ck": f(inp["cache_diff_k"][b]), "cv": f(inp["cache_diff_v"][b]),
        "sg": f(inp["state_gla"][b]), "sd": f(inp["state_dn"][b]), "sh": f(inp["state_hgrn"][b]),
        "pvec": pack_pvec(inp, b), "cst": make_consts(),
        "dn_al": f(inp["dn_a_log"].reshape(DEPTH, 8)), "dn_dt": f(inp["dn_dt_bias"].reshape(DEPTH, 8)),
        "dlam": f(inp["diff_lambda"].reshape(DEPTH, 128)), "gla_w2": f(inp["gla_w2"]),
    }
    for n_ in ("w_mod", "ffn1_in", "ffn2_in", "ffn1_down", "ffn2_down", "w_in", "w_branch", "w_mgate", "w_out"):
        m[n_] = f(inp[n_])
    return m


def program(k, mixers=("A", "B", "C", "D")):
    for g in range(2):
        k.xload(g)
        for l in range(DEPTH):
            k.layer(g, l, mixers)
        k.final(g)


_CACHE = {}


def get_nc():
    if "nc" not in _CACHE:
        k1 = build(None)
        program(k1)
        k2 = build(list(k1.wrec))
        program(k2)
        _CACHE["nc"] = finish(k2)
    return _CACHE["nc"]


def kernel(**inp):
    inp = {n: np.asarray(v) for n, v in inp.items()}
    nc = get_nc()
    in_maps = [host_inputs(inp, c) for c in range(8)]
    res = run_bass_kernel_spmd(nc, in_maps, core_ids=list(range(8)))
    R = res.results
    y_prompt = np.concatenate([R[c]["yp"].reshape(4, 256, D) for c in range(8)], axis=0)
    y_sample = np.stack([R[0]["ys"], R[4]["ys"]], axis=0)
    cat = lambda n_: np.concatenate([R[c][n_] for c in range(8)], axis=0)
    return (y_prompt.astype(np.float32), y_sample.astype(np.float32), cat("nk"), cat("nv"), cat("ng"), cat("nd"), cat("nh"))
```

```python
import math
from contextlib import ExitStack
import numpy as np
import concourse.bass as bass
import concourse.mybir as mybir
from concourse.bass_utils import run_bass_kernel_spmd

F32 = mybir.dt.float32
BF16 = mybir.dt.bfloat16
AF = mybir.ActivationFunctionType
ALU = mybir.AluOpType
ENGS = ("pe", "act", "dve", "pool", "sp")

D = 1024
NT = 1024
DFF = 2816
NIN = 3888
EPS = 1e-6
DEPTH = 2
CH = 64
NCH = NT // CH
FFN_STOP = 0
DF_STOP = 0
DN_STOP = 0


class Buf:
    __slots__ = ("name", "w", "r")

    def __init__(self, name=""):
        self.name = name
        self.w = None
        self.r = {}


class Prog:
    def __init__(self):
        self.ops = {e: [] for e in ENGS}
        self.cnt = {e: 0 for e in ENGS}
        self.seen = {e: {} for e in ENGS}
        self.dma_cnt = {}

    def _need(self, eng, tick, waits):
        if tick is None:
            return
        k, v = tick
        if self.seen[eng].get(k, 0) >= v:
            return
        waits[k] = max(waits.get(k, 0), v)

    def _deps(self, eng, reads, writes, is_dma):
        waits = {}
        for b in reads:
            self._need(eng, b.w, waits)
        for b in writes:
            if b.w is not None and (is_dma or b.w[0] != eng):
                self._need(eng, b.w, waits)
            for k, v in b.r.items():
                if is_dma or k != eng:
                    self._need(eng, (k, v), waits)
        for k, v in waits.items():
            self.seen[eng][k] = v
        return tuple(waits.items())

    def op(self, eng, fn, reads=(), writes=(), inc=True):
        waits = self._deps(eng, reads, writes, False)
        if inc:
            self.cnt[eng] += 1
            tv = self.cnt[eng]
        else:
            tv = self.cnt[eng] + 1
        self.ops[eng].append((waits, fn, (eng, 1) if inc else None))
        for b in reads:
            b.r[eng] = tv
        for b in writes:
            b.w = (eng, tv)
            b.r = {}

    def dma(self, eng, fn, semkey, reads=(), writes=(), n=1):
        waits = self._deps(eng, reads, writes, True)
        self.dma_cnt[semkey] = self.dma_cnt.get(semkey, 0) + 16 * n
        tick = (semkey, self.dma_cnt[semkey])
        self.ops[eng].append((waits, fn, (semkey, 16)))
        for b in reads:
            b.r[semkey] = tick[1]
        for b in writes:
            b.w = tick
            b.r = {}
        return tick

    def barrier(self, skip=lambda k: k.startswith("w") and not k.startswith("d_")):
        for e in ENGS:
            waits = {}
            for k in ENGS:
                if k != e and self.cnt[k] > 0:
                    self._need(e, (k, self.cnt[k]), waits)
            for k, v in self.dma_cnt.items():
                if not skip(k):
                    self._need(e, (k, v), waits)
            for k, v in waits.items():
                self.seen[e][k] = v
            if waits:
                self.ops[e].append((tuple(waits.items()), None, None))

    def final_wait(self, eng):
        waits = {}
        for k in ENGS:
            if k != eng and self.cnt[k] > 0:
                self._need(eng, (k, self.cnt[k]), waits)
        for k, v in self.dma_cnt.items():
            self._need(eng, (k, v), waits)
        self.ops[eng].append((tuple(waits.items()), None, None))


def replay(prog, block, sems):
    def run(name):
        def body(eng):
            for waits, fn, inc in prog.ops[name]:
                for k, v in waits:
                    eng.wait_ge(sems[k], v)
                if fn is None:
                    continue
                res = fn(eng)
                if inc is not None:
                    if isinstance(res, (list, tuple)):
                        for r in res:
                            r.then_inc(sems[inc[0]], inc[1])
                    else:
                        res.then_inc(sems[inc[0]], inc[1])
        return body
    block.tensor(run("pe"))
    block.scalar(run("act"))
    block.vector(run("dve"))
    block.gpsimd(run("pool"))
    block.sync(run("sp"))


IN_OFF = {}
_o = 0
for _n, _s in [("ga_q", 128), ("ga_k", 128), ("ga_v", 256), ("ga_r", 256), ("ga_lr", 32), ("dn_x", 768), ("dn_b", 8),
               ("dn_a", 8), ("dn_g", 256), ("hg_q", 256), ("hg_f", 512), ("hg_i", 256), ("hg_g", 256), ("df_q", 256),
               ("df_k", 256), ("df_v", 256)]:
    IN_OFF[_n] = _o
    _o += _s
assert _o == NIN

ROW = {}
_r = 0
for _n, _s in [("norm_w", DEPTH * 3 * 8), ("b_mod", DEPTH * 72), ("final", 8), ("c_ctx", 8), ("c_lat", 8), ("gla_b", DEPTH * 2),
               ("dn_conv", DEPTH * 3 * 6), ("hg_lb", DEPTH * 2 * 2), ("hnorm", DEPTH * 4)]:
    ROW[_n] = _r
    _r += _s
NROW = 384
assert _r <= NROW

CST = {}
_c = 0
for _n, _s in [("ident", 128), ("ones", 128), ("bd64", 128), ("rot", 128), ("sel_f", 128), ("sel_b", 128), ("id8", 512),
               ("mf", 256), ("mb", 256), ("rm", 1024), ("hm_gla", 4), ("hm_hg", 4),
               ("sm_gla", 256), ("sm_hg", 256), ("qm", 4), ("CSTA_END", 0),
               ("nm_c", 512), ("nm_b", 512), ("NM_END", 0),
               ("cos", 1024), ("sin", 1024)]:
    CST[_n] = (_c, _s)
    _c += _s
NCST = _c
NCSTA = CST["CSTA_END"][0]
NCSTB = CST["mf"][0]


def make_consts():
    c = np.zeros((128, NCST), np.float32)

    def put(name, arr):
        o, s = CST[name]
        a = np.zeros((128, s), np.float32)
        a[:arr.shape[0], :arr.shape[1]] = arr
        c[:, o:o + s] = a
    put("ident", np.eye(128))
    put("ones", np.ones((128, 128)))
    bd = np.zeros((128, 128)); bd[:64, :64] = 1; bd[64:, 64:] = 1
    put("bd64", bd)
    j = np.arange(64)[:, None]; i = np.arange(64)[None, :]
    put("mf", np.tile((j <= i).astype(np.float32), (1, 4)))
    put("mb", np.tile((j >= i).astype(np.float32), (1, 4)))
    rm = np.ones((128, 1024)); rm[:, ::64] = 0
    put("rm", rm)
    d = np.arange(128)[:, None]; h = np.arange(4)[None, :]
    put("hm_gla", (d // 32 == h).astype(np.float32))
    put("hm_hg", (d // 64 == h % 2).astype(np.float32))
    put("sm_gla", np.repeat((d // 32 == h).astype(np.float32), 64, axis=1))
    put("sm_hg", np.repeat((d // 64 == h % 2).astype(np.float32), 64, axis=1))
    NEG = -30000.0
    put("nm_c", np.concatenate([np.tile(np.where(j <= i, 0.0, NEG), (1, 4)), np.tile(np.where(j >= i, 0.0, NEG), (1, 4))], axis=1))
    put("nm_b", np.concatenate([np.tile(np.where(j > i, 0.0, NEG), (1, 4)), np.tile(np.where(j < i, 0.0, NEG), (1, 4))], axis=1))
    t = np.arange(1024)
    row = (t // 64).astype(np.float32); col = (t % 64).astype(np.float32)
    half = 16
    inv = (10000.0 ** (-np.arange(0, half, 2, dtype=np.float32) / half)).astype(np.float32)
    ang_r = np.concatenate([row[:, None] * inv[None, :]] * 2, axis=1)
    ang_c = np.concatenate([col[:, None] * inv[None, :]] * 2, axis=1)
    ang = np.concatenate([ang_r, ang_c], axis=1).astype(np.float32)
    cos32 = np.cos(ang).T; sin32 = np.sin(ang).T
    put("cos", np.tile(cos32, (4, 1)))
    put("sin", np.tile(sin32, (4, 1)))
    R = np.zeros((128, 128), np.float32)
    for p in range(128):
        b16 = (p // 16) * 16; dd = p % 16
        if dd < 8:
            R[b16 + dd + 8, p] = -1.0
        else:
            R[b16 + dd - 8, p] = 1.0
    put("rot", R)
    put("qm", (d // 32 == h).astype(np.float32))
    sf = np.zeros((64, 128)); sf[63, :] = 1
    sb_ = np.zeros((64, 128)); sb_[0, :] = 1
    put("sel_f", sf); put("sel_b", sb_)
    put("id8", np.tile(np.eye(64), (1, 8)))
    return c


def pack_pvec(inp, b):
    rows = np.zeros((NROW, 128), np.float32)

    def put(name, arr):
        a = np.asarray(arr, np.float32).reshape(-1, 128)
        rows[ROW[name]:ROW[name] + a.shape[0]] = a
    put("norm_w", inp["norm_w"])
    put("b_mod", inp["b_mod"])
    put("final", inp["final_norm"])
    put("c_ctx", inp["c_ctx"])
    put("c_lat", inp["c"][b])
    put("gla_b", inp["gla_b"])
    put("dn_conv", inp["dn_conv"])
    put("hg_lb", inp["hg_lb_logits"])
    hn = np.stack([np.stack([np.tile(inp[k][l], 2) for k in ("gla_norm", "dn_norm", "hg_norm", "diff_norm")]) for l in range(DEPTH)])
    put("hnorm", hn)
    return rows


class Tn:
    __slots__ = ("t", "b")

    def __init__(self, t, b):
        self.t = t
        self.b = b


class Phase:
    def __init__(self, k):
        self.k = k
        self.es = ExitStack()
        self.rots = {}

    def __enter__(self):
        self.es.__enter__()
        return self

    def sb(self, name, shape, dt=F32):
        self.k.uid += 1
        t = self.es.enter_context(self.k.nc.sbuf_tensor(f"{name}_{self.k.uid}", list(shape), dt))
        return Tn(t, Buf(name))

    def rot(self, name, shape, dt=F32, n=2):
        if name not in self.rots:
            self.rots[name] = [[self.sb(f"{name}{i}", shape, dt) for i in range(n)], 0]
        lst = self.rots[name]
        t = lst[0][lst[1] % n]
        lst[1] += 1
        return t

    def __exit__(self, *a):
        self.k.P.barrier()
        return self.es.__exit__(*a)


class K:
    def __init__(self, wplan=None, taps=(), stages=None):
        self.nc = bass.Bass("TRN2", target_bir_lowering=False)
        self.P = Prog()
        self.es = ExitStack()
        self.uid = 0
        self.wplan = wplan
        self.wrec = []
        self.wi = 0
        self.wissued = 0
        self.taps = set(taps)
        self.tap_out = {}
        self.stages = stages
        self.din = {}
        self.dout = {}
        self.bank_i = 0

    def inp(self, name, shape):
        self.din[name] = self.nc.dram_tensor(name, list(shape), F32, kind="ExternalInput").ap()
        return self.din[name]

    def outp(self, name, shape):
        self.dout[name] = self.nc.dram_tensor(name, list(shape), F32, kind="ExternalOutput").ap()
        return self.dout[name]

    def sb(self, name, shape, dt=F32):
        self.uid += 1
        t = self.es.enter_context(self.nc.sbuf_tensor(f"{name}_{self.uid}", list(shape), dt))
        return Tn(t, Buf(name))

    def phase(self):
        return Phase(self)

    def mm(self, out, lhsT, rhs, start, stop, rd, wr, inc=None):
        inc = True
        self.P.op("pe", lambda e: e.matmul(out, lhsT=lhsT, rhs=rhs, start=start, stop=stop), reads=rd, writes=wr, inc=inc)

    def tr(self, out, in_, ident, rd, wr):
        self.P.op("pe", lambda e: e.transpose(out, in_, ident), reads=rd, writes=wr)

    def act(self, out, in_, func, rd, wr, bias=0.0, scale=1.0, accum=None):
        if accum is None:
            self.P.op("act", lambda e: e.activation(out=out, in_=in_, func=func, bias=bias, scale=scale), reads=rd, writes=wr)
        else:
            self.P.op("act", lambda e: e.activation(out=out, in_=in_, func=func, bias=bias, scale=scale, accum_out=accum), reads=rd, writes=wr)

    def tt(self, eng, out, a, b, op, rd, wr):
        if eng == "pool" and op not in (ALU.add, ALU.subtract, ALU.mult):
            eng = "dve"
        self.P.op(eng, lambda e: e.tensor_tensor(out=out, in0=a, in1=b, op=op), reads=rd, writes=wr)

    def ts(self, eng, out, a, s1, s2, op0, op1, rd, wr):
        self.P.op(eng, lambda e: e.tensor_scalar(out=out, in0=a, scalar1=s1, scalar2=s2, op0=op0, op1=op1), reads=rd, writes=wr)

    def stt(self, eng, out, a, s, b, op0, op1, rd, wr):
        eng = "dve"
        self.P.op(eng, lambda e: e.scalar_tensor_tensor(out=out, in0=a, scalar=s, in1=b, op0=op0, op1=op1), reads=rd, writes=wr)

    def cp(self, eng, out, in_, rd, wr):
        if eng == "act":
            self.P.op("act", lambda e: e.copy(out=out, in_=in_), reads=rd, writes=wr)
        else:
            self.P.op(eng, lambda e: e.tensor_copy(out=out, in_=in_), reads=rd, writes=wr)

    def memset(self, eng, ap, val, wr):
        self.P.op(eng, lambda e: e.memset(ap, val), writes=wr)

    def dma(self, eng, out, in_, key, rd=(), wr=()):
        key = "d_" + (wr[0].name if wr else rd[0].name)
        return self.P.dma(eng, lambda e: e.dma_start(out=out, in_=in_), key, reads=rd, writes=wr)

    def bank(self):
        b = self.banks[self.bank_i % len(self.banks)]
        self.bank_i += 1
        return b

    def tap(self, name, tn_ap, shape, rd):
        if name not in self.taps:
            return
        o = self.outp("tap_" + name, shape)
        self.P.dma("pool", lambda e: e.dma_start(out=o, in_=tn_ap), "tap_" + name, reads=rd)

    def wsrc(self, spec, slot):
        kind = spec[0]
        if kind == "mat":
            _, name, idx, nk, ranges = spec
            W = self.din[name]
            for i in idx:
                W = W[i]
            tot = sum(n for _, n in ranges)
            view = slot.t[:, 0:nk * tot].rearrange("p (k c) -> p k c", c=tot)
            pieces = []
            o = 0
            for c0, n in ranges:
                pieces.append((view[:, :, o:o + n], W[:, c0:c0 + n].rearrange("(k p) c -> p k c", p=128)))
                o += n
            return pieces, view
        if kind == "branch":
            _, l, n = spec
            W = self.din["w_branch"][l][n]
            view = slot.t[0:64, 0:4096].rearrange("p (h f) -> p h f", f=1024)
            return [(view, W.rearrange("(h e) f -> e h f", e=64))], view
        raise ValueError(kind)

    def _wissue(self, j, spec):
        slot = self.wslots[j % len(self.wslots)]
        pieces, view = self.wsrc(spec, slot)
        key = f"w{j % len(self.wslots)}"
        self.P.dma("pool", lambda e: [e.dma_start(out=o, in_=i) for o, i in pieces], key, writes=[slot.b], n=len(pieces))

    def wget(self, spec):
        j = self.wi
        self.wi += 1
        self.wrec.append(spec)
        slot = self.wslots[j % len(self.wslots)]
        if self.wplan is None:
            self._wissue(j, spec)
        else:
            assert self.wplan[j] == spec, (j, spec, self.wplan[j])
            while self.wissued < min(j + len(self.wslots) - 1, len(self.wplan)):
                self._wissue(self.wissued, self.wplan[self.wissued])
                self.wissued += 1
        _, view = self.wsrc(spec, slot)
        return view, slot.b


def CS(k, name, lo=0, n=None):
    o, s = CST[name]
    n = s - lo if n is None else n
    return k.cst.t[:, o + lo:o + lo + n]


def CSB(k, name, lo=0, n=None):
    o, s = CST[name]
    n = s - lo if n is None else n
    return k.cstb.t[:, o + lo:o + lo + n]


def build(wplan=None, taps=(), stages=("all",)):
    k = K(wplan, taps, stages)
    nc = k.nc
    es = k.es
    st = set(stages)
    ALL = "all" in st
    xin = [k.inp("xp", [NT, D]), k.inp("xs", [NT, D])]
    k.inp("ck", [DEPTH, 4, 512, 64]); k.inp("cv", [DEPTH, 4, 512, 64])
    k.inp("sg", [DEPTH, 2, 4, 32, 64]); k.inp("sd", [DEPTH, 2, 4, 64, 64]); k.inp("sh", [DEPTH, 2, 4, 64, 64])
    k.inp("pvec", [NROW, 128]); k.inp("cst", [128, NCST])
    k.inp("dn_al", [DEPTH, 8]); k.inp("dn_dt", [DEPTH, 8]); k.inp("dlam", [DEPTH, 128]); k.inp("gla_w2", [DEPTH, 2, 16, 128])
    k.inp("w_mod", [DEPTH, D, 9 * D])
    for n_ in ("ffn1_in", "ffn2_in"):
        k.inp(n_, [DEPTH, D, 2 * DFF])
    for n_ in ("ffn1_down", "ffn2_down"):
        k.inp(n_, [DEPTH, DFF, D])
    k.inp("w_in", [DEPTH, D, NIN]); k.inp("w_branch", [DEPTH, 4, 256, D]); k.inp("w_mgate", [DEPTH, D, 4 * D]); k.inp("w_out", [DEPTH, D, D])
    yout = [k.outp("yp", [NT, D]), k.outp("ys", [NT, D])]
    k.outp("nk", [4, DEPTH, 4, 256, 64]); k.outp("nv", [4, DEPTH, 4, 256, 64])
    k.outp("ng", [4, DEPTH, 2, 4, 32, 64]); k.outp("nd", [4, DEPTH, 2, 4, 64, 64]); k.outp("nh", [4, DEPTH, 2, 4, 64, 64])

    k.cst = k.sb("cst", [128, NCSTA])
    k.cstb = k.sb("cstb", [128, NCSTB], BF16)
    k.pt = k.sb("pt", [128, NROW])
    k.mod = k.sb("mod", [128, DEPTH, 72, 2])
    k.der = k.sb("der", [128, DEPTH, 2, 9, 8])
    k.XT = k.sb("XT", [128, 8, NT])
    k.xtb = [Buf(f"xt{c}") for c in range(8)]
    k.wslots = [k.sb(f"wslot{i}", [128, 4096], BF16) for i in range(3)]
    k.xscr = nc.dram_tensor("xscr", [128, 8 * NT], F32, kind="Internal").ap()
    k.big = [es.enter_context(nc.psum_tensor(f"big{i}", [128, 1536], F32)) for i in range(2)]
    k.banks = [Tn(k.big[i // 3][:, (i % 3) * 512:(i % 3 + 1) * 512], Buf(f"bank{i}")) for i in range(6)]
    k.bbanks = [Tn(es.enter_context(nc.psum_tensor(f"bbank{i}", [128, 1024], BF16)), Buf(f"bbank{i}")) for i in range(2)]
    k.bb_i = 0
    P = k.P
    IDF = lambda: CS(k, "ident")
    IDB = lambda: CSB(k, "ident")
    ONESB = lambda: CSB(k, "ones")

    k.dma("sp", k.cst.t[:], k.din["cst"][:, 0:NCSTA], "ld", wr=[k.cst.b])
    k.dma("pool", k.cstb.t[:], k.din["cst"][:, 0:NCSTB], "ldb", wr=[k.cstb.b])
    with k.phase() as ph:
        for r in range(3):
            stg = ph.rot("pstg", [128, 128], F32, 3)
            k.dma("sp", stg.t[:], k.din["pvec"][r * 128:(r + 1) * 128, :], "ld", wr=[stg.b])
            bk = k.bank()
            k.tr(bk.t[:, 0:128], stg.t[:], IDF(), [stg.b, k.cst.b], [bk.b])
            k.cp("dve", k.pt.t[:, r * 128:(r + 1) * 128], bk.t[:, 0:128], [bk.b], [k.pt.b])
        cs = ph.sb("cs", [128, 8, 2], BF16)
        k.act(cs.t[:, :, 0], k.pt.t[:, ROW["c_ctx"]:ROW["c_ctx"] + 8], AF.Silu, [k.pt.b], [cs.b])
        k.act(cs.t[:, :, 1], k.pt.t[:, ROW["c_lat"]:ROW["c_lat"] + 8], AF.Silu, [k.pt.b], [cs.b])
        for l in range(DEPTH):
            bk = k.bank()
            for tl in range(18):
                wv, wb = k.wget(("mat", "w_mod", (l,), 8, ((tl * 512, 512),)))
                for s4 in range(4):
                    j = tl * 4 + s4
                    for kc in range(8):
                        k.mm(bk.t[:, j * 2:j * 2 + 2], wv[:, kc, s4 * 128:(s4 + 1) * 128], cs.t[:, kc, :], kc == 0, kc == 7, [wb, cs.b], [bk.b])
            o0 = ROW["b_mod"] + l * 72
            k.tt("dve", k.mod.t[:, l], bk.t[:, 0:144].rearrange("p (j w) -> p j w", w=2),
                 k.pt.t[:, o0:o0 + 72].unsqueeze(2).to_broadcast([128, 72, 2]), ALU.add, [bk.b, k.pt.b], [k.mod.b])
            for w in range(2):
                for i in range(3):
                    nw0 = ROW["norm_w"] + (l * 3 + i) * 8
                    k.stt("dve", k.der.t[:, l, w, i, :], k.mod.t[:, l, (3 * i + 1) * 8:(3 * i + 2) * 8, w], 1.0, k.pt.t[:, nw0:nw0 + 8],
                          ALU.add, ALU.mult, [k.mod.b, k.pt.b], [k.der.b])
                    k.cp("dve", k.der.t[:, l, w, 3 + i, :], k.mod.t[:, l, (3 * i) * 8:(3 * i + 1) * 8, w], [k.mod.b], [k.der.b])
                    k.ts("dve", k.der.t[:, l, w, 6 + i, :], k.mod.t[:, l, (3 * i + 2) * 8:(3 * i + 3) * 8, w], 1.0 if i == 1 else 0.5, 0.0,
                         ALU.mult, ALU.add, [k.mod.b], [k.der.b])
    k.tap("mod", k.mod.t[:].rearrange("p l j w -> p (l j w)"), [128, DEPTH * 144], [k.mod.b])

    XT = k.XT

    def xload(g):
        with k.phase() as ph:
            for tl in range(8):
                stg = ph.rot("xstg", [128, D], F32, 2)
                k.dma("sp", stg.t[:], xin[g][tl * 128:(tl + 1) * 128, :], "ldx", wr=[stg.b])
                for hb in range(2):
                    bk = k.bank()
                    for c4 in range(4):
                        c = hb * 4 + c4
                        k.tr(bk.t[:, c4 * 128:(c4 + 1) * 128], stg.t[:, c * 128:(c + 1) * 128], IDF(), [stg.b, k.cst.b], [bk.b])
                    eng = "act" if hb == 0 else "dve"
                    k.cp(eng, XT.t[:, hb * 4:(hb + 1) * 4, tl * 128:(tl + 1) * 128], bk.t[:].rearrange("p (c t) -> p c t", t=128), [bk.b],
                         [k.xtb[c] for c in range(hb * 4, hb * 4 + 4)])

    def rstd_of(ph, src_fn, nchunk, nparts, lhsT, scale, name, rd, dst=None):
        rstd = dst if dst is not None else ph.sb(name, [128, NT])
        for hb in range(2):
            bk = k.bank()
            for c in range(nchunk):
                sq = ph.rot("sq", [128, 512], BF16, 3)
                k.act(sq.t[0:nparts, :], src_fn(c, hb), AF.Square, rd, [sq.b])
                k.mm(bk.t[0:nparts, :], lhsT, sq.t[0:nparts, :], c == 0, c == nchunk - 1, [sq.b, k.cstb.b], [bk.b])
            sl = rstd.t[0:nparts, hb * 512:(hb + 1) * 512]
            k.ts("dve", sl, bk.t[0:nparts, :], scale, EPS, ALU.mult, ALU.add, [bk.b], [rstd.b])
            k.act(sl, sl, AF.Sqrt, [rstd.b], [rstd.b])
            k.P.op("dve", (lambda sl_: (lambda e: e.reciprocal(out=sl_, in_=sl_)))(sl), reads=[rstd.b], writes=[rstd.b])
        return rstd

    def norm_mod(ph, l, w, i, HN):
        rstd = rstd_of(ph, lambda c, hb: XT.t[:, c, hb * 512:(hb + 1) * 512], 8, 128, ONESB(), 1.0 / D, "rstd", k.xtb)
        for c in range(8):
            tmp = ph.rot("ntmp", [128, NT], F32, 2)
            k.tt("dve", tmp.t[:], XT.t[:, c, :], rstd.t[:], ALU.mult, [k.xtb[c], rstd.b], [tmp.b])
            k.act(HN.t[:, c, :], tmp.t[:], AF.Identity, [tmp.b, k.der.b], [HN.b],
                  bias=k.der.t[:, l, w, 3 + i, c:c + 1], scale=k.der.t[:, l, w, i, c:c + 1])

    def ffn(l, w, i, win, wdown):
        with k.phase() as ph:
            HN = ph.sb("HN", [128, 8, NT], BF16)
            FA = ph.sb("FA", [128, 22, NT], BF16)
            norm_mod(ph, l, w, i, HN)
            if FFN_STOP == 1:
                return
            for j in range(11 if FFN_STOP != 2 else 1):
                wv, wb = k.wget(("mat", win, (l,), 8, ((j * 256, 256), (DFF + j * 256, 256))))
                for sub in range(2):
                    for hb in range(2):
                        bg = k.bank(); bu = k.bank()
                        for kc in range(8):
                            k.mm(bg.t[:], wv[:, kc, sub * 128:(sub + 1) * 128], HN.t[:, kc, hb * 512:(hb + 1) * 512], kc == 0, kc == 7, [wb, HN.b], [bg.b])
                        for kc in range(8):
                            k.mm(bu.t[:], wv[:, kc, 256 + sub * 128:256 + (sub + 1) * 128], HN.t[:, kc, hb * 512:(hb + 1) * 512], kc == 0, kc == 7, [wb, HN.b], [bu.b])
                        sg = ph.rot("sg", [128, 512], F32, 3)
                        k.act(sg.t[:], bg.t[:], AF.Silu, [bg.b], [sg.b])
                        k.tt("dve", FA.t[:, j * 2 + sub, hb * 512:(hb + 1) * 512], sg.t[:], bu.t[:], ALU.mult, [sg.b, bu.b], [FA.b])
            if FFN_STOP in (2, 3):
                return
            for fo in range(8):
                wv, wb = k.wget(("mat", wdown, (l,), 22, ((fo * 128, 128),)))
                for hb in range(2):
                    bk = k.bank()
                    for kc in range(22):
                        k.mm(bk.t[:], wv[:, kc, :], FA.t[:, kc, hb * 512:(hb + 1) * 512], kc == 0, kc == 21, [wb, FA.b], [bk.b])
                    xs_ = XT.t[:, fo, hb * 512:(hb + 1) * 512]
                    k.stt("dve", xs_, bk.t[:], k.der.t[:, l, w, 6 + i, fo:fo + 1], xs_, ALU.mult, ALU.add, [bk.b, k.der.b, k.xtb[fo]], [k.xtb[fo]])

    def final(g):
        with k.phase() as ph:
            rstd = rstd_of(ph, lambda c, hb: XT.t[:, c, hb * 512:(hb + 1) * 512], 8, 128, ONESB(), 1.0 / D, "rstd", k.xtb)
            f0 = ROW["final"]
            for c in range(8):
                tmp = ph.rot("ntmp", [128, NT], F32, 2)
                k.stt("dve", tmp.t[:], XT.t[:, c, :], k.pt.t[:, f0 + c:f0 + c + 1], rstd.t[:], ALU.mult, ALU.mult, [k.xtb[c], rstd.b, k.pt.b], [tmp.b])
                for tl in range(8):
                    pass
                k.cp("pool", XT.t[:, c, :], tmp.t[:], [tmp.b], [k.xtb[c]])
            for tl in range(8):
                stg = ph.rot("ystg", [128, D], F32, 2)
                for hb in range(2):
                    bk = k.bank()
                    for c4 in range(4):
                        c = hb * 4 + c4
                        k.tr(bk.t[:, c4 * 128:(c4 + 1) * 128], XT.t[:, c, tl * 128:(tl + 1) * 128], IDF(), [k.xtb[c], k.cst.b], [bk.b])
                    k.cp("act" if hb == 0 else "dve", stg.t[:, hb * 512:(hb + 1) * 512], bk.t[:], [bk.b], [stg.b])
                k.dma("sp", yout[g][tl * 128:(tl + 1) * 128, :], stg.t[:], "sty", rd=[stg.b])


    def proj_fm(wv, wb, c0, n, HN, evac):
        for hb in range(2):
            bk = k.bank()
            for kc in range(8):
                k.mm(bk.t[0:n, :], wv[:, kc, c0:c0 + n], HN.t[:, kc, hb * 512:(hb + 1) * 512], kc == 0, kc == 7, [wb, HN.b], [bk.b])
            evac(hb, bk)

    def proj_tm(wv, wb, c0, n, HN, tok0, M, evac):
        bk = k.bank()
        for kc in range(8):
            k.mm(bk.t[0:M, 0:n], HN.t[:, kc, tok0:tok0 + M], wv[:, kc, c0:c0 + n], kc == 0, kc == 7, [HN.b, wb], [bk.b])
        evac(bk)

    def win_tile(l, c0, n):
        return k.wget(("mat", "w_in", (l,), 8, ((c0, n),)))

    def evac_to(dst, dstb, func=None, n=128, eng="act", scale=1.0):
        def f(hb, bk):
            o = dst[0:n, hb * 512:(hb + 1) * 512]
            if func is not None:
                k.act(o, bk.t[0:n, :], func, [bk.b], [dstb], scale=scale)
            else:
                k.cp(eng, o, bk.t[0:n, :], [bk.b], [dstb])
        return f

    def head_norm(ph, OACC, l, n, BR, gate=None, extra=1.0):
        w0 = ROW["hnorm"] + l * 4 + n
        with k.phase() as ph2:
            rstd = ph2.sb("hrstd", [128, NT])
            for h in range(4):
                k.rstd_of(ph2, lambda c, hb: OACC.t[0:64, h, hb * 512:(hb + 1) * 512], 1, 64, CSB(k, "ones")[0:64, 0:64], 1.0 / 64, "hrstd", [OACC.b], dst=rstd)
                t1 = ph2.rot("hn_t1", [64, NT], F32, 1)
                k.tt("dve", t1.t[:], OACC.t[0:64, h, :], rstd.t[0:64, :], ALU.mult, [OACC.b, rstd.b], [t1.b])
                if gate is not None:
                    k.stt("dve", BR.t[0:64, n, h, :], t1.t[:], k.pt.t[0:64, w0:w0 + 1], gate.t[0:64, h, :], ALU.mult, ALU.mult, [t1.b, k.pt.b, gate.b], [BR.b])
                else:
                    k.ts("pool", BR.t[0:64, n, h, :], t1.t[:], k.pt.t[0:64, w0:w0 + 1], extra, ALU.mult, ALU.mult, [t1.b, k.pt.b], [BR.b])

    def gate_proj(ph, l, c0, HN):
        gate = ph.sb("gate", [64, 4, NT], BF16)
        wv, wb = win_tile(l, c0, 256)
        for h in range(4):
            proj_fm(wv, wb, h * 64, 64, HN, evac_to(gate.t[:, h, :], gate.b, AF.Silu, 64))
        return gate

    def gla_scan(ph, g, l, n, KG, qT, qscale, kT, gT, vtm, hmname, smname, sin_name, sout_name, dk, OACC):
        nseq = 4 if g == 0 else 1
        cps = NCH // nseq
        HPG = 4 // KG
        MASK = {0: CS(k, "mf"), 1: CS(k, "mb")}
        SMv = CS(k, smname)
        for dr in range(2):
            with k.phase() as p2:
                qg = [p2.sb(f"qg{kg}", [128, NT], BF16) for kg in range(KG)]
                kx = [p2.sb(f"kx{h}", [128, NT], BF16) for h in range(4)]
                kd = [p2.sb(f"kd{kg}", [128, NT], BF16) for kg in range(KG)]
                eref = [p2.sb(f"eref{kg}", [128, NCH]) for kg in range(KG)]
                elast = [p2.sb(f"elast{kg}", [128, NCH]) for kg in range(KG)]
                tl = 63 if dr == 0 else 0
                for kg in range(KG):
                    gsrc = gT[dr][kg]; ksrc = kT[dr][kg]
                    cum = p2.rot("cum", [128, NT], F32, 2)
                    k.P.op("dve", (lambda o_, g_: (lambda e: e.tensor_tensor_scan(out=o_, data0=CS(k, "rm"), data1=g_, initial=0.0, op0=ALU.mult, op1=ALU.add)))(cum.t[:], gsrc.t[:]),
                           reads=[gsrc.b, k.cst.b], writes=[cum.b])
                    c3 = lambda t_: t_.t[:].rearrange("p (c j) -> p c j", j=64)
                    if dr == 1:
                        c2 = p2.rot("cum", [128, NT], F32, 2)
                        k.tt("dve", c2.t[:], gsrc.t[:], cum.t[:], ALU.subtract, [gsrc.b, cum.b], [c2.b])
                        k.tt("dve", c3(c2), c3(c2), c3(cum)[:, :, 63:64].to_broadcast([128, NCH, 64]), ALU.add, [c2.b, cum.b], [c2.b])
                        cum = c2
                    bm = p2.rot("bm", [128, NT], F32, 1)
                    k.tt("dve", c3(bm), c3(cum), c3(cum)[:, :, 32:33].to_broadcast([128, NCH, 64]), ALU.subtract, [cum.b], [bm.b])
                    e1 = p2.rot("ee", [128, NT], F32, 2)
                    k.act(e1.t[:], bm.t[:], AF.Exp, [bm.b], [e1.b])
                    k.stt("pool", qg[kg].t[:], qT[kg].t[:], qscale, e1.t[:], ALU.mult, ALU.mult, [qT[kg].b, e1.b], [qg[kg].b])
                    e2 = p2.rot("ee", [128, NT], F32, 2)
                    k.act(e2.t[:], bm.t[:], AF.Exp, [bm.b], [e2.b], scale=-1.0)
                    for hh in range(HPG):
                        h = kg * HPG + hh
                        k.stt("dve" if hh % 2 == 0 else "pool", kx[h].t[:], ksrc.t[:], CS(k, hmname, h, 1), e2.t[:], ALU.mult, ALU.mult, [ksrc.b, e2.b, k.cst.b], [kx[h].b])
                    bl = p2.rot("bm", [128, NT], F32, 1)
                    k.tt("dve", c3(bl), c3(cum)[:, :, tl:tl + 1].to_broadcast([128, NCH, 64]), c3(cum), ALU.subtract, [cum.b], [bl.b])
                    e3 = p2.rot("ee", [128, NT], F32, 2)
                    k.act(e3.t[:], bl.t[:], AF.Exp, [bl.b], [e3.b])
                    k.tt("pool", kd[kg].t[:], ksrc.t[:], e3.t[:], ALU.mult, [ksrc.b, e3.b], [kd[kg].b])
                    k.act(eref[kg].t[:], c3(cum)[:, :, 32], AF.Exp, [cum.b], [eref[kg].b])
                    k.act(elast[kg].t[:], c3(cum)[:, :, tl], AF.Exp, [cum.b], [elast[kg].b])
                S32 = p2.sb("S32", [128, 256])
                ATs = [p2.sb(f"AT{i_}", [64, 256], BF16) for i_ in range(3)]
                for a_ in ATs:
                    k.memset("pool", a_.t[:], 0.0, [a_.b])
                kdTs = [p2.sb(f"kdT{i_}", [64, 256], BF16) for i_ in range(4)]
                ATs = ATs + [p2.sb("AT3", [64, 256], BF16)]
                k.memset("pool", ATs[3].t[:], 0.0, [ATs[3].b])
                v4 = lambda ap_: ap_.rearrange("p (h i) -> p h i", i=64)

                def stageA(ch, slot):
                    tok = slice(ch * 64, (ch + 1) * 64)
                    bb = k.bbanks[k.bb_i % 2]; k.bb_i += 1
                    for kg in range(KG):
                        k.tr(bb.t[0:64, kg * 128:(kg + 1) * 128], kd[kg].t[:, tok], CSB(k, "ident"), [kd[kg].b, k.cstb.b], [bb.b])
                    kdT = kdTs[slot % 4]
                    k.cp("act", kdT.t[:, 0:KG * 128], bb.t[0:64, 0:KG * 128], [bb.b], [kdT.b])
                    pa = k.bank()
                    t0 = ch * 64
                    lo = slice(t0, t0 + 32); hi = slice(t0 + 32, t0 + 64)
                    for h in range(4):
                        qh = qg[h // HPG]
                        rdm = [kx[h].b, qh.b]
                        if dr == 0:
                            k.mm(pa.t[0:32, h * 64:(h + 1) * 64], kx[h].t[:, lo], qh.t[:, tok], True, True, rdm, [pa.b])
                            k.mm(pa.t[32:64, h * 64 + 32:(h + 1) * 64], kx[h].t[:, hi], qh.t[:, hi], True, True, rdm, [pa.b])
                        else:
                            k.mm(pa.t[0:32, h * 64:h * 64 + 32], kx[h].t[:, lo], qh.t[:, lo], True, True, rdm, [pa.b])
                            k.mm(pa.t[32:64, h * 64:(h + 1) * 64], kx[h].t[:, hi], qh.t[:, tok], True, True, rdm, [pa.b])
                    AT = ATs[slot % 4]
                    if dr == 0:
                        k.tt("dve", AT.t[0:32, :], pa.t[0:32, 0:256], MASK[dr][0:32, :], ALU.mult, [pa.b, k.cst.b], [AT.b])
                        k.tt("dve", v4(AT.t[32:64, :])[:, :, 32:64], v4(pa.t[32:64, 0:256])[:, :, 32:64], v4(MASK[dr][32:64, :])[:, :, 32:64], ALU.mult, [pa.b, k.cst.b], [AT.b])
                    else:
                        k.tt("dve", v4(AT.t[0:32, :])[:, :, 0:32], v4(pa.t[0:32, 0:256])[:, :, 0:32], v4(MASK[dr][0:32, :])[:, :, 0:32], ALU.mult, [pa.b, k.cst.b], [AT.b])
                        k.tt("dve", AT.t[32:64, :], pa.t[32:64, 0:256], MASK[dr][32:64, :], ALU.mult, [pa.b, k.cst.b], [AT.b])
                    return kdT, AT

                def stageB(ch, kdT, AT):
                    tok = slice(ch * 64, (ch + 1) * 64)
                    ps_ = k.bank()
                    if KG == 1:
                        k.mm(ps_.t[:, 0:256], kdT.t[:, 0:128], vtm.t[:, ch, :], True, True, [kdT.b, vtm.b], [ps_.b])
                    else:
                        for kg in range(2):
                            k.mm(ps_.t[:, kg * 128:(kg + 1) * 128], kdT.t[:, kg * 128:(kg + 1) * 128], vtm.t[:, ch, kg * 128:(kg + 1) * 128], True, True, [kdT.b, vtm.b], [ps_.b])
                    Sbf = p2.rot("Sbf", [128, 256], BF16, 3)
                    if KG == 1:
                        k.stt("dve", Sbf.t[:], S32.t[:], eref[0].t[:, ch:ch + 1], SMv, ALU.mult, ALU.mult, [S32.b, eref[0].b, k.cst.b], [Sbf.b])
                    else:
                        for kg in range(2):
                            k.stt("dve", Sbf.t[:, kg * 128:(kg + 1) * 128], S32.t[:, kg * 128:(kg + 1) * 128], eref[kg].t[:, ch:ch + 1], SMv[:, kg * 128:(kg + 1) * 128],
                                  ALU.mult, ALU.mult, [S32.b, eref[kg].b, k.cst.b], [Sbf.b])
                    if KG == 1:
                        k.stt("dve", S32.t[:], S32.t[:], elast[0].t[:, ch:ch + 1], ps_.t[:, 0:256], ALU.mult, ALU.add, [S32.b, elast[0].b, ps_.b], [S32.b])
                    else:
                        for kg in range(2):
                            sl = slice(kg * 128, (kg + 1) * 128)
                            k.stt("dve", S32.t[:, sl], S32.t[:, sl], elast[kg].t[:, ch:ch + 1], ps_.t[:, sl], ALU.mult, ALU.add, [S32.b, elast[kg].b, ps_.b], [S32.b])
                    po = k.bank()
                    for h in range(4):
                        o_ = po.t[0:64, h * 64:(h + 1) * 64]
                        k.mm(o_, vtm.t[:, ch, h * 64:(h + 1) * 64], AT.t[:, h * 64:(h + 1) * 64], True, False, [vtm.b, AT.b], [po.b])
                        k.mm(o_, Sbf.t[:, h * 64:(h + 1) * 64], qg[h // HPG].t[:, tok], False, True, [Sbf.b, qg[h // HPG].b], [po.b])
                    oview = OACC.t[0:64, :, tok]
                    pview = po.t[0:64, 0:256].rearrange("p (h i) -> p h i", i=64)
                    if dr == 0:
                        k.cp("act", oview, pview, [po.b], [OACC.b])
                    else:
                        k.tt("dve", oview, oview, pview, ALU.add, [po.b, OACC.b], [OACC.b])

                seqlist = []
                for s_ in range(nseq):
                    order = range(cps) if dr == 0 else range(cps - 1, -1, -1)
                    for ii, ci in enumerate(order):
                        seqlist.append((s_, s_ * cps + ci, ii == 0, ii == cps - 1))
                DEPTH_A = 2
                pend = {}
                for j in range(min(DEPTH_A, len(seqlist))):
                    pend[j] = stageA(seqlist[j][1], j)
                for j, (s_, ch, first, last) in enumerate(seqlist):
                    if first:
                        k.memset("pool", S32.t[:], 0.0, [S32.b])
                        if g == 1:
                            for h in range(4):
                                hh = h % HPG
                                k.dma("sp", S32.t[hh * dk:(hh + 1) * dk, h * 64:(h + 1) * 64] if KG == 2 else S32.t[h * dk:(h + 1) * dk, h * 64:(h + 1) * 64],
                                      k.din[sin_name][l, dr, h], "ldS", wr=[S32.b])
                    kdT, AT = pend.pop(j)
                    stageB(ch, kdT, AT)
                    if j + DEPTH_A < len(seqlist):
                        pend[j + DEPTH_A] = stageA(seqlist[j + DEPTH_A][1], j + DEPTH_A)
                    if last and g == 0:
                        for h in range(4):
                            hh = h % HPG
                            src = S32.t[hh * dk:(hh + 1) * dk, h * 64:(h + 1) * 64] if KG == 2 else S32.t[h * dk:(h + 1) * dk, h * 64:(h + 1) * 64]
                            k.dma("sp", k.dout[sout_name][s_, l, dr, h], src, "stS", rd=[S32.b])

    def mixer_gla(ph, g, l, HN, BR):
        with k.phase() as p1:
            o = IN_OFF["ga_q"]
            wv, wb = win_tile(l, 0, 512)
            qT = p1.sb("qT", [128, NT]); kTt = p1.sb("kT", [128, NT])
            proj_fm(wv, wb, 0, 128, HN, evac_to(qT.t, qT.b))
            proj_fm(wv, wb, 128, 128, HN, evac_to(kTt.t, kTt.b, eng="dve"))
            vtm = p1.sb("vtm", [64, NCH, 256], BF16)
            for ch in range(NCH):
                proj_tm(wv, wb, 256, 256, HN, ch * 64, 64, (lambda ch_: (lambda bk: k.cp("act" if ch_ % 2 else "dve", vtm.t[:, ch_, :], bk.t[0:64, 0:256], [bk.b], [vtm.b])))(ch))
            wv2, wb2 = win_tile(l, 768, 32)
            lrT = p1.sb("lrT", [32, NT], BF16)
            proj_fm(wv2, wb2, 0, 32, HN, evac_to(lrT.t, lrT.b, n=32))
            w2p = p1.sb("w2p", [32, 2, 128], BF16)
            k.memset("pool", w2p.t[:], 0.0, [w2p.b])
            for dr in range(2):
                k.dma("pool", w2p.t[dr * 16:(dr + 1) * 16, dr, :], k.din["gla_w2"][l, dr], "ldw2", wr=[w2p.b])
            negb = p1.sb("negb", [128, 2])
            gb0 = ROW["gla_b"] + l * 2
            k.ts("dve", negb.t[:], k.pt.t[:, gb0:gb0 + 2], -1.0, 0.0, ALU.mult, ALU.add, [k.pt.b], [negb.b])
            gT = [[Tn(k.XT.t[:, 4 + dr, :], Buf(f"gT{dr}"))] for dr in range(2)]
            for dr in range(2):
                for hb in range(2):
                    bk = k.bank()
                    k.mm(bk.t[:], w2p.t[:, dr, :], lrT.t[:, hb * 512:(hb + 1) * 512], True, True, [w2p.b, lrT.b], [bk.b])
                    sl = gT[dr][0].t[:, hb * 512:(hb + 1) * 512]
                    k.act(sl, bk.t[:], AF.Exp, [bk.b, negb.b], [gT[dr][0].b], bias=negb.t[:, dr:dr + 1], scale=-1.0)
                    k.act(sl, sl, AF.Ln, [gT[dr][0].b], [gT[dr][0].b], bias=1.0)
                    k.ts("dve", sl, sl, -1.0 / 16.0, 0.0, ALU.mult, ALU.add, [gT[dr][0].b], [gT[dr][0].b])
            OACC = Tn(k.XT.t[0:64, 0:4, :], Buf("OACC"))
            gla_scan(p1, g, l, 0, 1, [qT], 32 ** -0.5, [[kTt], [kTt]], gT, vtm, "hm_gla", "sm_gla", "sg", "ng", 32, OACC)
            gate = gate_proj(p1, l, 512, HN)
            head_norm(p1, OACC, l, 0, BR, gate)

    def mixer_hg(ph, g, l, HN, BR):
        with k.phase() as p1:
            c0 = IN_OFF["hg_q"]
            wv, wb = win_tile(l, c0, 512)
            wv2, wb2 = win_tile(l, c0 + 512, 512)
            qT = [p1.sb(f"hq{kg}", [128, NT], BF16) for kg in range(2)]
            for kg in range(2):
                proj_fm(wv, wb, kg * 128, 128, HN, evac_to(qT[kg].t, qT[kg].b, AF.Silu))
            vtm = p1.sb("vtm", [64, NCH, 256], BF16)
            for ch in range(NCH):
                proj_tm(wv2, wb2, 256, 256, HN, ch * 64, 64, (lambda ch_: (lambda bk: k.cp("act" if ch_ % 2 else "dve", vtm.t[:, ch_, :], bk.t[0:64, 0:256], [bk.b], [vtm.b])))(ch))
            lb = p1.sb("lb", [128, 4]); oml = p1.sb("oml", [128, 4])
            r0 = ROW["hg_lb"]
            if l == 0:
                k.memset("pool", lb.t[:], 0.0, [lb.b])
            else:
                ex = p1.sb("lbex", [128, 8])
                k.act(ex.t[:], k.pt.t[:, r0:r0 + 8], AF.Exp, [k.pt.b], [ex.b])
                sm_ = p1.sb("lbsum", [128, 4])
                k.tt("dve", sm_.t[:], ex.t[:, 0:4], ex.t[:, 4:8], ALU.add, [ex.b], [sm_.b])
                k.P.op("dve", lambda e: e.reciprocal(out=sm_.t[:], in_=sm_.t[:]), reads=[sm_.b], writes=[sm_.b])
                k.tt("dve", lb.t[:], ex.t[:, 4:8], sm_.t[:], ALU.mult, [ex.b, sm_.b], [lb.b])
            k.ts("dve", oml.t[:], lb.t[:], -1.0, 1.0, ALU.mult, ALU.add, [lb.b], [oml.b])
            kT = [[p1.sb(f"hk{dr}{kg}", [128, NT], BF16) for kg in range(2)] for dr in range(2)]
            gT = [[Tn(k.XT.t[:, 4 + dr * 2 + kg, :], Buf(f"hg{dr}{kg}")) for kg in range(2)] for dr in range(2)]
            for dr in range(2):
                for kg in range(2):
                    wsrc_, wbuf_, cc = (wv, wb, 256 + kg * 128) if dr == 0 else (wv2, wb2, kg * 128)
                    idx = dr * 2 + kg

                    def ev(hb, bk, dr=dr, kg=kg, idx=idx):
                        sl = slice(hb * 512, (hb + 1) * 512)
                        f_ = p1.rot("hf", [128, 512], F32, 1)
                        k.act(f_.t[:], bk.t[:], AF.Sigmoid, [bk.b], [f_.b])
                        k.ts("dve", f_.t[:], f_.t[:], oml.t[:, idx:idx + 1], lb.t[:, idx:idx + 1], ALU.mult, ALU.add, [f_.b, oml.b, lb.b], [f_.b])
                        k.act(gT[dr][kg].t[:, sl], f_.t[:], AF.Ln, [f_.b], [gT[dr][kg].b])
                        k.ts("pool", kT[dr][kg].t[:, sl], f_.t[:], -1.0, 1.0, ALU.mult, ALU.add, [f_.b], [kT[dr][kg].b])
                    proj_fm(wsrc_, wbuf_, cc, 128, HN, ev)
            OACC = Tn(k.XT.t[0:64, 0:4, :], Buf("OACC"))
            gla_scan(p1, g, l, 2, 2, qT, 0.125, kT, gT, vtm, "hm_hg", "sm_hg", "sh", "nh", 64, OACC)
            gate = gate_proj(p1, l, IN_OFF["hg_g"], HN)
            head_norm(p1, OACC, l, 2, BR, gate)


    def lam_setup(ph, l):
        dl = ph.sb("dl", [128, 128])
        k.dma("sp", dl.t[:], k.din["dlam"][l:l + 1, :].partition_broadcast(128), "lddl", wr=[dl.b])
        pr = ph.sb("dlp", [128, 2, 32])
        k.tt("dve", pr.t[:, 0, :], dl.t[:, 0:32], dl.t[:, 32:64], ALU.mult, [dl.b], [pr.b])
        k.tt("dve", pr.t[:, 1, :], dl.t[:, 64:96], dl.t[:, 96:128], ALU.mult, [dl.b], [pr.b])
        sm_ = ph.sb("dls", [128, 2])
        k.P.op("dve", lambda e: e.reduce_sum(out=sm_.t[:], in_=pr.t[:], axis=mybir.AxisListType.X), reads=[pr.b], writes=[sm_.b])
        k.act(sm_.t[:], sm_.t[:], AF.Exp, [sm_.b], [sm_.b])
        lam = ph.sb("lam", [128, 2])
        lam_init = 0.8 - 0.6 * math.exp(-0.3 * l)
        k.stt("dve", lam.t[:, 0:1], sm_.t[:, 0:1], lam_init, sm_.t[:, 1:2], ALU.add, ALU.subtract, [sm_.b], [lam.b])
        k.ts("dve", lam.t[:, 1:2], lam.t[:, 0:1], -1.0, 0.0, ALU.mult, ALU.add, [lam.b], [lam.b])
        return lam, lam_init

    def mixer_df(ph, g, l, HN, BR):
        with k.phase() as p1:
            lam, lam_init = lam_setup(p1, l)
            c0 = IN_OFF["df_q"]
            wv, wb = win_tile(l, c0, 512)
            wv2, wb2 = win_tile(l, c0 + 512, 256)
            NK = 256 if g == 0 else 1536
            NKT = NK // 128
            koff = 0 if g == 0 else 512
            qT = [Tn(k.XT.t[:, 6 + cp, :], Buf(f"dq{cp}")) for cp in range(2)]
            kTf = [Tn(k.XT.t[:, 4 + cp, :], Buf(f"dkf{cp}")) for cp in range(2)]
            KT = [p1.sb(f"dK{cp}", [128, koff + NT], BF16) for cp in range(2)]
            for cp in range(2):
                proj_fm(wv, wb, cp * 128, 128, HN, evac_to(qT[cp].t, qT[cp].b))
                proj_fm(wv, wb, 256 + cp * 128, 128, HN, evac_to(kTf[cp].t, kTf[cp].b, eng="dve"))
            if DF_STOP == 1:
                return
            if g == 1:
                rope = p1.sb("rope", [128, 2048])
                k.dma("sp", rope.t[:], k.din["cst"][:, CST["cos"][0]:CST["cos"][0] + 2048], "ldrope", wr=[rope.b])
                for src in qT + kTf:
                    xb_ = p1.rot("ropexb", [128, NT], BF16, 1)
                    k.cp("pool", xb_.t[:], src.t[:], [src.b], [xb_.b])
                    for hb in range(2):
                        sl = slice(hb * 512, (hb + 1) * 512)
                        bk = k.bank()
                        k.mm(bk.t[:], CSB(k, "rot"), xb_.t[:, sl], True, True, [xb_.b, k.cstb.b], [bk.b])
                        t2 = p1.rot("ropet", [128, 512], F32, 1)
                        k.tt("dve", t2.t[:], bk.t[:], rope.t[:, 1024 + hb * 512:1024 + (hb + 1) * 512], ALU.mult, [bk.b, rope.b], [t2.b])
                        k.tt("pool", src.t[:, sl], src.t[:, sl], rope.t[:, sl], ALU.mult, [src.b, rope.b], [src.b])
                        k.tt("dve", src.t[:, sl], src.t[:, sl], t2.t[:], ALU.add, [src.b, t2.b], [src.b])
                for cp in range(2):
                    for kt in range(4):
                        stg = p1.rot("ckstg", [128, 2, 64], F32, 2)
                        k.dma("sp", stg.t[:], k.din["ck"][l, 2 * cp:2 * cp + 2, kt * 128:(kt + 1) * 128, :].rearrange("h t d -> t h d"), "ldck", wr=[stg.b])
                        bk = k.bank()
                        k.tr(bk.t[:, 0:128], stg.t[:].rearrange("p h d -> p (h d)"), CS(k, "ident"), [stg.b, k.cst.b], [bk.b])
                        k.cp("act", KT[cp].t[:, kt * 128:(kt + 1) * 128], bk.t[:, 0:128], [bk.b], [KT[cp].b])
            for cp in range(2):
                k.cp("pool", KT[cp].t[:, koff:koff + NT], kTf[cp].t[:], [kTf[cp].b], [KT[cp].b])
            if DF_STOP == 6:
                return
            NVT = (koff + NT) // 128
            Vtm = p1.sb("Vtm", [128, NVT, 256], BF16)
            if g == 1:
                for kt in range(4):
                    k.dma("pool", Vtm.t[:, kt, :].rearrange("p (h d) -> p h d", d=64), k.din["cv"][l, :, kt * 128:(kt + 1) * 128, :].rearrange("h t d -> t h d"), "ldcv", wr=[Vtm.b])
            for tl in range(8):
                def evv(bk, tl=tl):
                    if g == 0:
                        so = p1.rot("vout", [128, 256], F32, 2)
                        k.cp("dve", so.t[:], bk.t[:, 0:256], [bk.b], [so.b])
                        k.cp("act", Vtm.t[:, koff // 128 + tl, :], so.t[:], [so.b], [Vtm.b])
                        s_ = tl // 2; t0 = (tl % 2) * 128
                        k.dma("sp", k.dout["nv"][s_, l, :, t0:t0 + 128, :].rearrange("h t d -> t h d"), so.t[:].rearrange("p (h d) -> p h d", d=64), "stv", rd=[so.b])
                    else:
                        k.cp("act", Vtm.t[:, koff // 128 + tl, :], bk.t[:, 0:256], [bk.b], [Vtm.b])
                proj_tm(wv2, wb2, 0, 256, HN, tl * 128, 128, evv)
                if g == 0 and DF_STOP not in (7, 8):
                    def evk(bk, tl=tl):
                        so = p1.rot("kout", [128, 256], F32, 2)
                        k.cp("dve", so.t[:], bk.t[:, 0:256], [bk.b], [so.b])
                        s_ = tl // 2; t0 = (tl % 2) * 128
                        if True:
                            k.dma("sp", k.dout["nk"][s_, l, :, t0:t0 + 128, :].rearrange("h t d -> t h d"), so.t[:].rearrange("p (h d) -> p h d", d=64), "stk", rd=[so.b])
                    proj_tm(wv, wb, 256, 256, HN, tl * 128, 128, evk)
            if DF_STOP in (2, 5, 7, 8):
                return
            QX = [[p1.sb(f"QX{cp}{i}", [128, NT], BF16) for i in range(4)] for cp in range(2)]
            for cp in range(2):
                for i in range(4):
                    k.ts("dve" if i % 2 else "pool", QX[cp][i].t[:], qT[cp].t[:], CS(k, "qm", i, 1), 0.0, ALU.mult, ALU.add, [qT[cp].b, k.cst.b], [QX[cp][i].b])
            if DF_STOP == 3:
                return
            OACC = Tn(k.XT.t[0:64, 0:4, :], Buf("OACC"))
            scale = 32 ** -0.5
            bigb = [[k.banks[i * 3 + j].b for j in range(3)] for i in range(2)]
            nbk = (NK + 511) // 512

            def stageS(qb, h):
                qs = slice(qb * 128, (qb + 1) * 128)
                k0 = (qb // 2) * 256 if g == 0 else 0
                cp = h // 2; hh = h % 2
                Pm = []; rr = []
                for m in range(2):
                    big = k.big[m]; bb_ = bigb[m][0:nbk]
                    for sg_ in range(nbk):
                        n_ = min(512, NK - sg_ * 512)
                        k.mm(big[:, sg_ * 512:sg_ * 512 + n_], QX[cp][hh * 2 + m].t[:, qs], KT[cp].t[:, k0 + sg_ * 512:k0 + sg_ * 512 + n_], True, True,
                             [QX[cp][hh * 2 + m].b, KT[cp].b], bb_)
                    mx = p1.rot("mx", [128, 1], F32, 4)
                    k.P.op("dve", (lambda o_, i_: (lambda e: e.reduce_max(out=o_, in_=i_, axis=mybir.AxisListType.X)))(mx.t[:], big[:, 0:NK]), reads=bb_, writes=[mx.b])
                    k.ts("dve", mx.t[:], mx.t[:], -scale, 0.0, ALU.mult, ALU.add, [mx.b], [mx.b])
                    Pt = p1.rot("Pt", [128, NK], BF16, 2)
                    rs = p1.rot("rs", [128, 1], F32, 4)
                    k.act(Pt.t[:], big[:, 0:NK], AF.Exp, bb_ + [mx.b], [Pt.b, rs.b], bias=mx.t[:], scale=scale, accum=rs.t[:])
                    k.P.op("dve", (lambda o_: (lambda e: e.reciprocal(out=o_, in_=o_)))(rs.t[:]), reads=[rs.b], writes=[rs.b])
                    if m == 1:
                        k.tt("dve", rs.t[:], rs.t[:], lam.t[:, 1:2], ALU.mult, [rs.b, lam.b], [rs.b])
                    Pm.append(Pt); rr.append(rs)
                A = p1.rot("Amat", [128, NK], BF16, 2)
                k.ts("pool", A.t[:], Pm[0].t[:], rr[0].t[:], 0.0, ALU.mult, ALU.add, [Pm[0].b, rr[0].b], [A.b])
                k.stt("dve", A.t[:], Pm[1].t[:], rr[1].t[:], A.t[:], ALU.mult, ALU.add, [Pm[1].b, rr[1].b, A.b], [A.b])
                return A

            def stageT(qb, h, A):
                qs = slice(qb * 128, (qb + 1) * 128)
                k0 = (qb // 2) * 256 if g == 0 else 0
                ATm = p1.rot("ATm", [128, NKT, 128], BF16, 1)
                for t8 in range((NKT + 7) // 8):
                    bb = k.bbanks[k.bb_i % 2]; k.bb_i += 1
                    nt_ = min(8, NKT - t8 * 8)
                    for kt in range(nt_):
                        k.tr(bb.t[:, kt * 128:(kt + 1) * 128], A.t[:, (t8 * 8 + kt) * 128:(t8 * 8 + kt + 1) * 128], CSB(k, "ident"), [A.b, k.cstb.b], [bb.b])
                    k.cp("act", ATm.t[:, t8 * 8:t8 * 8 + nt_, :], bb.t[:, 0:nt_ * 128].rearrange("p (t q) -> p t q", q=128), [bb.b], [ATm.b])
                po = k.bank()
                vt0 = k0 // 128
                for kt in range(NKT):
                    k.mm(po.t[0:64, 0:128], Vtm.t[:, vt0 + kt, h * 64:(h + 1) * 64], ATm.t[:, kt, :], kt == 0, kt == NKT - 1, [Vtm.b, ATm.b], [po.b])
                k.cp("act", OACC.t[0:64, h, qs], po.t[0:64, 0:128], [po.b], [OACC.b])

            units = [(qb, h) for qb in range(8) for h in range(4)]
            Acur = stageS(*units[0])
            for u_ in range(len(units)):
                Anext = stageS(*units[u_ + 1]) if u_ + 1 < len(units) else None
                stageT(units[u_][0], units[u_][1], Acur)
                Acur = Anext
            head_norm(p1, OACC, l, 3, BR, None, 1.0 - lam_init)

    def merge(g, l, HN, BR):
        w = g
        with k.phase() as p1:
            MG = p1.sb("MG", [128, 8, NT], BF16)
            for kq in range(2):
                acc = p1.sb(f"mgacc{kq}", [128, 4, NT])
                for n in range(4):
                    wv, wb = k.wget(("mat", "w_mgate", (l,), 8, ((n * 1024 + kq * 512, 512),)))
                    bv, bbuf = k.wget(("branch", l, n))
                    for k4 in range(4):
                        kc = kq * 4 + k4
                        for hb in range(2):
                            sl = slice(hb * 512, (hb + 1) * 512)
                            bg = k.bank(); bu = k.bank()
                            for kk in range(8):
                                k.mm(bg.t[:], wv[:, kk, k4 * 128:(k4 + 1) * 128], HN.t[:, kk, sl], kk == 0, kk == 7, [wb, HN.b], [bg.b])
                            for h in range(4):
                                k.mm(bu.t[:], bv[:, h, kc * 128:(kc + 1) * 128], BR.t[0:64, n, h, sl], h == 0, h == 3, [bbuf, BR.b], [bu.b])
                            sg = p1.rot("msg", [128, 512], F32, 3)
                            k.act(sg.t[:], bg.t[:], AF.Sigmoid, [bg.b], [sg.b])
                            if n == 0:
                                k.tt("dve", acc.t[:, k4, sl], sg.t[:], bu.t[:], ALU.mult, [sg.b, bu.b], [acc.b])
                            else:
                                k.tt("dve", sg.t[:], sg.t[:], bu.t[:], ALU.mult, [sg.b, bu.b], [sg.b])
                                if n < 3:
                                    k.tt("pool", acc.t[:, k4, sl], acc.t[:, k4, sl], sg.t[:], ALU.add, [acc.b, sg.b], [acc.b])
                                else:
                                    k.tt("pool", MG.t[:, kc, sl], acc.t[:, k4, sl], sg.t[:], ALU.add, [acc.b, sg.b], [MG.b])
            for t2 in range(2):
                wv, wb = k.wget(("mat", "w_out", (l,), 8, ((t2 * 512, 512),)))
                for f4 in range(4):
                    fo = t2 * 4 + f4
                    for hb in range(2):
                        sl = slice(hb * 512, (hb + 1) * 512)
                        bk = k.bank()
                        for kk in range(8):
                            k.mm(bk.t[:], wv[:, kk, f4 * 128:(f4 + 1) * 128], MG.t[:, kk, sl], kk == 0, kk == 7, [wb, MG.b], [bk.b])
                        xs_ = XT.t[:, fo, sl]
                        k.stt("dve", xs_, bk.t[:], k.der.t[:, l, w, 7, fo:fo + 1], xs_, ALU.mult, ALU.add, [bk.b, k.der.b, k.xtb[fo]], [k.xtb[fo]])

    def layer(g, l, mixers=("A", "B", "C", "D")):
        w = g
        ffn(l, w, 0, "ffn1_in", "ffn1_down")
        with k.phase() as ph:
            HN = ph.sb("HN", [128, 8, NT], BF16)
            BR = ph.sb("BR", [64, 4, 4, NT], BF16)
            norm_mod(ph, l, w, 1, HN)
            k.dma("sp", k.xscr[:, :], XT.t[:].rearrange("p c t -> p (c t)"), "spill", rd=k.xtb)
            k.P.barrier()
            if "A" in mixers:
                mixer_gla(ph, g, l, HN, BR)
            else:
                k.memset("pool", BR.t[:, 0], 0.0, [BR.b])
            if "B" in mixers:
                k.mixer_dn(ph, g, l, HN, BR)
            else:
                k.memset("pool", BR.t[:, 1], 0.0, [BR.b])
            if "C" in mixers:
                mixer_hg(ph, g, l, HN, BR)
            else:
                k.memset("pool", BR.t[:, 2], 0.0, [BR.b])
            if "D" in mixers:
                mixer_df(ph, g, l, HN, BR)
            else:
                k.memset("pool", BR.t[:, 3], 0.0, [BR.b])
            for n_, nm_ in enumerate("ABCD"):
                k.tap(f"br{nm_}{g}{l}", BR.t[0:64, n_].rearrange("p h t -> p (h t)"), [64, 4 * NT], [BR.b])
            k.P.barrier()
            k.P.dma("sp", lambda e: e.dma_start(out=XT.t[:].rearrange("p c t -> p (c t)"), in_=k.xscr[:, :]), "d_xt0", writes=k.xtb)
            merge(g, l, HN, BR)
        k.tap(f"xm{g}{l}", XT.t[:].rearrange("p c t -> p (c t)"), [128, 8 * NT], k.xtb)
        ffn(l, w, 2, "ffn2_in", "ffn2_down")

    k.layer = layer
    k.mixer_df = mixer_df
    k.merge = merge

    def mixer_dn(ph, g, l, HN, BR):
        nseq = 4 if g == 0 else 1
        cps = NCH // nseq
        T = NT // nseq
        with k.phase() as p1:
            c0 = IN_OFF["dn_x"]
            wv, wb = win_tile(l, c0, 512)
            wv2, wb2 = win_tile(l, c0 + 512, 272)
            nmc = p1.sb("nmc", [64, 1024], BF16)
            k.dma("pool", nmc.t[:], k.din["cst"][0:64, CST["nm_c"][0]:CST["nm_c"][0] + 1024], "ldnm", wr=[nmc.b])
            QT = [p1.sb(f"nQ{p}", [128, NT], BF16) for p in range(2)]
            KT = [p1.sb(f"nK{p}", [128, NT], BF16) for p in range(2)]
            Ktm = p1.sb("Ktm", [64, NCH, 256], BF16); Vtm = p1.sb("nVtm", [64, NCH, 256], BF16)
            with k.phase() as p2:
                VT = [p2.sb(f"nV{p}", [128, NT], BF16) for p in range(2)]
                for c in range(6):
                    src_w, src_b, cc = (wv, wb, c * 128) if c < 4 else (wv2, wb2, (c - 4) * 128)
                    x = p2.rot("cx", [128, NT], F32, 2)
                    proj_fm(src_w, src_b, cc, 128, HN, evac_to(x.t, x.b))
                    y = p2.rot("cy", [128, NT], F32, 2)
                    wrow = lambda tap: k.pt.t[:, ROW["dn_conv"] + (l * 3 + tap) * 6 + c:ROW["dn_conv"] + (l * 3 + tap) * 6 + c + 1]
                    k.ts("dve", y.t[:], x.t[:], wrow(1), 0.0, ALU.mult, ALU.add, [x.b, k.pt.b], [y.b])
                    x3 = x.t[:].rearrange("p (s t) -> p s t", t=T); y3 = y.t[:].rearrange("p (s t) -> p s t", t=T)
                    k.stt("dve", y3[:, :, 1:T], x3[:, :, 0:T - 1], wrow(0), y3[:, :, 1:T], ALU.mult, ALU.add, [x.b, y.b, k.pt.b], [y.b])
                    k.stt("dve", y3[:, :, 0:T - 1], x3[:, :, 1:T], wrow(2), y3[:, :, 0:T - 1], ALU.mult, ALU.add, [x.b, y.b, k.pt.b], [y.b])
                    if c >= 4:
                        k.act(VT[c - 4].t[:], y.t[:], AF.Silu, [y.b], [VT[c - 4].b])
                    else:
                        k.act(y.t[:], y.t[:], AF.Silu, [y.b], [y.b])
                        rn = k.rstd_of(p2, lambda c_, hb: y.t[:, hb * 512:(hb + 1) * 512], 1, 128, CSB(k, "bd64"), 1.0, f"rn{c % 2}", [y.b])
                        dst = QT[c] if c < 2 else KT[c - 2]
                        k.stt("dve", dst.t[:], y.t[:], 0.125 if c < 2 else 1.0, rn.t[:], ALU.mult, ALU.mult, [y.b, rn.b], [dst.b])
                for ch in range(NCH):
                    tok = slice(ch * 64, (ch + 1) * 64)
                    bb = k.bbanks[k.bb_i % 2]; k.bb_i += 1
                    for p in range(2):
                        k.tr(bb.t[0:64, p * 128:(p + 1) * 128], KT[p].t[:, tok], CSB(k, "ident"), [KT[p].b, k.cstb.b], [bb.b])
                        k.tr(bb.t[0:64, 256 + p * 128:256 + (p + 1) * 128], VT[p].t[:, tok], CSB(k, "ident"), [VT[p].b, k.cstb.b], [bb.b])
                    k.cp("act", Ktm.t[:, ch, :], bb.t[0:64, 0:256], [bb.b], [Ktm.b])
                    k.cp("act", Vtm.t[:, ch, :], bb.t[0:64, 256:512], [bb.b], [Vtm.b])

            if DN_STOP == 2:
                return
            zz = p1.sb("zz", [64, NCH, 16])
            for ch in range(NCH):
                proj_tm(wv2, wb2, 256, 16, HN, ch * 64, 64, (lambda ch_: (lambda bk: k.cp("dve", zz.t[:, ch_, :], bk.t[0:64, 0:16], [bk.b], [zz.b])))(ch))
            par = p1.sb("dnpar", [64, 16])
            k.dma("sp", par.t[:, 0:8], k.din["dn_al"][l:l + 1, :].partition_broadcast(64), "ldpar", wr=[par.b])
            k.dma("sp", par.t[:, 8:16], k.din["dn_dt"][l:l + 1, :].partition_broadcast(64), "ldpar", wr=[par.b])
            k.act(par.t[:, 0:8], par.t[:, 0:8], AF.Exp, [par.b], [par.b])
            k.ts("dve", par.t[:, 0:8], par.t[:, 0:8], -1.0, 0.0, ALU.mult, ALU.add, [par.b], [par.b])
            bcol = p1.sb("bcol", [64, NCH, 8]); lnb = p1.sb("lnb", [64, NCH, 8]); gcol = p1.sb("gcol", [64, NCH, 8])
            k.act(bcol.t[:], zz.t[:, :, 0:8], AF.Sigmoid, [zz.b], [bcol.b])
            k.act(lnb.t[:], bcol.t[:], AF.Ln, [bcol.b], [lnb.b])
            k.tt("dve", gcol.t[:], zz.t[:, :, 8:16], par.t[:, 8:16].unsqueeze(1).to_broadcast([64, NCH, 8]), ALU.add, [zz.b, par.b], [gcol.b])
            k.act(gcol.t[:], gcol.t[:], AF.Exp, [gcol.b], [gcol.b])
            k.act(gcol.t[:], gcol.t[:], AF.Ln, [gcol.b], [gcol.b], bias=1.0)
            k.tt("dve", gcol.t[:], gcol.t[:], par.t[:, 0:8].unsqueeze(1).to_broadcast([64, NCH, 8]), ALU.mult, [gcol.b, par.b], [gcol.b])
            dcol = p1.sb("dcol", [64, NCH, 8]); dlast = p1.sb("dlast", [128, NCH, 8])
            bk = k.bank()
            for ch in range(NCH):
                for dr in range(2):
                    k.mm(bk.t[0:64, ch * 8 + dr * 4:ch * 8 + dr * 4 + 4], CS(k, "mf" if dr == 0 else "mb")[0:64, 0:64], gcol.t[:, ch, dr * 4:dr * 4 + 4], True, True, [gcol.b, k.cst.b], [bk.b])
            k.cp("dve", dcol.t[:].rearrange("p c h -> p (c h)"), bk.t[0:64, 0:NCH * 8], [bk.b], [dcol.b])
            bk = k.bank()
            for ch in range(NCH):
                for dr in range(2):
                    k.mm(bk.t[:, ch * 8 + dr * 4:ch * 8 + dr * 4 + 4], CS(k, "sel_f" if dr == 0 else "sel_b")[0:64, :], dcol.t[:, ch, dr * 4:dr * 4 + 4], True, True, [dcol.b, k.cst.b], [bk.b])
            k.cp("dve", dlast.t[:].rearrange("p c h -> p (c h)"), bk.t[:, 0:NCH * 8], [bk.b], [dlast.b])
            acol = p1.sb("acol", [64, NCH, 8]); expd = p1.sb("expd", [64, NCH, 8]); nexpd = p1.sb("nexpd", [64, NCH, 8])
            edl = p1.sb("edl", [64, NCH, 8]); edlast = p1.sb("edlast", [128, NCH, 8])
            k.tt("dve", acol.t[:], dcol.t[:], lnb.t[:], ALU.add, [dcol.b, lnb.b], [acol.b])
            k.act(expd.t[:], dcol.t[:], AF.Exp, [dcol.b], [expd.b])
            k.ts("dve", nexpd.t[:], expd.t[:], -1.0, 0.0, ALU.mult, ALU.add, [expd.b], [nexpd.b])
            k.tt("dve", edl.t[:], dlast.t[0:64], dcol.t[:], ALU.subtract, [dlast.b, dcol.b], [edl.b])
            k.act(edl.t[:], edl.t[:], AF.Exp, [edl.b], [edl.b])
            k.act(edlast.t[:], dlast.t[:], AF.Exp, [dlast.b], [edlast.b])
            if DN_STOP == 3:
                return
            OACC = Tn(k.XT.t[0:64, 0:4, :], Buf("OACC"))
            bc8 = lambda t_, ch_, lo, n_: t_.t[:, ch_, lo:lo + n_].unsqueeze(2).to_broadcast([64, n_, 64])
            v3 = lambda ap_, n_: ap_.rearrange("p (h i) -> p h i", i=64)
            for dr in range(2):
                h0 = dr * 4
                with k.phase() as p2:
                    TbT = p2.sb("TbT", [64, NCH, 256], BF16); Aqk = p2.sb("Aqk", [64, NCH, 256], BF16)
                    NG = 3
                    atile = lambda i_: Tn(k.XT.t[0:64, 4 + i_ // 4, (i_ % 4) * 256:(i_ % 4 + 1) * 256], Buf(f"dnal{i_}"))
                    slots = []
                    for si in range(NG):
                        if si == 0:
                            tl_ = [p2.sb(f"tp{j_}", [64, 256], F32) for j_ in range(8)]
                        else:
                            tl_ = [atile((si - 1) * 8 + j_) for j_ in range(8)]
                        slots.append({"ec": tl_[0], "eb": tl_[1], "XN": tl_[2:4], "XT": tl_[4:6], "PM": tl_[6:8], "dg": tl_[7],
                                      "KTm": p2.sb(f"KTm{si}", [128, 256], BF16)})
                    hsl = lambda h: slice(h * 64, (h + 1) * 64)

                    def tprep(ch, B):
                        tok = slice(ch * 64, (ch + 1) * 64)
                        dg = B["dg"]; ec = B["ec"]; eb = B["eb"]; KTm = B["KTm"]
                        k.tt("pool", v3(dg.t[:], 4), v3(CS(k, "id8")[0:64, 0:256], 4), bc8(dcol, ch, h0, 4), ALU.mult, [dcol.b, k.cst.b], [dg.b])
                        for h in range(4):
                            k.ts("pool" if h % 2 else "dve", KTm.t[:, h * 64:(h + 1) * 64], KT[h // 2].t[:, tok], CS(k, "hm_hg", h, 1), 0.0, ALU.mult, ALU.add,
                                 [KT[h // 2].b, k.cst.b], [KTm.b])
                        yield
                        rb = k.bank()
                        k.mm(rb.t[0:64, 0:256], CS(k, "ones")[0:64, 0:64], dg.t[:], True, True, [dg.b, k.cst.b], [rb.b])
                        yield
                        k.tt("dve", v3(ec.t[:], 4), v3(rb.t[0:64, 0:256], 4), bc8(dcol, ch, h0, 4), ALU.subtract, [rb.b, dcol.b], [ec.b])
                        k.tt("dve", ec.t[:], ec.t[:], nmc.t[:, dr * 256:(dr + 1) * 256], ALU.min, [ec.b, nmc.b], [ec.b])
                        k.stt("dve", v3(eb.t[:], 4), v3(rb.t[0:64, 0:256], 4), -1.0, bc8(acol, ch, h0, 4), ALU.mult, ALU.add, [rb.b, acol.b], [eb.b])
                        k.tt("dve", eb.t[:], eb.t[:], nmc.t[:, 512 + dr * 256:512 + (dr + 1) * 256], ALU.min, [eb.b, nmc.b], [eb.b])
                        gk = k.bank(); gq = k.bank()
                        for h in range(4):
                            p = h // 2
                            k.mm(gk.t[0:64, hsl(h)], KTm.t[:, hsl(h)], KT[p].t[:, tok], True, True, [KT[p].b, KTm.b], [gk.b])
                            k.mm(gq.t[0:64, hsl(h)], KTm.t[:, hsl(h)], QT[p].t[:, tok], True, True, [KTm.b, QT[p].b], [gq.b])
                        yield
                        k.act(ec.t[:], ec.t[:], AF.Exp, [ec.b], [ec.b])
                        k.act(eb.t[:], eb.t[:], AF.Exp, [eb.b], [eb.b])
                        yield
                        XN = B["XN"][0]; XTt = B["XT"][0]; Pm = B["PM"][0]
                        k.stt("dve", XN.t[:], gk.t[0:64, 0:256], -1.0, eb.t[:], ALU.mult, ALU.mult, [gk.b, eb.b], [XN.b])
                        k.tt("dve", Aqk.t[:, ch, :], gq.t[0:64, 0:256], ec.t[:], ALU.mult, [gq.b, ec.b], [Aqk.b])
                        yield
                        bt = k.bank()
                        for h in range(4):
                            k.tr(bt.t[0:64, hsl(h)], XN.t[:, hsl(h)], CS(k, "ident")[0:64, 0:64], [XN.b, k.cst.b], [bt.b])
                        yield
                        k.cp("act", XTt.t[:], bt.t[0:64, 0:256], [bt.b], [XTt.b])
                        yield
                        k.tt("pool", Pm.t[:], XTt.t[:], CS(k, "id8")[0:64, 0:256], ALU.add, [XTt.b, k.cst.b], [Pm.b])
                        for lev in range(5):
                            b1 = None
                            if lev < 4:
                                b1 = k.bank()
                                for h in range(4):
                                    k.mm(b1.t[0:64, hsl(h)], XN.t[:, hsl(h)], XTt.t[:, hsl(h)], True, True, [XN.b, XTt.b], [b1.b])
                            b2 = k.bank()
                            for h in range(4):
                                k.mm(b2.t[0:64, hsl(h)], XTt.t[:, hsl(h)], XN.t[:, hsl(h)], True, True, [XN.b, XTt.b], [b2.b])
                            yield
                            XN2 = B["XN"][(lev + 1) % 2]
                            k.cp("act", XN2.t[:], b2.t[0:64, 0:256], [b2.b], [XN2.b])
                            if lev < 4:
                                XT2 = B["XT"][(lev + 1) % 2]
                                k.cp("act", XT2.t[:], b1.t[0:64, 0:256], [b1.b], [XT2.b])
                                XTt = XT2
                            XN = XN2
                            yield
                            b3 = k.bank()
                            for h in range(4):
                                k.mm(b3.t[0:64, hsl(h)], XN.t[:, hsl(h)], Pm.t[:, hsl(h)], True, True, [XN.b, Pm.b], [b3.b])
                            yield
                            Pn = B["PM"][(lev + 1) % 2]
                            k.tt("dve", Pn.t[:], Pm.t[:], b3.t[0:64, 0:256], ALU.add, [Pm.b, b3.b], [Pn.b])
                            Pm = Pn
                            yield
                        k.tt("dve", v3(TbT.t[:, ch, :], 4), v3(Pm.t[:], 4), bc8(bcol, ch, h0, 4), ALU.mult, [Pm.b, bcol.b], [TbT.b])

                    for c0_ in range(0, NCH, NG):
                        gens = [tprep(ch, slots[i_]) for i_, ch in enumerate(range(c0_, min(NCH, c0_ + NG)))]
                        while gens:
                            nxt = []
                            for g_ in gens:
                                try:
                                    next(g_)
                                    nxt.append(g_)
                                except StopIteration:
                                    pass
                            gens = nxt
                    if DN_STOP == 4:
                        return
                    S32 = p2.sb("S32", [128, 256])
                    for s_ in range(nseq):
                        k.memset("pool", S32.t[:], 0.0, [S32.b])
                        if g == 1:
                            for h in range(4):
                                k.dma("sp", S32.t[(h % 2) * 64:(h % 2 + 1) * 64, h * 64:(h + 1) * 64], k.din["sd"][l, dr, h], "ldS", wr=[S32.b])
                        order = range(cps) if dr == 0 else range(cps - 1, -1, -1)
                        for ci in order:
                            ch = s_ * cps + ci
                            tok = slice(ch * 64, (ch + 1) * 64)
                            Sbf = p2.rot("Sbf", [128, 256], BF16, 3)
                            k.tt("pool", Sbf.t[:], S32.t[:], CS(k, "sm_hg"), ALU.mult, [S32.b, k.cst.b], [Sbf.b])
                            pk = k.bank(); pq = k.bank()
                            for h in range(4):
                                p = h // 2
                                k.mm(pk.t[0:64, h * 64:(h + 1) * 64], KT[p].t[:, tok], Sbf.t[:, h * 64:(h + 1) * 64], True, True, [KT[p].b, Sbf.b], [pk.b])
                                k.mm(pq.t[0:64, h * 64:(h + 1) * 64], QT[p].t[:, tok], Sbf.t[:, h * 64:(h + 1) * 64], True, True, [QT[p].b, Sbf.b], [pq.b])
                            Y = p2.rot("Y", [64, 256], F32, 1); Yb = p2.rot("Yb", [64, 256], BF16, 2)
                            k.tt("dve", v3(Y.t[:], 4), v3(pk.t[0:64, 0:256], 4), bc8(nexpd, ch, h0, 4), ALU.mult, [pk.b, nexpd.b], [Y.b])
                            k.tt("dve", Yb.t[:], Y.t[:], Vtm.t[:, ch, :], ALU.add, [Y.b, Vtm.b], [Yb.b])
                            pv = k.bank()
                            for h in range(4):
                                k.mm(pv.t[0:64, h * 64:(h + 1) * 64], TbT.t[:, ch, h * 64:(h + 1) * 64], Yb.t[:, h * 64:(h + 1) * 64], True, True, [TbT.b, Yb.b], [pv.b])
                            VN = p2.rot("VN", [64, 256], BF16, 2); VNs = p2.rot("VNs", [64, 256], BF16, 2)
                            k.cp("dve", VN.t[:], pv.t[0:64, 0:256], [pv.b], [VN.b])
                            k.tt("dve", v3(VNs.t[:], 4), v3(pv.t[0:64, 0:256], 4), bc8(edl, ch, h0, 4), ALU.mult, [pv.b, edl.b], [VNs.b])
                            pa = k.bank()
                            for h in range(4):
                                k.mm(pa.t[0:64, h * 64:(h + 1) * 64], Aqk.t[:, ch, h * 64:(h + 1) * 64], VN.t[:, h * 64:(h + 1) * 64], True, True, [Aqk.b, VN.b], [pa.b])
                            ot = p2.rot("ot", [64, 256], F32, 1)
                            k.tt("dve", v3(ot.t[:], 4), v3(pq.t[0:64, 0:256], 4), bc8(expd, ch, h0, 4), ALU.mult, [pq.b, expd.b], [ot.b])
                            k.tt("dve", ot.t[:], ot.t[:], pa.t[0:64, 0:256], ALU.add, [ot.b, pa.b], [ot.b])
                            pt_ = k.bank()
                            for h in range(4):
                                k.tr(pt_.t[0:64, h * 64:(h + 1) * 64], ot.t[:, h * 64:(h + 1) * 64], CS(k, "ident")[0:64, 0:64], [ot.b, k.cst.b], [pt_.b])
                            oview = OACC.t[0:64, :, tok]
                            if dr == 0:
                                k.cp("act", oview, v3(pt_.t[0:64, 0:256], 4), [pt_.b], [OACC.b])
                            else:
                                k.tt("dve", oview, oview, v3(pt_.t[0:64, 0:256], 4), ALU.add, [pt_.b, OACC.b], [OACC.b])
                            pS = k.bank()
                            for p in range(2):
                                k.mm(pS.t[:, p * 128:(p + 1) * 128], Ktm.t[:, ch, p * 128:(p + 1) * 128], VNs.t[:, p * 128:(p + 1) * 128], True, True, [Ktm.b, VNs.b], [pS.b])
                            for h in range(4):
                                sl = slice(h * 64, (h + 1) * 64)
                                k.stt("dve", S32.t[:, sl], S32.t[:, sl], edlast.t[:, ch, h0 + h:h0 + h + 1], pS.t[:, sl], ALU.mult, ALU.add,
                                      [S32.b, edlast.b, pS.b], [S32.b])
                        if g == 0:
                            for h in range(4):
                                k.dma("sp", k.dout["nd"][s_, l, dr, h], S32.t[(h % 2) * 64:(h % 2 + 1) * 64, h * 64:(h + 1) * 64], "stS", rd=[S32.b])
            gate = gate_proj(p1, l, IN_OFF["dn_g"], HN)
            head_norm(p1, OACC, l, 1, BR, gate)

    k.mixer_dn = mixer_dn
    k.mixer_gla = mixer_gla
    k.mixer_hg = mixer_hg
    k.gla_scan = gla_scan
    k.head_norm = head_norm
    k.gate_proj = gate_proj
    k.proj_fm = proj_fm
    k.proj_tm = proj_tm
    k.win_tile = win_tile
    k.evac_to = evac_to
    k.ffn = ffn
    k.xload = xload
    k.final = final
    k.rstd_of = rstd_of
    k.norm_mod = norm_mod
    return k


def finish(k):
    k.P.final_wait("sp")
    keys = list(ENGS) + list(k.P.dma_cnt.keys())
    sems = {key: k.es.enter_context(k.nc.semaphore("s_" + key)) for key in keys}
    with k.nc.Block() as block:
        replay(k.P, block, sems)
    k.es.close()
    return k.nc


def host_inputs(inp, core):
    b = core // 4
    f = lambda a: np.ascontiguousarray(np.asarray(a, np.float32))
    m = {
        "xp": f(inp["x_prompt"][core * 4:(core + 1) * 4].reshape(NT, D)),
        "xs": f(inp["x_sample"][b]),
        "ck": f(inp["cache_diff_k"][b]), "cv": f(inp["cache_diff_v"][b]),
        "sg": f(inp["state_gla"][b]), "sd": f(inp["state_dn"][b]), "sh": f(inp["state_hgrn"][b]),
        "pvec": pack_pvec(inp, b), "cst": make_consts(),
        "dn_al": f(inp["dn_a_log"].reshape(DEPTH, 8)), "dn_dt": f(inp["dn_dt_bias"].reshape(DEPTH, 8)),
        "dlam": f(inp["diff_lambda"].reshape(DEPTH, 128)), "gla_w2": f(inp["gla_w2"]),
    }
    for n_ in ("w_mod", "ffn1_in", "ffn2_in", "ffn1_down", "ffn2_down", "w_in", "w_branch", "w_mgate", "w_out"):
        m[n_] = f(inp[n_])
    return m


def program(k, mixers=("A", "B", "C", "D")):
    for g in range(2):
        k.xload(g)
        for l in range(DEPTH):
            k.layer(g, l, mixers)
        k.final(g)


_CACHE = {}


def get_nc():
    if "nc" not in _CACHE:
        k1 = build(None)
        program(k1)
        k2 = build(list(k1.wrec))
        program(k2)
        _CACHE["nc"] = finish(k2)
    return _CACHE["nc"]


def kernel(**inp):
    inp = {n: np.asarray(v) for n, v in inp.items()}
    nc = get_nc()
    in_maps = [host_inputs(inp, c) for c in range(8)]
    res = run_bass_kernel_spmd(nc, in_maps, core_ids=list(range(8)))
    R = res.results
    y_prompt = np.concatenate([R[c]["yp"].reshape(4, 256, D) for c in range(8)], axis=0)
    y_sample = np.stack([R[0]["ys"], R[4]["ys"]], axis=0)
    cat = lambda n_: np.concatenate([R[c][n_] for c in range(8)], axis=0)
    return (y_prompt.astype(np.float32), y_sample.astype(np.float32), cat("nk"), cat("nv"), cat("ng"), cat("nd"), cat("nh"))
```

```python
import math
from contextlib import ExitStack
import numpy as np
import concourse.bass as bass
import concourse.mybir as mybir
from concourse.bass_utils import run_bass_kernel_spmd

F32 = mybir.dt.float32
BF16 = mybir.dt.bfloat16
AF = mybir.ActivationFunctionType
ALU = mybir.AluOpType
ENGS = ("pe", "act", "dve", "pool", "sp")

D = 1024
NT = 1024
DFF = 2816
NIN = 3888
EPS = 1e-6
DEPTH = 2
CH = 64
NCH = NT // CH
FFN_STOP = 0
DF_STOP = 0
DN_STOP = 0


class Buf:
    __slots__ = ("name", "w", "r")

    def __init__(self, name=""):
        self.name = name
        self.w = None
        self.r = {}


class Prog:
    def __init__(self):
        self.ops = {e: [] for e in ENGS}
        self.cnt = {e: 0 for e in ENGS}
        self.seen = {e: {} for e in ENGS}
        self.dma_cnt = {}

    def _need(self, eng, tick, waits):
        if tick is None:
            return
        k, v = tick
        if self.seen[eng].get(k, 0) >= v:
            return
        waits[k] = max(waits.get(k, 0), v)

    def _deps(self, eng, reads, writes, is_dma):
        waits = {}
        for b in reads:
            self._need(eng, b.w, waits)
        for b in writes:
            if b.w is not None and (is_dma or b.w[0] != eng):
                self._need(eng, b.w, waits)
            for k, v in b.r.items():
                if is_dma or k != eng:
                    self._need(eng, (k, v), waits)
        for k, v in waits.items():
            self.seen[eng][k] = v
        return tuple(waits.items())

    def op(self, eng, fn, reads=(), writes=(), inc=True):
        waits = self._deps(eng, reads, writes, False)
        if inc:
            self.cnt[eng] += 1
            tv = self.cnt[eng]
        else:
            tv = self.cnt[eng] + 1
        self.ops[eng].append((waits, fn, (eng, 1) if inc else None))
        for b in reads:
            b.r[eng] = tv
        for b in writes:
            b.w = (eng, tv)
            b.r = {}

    def dma(self, eng, fn, semkey, reads=(), writes=(), n=1):
        waits = self._deps(eng, reads, writes, True)
        self.dma_cnt[semkey] = self.dma_cnt.get(semkey, 0) + 16 * n
        tick = (semkey, self.dma_cnt[semkey])
        self.ops[eng].append((waits, fn, (semkey, 16)))
        for b in reads:
            b.r[semkey] = tick[1]
        for b in writes:
            b.w = tick
            b.r = {}
        return tick

    def barrier(self, skip=lambda k: k.startswith("w") and not k.startswith("d_")):
        for e in ENGS:
            waits = {}
            for k in ENGS:
                if k != e and self.cnt[k] > 0:
                    self._need(e, (k, self.cnt[k]), waits)
            for k, v in self.dma_cnt.items():
                if not skip(k):
                    self._need(e, (k, v), waits)
            for k, v in waits.items():
                self.seen[e][k] = v
            if waits:
                self.ops[e].append((tuple(waits.items()), None, None))

    def final_wait(self, eng):
        waits = {}
        for k in ENGS:
            if k != eng and self.cnt[k] > 0:
                self._need(eng, (k, self.cnt[k]), waits)
        for k, v in self.dma_cnt.items():
            self._need(eng, (k, v), waits)
        self.ops[eng].append((tuple(waits.items()), None, None))


def replay(prog, block, sems):
    def run(name):
        def body(eng):
            for waits, fn, inc in prog.ops[name]:
                for k, v in waits:
                    eng.wait_ge(sems[k], v)
                if fn is None:
                    continue
                res = fn(eng)
                if inc is not None:
                    if isinstance(res, (list, tuple)):
                        for r in res:
                            r.then_inc(sems[inc[0]], inc[1])
                    else:
                        res.then_inc(sems[inc[0]], inc[1])
        return body
    block.tensor(run("pe"))
    block.scalar(run("act"))
    block.vector(run("dve"))
    block.gpsimd(run("pool"))
    block.sync(run("sp"))


IN_OFF = {}
_o = 0
for _n, _s in [("ga_q", 128), ("ga_k", 128), ("ga_v", 256), ("ga_r", 256), ("ga_lr", 32), ("dn_x", 768), ("dn_b", 8),
               ("dn_a", 8), ("dn_g", 256), ("hg_q", 256), ("hg_f", 512), ("hg_i", 256), ("hg_g", 256), ("df_q", 256),
               ("df_k", 256), ("df_v", 256)]:
    IN_OFF[_n] = _o
    _o += _s
assert _o == NIN

ROW = {}
_r = 0
for _n, _s in [("norm_w", DEPTH * 3 * 8), ("b_mod", DEPTH * 72), ("final", 8), ("c_ctx", 8), ("c_lat", 8), ("gla_b", DEPTH * 2),
               ("dn_conv", DEPTH * 3 * 6), ("hg_lb", DEPTH * 2 * 2), ("hnorm", DEPTH * 4)]:
    ROW[_n] = _r
    _r += _s
NROW = 384
assert _r <= NROW

CST = {}
_c = 0
for _n, _s in [("ident", 128), ("ones", 128), ("bd64", 128), ("rot", 128), ("sel_f", 128), ("sel_b", 128), ("id8", 512),
               ("mf", 256), ("mb", 256), ("rm", 1024), ("hm_gla", 4), ("hm_hg", 4),
               ("sm_gla", 256), ("sm_hg", 256), ("qm", 4), ("CSTA_END", 0),
               ("nm_c", 512), ("nm_b", 512), ("NM_END", 0),
               ("cos", 1024), ("sin", 1024)]:
    CST[_n] = (_c, _s)
    _c += _s
NCST = _c
NCSTA = CST["CSTA_END"][0]
NCSTB = CST["mf"][0]


def make_consts():
    c = np.zeros((128, NCST), np.float32)

    def put(name, arr):
        o, s = CST[name]
        a = np.zeros((128, s), np.float32)
        a[:arr.shape[0], :arr.shape[1]] = arr
        c[:, o:o + s] = a
    put("ident", np.eye(128))
    put("ones", np.ones((128, 128)))
    bd = np.zeros((128, 128)); bd[:64, :64] = 1; bd[64:, 64:] = 1
    put("bd64", bd)
    j = np.arange(64)[:, None]; i = np.arange(64)[None, :]
    put("mf", np.tile((j <= i).astype(np.float32), (1, 4)))
    put("mb", np.tile((j >= i).astype(np.float32), (1, 4)))
    rm = np.ones((128, 1024)); rm[:, ::64] = 0
    put("rm", rm)
    d = np.arange(128)[:, None]; h = np.arange(4)[None, :]
    put("hm_gla", (d // 32 == h).astype(np.float32))
    put("hm_hg", (d // 64 == h % 2).astype(np.float32))
    put("sm_gla", np.repeat((d // 32 == h).astype(np.float32), 64, axis=1))
    put("sm_hg", np.repeat((d // 64 == h % 2).astype(np.float32), 64, axis=1))
    NEG = -30000.0
    put("nm_c", np.concatenate([np.tile(np.where(j <= i, 0.0, NEG), (1, 4)), np.tile(np.where(j >= i, 0.0, NEG), (1, 4))], axis=1))
    put("nm_b", np.concatenate([np.tile(np.where(j > i, 0.0, NEG), (1, 4)), np.tile(np.where(j < i, 0.0, NEG), (1, 4))], axis=1))
    t = np.arange(1024)
    row = (t // 64).astype(np.float32); col = (t % 64).astype(np.float32)
    half = 16
    inv = (10000.0 ** (-np.arange(0, half, 2, dtype=np.float32) / half)).astype(np.float32)
    ang_r = np.concatenate([row[:, None] * inv[None, :]] * 2, axis=1)
    ang_c = np.concatenate([col[:, None] * inv[None, :]] * 2, axis=1)
    ang = np.concatenate([ang_r, ang_c], axis=1).astype(np.float32)
    cos32 = np.cos(ang).T; sin32 = np.sin(ang).T
    put("cos", np.tile(cos32, (4, 1)))
    put("sin", np.tile(sin32, (4, 1)))
    R = np.zeros((128, 128), np.float32)
    for p in range(128):
        b16 = (p // 16) * 16; dd = p % 16
        if dd < 8:
            R[b16 + dd + 8, p] = -1.0
        else:
            R[b16 + dd - 8, p] = 1.0
    put("rot", R)
    put("qm", (d // 32 == h).astype(np.float32))
    sf = np.zeros((64, 128)); sf[63, :] = 1
    sb_ = np.zeros((64, 128)); sb_[0, :] = 1
    put("sel_f", sf); put("sel_b", sb_)
    put("id8", np.tile(np.eye(64), (1, 8)))
    return c


def pack_pvec(inp, b):
    rows = np.zeros((NROW, 128), np.float32)

    def put(name, arr):
        a = np.asarray(arr, np.float32).reshape(-1, 128)
        rows[ROW[name]:ROW[name] + a.shape[0]] = a
    put("norm_w", inp["norm_w"])
    put("b_mod", inp["b_mod"])
    put("final", inp["final_norm"])
    put("c_ctx", inp["c_ctx"])
    put("c_lat", inp["c"][b])
    put("gla_b", inp["gla_b"])
    put("dn_conv", inp["dn_conv"])
    put("hg_lb", inp["hg_lb_logits"])
    hn = np.stack([np.stack([np.tile(inp[k][l], 2) for k in ("gla_norm", "dn_norm", "hg_norm", "diff_norm")]) for l in range(DEPTH)])
    put("hnorm", hn)
    return rows


class Tn:
    __slots__ = ("t", "b")

    def __init__(self, t, b):
        self.t = t
        self.b = b


class Phase:
    def __init__(self, k):
        self.k = k
        self.es = ExitStack()
        self.rots = {}

    def __enter__(self):
        self.es.__enter__()
        return self

    def sb(self, name, shape, dt=F32):
        self.k.uid += 1
        t = self.es.enter_context(self.k.nc.sbuf_tensor(f"{name}_{self.k.uid}", list(shape), dt))
        return Tn(t, Buf(name))

    def rot(self, name, shape, dt=F32, n=2):
        if name not in self.rots:
            self.rots[name] = [[self.sb(f"{name}{i}", shape, dt) for i in range(n)], 0]
        lst = self.rots[name]
        t = lst[0][lst[1] % n]
        lst[1] += 1
        return t

    def __exit__(self, *a):
        self.k.P.barrier()
        return self.es.__exit__(*a)


class K:
    def __init__(self, wplan=None, taps=(), stages=None):
        self.nc = bass.Bass("TRN2", target_bir_lowering=False)
        self.P = Prog()
        self.es = ExitStack()
        self.uid = 0
        self.wplan = wplan
        self.wrec = []
        self.wi = 0
        self.wissued = 0
        self.taps = set(taps)
        self.tap_out = {}
        self.stages = stages
        self.din = {}
        self.dout = {}
        self.bank_i = 0

    def inp(self, name, shape):
        self.din[name] = self.nc.dram_tensor(name, list(shape), F32, kind="ExternalInput").ap()
        return self.din[name]

    def outp(self, name, shape):
        self.dout[name] = self.nc.dram_tensor(name, list(shape), F32, kind="ExternalOutput").ap()
        return self.dout[name]

    def sb(self, name, shape, dt=F32):
        self.uid += 1
        t = self.es.enter_context(self.nc.sbuf_tensor(f"{name}_{self.uid}", list(shape), dt))
        return Tn(t, Buf(name))

    def phase(self):
        return Phase(self)

    def mm(self, out, lhsT, rhs, start, stop, rd, wr, inc=None):
        inc = True
        self.P.op("pe", lambda e: e.matmul(out, lhsT=lhsT, rhs=rhs, start=start, stop=stop), reads=rd, writes=wr, inc=inc)

    def tr(self, out, in_, ident, rd, wr):
        self.P.op("pe", lambda e: e.transpose(out, in_, ident), reads=rd, writes=wr)

    def act(self, out, in_, func, rd, wr, bias=0.0, scale=1.0, accum=None):
        if accum is None:
            self.P.op("act", lambda e: e.activation(out=out, in_=in_, func=func, bias=bias, scale=scale), reads=rd, writes=wr)
        else:
            self.P.op("act", lambda e: e.activation(out=out, in_=in_, func=func, bias=bias, scale=scale, accum_out=accum), reads=rd, writes=wr)

    def tt(self, eng, out, a, b, op, rd, wr):
        if eng == "pool" and op not in (ALU.add, ALU.subtract, ALU.mult):
            eng = "dve"
        self.P.op(eng, lambda e: e.tensor_tensor(out=out, in0=a, in1=b, op=op), reads=rd, writes=wr)

    def ts(self, eng, out, a, s1, s2, op0, op1, rd, wr):
        self.P.op(eng, lambda e: e.tensor_scalar(out=out, in0=a, scalar1=s1, scalar2=s2, op0=op0, op1=op1), reads=rd, writes=wr)

    def stt(self, eng, out, a, s, b, op0, op1, rd, wr):
        eng = "dve"
        self.P.op(eng, lambda e: e.scalar_tensor_tensor(out=out, in0=a, scalar=s, in1=b, op0=op0, op1=op1), reads=rd, writes=wr)

    def cp(self, eng, out, in_, rd, wr):
        if eng == "act":
            self.P.op("act", lambda e: e.copy(out=out, in_=in_), reads=rd, writes=wr)
        else:
            self.P.op(eng, lambda e: e.tensor_copy(out=out, in_=in_), reads=rd, writes=wr)

    def memset(self, eng, ap, val, wr):
        self.P.op(eng, lambda e: e.memset(ap, val), writes=wr)

    def dma(self, eng, out, in_, key, rd=(), wr=()):
        key = "d_" + (wr[0].name if wr else rd[0].name)
        return self.P.dma(eng, lambda e: e.dma_start(out=out, in_=in_), key, reads=rd, writes=wr)

    def bank(self):
        b = self.banks[self.bank_i % len(self.banks)]
        self.bank_i += 1
        return b

    def tap(self, name, tn_ap, shape, rd):
        if name not in self.taps:
            return
        o = self.outp("tap_" + name, shape)
        self.P.dma("pool", lambda e: e.dma_start(out=o, in_=tn_ap), "tap_" + name, reads=rd)

    def wsrc(self, spec, slot):
        kind = spec[0]
        if kind == "mat":
            _, name, idx, nk, ranges = spec
            W = self.din[name]
            for i in idx:
                W = W[i]
            tot = sum(n for _, n in ranges)
            view = slot.t[:, 0:nk * tot].rearrange("p (k c) -> p k c", c=tot)
            pieces = []
            o = 0
            for c0, n in ranges:
                pieces.append((view[:, :, o:o + n], W[:, c0:c0 + n].rearrange("(k p) c -> p k c", p=128)))
                o += n
            return pieces, view
        if kind == "branch":
            _, l, n = spec
            W = self.din["w_branch"][l][n]
            view = slot.t[0:64, 0:4096].rearrange("p (h f) -> p h f", f=1024)
            return [(view, W.rearrange("(h e) f -> e h f", e=64))], view
        if kind == "merge":
            _, l, n, q = spec
            gv = slot.t[:, 0:2048].rearrange("p (k c) -> p k c", c=256)
            bv = slot.t[0:64, 2048:3072].rearrange("p (h f) -> p h f", f=256)
            Wg = self.din["w_mgate"][l]
            Wb = self.din["w_branch"][l][n]
            pieces = [(gv, Wg[:, n * 1024 + q * 256:n * 1024 + (q + 1) * 256].rearrange("(k p) c -> p k c", p=128)),
                      (bv, Wb[:, q * 256:(q + 1) * 256].rearrange("(h e) f -> e h f", e=64))]
            return pieces, (gv, bv)
        raise ValueError(kind)

    def _wissue(self, j, spec):
        slot = self.wslots[j % len(self.wslots)]
        pieces, view = self.wsrc(spec, slot)
        key = f"w{j % len(self.wslots)}"
        self.P.dma("pool", lambda e: [e.dma_start(out=o, in_=i) for o, i in pieces], key, writes=[slot.b], n=len(pieces))

    def wget(self, spec):
        j = self.wi
        self.wi += 1
        self.wrec.append(spec)
        slot = self.wslots[j % len(self.wslots)]
        if self.wplan is None:
            self._wissue(j, spec)
        else:
            assert self.wplan[j] == spec, (j, spec, self.wplan[j])
            while self.wissued < min(j + len(self.wslots) - 1, len(self.wplan)):
                self._wissue(self.wissued, self.wplan[self.wissued])
                self.wissued += 1
        _, view = self.wsrc(spec, slot)
        return view, slot.b


def CS(k, name, lo=0, n=None):
    o, s = CST[name]
    n = s - lo if n is None else n
    return k.cst.t[:, o + lo:o + lo + n]


def CSB(k, name, lo=0, n=None):
    o, s = CST[name]
    n = s - lo if n is None else n
    return k.cstb.t[:, o + lo:o + lo + n]


def build(wplan=None, taps=(), stages=("all",)):
    k = K(wplan, taps, stages)
    nc = k.nc
    es = k.es
    st = set(stages)
    ALL = "all" in st
    xin = [k.inp("xp", [NT, D]), k.inp("xs", [NT, D])]
    k.inp("ck", [DEPTH, 4, 512, 64]); k.inp("cv", [DEPTH, 4, 512, 64])
    k.inp("sg", [DEPTH, 2, 4, 32, 64]); k.inp("sd", [DEPTH, 2, 4, 64, 64]); k.inp("sh", [DEPTH, 2, 4, 64, 64])
    k.inp("pvec", [NROW, 128]); k.inp("cst", [128, NCST])
    k.inp("dn_al", [DEPTH, 8]); k.inp("dn_dt", [DEPTH, 8]); k.inp("dlam", [DEPTH, 128]); k.inp("gla_w2", [DEPTH, 2, 16, 128])
    k.inp("w_mod", [DEPTH, D, 9 * D])
    for n_ in ("ffn1_in", "ffn2_in"):
        k.inp(n_, [DEPTH, D, 2 * DFF])
    for n_ in ("ffn1_down", "ffn2_down"):
        k.inp(n_, [DEPTH, DFF, D])
    k.inp("w_in", [DEPTH, D, NIN]); k.inp("w_branch", [DEPTH, 4, 256, D]); k.inp("w_mgate", [DEPTH, D, 4 * D]); k.inp("w_out", [DEPTH, D, D])
    yout = [k.outp("yp", [NT, D]), k.outp("ys", [NT, D])]
    k.outp("nk", [4, DEPTH, 4, 256, 64]); k.outp("nv", [4, DEPTH, 4, 256, 64])
    k.outp("ng", [4, DEPTH, 2, 4, 32, 64]); k.outp("nd", [4, DEPTH, 2, 4, 64, 64]); k.outp("nh", [4, DEPTH, 2, 4, 64, 64])

    k.cst = k.sb("cst", [128, NCSTA])
    k.cstb = k.sb("cstb", [128, NCSTB], BF16)
    k.pt = k.sb("pt", [128, NROW])
    k.mod = k.sb("mod", [128, DEPTH, 72, 2])
    k.der = k.sb("der", [128, DEPTH, 2, 9, 8])
    k.XT = k.sb("XT", [128, 8, NT])
    k.xtb = [Buf(f"xt{c}") for c in range(8)]
    k.wslots = [k.sb(f"wslot{i}", [128, 4096], BF16) for i in range(3)]
    k.xscr = nc.dram_tensor("xscr", [128, 8 * NT], F32, kind="Internal").ap()
    k.big = [es.enter_context(nc.psum_tensor(f"big{i}", [128, 1536], F32)) for i in range(2)]
    k.banks = [Tn(k.big[i // 3][:, (i % 3) * 512:(i % 3 + 1) * 512], Buf(f"bank{i}")) for i in range(6)]
    k.bbanks = [Tn(es.enter_context(nc.psum_tensor(f"bbank{i}", [128, 1024], BF16)), Buf(f"bbank{i}")) for i in range(2)]
    k.bb_i = 0
    P = k.P
    IDF = lambda: CS(k, "ident")
    IDB = lambda: CSB(k, "ident")
    ONESB = lambda: CSB(k, "ones")

    k.dma("sp", k.cst.t[:], k.din["cst"][:, 0:NCSTA], "ld", wr=[k.cst.b])
    k.dma("pool", k.cstb.t[:], k.din["cst"][:, 0:NCSTB], "ldb", wr=[k.cstb.b])
    with k.phase() as ph:
        for r in range(3):
            stg = ph.rot("pstg", [128, 128], F32, 3)
            k.dma("sp", stg.t[:], k.din["pvec"][r * 128:(r + 1) * 128, :], "ld", wr=[stg.b])
            bk = k.bank()
            k.tr(bk.t[:, 0:128], stg.t[:], IDF(), [stg.b, k.cst.b], [bk.b])
            k.cp("dve", k.pt.t[:, r * 128:(r + 1) * 128], bk.t[:, 0:128], [bk.b], [k.pt.b])
        cs = ph.sb("cs", [128, 8, 2], BF16)
        k.act(cs.t[:, :, 0], k.pt.t[:, ROW["c_ctx"]:ROW["c_ctx"] + 8], AF.Silu, [k.pt.b], [cs.b])
        k.act(cs.t[:, :, 1], k.pt.t[:, ROW["c_lat"]:ROW["c_lat"] + 8], AF.Silu, [k.pt.b], [cs.b])
        for l in range(DEPTH):
            bk = k.bank()
            for tl in range(18):
                wv, wb = k.wget(("mat", "w_mod", (l,), 8, ((tl * 512, 512),)))
                for s4 in range(4):
                    j = tl * 4 + s4
                    for kc in range(8):
                        k.mm(bk.t[:, j * 2:j * 2 + 2], wv[:, kc, s4 * 128:(s4 + 1) * 128], cs.t[:, kc, :], kc == 0, kc == 7, [wb, cs.b], [bk.b])
            o0 = ROW["b_mod"] + l * 72
            k.tt("dve", k.mod.t[:, l], bk.t[:, 0:144].rearrange("p (j w) -> p j w", w=2),
                 k.pt.t[:, o0:o0 + 72].unsqueeze(2).to_broadcast([128, 72, 2]), ALU.add, [bk.b, k.pt.b], [k.mod.b])
            for w in range(2):
                for i in range(3):
                    nw0 = ROW["norm_w"] + (l * 3 + i) * 8
                    k.stt("dve", k.der.t[:, l, w, i, :], k.mod.t[:, l, (3 * i + 1) * 8:(3 * i + 2) * 8, w], 1.0, k.pt.t[:, nw0:nw0 + 8],
                          ALU.add, ALU.mult, [k.mod.b, k.pt.b], [k.der.b])
                    k.cp("dve", k.der.t[:, l, w, 3 + i, :], k.mod.t[:, l, (3 * i) * 8:(3 * i + 1) * 8, w], [k.mod.b], [k.der.b])
                    k.ts("dve", k.der.t[:, l, w, 6 + i, :], k.mod.t[:, l, (3 * i + 2) * 8:(3 * i + 3) * 8, w], 1.0 if i == 1 else 0.5, 0.0,
                         ALU.mult, ALU.add, [k.mod.b], [k.der.b])
    k.tap("mod", k.mod.t[:].rearrange("p l j w -> p (l j w)"), [128, DEPTH * 144], [k.mod.b])

    XT = k.XT

    def xload(g):
        with k.phase() as ph:
            for tl in range(8):
                stg = ph.rot("xstg", [128, D], F32, 2)
                k.dma("sp", stg.t[:], xin[g][tl * 128:(tl + 1) * 128, :], "ldx", wr=[stg.b])
                for hb in range(2):
                    bk = k.bank()
                    for c4 in range(4):
                        c = hb * 4 + c4
                        k.tr(bk.t[:, c4 * 128:(c4 + 1) * 128], stg.t[:, c * 128:(c + 1) * 128], IDF(), [stg.b, k.cst.b], [bk.b])
                    eng = "act" if hb == 0 else "dve"
                    k.cp(eng, XT.t[:, hb * 4:(hb + 1) * 4, tl * 128:(tl + 1) * 128], bk.t[:].rearrange("p (c t) -> p c t", t=128), [bk.b],
                         [k.xtb[c] for c in range(hb * 4, hb * 4 + 4)])

    def rstd_of(ph, src_fn, nchunk, nparts, lhsT, scale, name, rd, dst=None):
        rstd = dst if dst is not None else ph.sb(name, [128, NT])
        for hb in range(2):
            bk = k.bank()
            for c in range(nchunk):
                sq = ph.rot("sq", [128, 512], BF16, 3)
                k.act(sq.t[0:nparts, :], src_fn(c, hb), AF.Square, rd, [sq.b])
                k.mm(bk.t[0:nparts, :], lhsT, sq.t[0:nparts, :], c == 0, c == nchunk - 1, [sq.b, k.cstb.b], [bk.b])
            sl = rstd.t[0:nparts, hb * 512:(hb + 1) * 512]
            k.ts("dve", sl, bk.t[0:nparts, :], scale, EPS, ALU.mult, ALU.add, [bk.b], [rstd.b])
            k.act(sl, sl, AF.Sqrt, [rstd.b], [rstd.b])
            k.P.op("dve", (lambda sl_: (lambda e: e.reciprocal(out=sl_, in_=sl_)))(sl), reads=[rstd.b], writes=[rstd.b])
        return rstd

    def norm_mod(ph, l, w, i, HN):
        rstd = rstd_of(ph, lambda c, hb: XT.t[:, c, hb * 512:(hb + 1) * 512], 8, 128, ONESB(), 1.0 / D, "rstd", k.xtb)
        for c in range(8):
            tmp = ph.rot("ntmp", [128, NT], F32, 2)
            k.tt("dve", tmp.t[:], XT.t[:, c, :], rstd.t[:], ALU.mult, [k.xtb[c], rstd.b], [tmp.b])
            k.act(HN.t[:, c, :], tmp.t[:], AF.Identity, [tmp.b, k.der.b], [HN.b],
                  bias=k.der.t[:, l, w, 3 + i, c:c + 1], scale=k.der.t[:, l, w, i, c:c + 1])

    def ffn(l, w, i, win, wdown):
        with k.phase() as ph:
            HN = ph.sb("HN", [128, 8, NT], BF16)
            FA = ph.sb("FA", [128, 22, NT], BF16)
            norm_mod(ph, l, w, i, HN)
            if FFN_STOP == 1:
                return
            for j in range(11 if FFN_STOP != 2 else 1):
                wv, wb = k.wget(("mat", win, (l,), 8, ((j * 256, 256), (DFF + j * 256, 256))))
                for sub in range(2):
                    for hb in range(2):
                        bg = k.bank(); bu = k.bank()
                        for kc in range(8):
                            k.mm(bg.t[:], wv[:, kc, sub * 128:(sub + 1) * 128], HN.t[:, kc, hb * 512:(hb + 1) * 512], kc == 0, kc == 7, [wb, HN.b], [bg.b])
                        for kc in range(8):
                            k.mm(bu.t[:], wv[:, kc, 256 + sub * 128:256 + (sub + 1) * 128], HN.t[:, kc, hb * 512:(hb + 1) * 512], kc == 0, kc == 7, [wb, HN.b], [bu.b])
                        sg = ph.rot("sg", [128, 512], F32, 3)
                        k.act(sg.t[:], bg.t[:], AF.Silu, [bg.b], [sg.b])
                        k.tt("dve", FA.t[:, j * 2 + sub, hb * 512:(hb + 1) * 512], sg.t[:], bu.t[:], ALU.mult, [sg.b, bu.b], [FA.b])
            if FFN_STOP in (2, 3):
                return
            for fo in range(8):
                wv, wb = k.wget(("mat", wdown, (l,), 22, ((fo * 128, 128),)))
                for hb in range(2):
                    bk = k.bank()
                    for kc in range(22):
                        k.mm(bk.t[:], wv[:, kc, :], FA.t[:, kc, hb * 512:(hb + 1) * 512], kc == 0, kc == 21, [wb, FA.b], [bk.b])
                    xs_ = XT.t[:, fo, hb * 512:(hb + 1) * 512]
                    k.stt("dve", xs_, bk.t[:], k.der.t[:, l, w, 6 + i, fo:fo + 1], xs_, ALU.mult, ALU.add, [bk.b, k.der.b, k.xtb[fo]], [k.xtb[fo]])

    def final(g):
        with k.phase() as ph:
            rstd = rstd_of(ph, lambda c, hb: XT.t[:, c, hb * 512:(hb + 1) * 512], 8, 128, ONESB(), 1.0 / D, "rstd", k.xtb)
            f0 = ROW["final"]
            for c in range(8):
                tmp = ph.rot("ntmp", [128, NT], F32, 2)
                k.stt("dve", tmp.t[:], XT.t[:, c, :], k.pt.t[:, f0 + c:f0 + c + 1], rstd.t[:], ALU.mult, ALU.mult, [k.xtb[c], rstd.b, k.pt.b], [tmp.b])
                for tl in range(8):
                    pass
                k.cp("pool", XT.t[:, c, :], tmp.t[:], [tmp.b], [k.xtb[c]])
            for tl in range(8):
                stg = ph.rot("ystg", [128, D], F32, 2)
                for hb in range(2):
                    bk = k.bank()
                    for c4 in range(4):
                        c = hb * 4 + c4
                        k.tr(bk.t[:, c4 * 128:(c4 + 1) * 128], XT.t[:, c, tl * 128:(tl + 1) * 128], IDF(), [k.xtb[c], k.cst.b], [bk.b])
                    k.cp("act" if hb == 0 else "dve", stg.t[:, hb * 512:(hb + 1) * 512], bk.t[:], [bk.b], [stg.b])
                k.dma("sp", yout[g][tl * 128:(tl + 1) * 128, :], stg.t[:], "sty", rd=[stg.b])


    def proj_fm(wv, wb, c0, n, HN, evac):
        for hb in range(2):
            bk = k.bank()
            for kc in range(8):
                k.mm(bk.t[0:n, :], wv[:, kc, c0:c0 + n], HN.t[:, kc, hb * 512:(hb + 1) * 512], kc == 0, kc == 7, [wb, HN.b], [bk.b])
            evac(hb, bk)

    def proj_tm(wv, wb, c0, n, HN, tok0, M, evac):
        bk = k.bank()
        for kc in range(8):
            k.mm(bk.t[0:M, 0:n], HN.t[:, kc, tok0:tok0 + M], wv[:, kc, c0:c0 + n], kc == 0, kc == 7, [HN.b, wb], [bk.b])
        evac(bk)

    def win_tile(l, c0, n):
        return k.wget(("mat", "w_in", (l,), 8, ((c0, n),)))

    def evac_to(dst, dstb, func=None, n=128, eng="act", scale=1.0):
        def f(hb, bk):
            o = dst[0:n, hb * 512:(hb + 1) * 512]
            if func is not None:
                k.act(o, bk.t[0:n, :], func, [bk.b], [dstb], scale=scale)
            else:
                k.cp(eng, o, bk.t[0:n, :], [bk.b], [dstb])
        return f

    def head_norm(ph, OACC, l, n, BR, gate=None, extra=1.0):
        w0 = ROW["hnorm"] + l * 4 + n
        with k.phase() as ph2:
            sq = ph2.sb("hsq", [64, 2, NT], BF16)
            rstd = ph2.sb("hrstd", [64, 2, NT])
            for pr in range(2):
                osl = OACC.t[0:64, 2 * pr:2 * pr + 2, :]
                k.act(sq.t[:], osl, AF.Square, [OACC.b], [sq.b])
                bks = []
                for j in range(4):
                    bk = k.bank()
                    k.mm(bk.t[0:64, :], CSB(k, "ones")[0:64, 0:64], sq.t[:, j // 2, (j % 2) * 512:(j % 2 + 1) * 512], True, True, [sq.b, k.cstb.b], [bk.b])
                    bks.append(bk)
                for j in range(4):
                    k.ts("dve", rstd.t[:, j // 2, (j % 2) * 512:(j % 2 + 1) * 512], bks[j].t[0:64, :], 1.0 / 64, EPS, ALU.mult, ALU.add, [bks[j].b], [rstd.b])
                k.act(rstd.t[:], rstd.t[:], AF.Sqrt, [rstd.b], [rstd.b])
                k.P.op("dve", lambda e: e.reciprocal(out=rstd.t[:], in_=rstd.t[:]), reads=[rstd.b], writes=[rstd.b])
                k.tt("dve", osl, osl, rstd.t[:], ALU.mult, [OACC.b, rstd.b], [OACC.b])
                if gate is not None:
                    k.stt("dve", BR.t[0:64, n, 2 * pr:2 * pr + 2, :], osl, k.pt.t[0:64, w0:w0 + 1], gate.t[0:64, 2 * pr:2 * pr + 2, :], ALU.mult, ALU.mult,
                          [OACC.b, k.pt.b, gate.b], [BR.b])
                else:
                    k.ts("pool", BR.t[0:64, n, 2 * pr:2 * pr + 2, :], osl, k.pt.t[0:64, w0:w0 + 1], extra, ALU.mult, ALU.mult, [OACC.b, k.pt.b], [BR.b])

    def gate_proj(ph, l, c0, HN):
        gate = ph.sb("gate", [64, 4, NT], BF16)
        wv, wb = win_tile(l, c0, 256)
        for h in range(4):
            proj_fm(wv, wb, h * 64, 64, HN, evac_to(gate.t[:, h, :], gate.b, AF.Silu, 64))
        return gate

    def gla_scan(ph, g, l, n, KG, qT, qscale, kT, gT, vtm, hmname, smname, sin_name, sout_name, dk, OACC):
        nseq = 4 if g == 0 else 1
        cps = NCH // nseq
        HPG = 4 // KG
        MASK = {0: CS(k, "mf"), 1: CS(k, "mb")}
        SMv = CS(k, smname)
        for dr in range(2):
            with k.phase() as p2:
                qg = [p2.sb(f"qg{kg}", [128, NT], BF16) for kg in range(KG)]
                kx = [p2.sb(f"kx{h}", [128, NT], BF16) for h in range(4)]
                kd = [p2.sb(f"kd{kg}", [128, NT], BF16) for kg in range(KG)]
                eref = [p2.sb(f"eref{kg}", [128, NCH]) for kg in range(KG)]
                elast = [p2.sb(f"elast{kg}", [128, NCH]) for kg in range(KG)]
                tl = 63 if dr == 0 else 0
                for kg in range(KG):
                    gsrc = gT[dr][kg]; ksrc = kT[dr][kg]
                    cum = p2.rot("cum", [128, NT], F32, 2)
                    k.P.op("dve", (lambda o_, g_: (lambda e: e.tensor_tensor_scan(out=o_, data0=CS(k, "rm"), data1=g_, initial=0.0, op0=ALU.mult, op1=ALU.add)))(cum.t[:], gsrc.t[:]),
                           reads=[gsrc.b, k.cst.b], writes=[cum.b])
                    c3 = lambda t_: t_.t[:].rearrange("p (c j) -> p c j", j=64)
                    if dr == 1:
                        c2 = p2.rot("cum", [128, NT], F32, 2)
                        k.tt("dve", c2.t[:], gsrc.t[:], cum.t[:], ALU.subtract, [gsrc.b, cum.b], [c2.b])
                        k.tt("dve", c3(c2), c3(c2), c3(cum)[:, :, 63:64].to_broadcast([128, NCH, 64]), ALU.add, [c2.b, cum.b], [c2.b])
                        cum = c2
                    bm = p2.rot("bm", [128, NT], F32, 1)
                    k.tt("dve", c3(bm), c3(cum), c3(cum)[:, :, 32:33].to_broadcast([128, NCH, 64]), ALU.subtract, [cum.b], [bm.b])
                    e1 = p2.rot("ee", [128, NT], F32, 2)
                    k.act(e1.t[:], bm.t[:], AF.Exp, [bm.b], [e1.b])
                    k.stt("pool", qg[kg].t[:], qT[kg].t[:], qscale, e1.t[:], ALU.mult, ALU.mult, [qT[kg].b, e1.b], [qg[kg].b])
                    e2 = p2.rot("ee", [128, NT], F32, 2)
                    k.act(e2.t[:], bm.t[:], AF.Exp, [bm.b], [e2.b], scale=-1.0)
                    for hh in range(HPG):
                        h = kg * HPG + hh
                        k.stt("dve" if hh % 2 == 0 else "pool", kx[h].t[:], ksrc.t[:], CS(k, hmname, h, 1), e2.t[:], ALU.mult, ALU.mult, [ksrc.b, e2.b, k.cst.b], [kx[h].b])
                    bl = p2.rot("bm", [128, NT], F32, 1)
                    k.tt("dve", c3(bl), c3(cum)[:, :, tl:tl + 1].to_broadcast([128, NCH, 64]), c3(cum), ALU.subtract, [cum.b], [bl.b])
                    e3 = p2.rot("ee", [128, NT], F32, 2)
                    k.act(e3.t[:], bl.t[:], AF.Exp, [bl.b], [e3.b])
                    k.tt("pool", kd[kg].t[:], ksrc.t[:], e3.t[:], ALU.mult, [ksrc.b, e3.b], [kd[kg].b])
                    k.act(eref[kg].t[:], c3(cum)[:, :, 32], AF.Exp, [cum.b], [eref[kg].b])
                    k.act(elast[kg].t[:], c3(cum)[:, :, tl], AF.Exp, [cum.b], [elast[kg].b])
                S32 = p2.sb("S32", [128, 256])
                ATs = [p2.sb(f"AT{i_}", [64, 256], BF16) for i_ in range(3)]
                for a_ in ATs:
                    k.memset("pool", a_.t[:], 0.0, [a_.b])
                kdTs = [p2.sb(f"kdT{i_}", [64, 256], BF16) for i_ in range(4)]
                ATs = ATs + [p2.sb("AT3", [64, 256], BF16)]
                k.memset("pool", ATs[3].t[:], 0.0, [ATs[3].b])
                v4 = lambda ap_: ap_.rearrange("p (h i) -> p h i", i=64)

                def stageA(ch, slot):
                    tok = slice(ch * 64, (ch + 1) * 64)
                    bb = k.bbanks[k.bb_i % 2]; k.bb_i += 1
                    for kg in range(KG):
                        k.tr(bb.t[0:64, kg * 128:(kg + 1) * 128], kd[kg].t[:, tok], CSB(k, "ident"), [kd[kg].b, k.cstb.b], [bb.b])
                    kdT = kdTs[slot % 4]
                    k.cp("act", kdT.t[:, 0:KG * 128], bb.t[0:64, 0:KG * 128], [bb.b], [kdT.b])
                    pa = k.bank()
                    t0 = ch * 64
                    lo = slice(t0, t0 + 32); hi = slice(t0 + 32, t0 + 64)
                    for h in range(4):
                        qh = qg[h // HPG]
                        rdm = [kx[h].b, qh.b]
                        if dr == 0:
                            k.mm(pa.t[0:32, h * 64:(h + 1) * 64], kx[h].t[:, lo], qh.t[:, tok], True, True, rdm, [pa.b])
                            k.mm(pa.t[32:64, h * 64 + 32:(h + 1) * 64], kx[h].t[:, hi], qh.t[:, hi], True, True, rdm, [pa.b])
                        else:
                            k.mm(pa.t[0:32, h * 64:h * 64 + 32], kx[h].t[:, lo], qh.t[:, lo], True, True, rdm, [pa.b])
                            k.mm(pa.t[32:64, h * 64:(h + 1) * 64], kx[h].t[:, hi], qh.t[:, tok], True, True, rdm, [pa.b])
                    AT = ATs[slot % 4]
                    if dr == 0:
                        k.tt("dve", AT.t[0:32, :], pa.t[0:32, 0:256], MASK[dr][0:32, :], ALU.mult, [pa.b, k.cst.b], [AT.b])
                        k.tt("dve", v4(AT.t[32:64, :])[:, :, 32:64], v4(pa.t[32:64, 0:256])[:, :, 32:64], v4(MASK[dr][32:64, :])[:, :, 32:64], ALU.mult, [pa.b, k.cst.b], [AT.b])
                    else:
                        k.tt("dve", v4(AT.t[0:32, :])[:, :, 0:32], v4(pa.t[0:32, 0:256])[:, :, 0:32], v4(MASK[dr][0:32, :])[:, :, 0:32], ALU.mult, [pa.b, k.cst.b], [AT.b])
                        k.tt("dve", AT.t[32:64, :], pa.t[32:64, 0:256], MASK[dr][32:64, :], ALU.mult, [pa.b, k.cst.b], [AT.b])
                    return kdT, AT

                def stageB(ch, kdT, AT):
                    tok = slice(ch * 64, (ch + 1) * 64)
                    ps_ = k.bank()
                    if KG == 1:
                        k.mm(ps_.t[:, 0:256], kdT.t[:, 0:128], vtm.t[:, ch, :], True, True, [kdT.b, vtm.b], [ps_.b])
                    else:
                        for kg in range(2):
                            k.mm(ps_.t[:, kg * 128:(kg + 1) * 128], kdT.t[:, kg * 128:(kg + 1) * 128], vtm.t[:, ch, kg * 128:(kg + 1) * 128], True, True, [kdT.b, vtm.b], [ps_.b])
                    Sbf = p2.rot("Sbf", [128, 256], BF16, 3)
                    if KG == 1:
                        k.stt("dve", Sbf.t[:], S32.t[:], eref[0].t[:, ch:ch + 1], SMv, ALU.mult, ALU.mult, [S32.b, eref[0].b, k.cst.b], [Sbf.b])
                    else:
                        for kg in range(2):
                            k.stt("dve", Sbf.t[:, kg * 128:(kg + 1) * 128], S32.t[:, kg * 128:(kg + 1) * 128], eref[kg].t[:, ch:ch + 1], SMv[:, kg * 128:(kg + 1) * 128],
                                  ALU.mult, ALU.mult, [S32.b, eref[kg].b, k.cst.b], [Sbf.b])
                    if KG == 1:
                        k.stt("dve", S32.t[:], S32.t[:], elast[0].t[:, ch:ch + 1], ps_.t[:, 0:256], ALU.mult, ALU.add, [S32.b, elast[0].b, ps_.b], [S32.b])
                    else:
                        for kg in range(2):
                            sl = slice(kg * 128, (kg + 1) * 128)
                            k.stt("dve", S32.t[:, sl], S32.t[:, sl], elast[kg].t[:, ch:ch + 1], ps_.t[:, sl], ALU.mult, ALU.add, [S32.b, elast[kg].b, ps_.b], [S32.b])
                    po = k.bank()
                    for h in range(4):
                        o_ = po.t[0:64, h * 64:(h + 1) * 64]
                        k.mm(o_, vtm.t[:, ch, h * 64:(h + 1) * 64], AT.t[:, h * 64:(h + 1) * 64], True, False, [vtm.b, AT.b], [po.b])
                        k.mm(o_, Sbf.t[:, h * 64:(h + 1) * 64], qg[h // HPG].t[:, tok], False, True, [Sbf.b, qg[h // HPG].b], [po.b])
                    oview = OACC.t[0:64, :, tok]
                    pview = po.t[0:64, 0:256].rearrange("p (h i) -> p h i", i=64)
                    if dr == 0:
                        k.cp("act", oview, pview, [po.b], [OACC.b])
                    else:
                        k.tt("dve", oview, oview, pview, ALU.add, [po.b, OACC.b], [OACC.b])

                seqlist = []
                for s_ in range(nseq):
                    order = range(cps) if dr == 0 else range(cps - 1, -1, -1)
                    for ii, ci in enumerate(order):
                        seqlist.append((s_, s_ * cps + ci, ii == 0, ii == cps - 1))
                DEPTH_A = 2
                pend = {}
                for j in range(min(DEPTH_A, len(seqlist))):
                    pend[j] = stageA(seqlist[j][1], j)
                for j, (s_, ch, first, last) in enumerate(seqlist):
                    if first:
                        k.memset("pool", S32.t[:], 0.0, [S32.b])
                        if g == 1:
                            for h in range(4):
                                hh = h % HPG
                                k.dma("sp", S32.t[hh * dk:(hh + 1) * dk, h * 64:(h + 1) * 64] if KG == 2 else S32.t[h * dk:(h + 1) * dk, h * 64:(h + 1) * 64],
                                      k.din[sin_name][l, dr, h], "ldS", wr=[S32.b])
                    kdT, AT = pend.pop(j)
                    stageB(ch, kdT, AT)
                    if j + DEPTH_A < len(seqlist):
                        pend[j + DEPTH_A] = stageA(seqlist[j + DEPTH_A][1], j + DEPTH_A)
                    if last and g == 0:
                        for h in range(4):
                            hh = h % HPG
                            src = S32.t[hh * dk:(hh + 1) * dk, h * 64:(h + 1) * 64] if KG == 2 else S32.t[h * dk:(h + 1) * dk, h * 64:(h + 1) * 64]
                            k.dma("sp", k.dout[sout_name][s_, l, dr, h], src, "stS", rd=[S32.b])

    def mixer_gla(ph, g, l, HN, BR):
        with k.phase() as p1:
            o = IN_OFF["ga_q"]
            wv, wb = win_tile(l, 0, 512)
            qT = p1.sb("qT", [128, NT]); kTt = p1.sb("kT", [128, NT])
            proj_fm(wv, wb, 0, 128, HN, evac_to(qT.t, qT.b))
            proj_fm(wv, wb, 128, 128, HN, evac_to(kTt.t, kTt.b, eng="dve"))
            vtm = p1.sb("vtm", [64, NCH, 256], BF16)
            for ch in range(NCH):
                proj_tm(wv, wb, 256, 256, HN, ch * 64, 64, (lambda ch_: (lambda bk: k.cp("act" if ch_ % 2 else "dve", vtm.t[:, ch_, :], bk.t[0:64, 0:256], [bk.b], [vtm.b])))(ch))
            wv2, wb2 = win_tile(l, 768, 32)
            lrT = p1.sb("lrT", [32, NT], BF16)
            proj_fm(wv2, wb2, 0, 32, HN, evac_to(lrT.t, lrT.b, n=32))
            w2p = p1.sb("w2p", [32, 2, 128], BF16)
            k.memset("pool", w2p.t[:], 0.0, [w2p.b])
            for dr in range(2):
                k.dma("pool", w2p.t[dr * 16:(dr + 1) * 16, dr, :], k.din["gla_w2"][l, dr], "ldw2", wr=[w2p.b])
            negb = p1.sb("negb", [128, 2])
            gb0 = ROW["gla_b"] + l * 2
            k.ts("dve", negb.t[:], k.pt.t[:, gb0:gb0 + 2], -1.0, 0.0, ALU.mult, ALU.add, [k.pt.b], [negb.b])
            gT = [[Tn(k.XT.t[:, 4 + dr, :], Buf(f"gT{dr}"))] for dr in range(2)]
            for dr in range(2):
                for hb in range(2):
                    bk = k.bank()
                    k.mm(bk.t[:], w2p.t[:, dr, :], lrT.t[:, hb * 512:(hb + 1) * 512], True, True, [w2p.b, lrT.b], [bk.b])
                    sl = gT[dr][0].t[:, hb * 512:(hb + 1) * 512]
                    k.act(sl, bk.t[:], AF.Exp, [bk.b, negb.b], [gT[dr][0].b], bias=negb.t[:, dr:dr + 1], scale=-1.0)
                    k.act(sl, sl, AF.Ln, [gT[dr][0].b], [gT[dr][0].b], bias=1.0)
                    k.ts("dve", sl, sl, -1.0 / 16.0, 0.0, ALU.mult, ALU.add, [gT[dr][0].b], [gT[dr][0].b])
            OACC = Tn(k.XT.t[0:64, 0:4, :], Buf("OACC"))
            gla_scan(p1, g, l, 0, 1, [qT], 32 ** -0.5, [[kTt], [kTt]], gT, vtm, "hm_gla", "sm_gla", "sg", "ng", 32, OACC)
            gate = gate_proj(p1, l, 512, HN)
            head_norm(p1, OACC, l, 0, BR, gate)

    def mixer_hg(ph, g, l, HN, BR):
        with k.phase() as p1:
            c0 = IN_OFF["hg_q"]
            wv, wb = win_tile(l, c0, 512)
            wv2, wb2 = win_tile(l, c0 + 512, 512)
            qT = [p1.sb(f"hq{kg}", [128, NT], BF16) for kg in range(2)]
            for kg in range(2):
                proj_fm(wv, wb, kg * 128, 128, HN, evac_to(qT[kg].t, qT[kg].b, AF.Silu))
            vtm = p1.sb("vtm", [64, NCH, 256], BF16)
            for ch in range(NCH):
                proj_tm(wv2, wb2, 256, 256, HN, ch * 64, 64, (lambda ch_: (lambda bk: k.cp("act" if ch_ % 2 else "dve", vtm.t[:, ch_, :], bk.t[0:64, 0:256], [bk.b], [vtm.b])))(ch))
            lb = p1.sb("lb", [128, 4]); oml = p1.sb("oml", [128, 4])
            r0 = ROW["hg_lb"]
            if l == 0:
                k.memset("pool", lb.t[:], 0.0, [lb.b])
            else:
                ex = p1.sb("lbex", [128, 8])
                k.act(ex.t[:], k.pt.t[:, r0:r0 + 8], AF.Exp, [k.pt.b], [ex.b])
                sm_ = p1.sb("lbsum", [128, 4])
                k.tt("dve", sm_.t[:], ex.t[:, 0:4], ex.t[:, 4:8], ALU.add, [ex.b], [sm_.b])
                k.P.op("dve", lambda e: e.reciprocal(out=sm_.t[:], in_=sm_.t[:]), reads=[sm_.b], writes=[sm_.b])
                k.tt("dve", lb.t[:], ex.t[:, 4:8], sm_.t[:], ALU.mult, [ex.b, sm_.b], [lb.b])
            k.ts("dve", oml.t[:], lb.t[:], -1.0, 1.0, ALU.mult, ALU.add, [lb.b], [oml.b])
            kT = [[p1.sb(f"hk{dr}{kg}", [128, NT], BF16) for kg in range(2)] for dr in range(2)]
            gT = [[Tn(k.XT.t[:, 4 + dr * 2 + kg, :], Buf(f"hg{dr}{kg}")) for kg in range(2)] for dr in range(2)]
            for dr in range(2):
                for kg in range(2):
                    wsrc_, wbuf_, cc = (wv, wb, 256 + kg * 128) if dr == 0 else (wv2, wb2, kg * 128)
                    idx = dr * 2 + kg

                    def ev(hb, bk, dr=dr, kg=kg, idx=idx):
                        sl = slice(hb * 512, (hb + 1) * 512)
                        f_ = p1.rot("hf", [128, 512], F32, 1)
                        k.act(f_.t[:], bk.t[:], AF.Sigmoid, [bk.b], [f_.b])
                        k.ts("dve", f_.t[:], f_.t[:], oml.t[:, idx:idx + 1], lb.t[:, idx:idx + 1], ALU.mult, ALU.add, [f_.b, oml.b, lb.b], [f_.b])
                        k.act(gT[dr][kg].t[:, sl], f_.t[:], AF.Ln, [f_.b], [gT[dr][kg].b])
                        k.ts("pool", kT[dr][kg].t[:, sl], f_.t[:], -1.0, 1.0, ALU.mult, ALU.add, [f_.b], [kT[dr][kg].b])
                    proj_fm(wsrc_, wbuf_, cc, 128, HN, ev)
            OACC = Tn(k.XT.t[0:64, 0:4, :], Buf("OACC"))
            gla_scan(p1, g, l, 2, 2, qT, 0.125, kT, gT, vtm, "hm_hg", "sm_hg", "sh", "nh", 64, OACC)
            gate = gate_proj(p1, l, IN_OFF["hg_g"], HN)
            head_norm(p1, OACC, l, 2, BR, gate)


    def lam_setup(ph, l):
        dl = ph.sb("dl", [128, 128])
        k.dma("sp", dl.t[:], k.din["dlam"][l:l + 1, :].partition_broadcast(128), "lddl", wr=[dl.b])
        pr = ph.sb("dlp", [128, 2, 32])
        k.tt("dve", pr.t[:, 0, :], dl.t[:, 0:32], dl.t[:, 32:64], ALU.mult, [dl.b], [pr.b])
        k.tt("dve", pr.t[:, 1, :], dl.t[:, 64:96], dl.t[:, 96:128], ALU.mult, [dl.b], [pr.b])
        sm_ = ph.sb("dls", [128, 2])
        k.P.op("dve", lambda e: e.reduce_sum(out=sm_.t[:], in_=pr.t[:], axis=mybir.AxisListType.X), reads=[pr.b], writes=[sm_.b])
        k.act(sm_.t[:], sm_.t[:], AF.Exp, [sm_.b], [sm_.b])
        lam = ph.sb("lam", [128, 2])
        lam_init = 0.8 - 0.6 * math.exp(-0.3 * l)
        k.stt("dve", lam.t[:, 0:1], sm_.t[:, 0:1], lam_init, sm_.t[:, 1:2], ALU.add, ALU.subtract, [sm_.b], [lam.b])
        k.ts("dve", lam.t[:, 1:2], lam.t[:, 0:1], -1.0, 0.0, ALU.mult, ALU.add, [lam.b], [lam.b])
        return lam, lam_init

    def mixer_df(ph, g, l, HN, BR):
        with k.phase() as p1:
            lam, lam_init = lam_setup(p1, l)
            c0 = IN_OFF["df_q"]
            wv, wb = win_tile(l, c0, 512)
            wv2, wb2 = win_tile(l, c0 + 512, 256)
            NK = 256 if g == 0 else 1536
            NKT = NK // 128
            koff = 0 if g == 0 else 512
            qT = [Tn(k.XT.t[:, 6 + cp, :], Buf(f"dq{cp}")) for cp in range(2)]
            kTf = [Tn(k.XT.t[:, 4 + cp, :], Buf(f"dkf{cp}")) for cp in range(2)]
            KT = [p1.sb(f"dK{cp}", [128, koff + NT], BF16) for cp in range(2)]
            for cp in range(2):
                proj_fm(wv, wb, cp * 128, 128, HN, evac_to(qT[cp].t, qT[cp].b))
                proj_fm(wv, wb, 256 + cp * 128, 128, HN, evac_to(kTf[cp].t, kTf[cp].b, eng="dve"))
            if DF_STOP == 1:
                return
            if g == 1:
                rope = p1.sb("rope", [128, 2048])
                k.dma("sp", rope.t[:], k.din["cst"][:, CST["cos"][0]:CST["cos"][0] + 2048], "ldrope", wr=[rope.b])
                for src in qT + kTf:
                    xb_ = p1.rot("ropexb", [128, NT], BF16, 1)
                    k.cp("pool", xb_.t[:], src.t[:], [src.b], [xb_.b])
                    for hb in range(2):
                        sl = slice(hb * 512, (hb + 1) * 512)
                        bk = k.bank()
                        k.mm(bk.t[:], CSB(k, "rot"), xb_.t[:, sl], True, True, [xb_.b, k.cstb.b], [bk.b])
                        t2 = p1.rot("ropet", [128, 512], F32, 1)
                        k.tt("dve", t2.t[:], bk.t[:], rope.t[:, 1024 + hb * 512:1024 + (hb + 1) * 512], ALU.mult, [bk.b, rope.b], [t2.b])
                        k.tt("pool", src.t[:, sl], src.t[:, sl], rope.t[:, sl], ALU.mult, [src.b, rope.b], [src.b])
                        k.tt("dve", src.t[:, sl], src.t[:, sl], t2.t[:], ALU.add, [src.b, t2.b], [src.b])
                for cp in range(2):
                    for kt in range(4):
                        stg = p1.rot("ckstg", [128, 2, 64], F32, 2)
                        k.dma("sp", stg.t[:], k.din["ck"][l, 2 * cp:2 * cp + 2, kt * 128:(kt + 1) * 128, :].rearrange("h t d -> t h d"), "ldck", wr=[stg.b])
                        bk = k.bank()
                        k.tr(bk.t[:, 0:128], stg.t[:].rearrange("p h d -> p (h d)"), CS(k, "ident"), [stg.b, k.cst.b], [bk.b])
                        k.cp("act", KT[cp].t[:, kt * 128:(kt + 1) * 128], bk.t[:, 0:128], [bk.b], [KT[cp].b])
            for cp in range(2):
                k.cp("pool", KT[cp].t[:, koff:koff + NT], kTf[cp].t[:], [kTf[cp].b], [KT[cp].b])
            if DF_STOP == 6:
                return
            NVT = (koff + NT) // 128
            Vtm = p1.sb("Vtm", [128, NVT, 256], BF16)
            if g == 1:
                for kt in range(4):
                    k.dma("pool", Vtm.t[:, kt, :].rearrange("p (h d) -> p h d", d=64), k.din["cv"][l, :, kt * 128:(kt + 1) * 128, :].rearrange("h t d -> t h d"), "ldcv", wr=[Vtm.b])
            for tl in range(8):
                def evv(bk, tl=tl):
                    if g == 0:
                        so = p1.rot("vout", [128, 256], F32, 2)
                        k.cp("dve", so.t[:], bk.t[:, 0:256], [bk.b], [so.b])
                        k.cp("act", Vtm.t[:, koff // 128 + tl, :], so.t[:], [so.b], [Vtm.b])
                        s_ = tl // 2; t0 = (tl % 2) * 128
                        k.dma("sp", k.dout["nv"][s_, l, :, t0:t0 + 128, :].rearrange("h t d -> t h d"), so.t[:].rearrange("p (h d) -> p h d", d=64), "stv", rd=[so.b])
                    else:
                        k.cp("act", Vtm.t[:, koff // 128 + tl, :], bk.t[:, 0:256], [bk.b], [Vtm.b])
                proj_tm(wv2, wb2, 0, 256, HN, tl * 128, 128, evv)
                if g == 0 and DF_STOP not in (7, 8):
                    def evk(bk, tl=tl):
                        so = p1.rot("kout", [128, 256], F32, 2)
                        k.cp("dve", so.t[:], bk.t[:, 0:256], [bk.b], [so.b])
                        s_ = tl // 2; t0 = (tl % 2) * 128
                        if True:
                            k.dma("sp", k.dout["nk"][s_, l, :, t0:t0 + 128, :].rearrange("h t d -> t h d"), so.t[:].rearrange("p (h d) -> p h d", d=64), "stk", rd=[so.b])
                    proj_tm(wv, wb, 256, 256, HN, tl * 128, 128, evk)
            if DF_STOP in (2, 5, 7, 8):
                return
            QX = [[p1.sb(f"QX{cp}{i}", [128, NT], BF16) for i in range(4)] for cp in range(2)]
            for cp in range(2):
                for i in range(4):
                    k.ts("dve" if i % 2 else "pool", QX[cp][i].t[:], qT[cp].t[:], CS(k, "qm", i, 1), 0.0, ALU.mult, ALU.add, [qT[cp].b, k.cst.b], [QX[cp][i].b])
            if DF_STOP == 3:
                return
            OACC = Tn(k.XT.t[0:64, 0:4, :], Buf("OACC"))
            scale = 32 ** -0.5
            bigb = [[k.banks[i * 3 + j].b for j in range(3)] for i in range(2)]
            nbk = (NK + 511) // 512

            def stageS(qb, h):
                qs = slice(qb * 128, (qb + 1) * 128)
                k0 = (qb // 2) * 256 if g == 0 else 0
                cp = h // 2; hh = h % 2
                Pm = []; rr = []
                for m in range(2):
                    big = k.big[m]; bb_ = bigb[m][0:nbk]
                    for sg_ in range(nbk):
                        n_ = min(512, NK - sg_ * 512)
                        k.mm(big[:, sg_ * 512:sg_ * 512 + n_], QX[cp][hh * 2 + m].t[:, qs], KT[cp].t[:, k0 + sg_ * 512:k0 + sg_ * 512 + n_], True, True,
                             [QX[cp][hh * 2 + m].b, KT[cp].b], bb_)
                    mx = p1.rot("mx", [128, 1], F32, 4)
                    k.P.op("dve", (lambda o_, i_: (lambda e: e.reduce_max(out=o_, in_=i_, axis=mybir.AxisListType.X)))(mx.t[:], big[:, 0:NK]), reads=bb_, writes=[mx.b])
                    k.ts("dve", mx.t[:], mx.t[:], -scale, 0.0, ALU.mult, ALU.add, [mx.b], [mx.b])
                    Pt = p1.rot("Pt", [128, NK], BF16, 2)
                    rs = p1.rot("rs", [128, 1], F32, 4)
                    k.act(Pt.t[:], big[:, 0:NK], AF.Exp, bb_ + [mx.b], [Pt.b, rs.b], bias=mx.t[:], scale=scale, accum=rs.t[:])
                    k.P.op("dve", (lambda o_: (lambda e: e.reciprocal(out=o_, in_=o_)))(rs.t[:]), reads=[rs.b], writes=[rs.b])
                    if m == 1:
                        k.tt("dve", rs.t[:], rs.t[:], lam.t[:, 1:2], ALU.mult, [rs.b, lam.b], [rs.b])
                    Pm.append(Pt); rr.append(rs)
                A = p1.rot("Amat", [128, NK], BF16, 2)
                k.ts("pool", A.t[:], Pm[0].t[:], rr[0].t[:], 0.0, ALU.mult, ALU.add, [Pm[0].b, rr[0].b], [A.b])
                k.stt("dve", A.t[:], Pm[1].t[:], rr[1].t[:], A.t[:], ALU.mult, ALU.add, [Pm[1].b, rr[1].b, A.b], [A.b])
                return A

            def stageT(qb, h, A):
                qs = slice(qb * 128, (qb + 1) * 128)
                k0 = (qb // 2) * 256 if g == 0 else 0
                ATm = p1.rot("ATm", [128, NKT, 128], BF16, 1)
                for t8 in range((NKT + 7) // 8):
                    bb = k.bbanks[k.bb_i % 2]; k.bb_i += 1
                    nt_ = min(8, NKT - t8 * 8)
                    for kt in range(nt_):
                        k.tr(bb.t[:, kt * 128:(kt + 1) * 128], A.t[:, (t8 * 8 + kt) * 128:(t8 * 8 + kt + 1) * 128], CSB(k, "ident"), [A.b, k.cstb.b], [bb.b])
                    k.cp("act", ATm.t[:, t8 * 8:t8 * 8 + nt_, :], bb.t[:, 0:nt_ * 128].rearrange("p (t q) -> p t q", q=128), [bb.b], [ATm.b])
                po = k.bank()
                vt0 = k0 // 128
                for kt in range(NKT):
                    k.mm(po.t[0:64, 0:128], Vtm.t[:, vt0 + kt, h * 64:(h + 1) * 64], ATm.t[:, kt, :], kt == 0, kt == NKT - 1, [Vtm.b, ATm.b], [po.b])
                k.cp("act", OACC.t[0:64, h, qs], po.t[0:64, 0:128], [po.b], [OACC.b])

            units = [(qb, h) for qb in range(8) for h in range(4)]
            Acur = stageS(*units[0])
            for u_ in range(len(units)):
                Anext = stageS(*units[u_ + 1]) if u_ + 1 < len(units) else None
                stageT(units[u_][0], units[u_][1], Acur)
                Acur = Anext
            head_norm(p1, OACC, l, 3, BR, None, 1.0 - lam_init)

    def merge(g, l, HN, BR):
        w = g
        with k.phase() as p1:
            MG = p1.sb("MG", [128, 8, NT], BF16)
            for q in range(4):
                acc = p1.rot("mgacc", [128, 2, NT], F32, 2)
                for n in range(4):
                    (wv, bv), wb = k.wget(("merge", l, n, q))
                    for k2 in range(2):
                        kc = q * 2 + k2
                        for hb in range(2):
                            sl = slice(hb * 512, (hb + 1) * 512)
                            bg = k.bank(); bu = k.bank()
                            for kk in range(8):
                                k.mm(bg.t[:], wv[:, kk, k2 * 128:(k2 + 1) * 128], HN.t[:, kk, sl], kk == 0, kk == 7, [wb, HN.b], [bg.b])
                            for h in range(4):
                                k.mm(bu.t[:], bv[:, h, k2 * 128:(k2 + 1) * 128], BR.t[0:64, n, h, sl], h == 0, h == 3, [wb, BR.b], [bu.b])
                            sg = p1.rot("msg", [128, 512], F32, 3)
                            k.act(sg.t[:], bg.t[:], AF.Sigmoid, [bg.b], [sg.b])
                            if n == 0:
                                k.tt("dve", acc.t[:, k2, sl], sg.t[:], bu.t[:], ALU.mult, [sg.b, bu.b], [acc.b])
                            else:
                                k.tt("dve", sg.t[:], sg.t[:], bu.t[:], ALU.mult, [sg.b, bu.b], [sg.b])
                                if n < 3:
                                    k.tt("pool", acc.t[:, k2, sl], acc.t[:, k2, sl], sg.t[:], ALU.add, [acc.b, sg.b], [acc.b])
                                else:
                                    k.tt("pool", MG.t[:, kc, sl], acc.t[:, k2, sl], sg.t[:], ALU.add, [acc.b, sg.b], [MG.b])
            for t2 in range(2):
                wv, wb = k.wget(("mat", "w_out", (l,), 8, ((t2 * 512, 512),)))
                for f4 in range(4):
                    fo = t2 * 4 + f4
                    for hb in range(2):
                        sl = slice(hb * 512, (hb + 1) * 512)
                        bk = k.bank()
                        for kk in range(8):
                            k.mm(bk.t[:], wv[:, kk, f4 * 128:(f4 + 1) * 128], MG.t[:, kk, sl], kk == 0, kk == 7, [wb, MG.b], [bk.b])
                        xs_ = XT.t[:, fo, sl]
                        k.stt("dve", xs_, bk.t[:], k.der.t[:, l, w, 7, fo:fo + 1], xs_, ALU.mult, ALU.add, [bk.b, k.der.b, k.xtb[fo]], [k.xtb[fo]])

    def layer(g, l, mixers=("A", "B", "C", "D")):
        w = g
        ffn(l, w, 0, "ffn1_in", "ffn1_down")
        with k.phase() as ph:
            HN = ph.sb("HN", [128, 8, NT], BF16)
            BR = ph.sb("BR", [64, 4, 4, NT], BF16)
            norm_mod(ph, l, w, 1, HN)
            k.dma("sp", k.xscr[:, :], XT.t[:].rearrange("p c t -> p (c t)"), "spill", rd=k.xtb)
            k.P.barrier()
            if "A" in mixers:
                mixer_gla(ph, g, l, HN, BR)
            else:
                k.memset("pool", BR.t[:, 0], 0.0, [BR.b])
            if "B" in mixers:
                k.mixer_dn(ph, g, l, HN, BR)
            else:
                k.memset("pool", BR.t[:, 1], 0.0, [BR.b])
            if "C" in mixers:
                mixer_hg(ph, g, l, HN, BR)
            else:
                k.memset("pool", BR.t[:, 2], 0.0, [BR.b])
            if "D" in mixers:
                mixer_df(ph, g, l, HN, BR)
            else:
                k.memset("pool", BR.t[:, 3], 0.0, [BR.b])
            for n_, nm_ in enumerate("ABCD"):
                k.tap(f"br{nm_}{g}{l}", BR.t[0:64, n_].rearrange("p h t -> p (h t)"), [64, 4 * NT], [BR.b])
            k.P.barrier()
            k.P.dma("sp", lambda e: e.dma_start(out=XT.t[:].rearrange("p c t -> p (c t)"), in_=k.xscr[:, :]), "d_xt0", writes=k.xtb)
            merge(g, l, HN, BR)
        k.tap(f"xm{g}{l}", XT.t[:].rearrange("p c t -> p (c t)"), [128, 8 * NT], k.xtb)
        ffn(l, w, 2, "ffn2_in", "ffn2_down")

    k.layer = layer
    k.mixer_df = mixer_df
    k.merge = merge

    def mixer_dn(ph, g, l, HN, BR):
        nseq = 4 if g == 0 else 1
        cps = NCH // nseq
        T = NT // nseq
        with k.phase() as p1:
            c0 = IN_OFF["dn_x"]
            wv, wb = win_tile(l, c0, 512)
            wv2, wb2 = win_tile(l, c0 + 512, 272)
            nmc = p1.sb("nmc", [64, 1024], BF16)
            k.dma("pool", nmc.t[:], k.din["cst"][0:64, CST["nm_c"][0]:CST["nm_c"][0] + 1024], "ldnm", wr=[nmc.b])
            QT = [p1.sb(f"nQ{p}", [128, NT], BF16) for p in range(2)]
            KT = [p1.sb(f"nK{p}", [128, NT], BF16) for p in range(2)]
            Ktm = p1.sb("Ktm", [64, NCH, 256], BF16); Vtm = p1.sb("nVtm", [64, NCH, 256], BF16)
            with k.phase() as p2:
                VT = [p2.sb(f"nV{p}", [128, NT], BF16) for p in range(2)]
                for c in range(6):
                    src_w, src_b, cc = (wv, wb, c * 128) if c < 4 else (wv2, wb2, (c - 4) * 128)
                    x = p2.rot("cx", [128, NT], F32, 2)
                    proj_fm(src_w, src_b, cc, 128, HN, evac_to(x.t, x.b))
                    y = p2.rot("cy", [128, NT], F32, 2)
                    wrow = lambda tap: k.pt.t[:, ROW["dn_conv"] + (l * 3 + tap) * 6 + c:ROW["dn_conv"] + (l * 3 + tap) * 6 + c + 1]
                    k.ts("dve", y.t[:], x.t[:], wrow(1), 0.0, ALU.mult, ALU.add, [x.b, k.pt.b], [y.b])
                    x3 = x.t[:].rearrange("p (s t) -> p s t", t=T); y3 = y.t[:].rearrange("p (s t) -> p s t", t=T)
                    k.stt("dve", y3[:, :, 1:T], x3[:, :, 0:T - 1], wrow(0), y3[:, :, 1:T], ALU.mult, ALU.add, [x.b, y.b, k.pt.b], [y.b])
                    k.stt("dve", y3[:, :, 0:T - 1], x3[:, :, 1:T], wrow(2), y3[:, :, 0:T - 1], ALU.mult, ALU.add, [x.b, y.b, k.pt.b], [y.b])
                    if c >= 4:
                        k.act(VT[c - 4].t[:], y.t[:], AF.Silu, [y.b], [VT[c - 4].b])
                    else:
                        k.act(y.t[:], y.t[:], AF.Silu, [y.b], [y.b])
                        rn = k.rstd_of(p2, lambda c_, hb: y.t[:, hb * 512:(hb + 1) * 512], 1, 128, CSB(k, "bd64"), 1.0, f"rn{c % 2}", [y.b])
                        dst = QT[c] if c < 2 else KT[c - 2]
                        k.stt("dve", dst.t[:], y.t[:], 0.125 if c < 2 else 1.0, rn.t[:], ALU.mult, ALU.mult, [y.b, rn.b], [dst.b])
                for ch in range(NCH):
                    tok = slice(ch * 64, (ch + 1) * 64)
                    bb = k.bbanks[k.bb_i % 2]; k.bb_i += 1
                    for p in range(2):
                        k.tr(bb.t[0:64, p * 128:(p + 1) * 128], KT[p].t[:, tok], CSB(k, "ident"), [KT[p].b, k.cstb.b], [bb.b])
                        k.tr(bb.t[0:64, 256 + p * 128:256 + (p + 1) * 128], VT[p].t[:, tok], CSB(k, "ident"), [VT[p].b, k.cstb.b], [bb.b])
                    k.cp("act", Ktm.t[:, ch, :], bb.t[0:64, 0:256], [bb.b], [Ktm.b])
                    k.cp("act", Vtm.t[:, ch, :], bb.t[0:64, 256:512], [bb.b], [Vtm.b])

            if DN_STOP == 2:
                return
            zz = p1.sb("zz", [64, NCH, 16])
            for ch in range(NCH):
                proj_tm(wv2, wb2, 256, 16, HN, ch * 64, 64, (lambda ch_: (lambda bk: k.cp("dve", zz.t[:, ch_, :], bk.t[0:64, 0:16], [bk.b], [zz.b])))(ch))
            par = p1.sb("dnpar", [64, 16])
            k.dma("sp", par.t[:, 0:8], k.din["dn_al"][l:l + 1, :].partition_broadcast(64), "ldpar", wr=[par.b])
            k.dma("sp", par.t[:, 8:16], k.din["dn_dt"][l:l + 1, :].partition_broadcast(64), "ldpar", wr=[par.b])
            k.act(par.t[:, 0:8], par.t[:, 0:8], AF.Exp, [par.b], [par.b])
            k.ts("dve", par.t[:, 0:8], par.t[:, 0:8], -1.0, 0.0, ALU.mult, ALU.add, [par.b], [par.b])
            bcol = p1.sb("bcol", [64, NCH, 8]); lnb = p1.sb("lnb", [64, NCH, 8]); gcol = p1.sb("gcol", [64, NCH, 8])
            k.act(bcol.t[:], zz.t[:, :, 0:8], AF.Sigmoid, [zz.b], [bcol.b])
            k.act(lnb.t[:], bcol.t[:], AF.Ln, [bcol.b], [lnb.b])
            k.tt("dve", gcol.t[:], zz.t[:, :, 8:16], par.t[:, 8:16].unsqueeze(1).to_broadcast([64, NCH, 8]), ALU.add, [zz.b, par.b], [gcol.b])
            k.act(gcol.t[:], gcol.t[:], AF.Exp, [gcol.b], [gcol.b])
            k.act(gcol.t[:], gcol.t[:], AF.Ln, [gcol.b], [gcol.b], bias=1.0)
            k.tt("dve", gcol.t[:], gcol.t[:], par.t[:, 0:8].unsqueeze(1).to_broadcast([64, NCH, 8]), ALU.mult, [gcol.b, par.b], [gcol.b])
            dcol = p1.sb("dcol", [64, NCH, 8]); dlast = p1.sb("dlast", [128, NCH, 8])
            bk = k.bank()
            for ch in range(NCH):
                for dr in range(2):
                    k.mm(bk.t[0:64, ch * 8 + dr * 4:ch * 8 + dr * 4 + 4], CS(k, "mf" if dr == 0 else "mb")[0:64, 0:64], gcol.t[:, ch, dr * 4:dr * 4 + 4], True, True, [gcol.b, k.cst.b], [bk.b])
            k.cp("dve", dcol.t[:].rearrange("p c h -> p (c h)"), bk.t[0:64, 0:NCH * 8], [bk.b], [dcol.b])
            bk = k.bank()
            for ch in range(NCH):
                for dr in range(2):
                    k.mm(bk.t[:, ch * 8 + dr * 4:ch * 8 + dr * 4 + 4], CS(k, "sel_f" if dr == 0 else "sel_b")[0:64, :], dcol.t[:, ch, dr * 4:dr * 4 + 4], True, True, [dcol.b, k.cst.b], [bk.b])
            k.cp("dve", dlast.t[:].rearrange("p c h -> p (c h)"), bk.t[:, 0:NCH * 8], [bk.b], [dlast.b])
            acol = p1.sb("acol", [64, NCH, 8]); expd = p1.sb("expd", [64, NCH, 8]); nexpd = p1.sb("nexpd", [64, NCH, 8])
            edl = p1.sb("edl", [64, NCH, 8]); edlast = p1.sb("edlast", [128, NCH, 8])
            k.tt("dve", acol.t[:], dcol.t[:], lnb.t[:], ALU.add, [dcol.b, lnb.b], [acol.b])
            k.act(expd.t[:], dcol.t[:], AF.Exp, [dcol.b], [expd.b])
            k.ts("dve", nexpd.t[:], expd.t[:], -1.0, 0.0, ALU.mult, ALU.add, [expd.b], [nexpd.b])
            k.tt("dve", edl.t[:], dlast.t[0:64], dcol.t[:], ALU.subtract, [dlast.b, dcol.b], [edl.b])
            k.act(edl.t[:], edl.t[:], AF.Exp, [edl.b], [edl.b])
            k.act(edlast.t[:], dlast.t[:], AF.Exp, [dlast.b], [edlast.b])
            if DN_STOP == 3:
                return
            OACC = Tn(k.XT.t[0:64, 0:4, :], Buf("OACC"))
            bc8 = lambda t_, ch_, lo, n_: t_.t[:, ch_, lo:lo + n_].unsqueeze(2).to_broadcast([64, n_, 64])
            v3 = lambda ap_, n_: ap_.rearrange("p (h i) -> p h i", i=64)
            for dr in range(2):
                h0 = dr * 4
                with k.phase() as p2:
                    TbT = p2.sb("TbT", [64, NCH, 256], BF16); Aqk = p2.sb("Aqk", [64, NCH, 256], BF16)
                    NG = 3
                    atile = lambda i_: Tn(k.XT.t[0:64, 4 + i_ // 4, (i_ % 4) * 256:(i_ % 4 + 1) * 256], Buf(f"dnal{i_}"))
                    slots = []
                    for si in range(NG):
                        if si == 0:
                            tl_ = [p2.sb(f"tp{j_}", [64, 256], F32) for j_ in range(8)]
                        else:
                            tl_ = [atile((si - 1) * 8 + j_) for j_ in range(8)]
                        slots.append({"ec": tl_[0], "eb": tl_[1], "XN": tl_[2:4], "XT": tl_[4:6], "PM": tl_[6:8], "dg": tl_[7],
                                      "KTm": p2.sb(f"KTm{si}", [128, 256], BF16)})
                    hsl = lambda h: slice(h * 64, (h + 1) * 64)

                    def tprep(ch, B):
                        tok = slice(ch * 64, (ch + 1) * 64)
                        dg = B["dg"]; ec = B["ec"]; eb = B["eb"]; KTm = B["KTm"]
                        k.tt("pool", v3(dg.t[:], 4), v3(CS(k, "id8")[0:64, 0:256], 4), bc8(dcol, ch, h0, 4), ALU.mult, [dcol.b, k.cst.b], [dg.b])
                        for h in range(4):
                            k.ts("pool" if h % 2 else "dve", KTm.t[:, h * 64:(h + 1) * 64], KT[h // 2].t[:, tok], CS(k, "hm_hg", h, 1), 0.0, ALU.mult, ALU.add,
                                 [KT[h // 2].b, k.cst.b], [KTm.b])
                        yield
                        rb = k.bank()
                        k.mm(rb.t[0:64, 0:256], CS(k, "ones")[0:64, 0:64], dg.t[:], True, True, [dg.b, k.cst.b], [rb.b])
                        yield
                        k.tt("dve", v3(ec.t[:], 4), v3(rb.t[0:64, 0:256], 4), bc8(dcol, ch, h0, 4), ALU.subtract, [rb.b, dcol.b], [ec.b])
                        k.tt("dve", ec.t[:], ec.t[:], nmc.t[:, dr * 256:(dr + 1) * 256], ALU.min, [ec.b, nmc.b], [ec.b])
                        k.stt("dve", v3(eb.t[:], 4), v3(rb.t[0:64, 0:256], 4), -1.0, bc8(acol, ch, h0, 4), ALU.mult, ALU.add, [rb.b, acol.b], [eb.b])
                        k.tt("dve", eb.t[:], eb.t[:], nmc.t[:, 512 + dr * 256:512 + (dr + 1) * 256], ALU.min, [eb.b, nmc.b], [eb.b])
                        gk = k.bank(); gq = k.bank()
                        for h in range(4):
                            p = h // 2
                            k.mm(gk.t[0:64, hsl(h)], KTm.t[:, hsl(h)], KT[p].t[:, tok], True, True, [KT[p].b, KTm.b], [gk.b])
                            k.mm(gq.t[0:64, hsl(h)], KTm.t[:, hsl(h)], QT[p].t[:, tok], True, True, [KTm.b, QT[p].b], [gq.b])
                        yield
                        k.act(ec.t[:], ec.t[:], AF.Exp, [ec.b], [ec.b])
                        k.act(eb.t[:], eb.t[:], AF.Exp, [eb.b], [eb.b])
                        yield
                        XN = B["XN"][0]; XTt = B["XT"][0]; Pm = B["PM"][0]
                        k.stt("dve", XN.t[:], gk.t[0:64, 0:256], -1.0, eb.t[:], ALU.mult, ALU.mult, [gk.b, eb.b], [XN.b])
                        k.tt("dve", Aqk.t[:, ch, :], gq.t[0:64, 0:256], ec.t[:], ALU.mult, [gq.b, ec.b], [Aqk.b])
                        yield
                        bt = k.bank()
                        for h in range(4):
                            k.tr(bt.t[0:64, hsl(h)], XN.t[:, hsl(h)], CS(k, "ident")[0:64, 0:64], [XN.b, k.cst.b], [bt.b])
                        yield
                        k.cp("act", XTt.t[:], bt.t[0:64, 0:256], [bt.b], [XTt.b])
                        yield
                        k.tt("pool", Pm.t[:], XTt.t[:], CS(k, "id8")[0:64, 0:256], ALU.add, [XTt.b, k.cst.b], [Pm.b])
                        for lev in range(5):
                            b1 = None
                            if lev < 4:
                                b1 = k.bank()
                                for h in range(4):
                                    k.mm(b1.t[0:64, hsl(h)], XN.t[:, hsl(h)], XTt.t[:, hsl(h)], True, True, [XN.b, XTt.b], [b1.b])
                            b2 = k.bank()
                            for h in range(4):
                                k.mm(b2.t[0:64, hsl(h)], XTt.t[:, hsl(h)], XN.t[:, hsl(h)], True, True, [XN.b, XTt.b], [b2.b])
                            yield
                            XN2 = B["XN"][(lev + 1) % 2]
                            k.cp("act", XN2.t[:], b2.t[0:64, 0:256], [b2.b], [XN2.b])
                            if lev < 4:
                                XT2 = B["XT"][(lev + 1) % 2]
                                k.cp("act", XT2.t[:], b1.t[0:64, 0:256], [b1.b], [XT2.b])
                                XTt = XT2
                            XN = XN2
                            yield
                            b3 = k.bank()
                            for h in range(4):
                                k.mm(b3.t[0:64, hsl(h)], XN.t[:, hsl(h)], Pm.t[:, hsl(h)], True, True, [XN.b, Pm.b], [b3.b])
                            yield
                            Pn = B["PM"][(lev + 1) % 2]
                            k.tt("dve", Pn.t[:], Pm.t[:], b3.t[0:64, 0:256], ALU.add, [Pm.b, b3.b], [Pn.b])
                            Pm = Pn
                            yield
                        k.tt("dve", v3(TbT.t[:, ch, :], 4), v3(Pm.t[:], 4), bc8(bcol, ch, h0, 4), ALU.mult, [Pm.b, bcol.b], [TbT.b])

                    for c0_ in range(0, NCH, NG):
                        gens = [tprep(ch, slots[i_]) for i_, ch in enumerate(range(c0_, min(NCH, c0_ + NG)))]
                        while gens:
                            nxt = []
                            for g_ in gens:
                                try:
                                    next(g_)
                                    nxt.append(g_)
                                except StopIteration:
                                    pass
                            gens = nxt
                    if DN_STOP == 4:
                        return
                    S32 = p2.sb("S32", [128, 256])
                    for s_ in range(nseq):
                        k.memset("pool", S32.t[:], 0.0, [S32.b])
                        if g == 1:
                            for h in range(4):
                                k.dma("sp", S32.t[(h % 2) * 64:(h % 2 + 1) * 64, h * 64:(h + 1) * 64], k.din["sd"][l, dr, h], "ldS", wr=[S32.b])
                        order = range(cps) if dr == 0 else range(cps - 1, -1, -1)
                        for ci in order:
                            ch = s_ * cps + ci
                            tok = slice(ch * 64, (ch + 1) * 64)
                            Sbf = p2.rot("Sbf", [128, 256], BF16, 3)
                            k.tt("pool", Sbf.t[:], S32.t[:], CS(k, "sm_hg"), ALU.mult, [S32.b, k.cst.b], [Sbf.b])
                            pk = k.bank(); pq = k.bank()
                            for h in range(4):
                                p = h // 2
                                k.mm(pk.t[0:64, h * 64:(h + 1) * 64], KT[p].t[:, tok], Sbf.t[:, h * 64:(h + 1) * 64], True, True, [KT[p].b, Sbf.b], [pk.b])
                                k.mm(pq.t[0:64, h * 64:(h + 1) * 64], QT[p].t[:, tok], Sbf.t[:, h * 64:(h + 1) * 64], True, True, [QT[p].b, Sbf.b], [pq.b])
                            Y = p2.rot("Y", [64, 256], F32, 1); Yb = p2.rot("Yb", [64, 256], BF16, 2)
                            k.tt("dve", v3(Y.t[:], 4), v3(pk.t[0:64, 0:256], 4), bc8(nexpd, ch, h0, 4), ALU.mult, [pk.b, nexpd.b], [Y.b])
                            k.tt("dve", Yb.t[:], Y.t[:], Vtm.t[:, ch, :], ALU.add, [Y.b, Vtm.b], [Yb.b])
                            pv = k.bank()
                            for h in range(4):
                                k.mm(pv.t[0:64, h * 64:(h + 1) * 64], TbT.t[:, ch, h * 64:(h + 1) * 64], Yb.t[:, h * 64:(h + 1) * 64], True, True, [TbT.b, Yb.b], [pv.b])
                            VN = p2.rot("VN", [64, 256], BF16, 2); VNs = p2.rot("VNs", [64, 256], BF16, 2)
                            k.cp("dve", VN.t[:], pv.t[0:64, 0:256], [pv.b], [VN.b])
                            k.tt("dve", v3(VNs.t[:], 4), v3(pv.t[0:64, 0:256], 4), bc8(edl, ch, h0, 4), ALU.mult, [pv.b, edl.b], [VNs.b])
                            pa = k.bank()
                            for h in range(4):
                                k.mm(pa.t[0:64, h * 64:(h + 1) * 64], Aqk.t[:, ch, h * 64:(h + 1) * 64], VN.t[:, h * 64:(h + 1) * 64], True, True, [Aqk.b, VN.b], [pa.b])
                            ot = p2.rot("ot", [64, 256], F32, 1)
                            k.tt("dve", v3(ot.t[:], 4), v3(pq.t[0:64, 0:256], 4), bc8(expd, ch, h0, 4), ALU.mult, [pq.b, expd.b], [ot.b])
                            k.tt("dve", ot.t[:], ot.t[:], pa.t[0:64, 0:256], ALU.add, [ot.b, pa.b], [ot.b])
                            pt_ = k.bank()
                            for h in range(4):
                                k.tr(pt_.t[0:64, h * 64:(h + 1) * 64], ot.t[:, h * 64:(h + 1) * 64], CS(k, "ident")[0:64, 0:64], [ot.b, k.cst.b], [pt_.b])
                            oview = OACC.t[0:64, :, tok]
                            if dr == 0:
                                k.cp("act", oview, v3(pt_.t[0:64, 0:256], 4), [pt_.b], [OACC.b])
                            else:
                                k.tt("dve", oview, oview, v3(pt_.t[0:64, 0:256], 4), ALU.add, [pt_.b, OACC.b], [OACC.b])
                            pS = k.bank()
                            for p in range(2):
                                k.mm(pS.t[:, p * 128:(p + 1) * 128], Ktm.t[:, ch, p * 128:(p + 1) * 128], VNs.t[:, p * 128:(p + 1) * 128], True, True, [Ktm.b, VNs.b], [pS.b])
                            for h in range(4):
                                sl = slice(h * 64, (h + 1) * 64)
                                k.stt("dve", S32.t[:, sl], S32.t[:, sl], edlast.t[:, ch, h0 + h:h0 + h + 1], pS.t[:, sl], ALU.mult, ALU.add,
                                      [S32.b, edlast.b, pS.b], [S32.b])
                        if g == 0:
                            for h in range(4):
                                k.dma("sp", k.dout["nd"][s_, l, dr, h], S32.t[(h % 2) * 64:(h % 2 + 1) * 64, h * 64:(h + 1) * 64], "stS", rd=[S32.b])
            gate = gate_proj(p1, l, IN_OFF["dn_g"], HN)
            head_norm(p1, OACC, l, 1, BR, gate)

    k.mixer_dn = mixer_dn
    k.mixer_gla = mixer_gla
    k.mixer_hg = mixer_hg
    k.gla_scan = gla_scan
    k.head_norm = head_norm
    k.gate_proj = gate_proj
    k.proj_fm = proj_fm
    k.proj_tm = proj_tm
    k.win_tile = win_tile
    k.evac_to = evac_to
    k.ffn = ffn
    k.xload = xload
    k.final = final
    k.rstd_of = rstd_of
    k.norm_mod = norm_mod
    return k


def finish(k):
    k.P.final_wait("sp")
    keys = list(ENGS) + list(k.P.dma_cnt.keys())
    sems = {key: k.es.enter_context(k.nc.semaphore("s_" + key)) for key in keys}
    with k.nc.Block() as block:
        replay(k.P, block, sems)
    k.es.close()
    return k.nc


def host_inputs(inp, core):
    b = core // 4
    f = lambda a: np.ascontiguousarray(np.asarray(a, np.float32))
    m = {
        "xp": f(inp["x_prompt"][core * 4:(core + 1) * 4].reshape(NT, D)),
        "xs": f(inp["x_sample"][b]),
        "ck": f(inp["cache_diff_k"][b]), "cv": f(inp["cache_diff_v"][b]),
        "sg": f(inp["state_gla"][b]), "sd": f(inp["state_dn"][b]), "sh": f(inp["state_hgrn"][b]),
        "pvec": pack_pvec(inp, b), "cst": make_consts(),
        "dn_al": f(inp["dn_a_log"].reshape(DEPTH, 8)), "dn_dt": f(inp["dn_dt_bias"].reshape(DEPTH, 8)),
        "dlam": f(inp["diff_lambda"].reshape(DEPTH, 128)), "gla_w2": f(inp["gla_w2"]),
    }
    for n_ in ("w_mod", "ffn1_in", "ffn2_in", "ffn1_down", "ffn2_down", "w_in", "w_branch", "w_mgate", "w_out"):
        m[n_] = f(inp[n_])
    return m


def program(k, mixers=("A", "B", "C", "D")):
    for g in range(2):
        k.xload(g)
        for l in range(DEPTH):
            k.layer(g, l, mixers)
        k.final(g)


_CACHE = {}


def get_nc():
    if "nc" not in _CACHE:
        k1 = build(None)
        program(k1)
        k2 = build(list(k1.wrec))
        program(k2)
        _CACHE["nc"] = finish(k2)
    return _CACHE["nc"]


def kernel(**inp):
    inp = {n: np.asarray(v) for n, v in inp.items()}
    nc = get_nc()
    in_maps = [host_inputs(inp, c) for c in range(8)]
    res = run_bass_kernel_spmd(nc, in_maps, core_ids=list(range(8)))
    R = res.results
    y_prompt = np.concatenate([R[c]["yp"].reshape(4, 256, D) for c in range(8)], axis=0)
    y_sample = np.stack([R[0]["ys"], R[4]["ys"]], axis=0)
    cat = lambda n_: np.concatenate([R[c][n_] for c in range(8)], axis=0)
    return (y_prompt.astype(np.float32), y_sample.astype(np.float32), cat("nk"), cat("nv"), cat("ng"), cat("nd"), cat("nh"))
```

```python
import math
from contextlib import ExitStack
import numpy as np
import concourse.bass as bass
import concourse.mybir as mybir
from concourse.bass_utils import run_bass_kernel_spmd

F32 = mybir.dt.float32
BF16 = mybir.dt.bfloat16
AF = mybir.ActivationFunctionType
ALU = mybir.AluOpType
ENGS = ("pe", "act", "dve", "pool", "sp")

D = 1024
NT = 1024
DFF = 2816
NIN = 3888
EPS = 1e-6
DEPTH = 2
CH = 64
NCH = NT // CH
FFN_STOP = 0
DF_STOP = 0
DN_STOP = 0


class Buf:
    __slots__ = ("name", "w", "r")

    def __init__(self, name=""):
        self.name = name
        self.w = None
        self.r = {}


class Prog:
    def __init__(self):
        self.ops = {e: [] for e in ENGS}
        self.cnt = {e: 0 for e in ENGS}
        self.seen = {e: {} for e in ENGS}
        self.dma_cnt = {}

    def _need(self, eng, tick, waits):
        if tick is None:
            return
        k, v = tick
        if self.seen[eng].get(k, 0) >= v:
            return
        waits[k] = max(waits.get(k, 0), v)

    def _deps(self, eng, reads, writes, is_dma):
        waits = {}
        for b in reads:
            self._need(eng, b.w, waits)
        for b in writes:
            if b.w is not None and (is_dma or b.w[0] != eng):
                self._need(eng, b.w, waits)
            for k, v in b.r.items():
                if is_dma or k != eng:
                    self._need(eng, (k, v), waits)
        for k, v in waits.items():
            self.seen[eng][k] = v
        return tuple(waits.items())

    def op(self, eng, fn, reads=(), writes=(), inc=True):
        waits = self._deps(eng, reads, writes, False)
        if inc:
            self.cnt[eng] += 1
            tv = self.cnt[eng]
        else:
            tv = self.cnt[eng] + 1
        self.ops[eng].append((waits, fn, (eng, 1) if inc else None))
        for b in reads:
            b.r[eng] = tv
        for b in writes:
            b.w = (eng, tv)
            b.r = {}

    def dma(self, eng, fn, semkey, reads=(), writes=(), n=1):
        waits = self._deps(eng, reads, writes, True)
        self.dma_cnt[semkey] = self.dma_cnt.get(semkey, 0) + 16 * n
        tick = (semkey, self.dma_cnt[semkey])
        self.ops[eng].append((waits, fn, (semkey, 16)))
        for b in reads:
            b.r[semkey] = tick[1]
        for b in writes:
            b.w = tick
            b.r = {}
        return tick

    def barrier(self, skip=lambda k: k.startswith("w") and not k.startswith("d_")):
        for e in ENGS:
            waits = {}
            for k in ENGS:
                if k != e and self.cnt[k] > 0:
                    self._need(e, (k, self.cnt[k]), waits)
            for k, v in self.dma_cnt.items():
                if not skip(k):
                    self._need(e, (k, v), waits)
            for k, v in waits.items():
                self.seen[e][k] = v
            if waits:
                self.ops[e].append((tuple(waits.items()), None, None))

    def final_wait(self, eng):
        waits = {}
        for k in ENGS:
            if k != eng and self.cnt[k] > 0:
                self._need(eng, (k, self.cnt[k]), waits)
        for k, v in self.dma_cnt.items():
            self._need(eng, (k, v), waits)
        self.ops[eng].append((tuple(waits.items()), None, None))


def replay(prog, block, sems):
    def run(name):
        def body(eng):
            for waits, fn, inc in prog.ops[name]:
                for k, v in waits:
                    eng.wait_ge(sems[k], v)
                if fn is None:
                    continue
                res = fn(eng)
                if inc is not None:
                    if isinstance(res, (list, tuple)):
                        for r in res:
                            r.then_inc(sems[inc[0]], inc[1])
                    else:
                        res.then_inc(sems[inc[0]], inc[1])
        return body
    block.tensor(run("pe"))
    block.scalar(run("act"))
    block.vector(run("dve"))
    block.gpsimd(run("pool"))
    block.sync(run("sp"))


IN_OFF = {}
_o = 0
for _n, _s in [("ga_q", 128), ("ga_k", 128), ("ga_v", 256), ("ga_r", 256), ("ga_lr", 32), ("dn_x", 768), ("dn_b", 8),
               ("dn_a", 8), ("dn_g", 256), ("hg_q", 256), ("hg_f", 512), ("hg_i", 256), ("hg_g", 256), ("df_q", 256),
               ("df_k", 256), ("df_v", 256)]:
    IN_OFF[_n] = _o
    _o += _s
assert _o == NIN

ROW = {}
_r = 0
for _n, _s in [("norm_w", DEPTH * 3 * 8), ("b_mod", DEPTH * 72), ("final", 8), ("c_ctx", 8), ("c_lat", 8), ("gla_b", DEPTH * 2),
               ("dn_conv", DEPTH * 3 * 6), ("hg_lb", DEPTH * 2 * 2), ("hnorm", DEPTH * 4)]:
    ROW[_n] = _r
    _r += _s
NROW = 384
assert _r <= NROW

CST = {}
_c = 0
for _n, _s in [("ident", 128), ("ones", 128), ("bd64", 128), ("rot", 128), ("sel_f", 128), ("sel_b", 128), ("id8", 512),
               ("mf", 256), ("mb", 256), ("rm", 1024), ("hm_gla", 4), ("hm_hg", 4),
               ("sm_gla", 256), ("sm_hg", 256), ("qm", 4), ("CSTA_END", 0),
               ("nm_c", 512), ("nm_b", 512), ("NM_END", 0),
               ("cos", 1024), ("sin", 1024)]:
    CST[_n] = (_c, _s)
    _c += _s
NCST = _c
NCSTA = CST["CSTA_END"][0]
NCSTB = CST["mf"][0]


def make_consts():
    c = np.zeros((128, NCST), np.float32)

    def put(name, arr):
        o, s = CST[name]
        a = np.zeros((128, s), np.float32)
        a[:arr.shape[0], :arr.shape[1]] = arr
        c[:, o:o + s] = a
    put("ident", np.eye(128))
    put("ones", np.ones((128, 128)))
    bd = np.zeros((128, 128)); bd[:64, :64] = 1; bd[64:, 64:] = 1
    put("bd64", bd)
    j = np.arange(64)[:, None]; i = np.arange(64)[None, :]
    put("mf", np.tile((j <= i).astype(np.float32), (1, 4)))
    put("mb", np.tile((j >= i).astype(np.float32), (1, 4)))
    rm = np.ones((128, 1024)); rm[:, ::64] = 0
    put("rm", rm)
    d = np.arange(128)[:, None]; h = np.arange(4)[None, :]
    put("hm_gla", (d // 32 == h).astype(np.float32))
    put("hm_hg", (d // 64 == h % 2).astype(np.float32))
    put("sm_gla", np.repeat((d // 32 == h).astype(np.float32), 64, axis=1))
    put("sm_hg", np.repeat((d // 64 == h % 2).astype(np.float32), 64, axis=1))
    NEG = -30000.0
    put("nm_c", np.concatenate([np.tile(np.where(j <= i, 0.0, NEG), (1, 4)), np.tile(np.where(j >= i, 0.0, NEG), (1, 4))], axis=1))
    put("nm_b", np.concatenate([np.tile(np.where(j > i, 0.0, NEG), (1, 4)), np.tile(np.where(j < i, 0.0, NEG), (1, 4))], axis=1))
    t = np.arange(1024)
    row = (t // 64).astype(np.float32); col = (t % 64).astype(np.float32)
    half = 16
    inv = (10000.0 ** (-np.arange(0, half, 2, dtype=np.float32) / half)).astype(np.float32)
    ang_r = np.concatenate([row[:, None] * inv[None, :]] * 2, axis=1)
    ang_c = np.concatenate([col[:, None] * inv[None, :]] * 2, axis=1)
    ang = np.concatenate([ang_r, ang_c], axis=1).astype(np.float32)
    cos32 = np.cos(ang).T; sin32 = np.sin(ang).T
    put("cos", np.tile(cos32, (4, 1)))
    put("sin", np.tile(sin32, (4, 1)))
    R = np.zeros((128, 128), np.float32)
    for p in range(128):
        b16 = (p // 16) * 16; dd = p % 16
        if dd < 8:
            R[b16 + dd + 8, p] = -1.0
        else:
            R[b16 + dd - 8, p] = 1.0
    put("rot", R)
    put("qm", (d // 32 == h).astype(np.float32))
    sf = np.zeros((64, 128)); sf[63, :] = 1
    sb_ = np.zeros((64, 128)); sb_[0, :] = 1
    put("sel_f", sf); put("sel_b", sb_)
    put("id8", np.tile(np.eye(64), (1, 8)))
    return c


def pack_pvec(inp, b):
    rows = np.zeros((NROW, 128), np.float32)

    def put(name, arr):
        a = np.asarray(arr, np.float32).reshape(-1, 128)
        rows[ROW[name]:ROW[name] + a.shape[0]] = a
    put("norm_w", inp["norm_w"])
    put("b_mod", inp["b_mod"])
    put("final", inp["final_norm"])
    put("c_ctx", inp["c_ctx"])
    put("c_lat", inp["c"][b])
    put("gla_b", inp["gla_b"])
    put("dn_conv", inp["dn_conv"])
    put("hg_lb", inp["hg_lb_logits"])
    hn = np.stack([np.stack([np.tile(inp[k][l], 2) for k in ("gla_norm", "dn_norm", "hg_norm", "diff_norm")]) for l in range(DEPTH)])
    put("hnorm", hn)
    return rows


class Tn:
    __slots__ = ("t", "b")

    def __init__(self, t, b):
        self.t = t
        self.b = b


class Phase:
    def __init__(self, k):
        self.k = k
        self.es = ExitStack()
        self.rots = {}

    def __enter__(self):
        self.es.__enter__()
        return self

    def sb(self, name, shape, dt=F32):
        self.k.uid += 1
        t = self.es.enter_context(self.k.nc.sbuf_tensor(f"{name}_{self.k.uid}", list(shape), dt))
        return Tn(t, Buf(name))

    def rot(self, name, shape, dt=F32, n=2):
        if name not in self.rots:
            self.rots[name] = [[self.sb(f"{name}{i}", shape, dt) for i in range(n)], 0]
        lst = self.rots[name]
        t = lst[0][lst[1] % n]
        lst[1] += 1
        return t

    def __exit__(self, *a):
        self.k.P.barrier()
        return self.es.__exit__(*a)


class K:
    def __init__(self, wplan=None, taps=(), stages=None):
        self.nc = bass.Bass("TRN2", target_bir_lowering=False)
        self.P = Prog()
        self.es = ExitStack()
        self.uid = 0
        self.wplan = wplan
        self.wrec = []
        self.wi = 0
        self.wissued = 0
        self.taps = set(taps)
        self.tap_out = {}
        self.stages = stages
        self.din = {}
        self.dout = {}
        self.bank_i = 0

    def inp(self, name, shape):
        self.din[name] = self.nc.dram_tensor(name, list(shape), F32, kind="ExternalInput").ap()
        return self.din[name]

    def outp(self, name, shape):
        self.dout[name] = self.nc.dram_tensor(name, list(shape), F32, kind="ExternalOutput").ap()
        return self.dout[name]

    def sb(self, name, shape, dt=F32):
        self.uid += 1
        t = self.es.enter_context(self.nc.sbuf_tensor(f"{name}_{self.uid}", list(shape), dt))
        return Tn(t, Buf(name))

    def phase(self):
        return Phase(self)

    def mm(self, out, lhsT, rhs, start, stop, rd, wr, inc=None):
        inc = True
        self.P.op("pe", lambda e: e.matmul(out, lhsT=lhsT, rhs=rhs, start=start, stop=stop), reads=rd, writes=wr, inc=inc)

    def tr(self, out, in_, ident, rd, wr):
        self.P.op("pe", lambda e: e.transpose(out, in_, ident), reads=rd, writes=wr)

    def act(self, out, in_, func, rd, wr, bias=0.0, scale=1.0, accum=None):
        if accum is None:
            self.P.op("act", lambda e: e.activation(out=out, in_=in_, func=func, bias=bias, scale=scale), reads=rd, writes=wr)
        else:
            self.P.op("act", lambda e: e.activation(out=out, in_=in_, func=func, bias=bias, scale=scale, accum_out=accum), reads=rd, writes=wr)

    def tt(self, eng, out, a, b, op, rd, wr):
        if eng == "pool" and op not in (ALU.add, ALU.subtract, ALU.mult):
            eng = "dve"
        self.P.op(eng, lambda e: e.tensor_tensor(out=out, in0=a, in1=b, op=op), reads=rd, writes=wr)

    def ts(self, eng, out, a, s1, s2, op0, op1, rd, wr):
        self.P.op(eng, lambda e: e.tensor_scalar(out=out, in0=a, scalar1=s1, scalar2=s2, op0=op0, op1=op1), reads=rd, writes=wr)

    def stt(self, eng, out, a, s, b, op0, op1, rd, wr):
        eng = "dve"
        self.P.op(eng, lambda e: e.scalar_tensor_tensor(out=out, in0=a, scalar=s, in1=b, op0=op0, op1=op1), reads=rd, writes=wr)

    def cp(self, eng, out, in_, rd, wr):
        if eng == "act":
            self.P.op("act", lambda e: e.copy(out=out, in_=in_), reads=rd, writes=wr)
        else:
            self.P.op(eng, lambda e: e.tensor_copy(out=out, in_=in_), reads=rd, writes=wr)

    def memset(self, eng, ap, val, wr):
        self.P.op(eng, lambda e: e.memset(ap, val), writes=wr)

    def dma(self, eng, out, in_, key, rd=(), wr=()):
        key = "d_" + (wr[0].name if wr else rd[0].name)
        return self.P.dma(eng, lambda e: e.dma_start(out=out, in_=in_), key, reads=rd, writes=wr)

    def bank(self):
        b = self.banks[self.bank_i % len(self.banks)]
        self.bank_i += 1
        return b

    def tap(self, name, tn_ap, shape, rd):
        if name not in self.taps:
            return
        o = self.outp("tap_" + name, shape)
        self.P.dma("pool", lambda e: e.dma_start(out=o, in_=tn_ap), "tap_" + name, reads=rd)

    def wsrc(self, spec, slot):
        kind = spec[0]
        if kind == "mat":
            _, name, idx, nk, ranges = spec
            W = self.din[name]
            for i in idx:
                W = W[i]
            tot = sum(n for _, n in ranges)
            view = slot.t[:, 0:nk * tot].rearrange("p (k c) -> p k c", c=tot)
            pieces = []
            o = 0
            for c0, n in ranges:
                pieces.append((view[:, :, o:o + n], W[:, c0:c0 + n].rearrange("(k p) c -> p k c", p=128)))
                o += n
            return pieces, view
        if kind == "branch":
            _, l, n = spec
            W = self.din["w_branch"][l][n]
            view = slot.t[0:64, 0:4096].rearrange("p (h f) -> p h f", f=1024)
            return [(view, W.rearrange("(h e) f -> e h f", e=64))], view
        if kind == "merge":
            _, l, n, q = spec
            gv = slot.t[:, 0:2048].rearrange("p (k c) -> p k c", c=256)
            bv = slot.t[0:64, 2048:3072].rearrange("p (h f) -> p h f", f=256)
            Wg = self.din["w_mgate"][l]
            Wb = self.din["w_branch"][l][n]
            pieces = [(gv, Wg[:, n * 1024 + q * 256:n * 1024 + (q + 1) * 256].rearrange("(k p) c -> p k c", p=128)),
                      (bv, Wb[:, q * 256:(q + 1) * 256].rearrange("(h e) f -> e h f", e=64))]
            return pieces, (gv, bv)
        raise ValueError(kind)

    def _wissue(self, j, spec):
        slot = self.wslots[j % len(self.wslots)]
        pieces, view = self.wsrc(spec, slot)
        key = f"w{j % len(self.wslots)}"
        self.P.dma("pool", lambda e: [e.dma_start(out=o, in_=i) for o, i in pieces], key, writes=[slot.b], n=len(pieces))

    def wget(self, spec):
        j = self.wi
        self.wi += 1
        self.wrec.append(spec)
        slot = self.wslots[j % len(self.wslots)]
        if self.wplan is None:
            self._wissue(j, spec)
        else:
            assert self.wplan[j] == spec, (j, spec, self.wplan[j])
            while self.wissued < min(j + len(self.wslots) - 1, len(self.wplan)):
                self._wissue(self.wissued, self.wplan[self.wissued])
                self.wissued += 1
        _, view = self.wsrc(spec, slot)
        return view, slot.b


def CS(k, name, lo=0, n=None):
    o, s = CST[name]
    n = s - lo if n is None else n
    return k.cst.t[:, o + lo:o + lo + n]


def CSB(k, name, lo=0, n=None):
    o, s = CST[name]
    n = s - lo if n is None else n
    return k.cstb.t[:, o + lo:o + lo + n]


def build(wplan=None, taps=(), stages=("all",)):
    k = K(wplan, taps, stages)
    nc = k.nc
    es = k.es
    st = set(stages)
    ALL = "all" in st
    xin = [k.inp("xp", [NT, D]), k.inp("xs", [NT, D])]
    k.inp("ck", [DEPTH, 4, 512, 64]); k.inp("cv", [DEPTH, 4, 512, 64])
    k.inp("sg", [DEPTH, 2, 4, 32, 64]); k.inp("sd", [DEPTH, 2, 4, 64, 64]); k.inp("sh", [DEPTH, 2, 4, 64, 64])
    k.inp("pvec", [NROW, 128]); k.inp("cst", [128, NCST])
    k.inp("dn_al", [DEPTH, 8]); k.inp("dn_dt", [DEPTH, 8]); k.inp("dlam", [DEPTH, 128]); k.inp("gla_w2", [DEPTH, 2, 16, 128])
    k.inp("w_mod", [DEPTH, D, 9 * D])
    for n_ in ("ffn1_in", "ffn2_in"):
        k.inp(n_, [DEPTH, D, 2 * DFF])
    for n_ in ("ffn1_down", "ffn2_down"):
        k.inp(n_, [DEPTH, DFF, D])
    k.inp("w_in", [DEPTH, D, NIN]); k.inp("w_branch", [DEPTH, 4, 256, D]); k.inp("w_mgate", [DEPTH, D, 4 * D]); k.inp("w_out", [DEPTH, D, D])
    yout = [k.outp("yp", [NT, D]), k.outp("ys", [NT, D])]
    k.outp("nk", [4, DEPTH, 4, 256, 64]); k.outp("nv", [4, DEPTH, 4, 256, 64])
    k.outp("ng", [4, DEPTH, 2, 4, 32, 64]); k.outp("nd", [4, DEPTH, 2, 4, 64, 64]); k.outp("nh", [4, DEPTH, 2, 4, 64, 64])

    k.cst = k.sb("cst", [128, NCSTA])
    k.cstb = k.sb("cstb", [128, NCSTB], BF16)
    k.pt = k.sb("pt", [128, NROW])
    k.mod = k.sb("mod", [128, DEPTH, 72, 2])
    k.der = k.sb("der", [128, DEPTH, 2, 9, 8])
    k.XT = k.sb("XT", [128, 8, NT])
    k.xtb = [Buf(f"xt{c}") for c in range(8)]
    k.wslots = [k.sb(f"wslot{i}", [128, 4096], BF16) for i in range(3)]
    k.xscr = nc.dram_tensor("xscr", [128, 8 * NT], F32, kind="Internal").ap()
    k.big = [es.enter_context(nc.psum_tensor(f"big{i}", [128, 1536], F32)) for i in range(2)]
    k.banks = [Tn(k.big[i // 3][:, (i % 3) * 512:(i % 3 + 1) * 512], Buf(f"bank{i}")) for i in range(6)]
    k.bbanks = [Tn(es.enter_context(nc.psum_tensor(f"bbank{i}", [128, 1024], BF16)), Buf(f"bbank{i}")) for i in range(2)]
    k.bb_i = 0
    P = k.P
    IDF = lambda: CS(k, "ident")
    IDB = lambda: CSB(k, "ident")
    ONESB = lambda: CSB(k, "ones")

    k.dma("sp", k.cst.t[:], k.din["cst"][:, 0:NCSTA], "ld", wr=[k.cst.b])
    k.dma("pool", k.cstb.t[:], k.din["cst"][:, 0:NCSTB], "ldb", wr=[k.cstb.b])
    with k.phase() as ph:
        for r in range(3):
            stg = ph.rot("pstg", [128, 128], F32, 3)
            k.dma("sp", stg.t[:], k.din["pvec"][r * 128:(r + 1) * 128, :], "ld", wr=[stg.b])
            bk = k.bank()
            k.tr(bk.t[:, 0:128], stg.t[:], IDF(), [stg.b, k.cst.b], [bk.b])
            k.cp("dve", k.pt.t[:, r * 128:(r + 1) * 128], bk.t[:, 0:128], [bk.b], [k.pt.b])
        cs = ph.sb("cs", [128, 8, 2], BF16)
        k.act(cs.t[:, :, 0], k.pt.t[:, ROW["c_ctx"]:ROW["c_ctx"] + 8], AF.Silu, [k.pt.b], [cs.b])
        k.act(cs.t[:, :, 1], k.pt.t[:, ROW["c_lat"]:ROW["c_lat"] + 8], AF.Silu, [k.pt.b], [cs.b])
        for l in range(DEPTH):
            bk = k.bank()
            for tl in range(18):
                wv, wb = k.wget(("mat", "w_mod", (l,), 8, ((tl * 512, 512),)))
                for s4 in range(4):
                    j = tl * 4 + s4
                    for kc in range(8):
                        k.mm(bk.t[:, j * 2:j * 2 + 2], wv[:, kc, s4 * 128:(s4 + 1) * 128], cs.t[:, kc, :], kc == 0, kc == 7, [wb, cs.b], [bk.b])
            o0 = ROW["b_mod"] + l * 72
            k.tt("dve", k.mod.t[:, l], bk.t[:, 0:144].rearrange("p (j w) -> p j w", w=2),
                 k.pt.t[:, o0:o0 + 72].unsqueeze(2).to_broadcast([128, 72, 2]), ALU.add, [bk.b, k.pt.b], [k.mod.b])
            for w in range(2):
                for i in range(3):
                    nw0 = ROW["norm_w"] + (l * 3 + i) * 8
                    k.stt("dve", k.der.t[:, l, w, i, :], k.mod.t[:, l, (3 * i + 1) * 8:(3 * i + 2) * 8, w], 1.0, k.pt.t[:, nw0:nw0 + 8],
                          ALU.add, ALU.mult, [k.mod.b, k.pt.b], [k.der.b])
                    k.cp("dve", k.der.t[:, l, w, 3 + i, :], k.mod.t[:, l, (3 * i) * 8:(3 * i + 1) * 8, w], [k.mod.b], [k.der.b])
                    k.ts("dve", k.der.t[:, l, w, 6 + i, :], k.mod.t[:, l, (3 * i + 2) * 8:(3 * i + 3) * 8, w], 1.0 if i == 1 else 0.5, 0.0,
                         ALU.mult, ALU.add, [k.mod.b], [k.der.b])
    k.tap("mod", k.mod.t[:].rearrange("p l j w -> p (l j w)"), [128, DEPTH * 144], [k.mod.b])

    XT = k.XT

    def xload(g):
        with k.phase() as ph:
            for tl in range(8):
                stg = ph.rot("xstg", [128, D], F32, 2)
                k.dma("sp", stg.t[:], xin[g][tl * 128:(tl + 1) * 128, :], "ldx", wr=[stg.b])
                for hb in range(2):
                    bk = k.bank()
                    for c4 in range(4):
                        c = hb * 4 + c4
                        k.tr(bk.t[:, c4 * 128:(c4 + 1) * 128], stg.t[:, c * 128:(c + 1) * 128], IDF(), [stg.b, k.cst.b], [bk.b])
                    eng = "act" if hb == 0 else "dve"
                    k.cp(eng, XT.t[:, hb * 4:(hb + 1) * 4, tl * 128:(tl + 1) * 128], bk.t[:].rearrange("p (c t) -> p c t", t=128), [bk.b],
                         [k.xtb[c] for c in range(hb * 4, hb * 4 + 4)])

    def rstd_of(ph, src_fn, nchunk, nparts, lhsT, scale, name, rd, dst=None):
        rstd = dst if dst is not None else ph.sb(name, [128, NT])
        for hb in range(2):
            bk = k.bank()
            for c in range(nchunk):
                sq = ph.rot("sq", [128, 512], BF16, 3)
                k.act(sq.t[0:nparts, :], src_fn(c, hb), AF.Square, rd, [sq.b])
                k.mm(bk.t[0:nparts, :], lhsT, sq.t[0:nparts, :], c == 0, c == nchunk - 1, [sq.b, k.cstb.b], [bk.b])
            sl = rstd.t[0:nparts, hb * 512:(hb + 1) * 512]
            k.ts("dve", sl, bk.t[0:nparts, :], scale, EPS, ALU.mult, ALU.add, [bk.b], [rstd.b])
            k.act(sl, sl, AF.Sqrt, [rstd.b], [rstd.b])
            k.P.op("dve", (lambda sl_: (lambda e: e.reciprocal(out=sl_, in_=sl_)))(sl), reads=[rstd.b], writes=[rstd.b])
        return rstd

    def norm_mod(ph, l, w, i, HN):
        rstd = rstd_of(ph, lambda c, hb: XT.t[:, c, hb * 512:(hb + 1) * 512], 8, 128, ONESB(), 1.0 / D, "rstd", k.xtb)
        for c in range(8):
            tmp = ph.rot("ntmp", [128, NT], F32, 2)
            k.tt("dve", tmp.t[:], XT.t[:, c, :], rstd.t[:], ALU.mult, [k.xtb[c], rstd.b], [tmp.b])
            k.act(HN.t[:, c, :], tmp.t[:], AF.Identity, [tmp.b, k.der.b], [HN.b],
                  bias=k.der.t[:, l, w, 3 + i, c:c + 1], scale=k.der.t[:, l, w, i, c:c + 1])

    def ffn(l, w, i, win, wdown):
        with k.phase() as ph:
            HN = ph.sb("HN", [128, 8, NT], BF16)
            FA = ph.sb("FA", [128, 22, NT], BF16)
            norm_mod(ph, l, w, i, HN)
            if FFN_STOP == 1:
                return
            for j in range(11 if FFN_STOP != 2 else 1):
                wv, wb = k.wget(("mat", win, (l,), 8, ((j * 256, 256), (DFF + j * 256, 256))))
                for sub in range(2):
                    for hb in range(2):
                        bg = k.bank(); bu = k.bank()
                        for kc in range(8):
                            k.mm(bg.t[:], wv[:, kc, sub * 128:(sub + 1) * 128], HN.t[:, kc, hb * 512:(hb + 1) * 512], kc == 0, kc == 7, [wb, HN.b], [bg.b])
                        for kc in range(8):
                            k.mm(bu.t[:], wv[:, kc, 256 + sub * 128:256 + (sub + 1) * 128], HN.t[:, kc, hb * 512:(hb + 1) * 512], kc == 0, kc == 7, [wb, HN.b], [bu.b])
                        sg = ph.rot("sg", [128, 512], F32, 3)
                        k.act(sg.t[:], bg.t[:], AF.Silu, [bg.b], [sg.b])
                        k.tt("dve", FA.t[:, j * 2 + sub, hb * 512:(hb + 1) * 512], sg.t[:], bu.t[:], ALU.mult, [sg.b, bu.b], [FA.b])
            if FFN_STOP in (2, 3):
                return
            for fo in range(8):
                wv, wb = k.wget(("mat", wdown, (l,), 22, ((fo * 128, 128),)))
                for hb in range(2):
                    bk = k.bank()
                    for kc in range(22):
                        k.mm(bk.t[:], wv[:, kc, :], FA.t[:, kc, hb * 512:(hb + 1) * 512], kc == 0, kc == 21, [wb, FA.b], [bk.b])
                    xs_ = XT.t[:, fo, hb * 512:(hb + 1) * 512]
                    k.stt("dve", xs_, bk.t[:], k.der.t[:, l, w, 6 + i, fo:fo + 1], xs_, ALU.mult, ALU.add, [bk.b, k.der.b, k.xtb[fo]], [k.xtb[fo]])

    def final(g):
        with k.phase() as ph:
            rstd = rstd_of(ph, lambda c, hb: XT.t[:, c, hb * 512:(hb + 1) * 512], 8, 128, ONESB(), 1.0 / D, "rstd", k.xtb)
            f0 = ROW["final"]
            for c in range(8):
                tmp = ph.rot("ntmp", [128, NT], F32, 2)
                k.stt("dve", tmp.t[:], XT.t[:, c, :], k.pt.t[:, f0 + c:f0 + c + 1], rstd.t[:], ALU.mult, ALU.mult, [k.xtb[c], rstd.b, k.pt.b], [tmp.b])
                for tl in range(8):
                    pass
                k.cp("pool", XT.t[:, c, :], tmp.t[:], [tmp.b], [k.xtb[c]])
            for tl in range(8):
                stg = ph.rot("ystg", [128, D], F32, 2)
                for hb in range(2):
                    bk = k.bank()
                    for c4 in range(4):
                        c = hb * 4 + c4
                        k.tr(bk.t[:, c4 * 128:(c4 + 1) * 128], XT.t[:, c, tl * 128:(tl + 1) * 128], IDF(), [k.xtb[c], k.cst.b], [bk.b])
                    k.cp("act" if hb == 0 else "dve", stg.t[:, hb * 512:(hb + 1) * 512], bk.t[:], [bk.b], [stg.b])
                k.dma("sp", yout[g][tl * 128:(tl + 1) * 128, :], stg.t[:], "sty", rd=[stg.b])


    def proj_fm(wv, wb, c0, n, HN, evac):
        for hb in range(2):
            bk = k.bank()
            for kc in range(8):
                k.mm(bk.t[0:n, :], wv[:, kc, c0:c0 + n], HN.t[:, kc, hb * 512:(hb + 1) * 512], kc == 0, kc == 7, [wb, HN.b], [bk.b])
            evac(hb, bk)

    def proj_tm(wv, wb, c0, n, HN, tok0, M, evac):
        bk = k.bank()
        for kc in range(8):
            k.mm(bk.t[0:M, 0:n], HN.t[:, kc, tok0:tok0 + M], wv[:, kc, c0:c0 + n], kc == 0, kc == 7, [HN.b, wb], [bk.b])
        evac(bk)

    def win_tile(l, c0, n):
        return k.wget(("mat", "w_in", (l,), 8, ((c0, n),)))

    def evac_to(dst, dstb, func=None, n=128, eng="act", scale=1.0):
        def f(hb, bk):
            o = dst[0:n, hb * 512:(hb + 1) * 512]
            if func is not None:
                k.act(o, bk.t[0:n, :], func, [bk.b], [dstb], scale=scale)
            else:
                k.cp(eng, o, bk.t[0:n, :], [bk.b], [dstb])
        return f

    def head_norm(ph, OACC, l, n, BR, gate=None, extra=1.0):
        w0 = ROW["hnorm"] + l * 4 + n
        with k.phase() as ph2:
            sq = ph2.sb("hsq", [64, 2, NT], BF16)
            rstd = ph2.sb("hrstd", [64, 2, NT])
            for pr in range(2):
                osl = OACC.t[0:64, 2 * pr:2 * pr + 2, :]
                k.act(sq.t[:], osl, AF.Square, [OACC.b], [sq.b])
                bks = []
                for j in range(4):
                    bk = k.bank()
                    k.mm(bk.t[0:64, :], CSB(k, "ones")[0:64, 0:64], sq.t[:, j // 2, (j % 2) * 512:(j % 2 + 1) * 512], True, True, [sq.b, k.cstb.b], [bk.b])
                    bks.append(bk)
                for j in range(4):
                    k.ts("dve", rstd.t[:, j // 2, (j % 2) * 512:(j % 2 + 1) * 512], bks[j].t[0:64, :], 1.0 / 64, EPS, ALU.mult, ALU.add, [bks[j].b], [rstd.b])
                k.act(rstd.t[:], rstd.t[:], AF.Sqrt, [rstd.b], [rstd.b])
                k.P.op("dve", lambda e: e.reciprocal(out=rstd.t[:], in_=rstd.t[:]), reads=[rstd.b], writes=[rstd.b])
                k.tt("dve", osl, osl, rstd.t[:], ALU.mult, [OACC.b, rstd.b], [OACC.b])
                if gate is not None:
                    k.stt("dve", BR.t[0:64, n, 2 * pr:2 * pr + 2, :], osl, k.pt.t[0:64, w0:w0 + 1], gate.t[0:64, 2 * pr:2 * pr + 2, :], ALU.mult, ALU.mult,
                          [OACC.b, k.pt.b, gate.b], [BR.b])
                else:
                    k.ts("pool", BR.t[0:64, n, 2 * pr:2 * pr + 2, :], osl, k.pt.t[0:64, w0:w0 + 1], extra, ALU.mult, ALU.mult, [OACC.b, k.pt.b], [BR.b])

    def gate_proj(ph, l, c0, HN):
        gate = ph.sb("gate", [64, 4, NT], BF16)
        wv, wb = win_tile(l, c0, 256)
        for h in range(4):
            proj_fm(wv, wb, h * 64, 64, HN, evac_to(gate.t[:, h, :], gate.b, AF.Silu, 64))
        return gate

    def gla_scan(ph, g, l, n, KG, qT, qscale, kT, gT, vtm, hmname, smname, sin_name, sout_name, dk, OACC):
        nseq = 4 if g == 0 else 1
        cps = NCH // nseq
        HPG = 4 // KG
        MASK = {0: CS(k, "mf"), 1: CS(k, "mb")}
        SMv = CS(k, smname)
        for dr in range(2):
            with k.phase() as p2:
                qg = [p2.sb(f"qg{kg}", [128, NT], BF16) for kg in range(KG)]
                kx = [p2.sb(f"kx{h}", [128, NT], BF16) for h in range(4)]
                kd = [p2.sb(f"kd{kg}", [128, NT], BF16) for kg in range(KG)]
                eref = [p2.sb(f"eref{kg}", [128, NCH]) for kg in range(KG)]
                elast = [p2.sb(f"elast{kg}", [128, NCH]) for kg in range(KG)]
                tl = 63 if dr == 0 else 0
                for kg in range(KG):
                    gsrc = gT[dr][kg]; ksrc = kT[dr][kg]
                    cum = p2.rot("cum", [128, NT], F32, 2)
                    k.P.op("dve", (lambda o_, g_: (lambda e: e.tensor_tensor_scan(out=o_, data0=CS(k, "rm"), data1=g_, initial=0.0, op0=ALU.mult, op1=ALU.add)))(cum.t[:], gsrc.t[:]),
                           reads=[gsrc.b, k.cst.b], writes=[cum.b])
                    c3 = lambda t_: t_.t[:].rearrange("p (c j) -> p c j", j=64)
                    if dr == 1:
                        c2 = p2.rot("cum", [128, NT], F32, 2)
                        k.tt("dve", c2.t[:], gsrc.t[:], cum.t[:], ALU.subtract, [gsrc.b, cum.b], [c2.b])
                        k.tt("dve", c3(c2), c3(c2), c3(cum)[:, :, 63:64].to_broadcast([128, NCH, 64]), ALU.add, [c2.b, cum.b], [c2.b])
                        cum = c2
                    bm = p2.rot("bm", [128, NT], F32, 1)
                    k.tt("dve", c3(bm), c3(cum), c3(cum)[:, :, 32:33].to_broadcast([128, NCH, 64]), ALU.subtract, [cum.b], [bm.b])
                    e1 = p2.rot("ee", [128, NT], F32, 2)
                    k.act(e1.t[:], bm.t[:], AF.Exp, [bm.b], [e1.b])
                    k.stt("pool", qg[kg].t[:], qT[kg].t[:], qscale, e1.t[:], ALU.mult, ALU.mult, [qT[kg].b, e1.b], [qg[kg].b])
                    e2 = p2.rot("ee", [128, NT], F32, 2)
                    k.act(e2.t[:], bm.t[:], AF.Exp, [bm.b], [e2.b], scale=-1.0)
                    for hh in range(HPG):
                        h = kg * HPG + hh
                        k.stt("dve" if hh % 2 == 0 else "pool", kx[h].t[:], ksrc.t[:], CS(k, hmname, h, 1), e2.t[:], ALU.mult, ALU.mult, [ksrc.b, e2.b, k.cst.b], [kx[h].b])
                    bl = p2.rot("bm", [128, NT], F32, 1)
                    k.tt("dve", c3(bl), c3(cum)[:, :, tl:tl + 1].to_broadcast([128, NCH, 64]), c3(cum), ALU.subtract, [cum.b], [bl.b])
                    e3 = p2.rot("ee", [128, NT], F32, 2)
                    k.act(e3.t[:], bl.t[:], AF.Exp, [bl.b], [e3.b])
                    k.tt("pool", kd[kg].t[:], ksrc.t[:], e3.t[:], ALU.mult, [ksrc.b, e3.b], [kd[kg].b])
                    k.act(eref[kg].t[:], c3(cum)[:, :, 32], AF.Exp, [cum.b], [eref[kg].b])
                    k.act(elast[kg].t[:], c3(cum)[:, :, tl], AF.Exp, [cum.b], [elast[kg].b])
                S32 = p2.sb("S32", [128, 256])
                ATs = [p2.sb(f"AT{i_}", [64, 256], BF16) for i_ in range(3)]
                for a_ in ATs:
                    k.memset("pool", a_.t[:], 0.0, [a_.b])
                kdTs = [p2.sb(f"kdT{i_}", [64, 256], BF16) for i_ in range(4)]
                ATs = ATs + [p2.sb("AT3", [64, 256], BF16)]
                k.memset("pool", ATs[3].t[:], 0.0, [ATs[3].b])
                v4 = lambda ap_: ap_.rearrange("p (h i) -> p h i", i=64)

                def stageA(ch, slot):
                    tok = slice(ch * 64, (ch + 1) * 64)
                    bb = k.bbanks[k.bb_i % 2]; k.bb_i += 1
                    for kg in range(KG):
                        k.tr(bb.t[0:64, kg * 128:(kg + 1) * 128], kd[kg].t[:, tok], CSB(k, "ident"), [kd[kg].b, k.cstb.b], [bb.b])
                    kdT = kdTs[slot % 4]
                    k.cp("act", kdT.t[:, 0:KG * 128], bb.t[0:64, 0:KG * 128], [bb.b], [kdT.b])
                    pa = k.bank()
                    t0 = ch * 64
                    lo = slice(t0, t0 + 32); hi = slice(t0 + 32, t0 + 64)
                    for h in range(4):
                        qh = qg[h // HPG]
                        rdm = [kx[h].b, qh.b]
                        if dr == 0:
                            k.mm(pa.t[0:32, h * 64:(h + 1) * 64], kx[h].t[:, lo], qh.t[:, tok], True, True, rdm, [pa.b])
                            k.mm(pa.t[32:64, h * 64 + 32:(h + 1) * 64], kx[h].t[:, hi], qh.t[:, hi], True, True, rdm, [pa.b])
                        else:
                            k.mm(pa.t[0:32, h * 64:h * 64 + 32], kx[h].t[:, lo], qh.t[:, lo], True, True, rdm, [pa.b])
                            k.mm(pa.t[32:64, h * 64:(h + 1) * 64], kx[h].t[:, hi], qh.t[:, tok], True, True, rdm, [pa.b])
                    AT = ATs[slot % 4]
                    if dr == 0:
                        k.tt("dve", AT.t[0:32, :], pa.t[0:32, 0:256], MASK[dr][0:32, :], ALU.mult, [pa.b, k.cst.b], [AT.b])
                        k.tt("dve", v4(AT.t[32:64, :])[:, :, 32:64], v4(pa.t[32:64, 0:256])[:, :, 32:64], v4(MASK[dr][32:64, :])[:, :, 32:64], ALU.mult, [pa.b, k.cst.b], [AT.b])
                    else:
                        k.tt("dve", v4(AT.t[0:32, :])[:, :, 0:32], v4(pa.t[0:32, 0:256])[:, :, 0:32], v4(MASK[dr][0:32, :])[:, :, 0:32], ALU.mult, [pa.b, k.cst.b], [AT.b])
                        k.tt("dve", AT.t[32:64, :], pa.t[32:64, 0:256], MASK[dr][32:64, :], ALU.mult, [pa.b, k.cst.b], [AT.b])
                    return kdT, AT

                def stageB(ch, kdT, AT):
                    tok = slice(ch * 64, (ch + 1) * 64)
                    ps_ = k.bank()
                    if KG == 1:
                        k.mm(ps_.t[:, 0:256], kdT.t[:, 0:128], vtm.t[:, ch, :], True, True, [kdT.b, vtm.b], [ps_.b])
                    else:
                        for kg in range(2):
                            k.mm(ps_.t[:, kg * 128:(kg + 1) * 128], kdT.t[:, kg * 128:(kg + 1) * 128], vtm.t[:, ch, kg * 128:(kg + 1) * 128], True, True, [kdT.b, vtm.b], [ps_.b])
                    Sbf = p2.rot("Sbf", [128, 256], BF16, 3)
                    if KG == 1:
                        k.stt("dve", Sbf.t[:], S32.t[:], eref[0].t[:, ch:ch + 1], SMv, ALU.mult, ALU.mult, [S32.b, eref[0].b, k.cst.b], [Sbf.b])
                    else:
                        for kg in range(2):
                            k.stt("dve", Sbf.t[:, kg * 128:(kg + 1) * 128], S32.t[:, kg * 128:(kg + 1) * 128], eref[kg].t[:, ch:ch + 1], SMv[:, kg * 128:(kg + 1) * 128],
                                  ALU.mult, ALU.mult, [S32.b, eref[kg].b, k.cst.b], [Sbf.b])
                    if KG == 1:
                        k.stt("dve", S32.t[:], S32.t[:], elast[0].t[:, ch:ch + 1], ps_.t[:, 0:256], ALU.mult, ALU.add, [S32.b, elast[0].b, ps_.b], [S32.b])
                    else:
                        for kg in range(2):
                            sl = slice(kg * 128, (kg + 1) * 128)
                            k.stt("dve", S32.t[:, sl], S32.t[:, sl], elast[kg].t[:, ch:ch + 1], ps_.t[:, sl], ALU.mult, ALU.add, [S32.b, elast[kg].b, ps_.b], [S32.b])
                    po = k.bank()
                    for h in range(4):
                        o_ = po.t[0:64, h * 64:(h + 1) * 64]
                        k.mm(o_, vtm.t[:, ch, h * 64:(h + 1) * 64], AT.t[:, h * 64:(h + 1) * 64], True, False, [vtm.b, AT.b], [po.b])
                        k.mm(o_, Sbf.t[:, h * 64:(h + 1) * 64], qg[h // HPG].t[:, tok], False, True, [Sbf.b, qg[h // HPG].b], [po.b])
                    oview = OACC.t[0:64, :, tok]
                    pview = po.t[0:64, 0:256].rearrange("p (h i) -> p h i", i=64)
                    if dr == 0:
                        k.cp("act", oview, pview, [po.b], [OACC.b])
                    else:
                        k.tt("dve", oview, oview, pview, ALU.add, [po.b, OACC.b], [OACC.b])

                seqlist = []
                for s_ in range(nseq):
                    order = range(cps) if dr == 0 else range(cps - 1, -1, -1)
                    for ii, ci in enumerate(order):
                        seqlist.append((s_, s_ * cps + ci, ii == 0, ii == cps - 1))
                DEPTH_A = 2
                pend = {}
                for j in range(min(DEPTH_A, len(seqlist))):
                    pend[j] = stageA(seqlist[j][1], j)
                for j, (s_, ch, first, last) in enumerate(seqlist):
                    if first:
                        k.memset("pool", S32.t[:], 0.0, [S32.b])
                        if g == 1:
                            for h in range(4):
                                hh = h % HPG
                                k.dma("sp", S32.t[hh * dk:(hh + 1) * dk, h * 64:(h + 1) * 64] if KG == 2 else S32.t[h * dk:(h + 1) * dk, h * 64:(h + 1) * 64],
                                      k.din[sin_name][l, dr, h], "ldS", wr=[S32.b])
                    kdT, AT = pend.pop(j)
                    stageB(ch, kdT, AT)
                    if j + DEPTH_A < len(seqlist):
                        pend[j + DEPTH_A] = stageA(seqlist[j + DEPTH_A][1], j + DEPTH_A)
                    if last and g == 0:
                        for h in range(4):
                            hh = h % HPG
                            src = S32.t[hh * dk:(hh + 1) * dk, h * 64:(h + 1) * 64] if KG == 2 else S32.t[h * dk:(h + 1) * dk, h * 64:(h + 1) * 64]
                            k.dma("sp", k.dout[sout_name][s_, l, dr, h], src, "stS", rd=[S32.b])

    def mixer_gla(ph, g, l, HN, BR):
        with k.phase() as p1:
            o = IN_OFF["ga_q"]
            wv, wb = win_tile(l, 0, 512)
            qT = p1.sb("qT", [128, NT]); kTt = p1.sb("kT", [128, NT])
            proj_fm(wv, wb, 0, 128, HN, evac_to(qT.t, qT.b))
            proj_fm(wv, wb, 128, 128, HN, evac_to(kTt.t, kTt.b, eng="dve"))
            vtm = p1.sb("vtm", [64, NCH, 256], BF16)
            for ch in range(NCH):
                proj_tm(wv, wb, 256, 256, HN, ch * 64, 64, (lambda ch_: (lambda bk: k.cp("act" if ch_ % 2 else "dve", vtm.t[:, ch_, :], bk.t[0:64, 0:256], [bk.b], [vtm.b])))(ch))
            wv2, wb2 = win_tile(l, 768, 32)
            lrT = p1.sb("lrT", [32, NT], BF16)
            proj_fm(wv2, wb2, 0, 32, HN, evac_to(lrT.t, lrT.b, n=32))
            w2p = p1.sb("w2p", [32, 2, 128], BF16)
            k.memset("pool", w2p.t[:], 0.0, [w2p.b])
            for dr in range(2):
                k.dma("pool", w2p.t[dr * 16:(dr + 1) * 16, dr, :], k.din["gla_w2"][l, dr], "ldw2", wr=[w2p.b])
            negb = p1.sb("negb", [128, 2])
            gb0 = ROW["gla_b"] + l * 2
            k.ts("dve", negb.t[:], k.pt.t[:, gb0:gb0 + 2], -1.0, 0.0, ALU.mult, ALU.add, [k.pt.b], [negb.b])
            gT = [[Tn(k.XT.t[:, 4 + dr, :], Buf(f"gT{dr}"))] for dr in range(2)]
            for dr in range(2):
                for hb in range(2):
                    bk = k.bank()
                    k.mm(bk.t[:], w2p.t[:, dr, :], lrT.t[:, hb * 512:(hb + 1) * 512], True, True, [w2p.b, lrT.b], [bk.b])
                    sl = gT[dr][0].t[:, hb * 512:(hb + 1) * 512]
                    k.act(sl, bk.t[:], AF.Exp, [bk.b, negb.b], [gT[dr][0].b], bias=negb.t[:, dr:dr + 1], scale=-1.0)
                    k.act(sl, sl, AF.Ln, [gT[dr][0].b], [gT[dr][0].b], bias=1.0)
                    k.ts("dve", sl, sl, -1.0 / 16.0, 0.0, ALU.mult, ALU.add, [gT[dr][0].b], [gT[dr][0].b])
            OACC = Tn(k.XT.t[0:64, 0:4, :], Buf("OACC"))
            gla_scan(p1, g, l, 0, 1, [qT], 32 ** -0.5, [[kTt], [kTt]], gT, vtm, "hm_gla", "sm_gla", "sg", "ng", 32, OACC)
            gate = gate_proj(p1, l, 512, HN)
            head_norm(p1, OACC, l, 0, BR, gate)

    def mixer_hg(ph, g, l, HN, BR):
        with k.phase() as p1:
            c0 = IN_OFF["hg_q"]
            wv, wb = win_tile(l, c0, 512)
            wv2, wb2 = win_tile(l, c0 + 512, 512)
            qT = [p1.sb(f"hq{kg}", [128, NT], BF16) for kg in range(2)]
            for kg in range(2):
                proj_fm(wv, wb, kg * 128, 128, HN, evac_to(qT[kg].t, qT[kg].b, AF.Silu))
            vtm = p1.sb("vtm", [64, NCH, 256], BF16)
            for ch in range(NCH):
                proj_tm(wv2, wb2, 256, 256, HN, ch * 64, 64, (lambda ch_: (lambda bk: k.cp("act" if ch_ % 2 else "dve", vtm.t[:, ch_, :], bk.t[0:64, 0:256], [bk.b], [vtm.b])))(ch))
            lb = p1.sb("lb", [128, 4]); oml = p1.sb("oml", [128, 4])
            r0 = ROW["hg_lb"]
            if l == 0:
                k.memset("pool", lb.t[:], 0.0, [lb.b])
            else:
                ex = p1.sb("lbex", [128, 8])
                k.act(ex.t[:], k.pt.t[:, r0:r0 + 8], AF.Exp, [k.pt.b], [ex.b])
                sm_ = p1.sb("lbsum", [128, 4])
                k.tt("dve", sm_.t[:], ex.t[:, 0:4], ex.t[:, 4:8], ALU.add, [ex.b], [sm_.b])
                k.P.op("dve", lambda e: e.reciprocal(out=sm_.t[:], in_=sm_.t[:]), reads=[sm_.b], writes=[sm_.b])
                k.tt("dve", lb.t[:], ex.t[:, 4:8], sm_.t[:], ALU.mult, [ex.b, sm_.b], [lb.b])
            k.ts("dve", oml.t[:], lb.t[:], -1.0, 1.0, ALU.mult, ALU.add, [lb.b], [oml.b])
            kT = [[p1.sb(f"hk{dr}{kg}", [128, NT], BF16) for kg in range(2)] for dr in range(2)]
            gT = [[Tn(k.XT.t[:, 4 + dr * 2 + kg, :], Buf(f"hg{dr}{kg}")) for kg in range(2)] for dr in range(2)]
            for dr in range(2):
                for kg in range(2):
                    wsrc_, wbuf_, cc = (wv, wb, 256 + kg * 128) if dr == 0 else (wv2, wb2, kg * 128)
                    idx = dr * 2 + kg

                    def ev(hb, bk, dr=dr, kg=kg, idx=idx):
                        sl = slice(hb * 512, (hb + 1) * 512)
                        f_ = p1.rot("hf", [128, 512], F32, 1)
                        k.act(f_.t[:], bk.t[:], AF.Sigmoid, [bk.b], [f_.b])
                        k.ts("dve", f_.t[:], f_.t[:], oml.t[:, idx:idx + 1], lb.t[:, idx:idx + 1], ALU.mult, ALU.add, [f_.b, oml.b, lb.b], [f_.b])
                        k.act(gT[dr][kg].t[:, sl], f_.t[:], AF.Ln, [f_.b], [gT[dr][kg].b])
                        k.ts("pool", kT[dr][kg].t[:, sl], f_.t[:], -1.0, 1.0, ALU.mult, ALU.add, [f_.b], [kT[dr][kg].b])
                    proj_fm(wsrc_, wbuf_, cc, 128, HN, ev)
            OACC = Tn(k.XT.t[0:64, 0:4, :], Buf("OACC"))
            gla_scan(p1, g, l, 2, 2, qT, 0.125, kT, gT, vtm, "hm_hg", "sm_hg", "sh", "nh", 64, OACC)
            gate = gate_proj(p1, l, IN_OFF["hg_g"], HN)
            head_norm(p1, OACC, l, 2, BR, gate)


    def lam_setup(ph, l):
        dl = ph.sb("dl", [128, 128])
        k.dma("sp", dl.t[:], k.din["dlam"][l:l + 1, :].partition_broadcast(128), "lddl", wr=[dl.b])
        pr = ph.sb("dlp", [128, 2, 32])
        k.tt("dve", pr.t[:, 0, :], dl.t[:, 0:32], dl.t[:, 32:64], ALU.mult, [dl.b], [pr.b])
        k.tt("dve", pr.t[:, 1, :], dl.t[:, 64:96], dl.t[:, 96:128], ALU.mult, [dl.b], [pr.b])
        sm_ = ph.sb("dls", [128, 2])
        k.P.op("dve", lambda e: e.reduce_sum(out=sm_.t[:], in_=pr.t[:], axis=mybir.AxisListType.X), reads=[pr.b], writes=[sm_.b])
        k.act(sm_.t[:], sm_.t[:], AF.Exp, [sm_.b], [sm_.b])
        lam = ph.sb("lam", [128, 2])
        lam_init = 0.8 - 0.6 * math.exp(-0.3 * l)
        k.stt("dve", lam.t[:, 0:1], sm_.t[:, 0:1], lam_init, sm_.t[:, 1:2], ALU.add, ALU.subtract, [sm_.b], [lam.b])
        k.ts("dve", lam.t[:, 1:2], lam.t[:, 0:1], -1.0, 0.0, ALU.mult, ALU.add, [lam.b], [lam.b])
        return lam, lam_init

    def mixer_df(ph, g, l, HN, BR):
        with k.phase() as p1:
            lam, lam_init = lam_setup(p1, l)
            c0 = IN_OFF["df_q"]
            wv, wb = win_tile(l, c0, 512)
            wv2, wb2 = win_tile(l, c0 + 512, 256)
            NK = 256 if g == 0 else 1536
            NKT = NK // 128
            koff = 0 if g == 0 else 512
            qT = [Tn(k.XT.t[:, 6 + cp, :], Buf(f"dq{cp}")) for cp in range(2)]
            kTf = [Tn(k.XT.t[:, 4 + cp, :], Buf(f"dkf{cp}")) for cp in range(2)]
            KT = [p1.sb(f"dK{cp}", [128, koff + NT], BF16) for cp in range(2)]
            for cp in range(2):
                proj_fm(wv, wb, cp * 128, 128, HN, evac_to(qT[cp].t, qT[cp].b))
                proj_fm(wv, wb, 256 + cp * 128, 128, HN, evac_to(kTf[cp].t, kTf[cp].b, eng="dve"))
            if DF_STOP == 1:
                return
            if g == 1:
                rope = p1.sb("rope", [128, 2048])
                k.dma("sp", rope.t[:], k.din["cst"][:, CST["cos"][0]:CST["cos"][0] + 2048], "ldrope", wr=[rope.b])
                for src in qT + kTf:
                    xb_ = p1.rot("ropexb", [128, NT], BF16, 1)
                    k.cp("pool", xb_.t[:], src.t[:], [src.b], [xb_.b])
                    for hb in range(2):
                        sl = slice(hb * 512, (hb + 1) * 512)
                        bk = k.bank()
                        k.mm(bk.t[:], CSB(k, "rot"), xb_.t[:, sl], True, True, [xb_.b, k.cstb.b], [bk.b])
                        t2 = p1.rot("ropet", [128, 512], F32, 1)
                        k.tt("dve", t2.t[:], bk.t[:], rope.t[:, 1024 + hb * 512:1024 + (hb + 1) * 512], ALU.mult, [bk.b, rope.b], [t2.b])
                        k.tt("pool", src.t[:, sl], src.t[:, sl], rope.t[:, sl], ALU.mult, [src.b, rope.b], [src.b])
                        k.tt("dve", src.t[:, sl], src.t[:, sl], t2.t[:], ALU.add, [src.b, t2.b], [src.b])
                for cp in range(2):
                    for kt in range(4):
                        stg = p1.rot("ckstg", [128, 2, 64], F32, 2)
                        k.dma("sp", stg.t[:], k.din["ck"][l, 2 * cp:2 * cp + 2, kt * 128:(kt + 1) * 128, :].rearrange("h t d -> t h d"), "ldck", wr=[stg.b])
                        bk = k.bank()
                        k.tr(bk.t[:, 0:128], stg.t[:].rearrange("p h d -> p (h d)"), CS(k, "ident"), [stg.b, k.cst.b], [bk.b])
                        k.cp("act", KT[cp].t[:, kt * 128:(kt + 1) * 128], bk.t[:, 0:128], [bk.b], [KT[cp].b])
            for cp in range(2):
                k.cp("pool", KT[cp].t[:, koff:koff + NT], kTf[cp].t[:], [kTf[cp].b], [KT[cp].b])
            if DF_STOP == 6:
                return
            NVT = (koff + NT) // 128
            Vtm = p1.sb("Vtm", [128, NVT, 256], BF16)
            if g == 1:
                for kt in range(4):
                    k.dma("pool", Vtm.t[:, kt, :].rearrange("p (h d) -> p h d", d=64), k.din["cv"][l, :, kt * 128:(kt + 1) * 128, :].rearrange("h t d -> t h d"), "ldcv", wr=[Vtm.b])
            for tl in range(8):
                def evv(bk, tl=tl):
                    if g == 0:
                        so = p1.rot("vout", [128, 256], F32, 2)
                        k.cp("dve", so.t[:], bk.t[:, 0:256], [bk.b], [so.b])
                        k.cp("act", Vtm.t[:, koff // 128 + tl, :], so.t[:], [so.b], [Vtm.b])
                        s_ = tl // 2; t0 = (tl % 2) * 128
                        k.dma("sp", k.dout["nv"][s_, l, :, t0:t0 + 128, :].rearrange("h t d -> t h d"), so.t[:].rearrange("p (h d) -> p h d", d=64), "stv", rd=[so.b])
                    else:
                        k.cp("act", Vtm.t[:, koff // 128 + tl, :], bk.t[:, 0:256], [bk.b], [Vtm.b])
                proj_tm(wv2, wb2, 0, 256, HN, tl * 128, 128, evv)
                if g == 0 and DF_STOP not in (7, 8):
                    def evk(bk, tl=tl):
                        so = p1.rot("kout", [128, 256], F32, 2)
                        k.cp("dve", so.t[:], bk.t[:, 0:256], [bk.b], [so.b])
                        s_ = tl // 2; t0 = (tl % 2) * 128
                        if True:
                            k.dma("sp", k.dout["nk"][s_, l, :, t0:t0 + 128, :].rearrange("h t d -> t h d"), so.t[:].rearrange("p (h d) -> p h d", d=64), "stk", rd=[so.b])
                    proj_tm(wv, wb, 256, 256, HN, tl * 128, 128, evk)
            if DF_STOP in (2, 5, 7, 8):
                return
            QX = [[p1.sb(f"QX{cp}{i}", [128, NT], BF16) for i in range(4)] for cp in range(2)]
            for cp in range(2):
                for i in range(4):
                    k.ts("dve" if i % 2 else "pool", QX[cp][i].t[:], qT[cp].t[:], CS(k, "qm", i, 1), 0.0, ALU.mult, ALU.add, [qT[cp].b, k.cst.b], [QX[cp][i].b])
            if DF_STOP == 3:
                return
            OACC = Tn(k.XT.t[0:64, 0:4, :], Buf("OACC"))
            scale = 32 ** -0.5
            bigb = [[k.banks[i * 3 + j].b for j in range(3)] for i in range(2)]
            nbk = (NK + 511) // 512

            def stageS(qb, h):
                qs = slice(qb * 128, (qb + 1) * 128)
                k0 = (qb // 2) * 256 if g == 0 else 0
                cp = h // 2; hh = h % 2
                Pm = []; rr = []
                for m in range(2):
                    big = k.big[m]; bb_ = bigb[m][0:nbk]
                    for sg_ in range(nbk):
                        n_ = min(512, NK - sg_ * 512)
                        k.mm(big[:, sg_ * 512:sg_ * 512 + n_], QX[cp][hh * 2 + m].t[:, qs], KT[cp].t[:, k0 + sg_ * 512:k0 + sg_ * 512 + n_], True, True,
                             [QX[cp][hh * 2 + m].b, KT[cp].b], bb_)
                    mx = p1.rot("mx", [128, 1], F32, 4)
                    k.P.op("dve", (lambda o_, i_: (lambda e: e.reduce_max(out=o_, in_=i_, axis=mybir.AxisListType.X)))(mx.t[:], big[:, 0:NK]), reads=bb_, writes=[mx.b])
                    k.ts("dve", mx.t[:], mx.t[:], -scale, 0.0, ALU.mult, ALU.add, [mx.b], [mx.b])
                    Pt = p1.rot("Pt", [128, NK], BF16, 2)
                    rs = p1.rot("rs", [128, 1], F32, 4)
                    k.act(Pt.t[:], big[:, 0:NK], AF.Exp, bb_ + [mx.b], [Pt.b, rs.b], bias=mx.t[:], scale=scale, accum=rs.t[:])
                    k.P.op("dve", (lambda o_: (lambda e: e.reciprocal(out=o_, in_=o_)))(rs.t[:]), reads=[rs.b], writes=[rs.b])
                    if m == 1:
                        k.tt("dve", rs.t[:], rs.t[:], lam.t[:, 1:2], ALU.mult, [rs.b, lam.b], [rs.b])
                    Pm.append(Pt); rr.append(rs)
                A = p1.rot("Amat", [128, NK], BF16, 2)
                k.ts("pool", A.t[:], Pm[0].t[:], rr[0].t[:], 0.0, ALU.mult, ALU.add, [Pm[0].b, rr[0].b], [A.b])
                k.stt("dve", A.t[:], Pm[1].t[:], rr[1].t[:], A.t[:], ALU.mult, ALU.add, [Pm[1].b, rr[1].b, A.b], [A.b])
                return A

            def stageT(qb, h, A):
                qs = slice(qb * 128, (qb + 1) * 128)
                k0 = (qb // 2) * 256 if g == 0 else 0
                ATm = p1.rot("ATm", [128, NKT, 128], BF16, 1)
                for t8 in range((NKT + 7) // 8):
                    bb = k.bbanks[k.bb_i % 2]; k.bb_i += 1
                    nt_ = min(8, NKT - t8 * 8)
                    for kt in range(nt_):
                        k.tr(bb.t[:, kt * 128:(kt + 1) * 128], A.t[:, (t8 * 8 + kt) * 128:(t8 * 8 + kt + 1) * 128], CSB(k, "ident"), [A.b, k.cstb.b], [bb.b])
                    k.cp("act", ATm.t[:, t8 * 8:t8 * 8 + nt_, :], bb.t[:, 0:nt_ * 128].rearrange("p (t q) -> p t q", q=128), [bb.b], [ATm.b])
                po = k.bank()
                vt0 = k0 // 128
                for kt in range(NKT):
                    k.mm(po.t[0:64, 0:128], Vtm.t[:, vt0 + kt, h * 64:(h + 1) * 64], ATm.t[:, kt, :], kt == 0, kt == NKT - 1, [Vtm.b, ATm.b], [po.b])
                k.cp("act", OACC.t[0:64, h, qs], po.t[0:64, 0:128], [po.b], [OACC.b])

            units = [(qb, h) for qb in range(8) for h in range(4)]
            Acur = stageS(*units[0])
            for u_ in range(len(units)):
                Anext = stageS(*units[u_ + 1]) if u_ + 1 < len(units) else None
                stageT(units[u_][0], units[u_][1], Acur)
                Acur = Anext
            head_norm(p1, OACC, l, 3, BR, None, 1.0 - lam_init)

    def merge(g, l, HN, BR):
        w = g
        with k.phase() as p1:
            MG = p1.sb("MG", [128, 8, NT], BF16)
            for q in range(4):
                acc = p1.rot("mgacc", [128, 2, NT], F32, 2)
                for n in range(4):
                    (wv, bv), wb = k.wget(("merge", l, n, q))
                    for k2 in range(2):
                        kc = q * 2 + k2
                        for hb in range(2):
                            sl = slice(hb * 512, (hb + 1) * 512)
                            bg = k.bank(); bu = k.bank()
                            for kk in range(8):
                                k.mm(bg.t[:], wv[:, kk, k2 * 128:(k2 + 1) * 128], HN.t[:, kk, sl], kk == 0, kk == 7, [wb, HN.b], [bg.b])
                            for h in range(4):
                                k.mm(bu.t[:], bv[:, h, k2 * 128:(k2 + 1) * 128], BR.t[0:64, n, h, sl], h == 0, h == 3, [wb, BR.b], [bu.b])
                            sg = p1.rot("msg", [128, 512], F32, 3)
                            k.act(sg.t[:], bg.t[:], AF.Sigmoid, [bg.b], [sg.b])
                            if n == 0:
                                k.tt("dve", acc.t[:, k2, sl], sg.t[:], bu.t[:], ALU.mult, [sg.b, bu.b], [acc.b])
                            else:
                                k.tt("dve", sg.t[:], sg.t[:], bu.t[:], ALU.mult, [sg.b, bu.b], [sg.b])
                                if n < 3:
                                    k.tt("pool", acc.t[:, k2, sl], acc.t[:, k2, sl], sg.t[:], ALU.add, [acc.b, sg.b], [acc.b])
                                else:
                                    k.tt("pool", MG.t[:, kc, sl], acc.t[:, k2, sl], sg.t[:], ALU.add, [acc.b, sg.b], [MG.b])
            for t2 in range(2):
                wv, wb = k.wget(("mat", "w_out", (l,), 8, ((t2 * 512, 512),)))
                for f4 in range(4):
                    fo = t2 * 4 + f4
                    for hb in range(2):
                        sl = slice(hb * 512, (hb + 1) * 512)
                        bk = k.bank()
                        for kk in range(8):
                            k.mm(bk.t[:], wv[:, kk, f4 * 128:(f4 + 1) * 128], MG.t[:, kk, sl], kk == 0, kk == 7, [wb, MG.b], [bk.b])
                        xs_ = XT.t[:, fo, sl]
                        k.stt("dve", xs_, bk.t[:], k.der.t[:, l, w, 7, fo:fo + 1], xs_, ALU.mult, ALU.add, [bk.b, k.der.b, k.xtb[fo]], [k.xtb[fo]])

    def layer(g, l, mixers=("A", "B", "C", "D")):
        w = g
        ffn(l, w, 0, "ffn1_in", "ffn1_down")
        with k.phase() as ph:
            HN = ph.sb("HN", [128, 8, NT], BF16)
            BR = ph.sb("BR", [64, 4, 4, NT], BF16)
            norm_mod(ph, l, w, 1, HN)
            k.dma("sp", k.xscr[:, :], XT.t[:].rearrange("p c t -> p (c t)"), "spill", rd=k.xtb)
            k.P.barrier()
            if "A" in mixers:
                mixer_gla(ph, g, l, HN, BR)
            else:
                k.memset("pool", BR.t[:, 0], 0.0, [BR.b])
            if "B" in mixers:
                k.mixer_dn(ph, g, l, HN, BR)
            else:
                k.memset("pool", BR.t[:, 1], 0.0, [BR.b])
            if "C" in mixers:
                mixer_hg(ph, g, l, HN, BR)
            else:
                k.memset("pool", BR.t[:, 2], 0.0, [BR.b])
            if "D" in mixers:
                mixer_df(ph, g, l, HN, BR)
            else:
                k.memset("pool", BR.t[:, 3], 0.0, [BR.b])
            for n_, nm_ in enumerate("ABCD"):
                k.tap(f"br{nm_}{g}{l}", BR.t[0:64, n_].rearrange("p h t -> p (h t)"), [64, 4 * NT], [BR.b])
            k.P.barrier()
            k.P.dma("sp", lambda e: e.dma_start(out=XT.t[:].rearrange("p c t -> p (c t)"), in_=k.xscr[:, :]), "d_xt0", writes=k.xtb)
            merge(g, l, HN, BR)
        k.tap(f"xm{g}{l}", XT.t[:].rearrange("p c t -> p (c t)"), [128, 8 * NT], k.xtb)
        ffn(l, w, 2, "ffn2_in", "ffn2_down")

    k.layer = layer
    k.mixer_df = mixer_df
    k.merge = merge

    def mixer_dn(ph, g, l, HN, BR):
        nseq = 4 if g == 0 else 1
        cps = NCH // nseq
        T = NT // nseq
        with k.phase() as p1:
            c0 = IN_OFF["dn_x"]
            wv, wb = win_tile(l, c0, 512)
            wv2, wb2 = win_tile(l, c0 + 512, 272)
            nmc = p1.sb("nmc", [64, 1024], BF16)
            k.dma("pool", nmc.t[:], k.din["cst"][0:64, CST["nm_c"][0]:CST["nm_c"][0] + 1024], "ldnm", wr=[nmc.b])
            QT = [p1.sb(f"nQ{p}", [128, NT], BF16) for p in range(2)]
            KT = [p1.sb(f"nK{p}", [128, NT], BF16) for p in range(2)]
            Ktm = p1.sb("Ktm", [64, NCH, 256], BF16); Vtm = p1.sb("nVtm", [64, NCH, 256], BF16)
            with k.phase() as p2:
                VT = [p2.sb(f"nV{p}", [128, NT], BF16) for p in range(2)]
                for c in range(6):
                    src_w, src_b, cc = (wv, wb, c * 128) if c < 4 else (wv2, wb2, (c - 4) * 128)
                    x = p2.rot("cx", [128, NT], F32, 2)
                    proj_fm(src_w, src_b, cc, 128, HN, evac_to(x.t, x.b))
                    y = p2.rot("cy", [128, NT], F32, 2)
                    wrow = lambda tap: k.pt.t[:, ROW["dn_conv"] + (l * 3 + tap) * 6 + c:ROW["dn_conv"] + (l * 3 + tap) * 6 + c + 1]
                    k.ts("dve", y.t[:], x.t[:], wrow(1), 0.0, ALU.mult, ALU.add, [x.b, k.pt.b], [y.b])
                    x3 = x.t[:].rearrange("p (s t) -> p s t", t=T); y3 = y.t[:].rearrange("p (s t) -> p s t", t=T)
                    k.stt("dve", y3[:, :, 1:T], x3[:, :, 0:T - 1], wrow(0), y3[:, :, 1:T], ALU.mult, ALU.add, [x.b, y.b, k.pt.b], [y.b])
                    k.stt("dve", y3[:, :, 0:T - 1], x3[:, :, 1:T], wrow(2), y3[:, :, 0:T - 1], ALU.mult, ALU.add, [x.b, y.b, k.pt.b], [y.b])
                    if c >= 4:
                        k.act(VT[c - 4].t[:], y.t[:], AF.Silu, [y.b], [VT[c - 4].b])
                    else:
                        k.act(y.t[:], y.t[:], AF.Silu, [y.b], [y.b])
                        rn = k.rstd_of(p2, lambda c_, hb: y.t[:, hb * 512:(hb + 1) * 512], 1, 128, CSB(k, "bd64"), 1.0, f"rn{c % 2}", [y.b])
                        dst = QT[c] if c < 2 else KT[c - 2]
                        k.stt("dve", dst.t[:], y.t[:], 0.125 if c < 2 else 1.0, rn.t[:], ALU.mult, ALU.mult, [y.b, rn.b], [dst.b])
                for ch in range(NCH):
                    tok = slice(ch * 64, (ch + 1) * 64)
                    bb = k.bbanks[k.bb_i % 2]; k.bb_i += 1
                    for p in range(2):
                        k.tr(bb.t[0:64, p * 128:(p + 1) * 128], KT[p].t[:, tok], CSB(k, "ident"), [KT[p].b, k.cstb.b], [bb.b])
                        k.tr(bb.t[0:64, 256 + p * 128:256 + (p + 1) * 128], VT[p].t[:, tok], CSB(k, "ident"), [VT[p].b, k.cstb.b], [bb.b])
                    k.cp("act", Ktm.t[:, ch, :], bb.t[0:64, 0:256], [bb.b], [Ktm.b])
                    k.cp("act", Vtm.t[:, ch, :], bb.t[0:64, 256:512], [bb.b], [Vtm.b])

            if DN_STOP == 2:
                return
            zz = p1.sb("zz", [64, NCH, 16])
            for ch in range(NCH):
                proj_tm(wv2, wb2, 256, 16, HN, ch * 64, 64, (lambda ch_: (lambda bk: k.cp("dve", zz.t[:, ch_, :], bk.t[0:64, 0:16], [bk.b], [zz.b])))(ch))
            par = p1.sb("dnpar", [64, 16])
            k.dma("sp", par.t[:, 0:8], k.din["dn_al"][l:l + 1, :].partition_broadcast(64), "ldpar", wr=[par.b])
            k.dma("sp", par.t[:, 8:16], k.din["dn_dt"][l:l + 1, :].partition_broadcast(64), "ldpar", wr=[par.b])
            k.act(par.t[:, 0:8], par.t[:, 0:8], AF.Exp, [par.b], [par.b])
            k.ts("dve", par.t[:, 0:8], par.t[:, 0:8], -1.0, 0.0, ALU.mult, ALU.add, [par.b], [par.b])
            bcol = p1.sb("bcol", [64, NCH, 8]); lnb = p1.sb("lnb", [64, NCH, 8]); gcol = p1.sb("gcol", [64, NCH, 8])
            k.act(bcol.t[:], zz.t[:, :, 0:8], AF.Sigmoid, [zz.b], [bcol.b])
            k.act(lnb.t[:], bcol.t[:], AF.Ln, [bcol.b], [lnb.b])
            k.tt("dve", gcol.t[:], zz.t[:, :, 8:16], par.t[:, 8:16].unsqueeze(1).to_broadcast([64, NCH, 8]), ALU.add, [zz.b, par.b], [gcol.b])
            k.act(gcol.t[:], gcol.t[:], AF.Exp, [gcol.b], [gcol.b])
            k.act(gcol.t[:], gcol.t[:], AF.Ln, [gcol.b], [gcol.b], bias=1.0)
            k.tt("dve", gcol.t[:], gcol.t[:], par.t[:, 0:8].unsqueeze(1).to_broadcast([64, NCH, 8]), ALU.mult, [gcol.b, par.b], [gcol.b])
            dcol = p1.sb("dcol", [64, NCH, 8]); dlast = p1.sb("dlast", [128, NCH, 8])
            bk = k.bank()
            for ch in range(NCH):
                for dr in range(2):
                    k.mm(bk.t[0:64, ch * 8 + dr * 4:ch * 8 + dr * 4 + 4], CS(k, "mf" if dr == 0 else "mb")[0:64, 0:64], gcol.t[:, ch, dr * 4:dr * 4 + 4], True, True, [gcol.b, k.cst.b], [bk.b])
            k.cp("dve", dcol.t[:].rearrange("p c h -> p (c h)"), bk.t[0:64, 0:NCH * 8], [bk.b], [dcol.b])
            bk = k.bank()
            for ch in range(NCH):
                for dr in range(2):
                    k.mm(bk.t[:, ch * 8 + dr * 4:ch * 8 + dr * 4 + 4], CS(k, "sel_f" if dr == 0 else "sel_b")[0:64, :], dcol.t[:, ch, dr * 4:dr * 4 + 4], True, True, [dcol.b, k.cst.b], [bk.b])
            k.cp("dve", dlast.t[:].rearrange("p c h -> p (c h)"), bk.t[:, 0:NCH * 8], [bk.b], [dlast.b])
            acol = p1.sb("acol", [64, NCH, 8]); expd = p1.sb("expd", [64, NCH, 8]); nexpd = p1.sb("nexpd", [64, NCH, 8])
            edl = p1.sb("edl", [64, NCH, 8]); edlast = p1.sb("edlast", [128, NCH, 8])
            k.tt("dve", acol.t[:], dcol.t[:], lnb.t[:], ALU.add, [dcol.b, lnb.b], [acol.b])
            k.act(expd.t[:], dcol.t[:], AF.Exp, [dcol.b], [expd.b])
            k.ts("dve", nexpd.t[:], expd.t[:], -1.0, 0.0, ALU.mult, ALU.add, [expd.b], [nexpd.b])
            k.tt("dve", edl.t[:], dlast.t[0:64], dcol.t[:], ALU.subtract, [dlast.b, dcol.b], [edl.b])
            k.act(edl.t[:], edl.t[:], AF.Exp, [edl.b], [edl.b])
            k.act(edlast.t[:], dlast.t[:], AF.Exp, [dlast.b], [edlast.b])
            if DN_STOP == 3:
                return
            OACC = Tn(k.XT.t[0:64, 0:4, :], Buf("OACC"))
            bc8 = lambda t_, ch_, lo, n_: t_.t[:, ch_, lo:lo + n_].unsqueeze(2).to_broadcast([64, n_, 64])
            v3 = lambda ap_, n_: ap_.rearrange("p (h i) -> p h i", i=64)
            with k.phase() as p2:
                TbT = p2.sb("TbT", [64, NCH, 256], BF16); Aqk = p2.sb("Aqk", [64, NCH, 256], BF16)
                TbTb = [Buf(f"TbT{c_}") for c_ in range(NCH)]; Aqkb = [Buf(f"Aqk{c_}") for c_ in range(NCH)]
                NG = 3
                atile = lambda i_: Tn(k.XT.t[0:64, 4 + i_ // 4, (i_ % 4) * 256:(i_ % 4 + 1) * 256], Buf(f"dnal{i_}"))
                slots = []
                for si in range(NG):
                    if si == 0:
                        tl_ = [p2.sb(f"tp{j_}", [64, 256], F32) for j_ in range(8)]
                    else:
                        tl_ = [atile((si - 1) * 8 + j_) for j_ in range(8)]
                    slots.append({"ec": tl_[0], "eb": tl_[1], "XN": tl_[2:4], "XT": tl_[4:6], "PM": tl_[6:8], "dg": tl_[7],
                                  "KTm": p2.sb(f"KTm{si}", [128, 256], BF16)})
                hsl = lambda h: slice(h * 64, (h + 1) * 64)

                def tprep(ch, B, dr):
                    h0 = dr * 4
                    tok = slice(ch * 64, (ch + 1) * 64)
                    dg = B["dg"]; ec = B["ec"]; eb = B["eb"]; KTm = B["KTm"]
                    k.tt("pool", v3(dg.t[:], 4), v3(CS(k, "id8")[0:64, 0:256], 4), bc8(dcol, ch, h0, 4), ALU.mult, [dcol.b, k.cst.b], [dg.b])
                    for h in range(4):
                        k.ts("pool" if h % 2 else "dve", KTm.t[:, h * 64:(h + 1) * 64], KT[h // 2].t[:, tok], CS(k, "hm_hg", h, 1), 0.0, ALU.mult, ALU.add,
                             [KT[h // 2].b, k.cst.b], [KTm.b])
                    yield
                    rb = k.bank()
                    k.mm(rb.t[0:64, 0:256], CS(k, "ones")[0:64, 0:64], dg.t[:], True, True, [dg.b, k.cst.b], [rb.b])
                    yield
                    k.tt("dve", v3(ec.t[:], 4), v3(rb.t[0:64, 0:256], 4), bc8(dcol, ch, h0, 4), ALU.subtract, [rb.b, dcol.b], [ec.b])
                    k.tt("dve", ec.t[:], ec.t[:], nmc.t[:, dr * 256:(dr + 1) * 256], ALU.min, [ec.b, nmc.b], [ec.b])
                    k.stt("dve", v3(eb.t[:], 4), v3(rb.t[0:64, 0:256], 4), -1.0, bc8(acol, ch, h0, 4), ALU.mult, ALU.add, [rb.b, acol.b], [eb.b])
                    k.tt("dve", eb.t[:], eb.t[:], nmc.t[:, 512 + dr * 256:512 + (dr + 1) * 256], ALU.min, [eb.b, nmc.b], [eb.b])
                    gk = k.bank(); gq = k.bank()
                    for h in range(4):
                        p = h // 2
                        k.mm(gk.t[0:64, hsl(h)], KTm.t[:, hsl(h)], KT[p].t[:, tok], True, True, [KT[p].b, KTm.b], [gk.b])
                        k.mm(gq.t[0:64, hsl(h)], KTm.t[:, hsl(h)], QT[p].t[:, tok], True, True, [KTm.b, QT[p].b], [gq.b])
                    yield
                    k.act(ec.t[:], ec.t[:], AF.Exp, [ec.b], [ec.b])
                    k.act(eb.t[:], eb.t[:], AF.Exp, [eb.b], [eb.b])
                    yield
                    XN = B["XN"][0]; XTt = B["XT"][0]; Pm = B["PM"][0]
                    k.stt("dve", XN.t[:], gk.t[0:64, 0:256], -1.0, eb.t[:], ALU.mult, ALU.mult, [gk.b, eb.b], [XN.b])
                    k.tt("dve", Aqk.t[:, ch, :], gq.t[0:64, 0:256], ec.t[:], ALU.mult, [gq.b, ec.b], [Aqkb[ch]])
                    yield
                    bt = k.bank()
                    for h in range(4):
                        k.tr(bt.t[0:64, hsl(h)], XN.t[:, hsl(h)], CS(k, "ident")[0:64, 0:64], [XN.b, k.cst.b], [bt.b])
                    yield
                    k.cp("act", XTt.t[:], bt.t[0:64, 0:256], [bt.b], [XTt.b])
                    yield
                    k.tt("pool", Pm.t[:], XTt.t[:], CS(k, "id8")[0:64, 0:256], ALU.add, [XTt.b, k.cst.b], [Pm.b])
                    for lev in range(5):
                        b1 = None
                        if lev < 4:
                            b1 = k.bank()
                            for h in range(4):
                                k.mm(b1.t[0:64, hsl(h)], XN.t[:, hsl(h)], XTt.t[:, hsl(h)], True, True, [XN.b, XTt.b], [b1.b])
                        b2 = k.bank()
                        for h in range(4):
                            k.mm(b2.t[0:64, hsl(h)], XTt.t[:, hsl(h)], XN.t[:, hsl(h)], True, True, [XN.b, XTt.b], [b2.b])
                        yield
                        XN2 = B["XN"][(lev + 1) % 2]
                        k.cp("act", XN2.t[:], b2.t[0:64, 0:256], [b2.b], [XN2.b])
                        if lev < 4:
                            XT2 = B["XT"][(lev + 1) % 2]
                            k.cp("act", XT2.t[:], b1.t[0:64, 0:256], [b1.b], [XT2.b])
                            XTt = XT2
                        XN = XN2
                        yield
                        b3 = k.bank()
                        for h in range(4):
                            k.mm(b3.t[0:64, hsl(h)], XN.t[:, hsl(h)], Pm.t[:, hsl(h)], True, True, [XN.b, Pm.b], [b3.b])
                        yield
                        Pn = B["PM"][(lev + 1) % 2]
                        k.tt("dve", Pn.t[:], Pm.t[:], b3.t[0:64, 0:256], ALU.add, [Pm.b, b3.b], [Pn.b])
                        Pm = Pn
                        yield
                    k.tt("dve", v3(TbT.t[:, ch, :], 4), v3(Pm.t[:], 4), bc8(bcol, ch, h0, 4), ALU.mult, [Pm.b, bcol.b], [TbTb[ch]])


                S32 = p2.sb("S32", [128, 256])

                def rec(dr):
                    h0 = dr * 4
                    for s_ in range(nseq):
                        k.memset("pool", S32.t[:], 0.0, [S32.b])
                        if g == 1:
                            for h in range(4):
                                k.dma("sp", S32.t[(h % 2) * 64:(h % 2 + 1) * 64, h * 64:(h + 1) * 64], k.din["sd"][l, dr, h], "ldS", wr=[S32.b])
                        order = range(cps) if dr == 0 else range(cps - 1, -1, -1)
                        for ci in order:
                            ch = s_ * cps + ci
                            tok = slice(ch * 64, (ch + 1) * 64)
                            Sbf = p2.rot("Sbf", [128, 256], BF16, 2)
                            k.tt("pool", Sbf.t[:], S32.t[:], CS(k, "sm_hg"), ALU.mult, [S32.b, k.cst.b], [Sbf.b])
                            yield None
                            pk = k.bank(); pq = k.bank()
                            for h in range(4):
                                p = h // 2
                                k.mm(pk.t[0:64, hsl(h)], KT[p].t[:, tok], Sbf.t[:, hsl(h)], True, True, [KT[p].b, Sbf.b], [pk.b])
                                k.mm(pq.t[0:64, hsl(h)], QT[p].t[:, tok], Sbf.t[:, hsl(h)], True, True, [QT[p].b, Sbf.b], [pq.b])
                            yield None
                            Y = p2.rot("Y", [64, 256], F32, 1); Yb = p2.rot("Yb", [64, 256], BF16, 2); ot = p2.rot("ot", [64, 256], F32, 1)
                            k.tt("dve", v3(Y.t[:], 4), v3(pk.t[0:64, 0:256], 4), bc8(nexpd, ch, h0, 4), ALU.mult, [pk.b, nexpd.b], [Y.b])
                            k.tt("dve", Yb.t[:], Y.t[:], Vtm.t[:, ch, :], ALU.add, [Y.b, Vtm.b], [Yb.b])
                            k.tt("dve", v3(ot.t[:], 4), v3(pq.t[0:64, 0:256], 4), bc8(expd, ch, h0, 4), ALU.mult, [pq.b, expd.b], [ot.b])
                            pv = k.bank()
                            for h in range(4):
                                k.mm(pv.t[0:64, hsl(h)], TbT.t[:, ch, hsl(h)], Yb.t[:, hsl(h)], True, True, [TbTb[ch], Yb.b], [pv.b])
                            yield None
                            VN = p2.rot("VN", [64, 256], BF16, 2); VNs = p2.rot("VNs", [64, 256], BF16, 2)
                            k.cp("dve", VN.t[:], pv.t[0:64, 0:256], [pv.b], [VN.b])
                            k.tt("dve", v3(VNs.t[:], 4), v3(pv.t[0:64, 0:256], 4), bc8(edl, ch, h0, 4), ALU.mult, [pv.b, edl.b], [VNs.b])
                            pa = k.bank(); pS = k.bank()
                            for h in range(4):
                                k.mm(pa.t[0:64, hsl(h)], Aqk.t[:, ch, hsl(h)], VN.t[:, hsl(h)], True, True, [Aqkb[ch], VN.b], [pa.b])
                            for p in range(2):
                                k.mm(pS.t[:, p * 128:(p + 1) * 128], Ktm.t[:, ch, p * 128:(p + 1) * 128], VNs.t[:, p * 128:(p + 1) * 128], True, True, [Ktm.b, VNs.b], [pS.b])
                            yield None
                            k.tt("dve", ot.t[:], ot.t[:], pa.t[0:64, 0:256], ALU.add, [ot.b, pa.b], [ot.b])
                            for h in range(4):
                                k.stt("dve", S32.t[:, hsl(h)], S32.t[:, hsl(h)], edlast.t[:, ch, h0 + h:h0 + h + 1], pS.t[:, hsl(h)], ALU.mult, ALU.add,
                                      [S32.b, edlast.b, pS.b], [S32.b])
                            pt_ = k.bank()
                            for h in range(4):
                                k.tr(pt_.t[0:64, hsl(h)], ot.t[:, hsl(h)], CS(k, "ident")[0:64, 0:64], [ot.b, k.cst.b], [pt_.b])
                            yield None
                            oview = OACC.t[0:64, :, tok]
                            if dr == 0:
                                k.cp("act", oview, v3(pt_.t[0:64, 0:256], 4), [pt_.b], [OACC.b])
                            else:
                                k.tt("dve", oview, oview, v3(pt_.t[0:64, 0:256], 4), ALU.add, [pt_.b, OACC.b], [OACC.b])
                            yield ch
                        if g == 0:
                            for h in range(4):
                                k.dma("sp", k.dout["nd"][s_, l, dr, h], S32.t[(h % 2) * 64:(h % 2 + 1) * 64, h * 64:(h + 1) * 64], "stS", rd=[S32.b])

                def lockstep(pairs):
                    while pairs:
                        nxt = []
                        for g_ in pairs:
                            try:
                                next(g_)
                                nxt.append(g_)
                            except StopIteration:
                                pass
                        pairs = nxt

                for c0_ in range(0, NCH, NG):
                    lockstep([tprep(ch, slots[i_], 0) for i_, ch in enumerate(range(c0_, min(NCH, c0_ + NG)))])
                rg = rec(0)
                avail = []; active = []; free = [0, 1]; rec_done = False
                while (not rec_done) or avail or active:
                    if not rec_done:
                        try:
                            r_ = next(rg)
                            if r_ is not None:
                                avail.append(r_)
                        except StopIteration:
                            rec_done = True
                            free.append(2)
                    while avail and free:
                        active.append((tprep(avail.pop(0), slots[free[0]], 1), free.pop(0)))
                    nxt = []
                    for gen_, sl_ in active:
                        try:
                            next(gen_)
                            nxt.append((gen_, sl_))
                        except StopIteration:
                            free.append(sl_)
                    active = nxt
                for _ in rec(1):
                    pass
            gate = gate_proj(p1, l, IN_OFF["dn_g"], HN)
            head_norm(p1, OACC, l, 1, BR, gate)

    k.mixer_dn = mixer_dn
    k.mixer_gla = mixer_gla
    k.mixer_hg = mixer_hg
    k.gla_scan = gla_scan
    k.head_norm = head_norm
    k.gate_proj = gate_proj
    k.proj_fm = proj_fm
    k.proj_tm = proj_tm
    k.win_tile = win_tile
    k.evac_to = evac_to
    k.ffn = ffn
    k.xload = xload
    k.final = final
    k.rstd_of = rstd_of
    k.norm_mod = norm_mod
    return k


def finish(k):
    k.P.final_wait("sp")
    keys = list(ENGS) + list(k.P.dma_cnt.keys())
    sems = {key: k.es.enter_context(k.nc.semaphore("s_" + key)) for key in keys}
    with k.nc.Block() as block:
        replay(k.P, block, sems)
    k.es.close()
    return k.nc


def host_inputs(inp, core):
    b = core // 4
    f = lambda a: np.ascontiguousarray(np.asarray(a, np.float32))
    m = {
        "xp": f(inp["x_prompt"][core * 4:(core + 1) * 4].reshape(NT, D)),
        "xs": f(inp["x_sample"][b]),
        "ck": f(inp["cache_diff_k"][b]), "cv": f(inp["cache_diff_v"][b]),
        "sg": f(inp["state_gla"][b]), "sd": f(inp["state_dn"][b]), "sh": f(inp["state_hgrn"][b]),
        "pvec": pack_pvec(inp, b), "cst": make_consts(),
        "dn_al": f(inp["dn_a_log"].reshape(DEPTH, 8)), "dn_dt": f(inp["dn_dt_bias"].reshape(DEPTH, 8)),
        "dlam": f(inp["diff_lambda"].reshape(DEPTH, 128)), "gla_w2": f(inp["gla_w2"]),
    }
    for n_ in ("w_mod", "ffn1_in", "ffn2_in", "ffn1_down", "ffn2_down", "w_in", "w_branch", "w_mgate", "w_out"):
        m[n_] = f(inp[n_])
    return m


def program(k, mixers=("A", "B", "C", "D")):
    for g in range(2):
        k.xload(g)
        for l in range(DEPTH):
            k.layer(g, l, mixers)
        k.final(g)


_CACHE = {}


def get_nc():
    if "nc" not in _CACHE:
        k1 = build(None)
        program(k1)
        k2 = build(list(k1.wrec))
        program(k2)
        _CACHE["nc"] = finish(k2)
    return _CACHE["nc"]


def kernel(**inp):
    inp = {n: np.asarray(v) for n, v in inp.items()}
    nc = get_nc()
    in_maps = [host_inputs(inp, c) for c in range(8)]
    res = run_bass_kernel_spmd(nc, in_maps, core_ids=list(range(8)))
    R = res.results
    y_prompt = np.concatenate([R[c]["yp"].reshape(4, 256, D) for c in range(8)], axis=0)
    y_sample = np.stack([R[0]["ys"], R[4]["ys"]], axis=0)
    cat = lambda n_: np.concatenate([R[c][n_] for c in range(8)], axis=0)
    return (y_prompt.astype(np.float32), y_sample.astype(np.float32), cat("nk"), cat("nv"), cat("ng"), cat("nd"), cat("nh"))
```
